# Optimizing a Trainium2 kernel written in Bass

```python
import jax, jax.numpy as jnp
from jax import lax
import numpy as np

D_MODEL = 2048
BATCH = 2
SEQ = 16384
DEPTH = 2

CTX_LEN = 256
GRID_W = 64
N_MOD = 9
D_FF = 5632
FNET_GROUPS = 8
FNET_GROUP_DIM = D_MODEL // FNET_GROUPS
RWKV_HEAD = 64
RWKV_HEADS = D_MODEL // RWKV_HEAD
DECAY_LORA = 96
AAA_LORA = 96
GATE_LORA = 256
N_DIR = 2
NORM_EPS = 1e-6
GN_EPS = 64e-5

kernel_name = 'hybrid_fnet_rwkv7_dit_trunk'


def rms_norm(x, g):
    x32 = x.astype(jnp.float32)
    y = x32 * lax.rsqrt(jnp.mean(x32 * x32, axis=-1, keepdims=True) + NORM_EPS)
    return (y * g.astype(jnp.float32)).astype(x.dtype)


def modulate(x, g, shift, scale):
    return rms_norm(x, g) * (1 + scale) + shift


def swiglu(h, w13, w2):
    gate, up = jnp.split(h @ w13, 2, axis=-1)
    return (jax.nn.silu(gate) * up) @ w2


def fourier_mix(h, w_o, b_o):
    B_, L, D = h.shape
    hg = h.astype(jnp.float32).reshape(B_, L, FNET_GROUPS, FNET_GROUP_DIM)
    f = jnp.fft.fftn(hg, axes=(1, 3), norm='ortho').real
    return f.reshape(B_, L, D).astype(h.dtype) @ w_o + b_o


def grid_shift(h):
    B_, L, D = h.shape
    rows = L // GRID_W
    q = D // 4
    g = h.reshape(B_, rows, GRID_W, D)
    left = jnp.pad(g[:, :, :-1, :q], ((0, 0), (0, 0), (1, 0), (0, 0)))
    right = jnp.pad(g[:, :, 1:, q:2 * q], ((0, 0), (0, 0), (0, 1), (0, 0)))
    up = jnp.pad(g[:, :-1, :, 2 * q:3 * q], ((0, 0), (1, 0), (0, 0), (0, 0)))
    down = jnp.pad(g[:, 1:, :, 3 * q:], ((0, 0), (0, 1), (0, 0), (0, 0)))
    return jnp.concatenate([left, right, up, down], axis=-1).reshape(B_, L, D)


def seq_shift(h):
    q = h.shape[-1] // 4
    prev = jnp.pad(h[:, :-1], ((0, 0), (1, 0), (0, 0)))
    nxt = jnp.pad(h[:, 1:], ((0, 0), (0, 1), (0, 0)))
    return jnp.concatenate([prev[..., :q], nxt[..., q:2 * q], prev[..., 2 * q:3 * q], nxt[..., 3 * q:]], axis=-1)


def rwkv_project(h, hs, mu, w_rkv, w0, w1, w2, a0, a1, a2, g1, g2, k_k, k_a):
    B_, L, D = h.shape
    hm = h[None] + (hs - h)[None] * mu[:, None, None, :]
    r, k, v = jnp.einsum('pbld,pde->pble', hm[:3], w_rkv)
    lora_w = jnp.einsum('zblr,zrd->zbld', jnp.tanh(jnp.einsum('bld,zdr->zblr', hm[3], w1)), w2)
    log_w = -jax.nn.softplus(-(w0[:, None, None, :] + lora_w)) - 0.5
    decay = jnp.exp(-jnp.exp(log_w))
    a = jax.nn.sigmoid(a0[:, None, None, :] + jnp.einsum('zblr,zrd->zbld', jnp.einsum('bld,zdr->zblr', hm[4], a1), a2))
    g = jax.nn.sigmoid(hm[5] @ g1) @ g2
    kk32 = (k * k_k).astype(jnp.float32).reshape(B_, L, RWKV_HEADS, RWKV_HEAD)
    kk32 = kk32 * lax.rsqrt(jnp.maximum(jnp.sum(kk32 * kk32, axis=-1, keepdims=True), 1e-24))
    kk = kk32.reshape(B_, L, D).astype(h.dtype)
    k_dir = k[None] * (1 + (a - 1) * k_a)
    b_dir = kk[None] * a
    return r, v, g, kk, decay, k_dir, b_dir


def wkv_step(S, inp):
    r, w, k, v, a, b = inp
    sa = jnp.einsum('bhij,bhj->bhi', S, a)
    S = S * w[:, :, None, :] + sa[..., None] * b[:, :, None, :] + v[..., None] * k[:, :, None, :]
    return S, jnp.einsum('bhij,bhj->bhi', S, r)


def wkv_scan(S0, reverse, r, w, k, v, a, b):
    B_, L, _ = r.shape
    seq = tuple(jnp.moveaxis(t.astype(jnp.float32).reshape(B_, L, RWKV_HEADS, RWKV_HEAD), 1, 0)
                for t in (r, w, k, v, a, b))
    S, ys = lax.scan(wkv_step, S0, seq, reverse=reverse)
    return S, jnp.moveaxis(ys, 0, 1)


def rwkv_output(y, r, k_bonus, v, g, r_k, ln_w, ln_b, w_o):
    B_, L = y.shape[:2]
    mean = jnp.mean(y, axis=-1, keepdims=True)
    var = jnp.mean(jnp.square(y - mean), axis=-1, keepdims=True)
    yn = (y - mean) * lax.rsqrt(var + GN_EPS)
    yn = yn * ln_w.reshape(RWKV_HEADS, RWKV_HEAD).astype(jnp.float32) + ln_b.reshape(RWKV_HEADS, RWKV_HEAD).astype(jnp.float32)
    rh = r.reshape(B_, L, RWKV_HEADS, RWKV_HEAD)
    kh = k_bonus.reshape(B_, L, RWKV_HEADS, RWKV_HEAD)
    vh = v.reshape(B_, L, RWKV_HEADS, RWKV_HEAD)
    bonus = jnp.sum(rh * kh * r_k, axis=-1, keepdims=True) * vh
    o = (yn.astype(r.dtype) + bonus).reshape(B_, L, D_MODEL) * g
    return o @ w_o


def rwkv_mix(h_ctx, h_lat, need_ctx_out, mu, w_rkv, w0, w1, w2, a0, a1, a2, g1, g2, k_k, k_a, r_k, ln_w, ln_b, w_o):
    p = (mu, w_rkv, w0, w1, w2, a0, a1, a2, g1, g2, k_k, k_a)
    r_c, v_c, g_c, kk_c, w_c, k_c, b_c = rwkv_project(h_ctx, seq_shift(h_ctx), *p)
    r_l, v_l, g_l, kk_l, w_l, k_l, b_l = rwkv_project(h_lat, grid_shift(h_lat), *p)
    B_ = h_lat.shape[0]
    S0 = jnp.zeros((B_, RWKV_HEADS, RWKV_HEAD, RWKV_HEAD), jnp.float32)
    S_f, yf_c = wkv_scan(S0, False, r_c, w_c[0], k_c[0], v_c, -kk_c, b_c[0])
    S_b, yb_c = wkv_scan(S0, True, r_c, w_c[1], k_c[1], v_c, -kk_c, b_c[1])
    _, yf_l = wkv_scan(S_f, False, r_l, w_l[0], k_l[0], v_l, -kk_l, b_l[0])
    _, yb_l = wkv_scan(S_b, True, r_l, w_l[1], k_l[1], v_l, -kk_l, b_l[1])
    o_lat = rwkv_output(yf_l + yb_l, r_l, 0.5 * (k_l[0] + k_l[1]), v_l, g_l, r_k, ln_w, ln_b, w_o)
    o_ctx = None
    if need_ctx_out:
        o_ctx = rwkv_output(yf_c + yb_c, r_c, 0.5 * (k_c[0] + k_c[1]), v_c, g_c, r_k, ln_w, ln_b, w_o)
    return o_ctx, o_lat


def setup_inputs(seed: int = 0) -> dict:
    key = jax.random.key(seed)
    ks = jax.random.split(key, 32)
    n_a = (DEPTH + 1) // 2
    n_b = DEPTH // 2
    D = D_MODEL

    def nrm(k, shape, s):
        return jax.random.normal(k, shape, jnp.float32) * s

    return {
        'x': nrm(ks[0], (BATCH, SEQ, D), 1.0),
        'c': nrm(ks[1], (BATCH, D), 1.0),
        'ctx': nrm(ks[2], (BATCH, CTX_LEN, D), 1.0),
        'c_ctx': nrm(ks[3], (D,), 1.0),
        'mod_w': nrm(ks[4], (DEPTH, D, N_MOD * D), 0.5 * D ** -0.5),
        'mod_b': nrm(ks[5], (DEPTH, N_MOD * D), 0.01),
        'norm_w': 1.0 + nrm(ks[6], (DEPTH, 3, D), 0.02),
        'ffn_w13': nrm(ks[7], (DEPTH, 2, D, 2 * D_FF), D ** -0.5),
        'ffn_w2': nrm(ks[8], (DEPTH, 2, D_FF, D), D_FF ** -0.5),
        'fnet_w_o': nrm(ks[9], (n_a, D, D), D ** -0.5),
        'fnet_b_o': nrm(ks[10], (n_a, D), 0.01),
        'rwkv_mu': jax.random.uniform(ks[11], (n_b, 6, D), jnp.float32),
        'rwkv_w_rkv': nrm(ks[12], (n_b, 3, D, D), D ** -0.5),
        'rwkv_w0': jax.random.uniform(ks[13], (n_b, N_DIR, D), jnp.float32, minval=-6.5, maxval=-1.5),
        'rwkv_w1': nrm(ks[14], (n_b, N_DIR, D, DECAY_LORA), D ** -0.5),
        'rwkv_w2': nrm(ks[15], (n_b, N_DIR, DECAY_LORA, D), 0.1 * DECAY_LORA ** -0.5),
        'rwkv_a0': nrm(ks[16], (n_b, N_DIR, D), 0.1),
        'rwkv_a1': nrm(ks[17], (n_b, N_DIR, D, AAA_LORA), D ** -0.5),
        'rwkv_a2': nrm(ks[18], (n_b, N_DIR, AAA_LORA, D), 0.1 * AAA_LORA ** -0.5),
        'rwkv_g1': nrm(ks[19], (n_b, D, GATE_LORA), D ** -0.5),
        'rwkv_g2': nrm(ks[20], (n_b, GATE_LORA, D), GATE_LORA ** -0.5),
        'rwkv_k_k': 0.85 + nrm(ks[21], (n_b, D), 0.05),
        'rwkv_k_a': 1.0 + nrm(ks[22], (n_b, D), 0.05),
        'rwkv_r_k': nrm(ks[23], (n_b, RWKV_HEADS, RWKV_HEAD), 0.1),
        'rwkv_ln_w': 1.0 + nrm(ks[24], (n_b, D), 0.02),
        'rwkv_ln_b': nrm(ks[25], (n_b, D), 0.01),
        'rwkv_w_o': nrm(ks[26], (n_b, D, D), D ** -0.5),
        'final_norm_w': 1.0 + nrm(ks[27], (D,), 0.02),
    }


def reference(x, c, ctx, c_ctx, mod_w, mod_b, norm_w, ffn_w13, ffn_w2, fnet_w_o, fnet_b_o,
              rwkv_mu, rwkv_w_rkv, rwkv_w0, rwkv_w1, rwkv_w2, rwkv_a0, rwkv_a1, rwkv_a2,
              rwkv_g1, rwkv_g2, rwkv_k_k, rwkv_k_a, rwkv_r_k, rwkv_ln_w, rwkv_ln_b, rwkv_w_o,
              final_norm_w):
    B_ = x.shape[0]
    sc = jax.nn.silu(c)
    sc_ctx = jax.nn.silu(c_ctx)
    xl, xc = x, ctx
    for i in range(DEPTH):
        last = i == DEPTH - 1
        kind = i % 2
        j = i // 2
        ml = (sc @ mod_w[i] + mod_b[i]).reshape(B_, N_MOD, D_MODEL)[:, :, None, :]
        mc = (sc_ctx @ mod_w[i] + mod_b[i]).reshape(N_MOD, D_MODEL)
        ctx_live = (not last) or kind == 1

        xl = xl + 0.5 * ml[:, 2] * swiglu(modulate(xl, norm_w[i, 0], ml[:, 0], ml[:, 1]), ffn_w13[i, 0], ffn_w2[i, 0])
        if ctx_live:
            xc = xc + 0.5 * mc[2] * swiglu(modulate(xc, norm_w[i, 0], mc[0], mc[1]), ffn_w13[i, 0], ffn_w2[i, 0])

        hl = modulate(xl, norm_w[i, 1], ml[:, 3], ml[:, 4])
        if kind == 0:
            o_l = fourier_mix(hl, fnet_w_o[j], fnet_b_o[j])
            if not last:
                hc = modulate(xc, norm_w[i, 1], mc[3], mc[4])
                xc = xc + mc[5] * fourier_mix(hc, fnet_w_o[j], fnet_b_o[j])
        else:
            hc = modulate(xc, norm_w[i, 1], mc[3], mc[4])
            o_c, o_l = rwkv_mix(hc, hl, not last, rwkv_mu[j], rwkv_w_rkv[j], rwkv_w0[j], rwkv_w1[j], rwkv_w2[j],
                                rwkv_a0[j], rwkv_a1[j], rwkv_a2[j], rwkv_g1[j], rwkv_g2[j], rwkv_k_k[j],
                                rwkv_k_a[j], rwkv_r_k[j], rwkv_ln_w[j], rwkv_ln_b[j], rwkv_w_o[j])
            if not last:
                xc = xc + mc[5] * o_c
        xl = xl + ml[:, 5] * o_l

        xl = xl + 0.5 * ml[:, 8] * swiglu(modulate(xl, norm_w[i, 2], ml[:, 6], ml[:, 7]), ffn_w13[i, 1], ffn_w2[i, 1])
        if not last:
            xc = xc + 0.5 * mc[8] * swiglu(modulate(xc, norm_w[i, 2], mc[6], mc[7]), ffn_w13[i, 1], ffn_w2[i, 1])
    return rms_norm(xl, final_norm_w)
```

```python
import contextlib
import types
import numpy as np
import ml_dtypes
import concourse.bass as bass
import concourse.mybir as mybir
from concourse.bass_utils import run_bass_kernel_spmd

F32 = mybir.dt.float32
BF16 = mybir.dt.bfloat16
ALU = mybir.AluOpType
AF = mybir.ActivationFunctionType
AX = mybir.AxisListType

D = 2048
KC = 16
DFF = 5632
JC = 44
NMOD = 9
SEQ = 16384
CTX = 256
EPS = 1e-6
NCORES = 8


class Res:
    __slots__ = ("lastw", "readers")

    def __init__(self):
        self.lastw = None
        self.readers = []


class Op:
    __slots__ = ("eng", "fn", "deps", "sig", "cnt", "chan")

    def __init__(self, eng, fn, chan=None):
        self.eng = eng
        self.fn = fn
        self.deps = []
        self.sig = False
        self.cnt = 0
        self.chan = chan


ENGS = ("pe", "act", "dve", "pool", "sp")


def _freeze(fn):
    cl = fn.__closure__
    if cl is None:
        return fn
    cells = []
    for c in cl:
        try:
            cells.append(types.CellType(c.cell_contents))
        except ValueError:
            cells.append(c)
    return types.FunctionType(fn.__code__, fn.__globals__, fn.__name__, fn.__defaults__, tuple(cells))


class Prog:
    def __init__(self, nc, same_engine_sync=True):
        self.nc = nc
        self.ops = {e: [] for e in ENGS}
        self.chan_last = {}
        self.same = same_engine_sync

    def emit(self, eng, fn, reads=(), writes=(), chan=None):
        op = Op(eng, _freeze(fn), chan)
        deps = []
        for r in reads:
            if r.lastw is not None:
                deps.append(r.lastw)
        for w in writes:
            if w.lastw is not None:
                deps.append(w.lastw)
            deps.extend(w.readers)
        if chan is not None:
            prev = self.chan_last.get(chan)
            if prev is not None:
                deps.append(prev)
                op.cnt = prev.cnt + 16
            else:
                op.cnt = 16
            self.chan_last[chan] = op
        seen = set()
        for d in deps:
            if id(d) in seen or d is op:
                continue
            seen.add(id(d))
            if d.chan is None and d.eng == eng and (eng == "pe" or not self.same):
                continue
            op.deps.append(d)
            d.sig = True
        for r in reads:
            r.readers.append(op)
        for w in writes:
            w.lastw = op
            w.readers = []
        self.ops[eng].append(op)
        return op

    def run(self, final_waits=()):
        nc = self.nc
        chans = dict(self.chan_last)
        for e in ENGS:
            c = 0
            for op in self.ops[e]:
                if op.chan is None and op.sig:
                    c += 1
                    op.cnt = c
        with contextlib.ExitStack() as st:
            esem = {e: st.enter_context(nc.semaphore("s_" + e)) for e in ENGS}
            csem = {c: st.enter_context(nc.semaphore("c_%s" % (str(c),))) for c in chans}
            block = st.enter_context(nc.Block())

            def mk(ename):
                oplist = self.ops[ename]
                fw = list(final_waits) if ename == "sp" else []

                def body(eng):
                    known = {}
                    for op in oplist:
                        need = {}
                        for d in op.deps:
                            if d.chan is not None:
                                key = ("c", d.chan)
                                sem = csem[d.chan]
                            else:
                                key = ("e", d.eng)
                                sem = esem[d.eng]
                            if known.get(key, 0) >= d.cnt:
                                continue
                            if key not in need or need[key][1] < d.cnt:
                                need[key] = (sem, d.cnt)
                        for key, (sem, cnt) in need.items():
                            known[key] = cnt
                            eng.wait_ge(sem, cnt)
                        ins = op.fn(eng)
                        if op.chan is not None:
                            ins.then_inc(csem[op.chan], 16)
                        elif op.sig:
                            ins.then_inc(esem[op.eng], 1)
                    for d in fw:
                        eng.wait_ge(csem[d.chan], d.cnt)
                return body

            block.tensor(mk("pe"))
            block.scalar(mk("act"))
            block.vector(mk("dve"))
            block.gpsimd(mk("pool"))
            block.sync(mk("sp"))


class Rot:
    def __init__(self, bufs, name):
        self.bufs = bufs
        self.res = [Res() for _ in bufs]
        self.name = name
        self.i = 0

    def next(self):
        k = self.i % len(self.bufs)
        self.i += 1
        return self.bufs[k], self.res[k], "%s%d" % (self.name, k)


class Dense:
    def __init__(self, nc, P, st, tmax, batch_row):
        self.nc, self.P, self.st = nc, P, st
        self.tmax = tmax
        self.brow = batch_row
        sb = lambda name, shape, dt: st.enter_context(nc.sbuf_tensor("sb_" + name, shape, dt))
        self.h = sb("h", [128, KC, tmax], BF16)
        self.r_h = Res()
        self.hid = sb("hid", [128, JC, tmax], BF16)
        self.r_hid = Res()
        self.xs = Rot([sb("xs%d" % i, [128, tmax], F32) for i in range(3)], "xs")
        self.sq = Rot([sb("sq%d" % i, [128, tmax], BF16) for i in range(2)], "sq")
        self.rstd = sb("rstd", [128, tmax], F32)
        self.r_rstd = Res()
        self.tmp = Rot([sb("tmp%d" % i, [128, tmax], F32) for i in range(2)], "tmp")
        self.sg = Rot([sb("sg%d" % i, [128, 512], F32) for i in range(2)], "sg")
        self.w13t = Rot([sb("w13t%d" % i, [128, KC, 256], BF16) for i in range(3)], "w13t")
        self.w2t = Rot([sb("w2t%d" % i, [128, JC, 128], BF16) for i in range(2)], "w2t")
        self.xo = Rot([sb("xo%d" % i, [128, tmax], F32) for i in range(2)], "xo")
        self.ones = sb("ones", [128, 128], BF16)
        self.r_ones = Res()
        P.emit("pool", lambda e: e.memset(self.ones[:], 1.0 / D), writes=[self.r_ones])
        self.epsb = sb("epsb", [128, 1], F32)
        P.emit("pool", lambda e: e.memset(self.epsb[:], EPS), writes=[self.r_ones])
        self.ps = Rot([st.enter_context(nc.psum_tensor("ps%d" % i, [128, 512], F32)) for i in range(8)], "ps")

    def mod_compute(self, sT_d, modw_d, modb_d, nlayers=2, nchunks=144):
        nc, P, st = self.nc, self.P, self.st
        sb = lambda name, shape, dt: st.enter_context(nc.sbuf_tensor("sb_" + name, shape, dt))
        sT = sb("sT", [128, KC, 3], F32)
        r_sT = Res()
        self.mod = sb("mod", [128, nlayers, nchunks, 3], F32)
        self.r_mod = Res()
        modb = sb("modb", [128, nlayers, nchunks], F32)
        r_modb = Res()
        wm = Rot([sb("wm%d" % i, [128, KC, 256], F32) for i in range(2)], "wm")
        P.emit("sp", lambda e: e.dma_start(out=sT[:], in_=sT_d), writes=[r_sT], chan="msc0")
        P.emit("sp", lambda e: e.dma_start(out=modb[:], in_=modb_d), writes=[r_modb], chan="msc1")
        P.emit("act", lambda e: e.activation(out=sT[:], in_=sT[:], func=AF.Silu), reads=[r_sT], writes=[r_sT])
        for l in range(nlayers):
            pst, r_ps, _ = self.ps.next()
            psv = pst[:, 0:nchunks * 3].rearrange("p (n r) -> p n r", r=3)
            for nb in range(nchunks // 2):
                wt, r_wt, ch = wm.next()
                P.emit("sp" if nb % 2 else "act",
                       lambda e, wt=wt, l=l, nb=nb: e.dma_start(out=wt[:], in_=modw_d[l, :, :, nb * 256:(nb + 1) * 256]),
                       writes=[r_wt], chan=ch)
                for q in range(2):
                    n = nb * 2 + q
                    for kc in range(KC):
                        P.emit("pe", lambda e, wt=wt, q=q, kc=kc, n=n, psv=psv: e.matmul(
                            psv[:, n, :], lhsT=wt[:, kc, q * 128:(q + 1) * 128], rhs=sT[:, kc, :],
                            start=(kc == 0), stop=(kc == KC - 1)), reads=[r_wt, r_sT], writes=[r_ps])
            P.emit("dve", lambda e, l=l, psv=psv: e.tensor_tensor(
                out=self.mod[:, l], in0=psv, in1=modb[:, l, :].unsqueeze(2).to_broadcast([128, nchunks, 3]), op=ALU.add),
                reads=[r_ps, r_modb], writes=[self.r_mod])

    def mod_derive(self, mod_d, normw_d, nlayers=2):
        nc, P, st = self.nc, self.P, self.st
        sb = lambda name, shape, dt: st.enter_context(nc.sbuf_tensor("sb_" + name, shape, dt))
        self.mod = sb("mod", [128, nlayers, 144, 3], F32)
        self.r_mod = Res()
        self.normw = sb("normw", [128, nlayers, 3, KC], F32)
        r_nw = Res()
        self.gs = sb("gs", [128, nlayers, 3, KC, 3], F32)
        self.hg = sb("hg", [128, nlayers, 3, KC, 3], F32)
        P.emit("sp", lambda e: e.dma_start(out=self.mod[:].rearrange("p a b c -> p (a b c)"), in_=mod_d), writes=[self.r_mod], chan="md0")
        P.emit("sp", lambda e: e.dma_start(out=self.normw[:], in_=normw_d), writes=[r_nw], chan="md1")
        for l in range(nlayers):
            for s in range(3):
                sc = self.mod[:, l, (3 * s + 1) * 16:(3 * s + 2) * 16, :]
                gt = self.mod[:, l, (3 * s + 2) * 16:(3 * s + 3) * 16, :]
                P.emit("dve", lambda e, l=l, s=s, sc=sc: e.scalar_tensor_tensor(
                    out=self.gs[:, l, s], in0=sc, scalar=1.0, in1=self.normw[:, l, s, :].unsqueeze(2).to_broadcast([128, KC, 3]),
                    op0=ALU.add, op1=ALU.mult), reads=[self.r_mod, r_nw], writes=[self.r_mod])
                P.emit("dve", lambda e, l=l, s=s, gt=gt: e.tensor_scalar(
                    out=self.hg[:, l, s], in0=gt, scalar1=(1.0 if s == 1 else 0.5), scalar2=None, op0=ALU.mult),
                    reads=[self.r_mod], writes=[self.r_mod])

    def mod_load(self, modo, gso, hgo, nlayers=2):
        nc, P, st = self.nc, self.P, self.st
        sb = lambda name, shape, dt: st.enter_context(nc.sbuf_tensor("sb_" + name, shape, dt))
        self.mod = sb("mod", [128, nlayers, 144, 3], F32)
        self.gs = sb("gs", [128, nlayers, 3, KC, 3], F32)
        self.hg = sb("hg", [128, nlayers, 3, KC, 3], F32)
        self.r_mod = Res()
        P.emit("sp", lambda e: e.dma_start(out=self.mod[:].rearrange("p a b c -> p (a b c)"), in_=modo), writes=[self.r_mod], chan="ml0")
        P.emit("sp", lambda e: e.dma_start(out=self.gs[:].rearrange("p a b c d -> p (a b c d)"), in_=gso), writes=[self.r_mod], chan="ml1")
        P.emit("sp", lambda e: e.dma_start(out=self.hg[:].rearrange("p a b c d -> p (a b c d)"), in_=hgo), writes=[self.r_mod], chan="ml2")

    def linear_res(self, blocks, src_d, r_src, w_d, bias_sb, X, r_X, Xo, r_Xo, l):
        P = self.P
        offs = np.cumsum([0] + [b[1] for b in blocks])
        sv = src_d.rearrange("(c p) t -> p c t", p=128)
        for bi, (c0, n, row) in enumerate(blocks):
            for half in range(2):
                P.emit("sp", lambda e, c0=c0, n=n, o=offs[bi], half=half: e.dma_start(
                    out=self.h[:, half * 8:(half + 1) * 8, o:o + n], in_=sv[:, half * 8:(half + 1) * 8, c0:c0 + n]),
                    reads=[r_src], writes=[self.r_h], chan="lrh%d_%d" % (bi, half))
        Xv = X.rearrange("(c p) t -> p c t", p=128)
        Xov = Xo.rearrange("(c p) t -> p c t", p=128)
        for n2 in range(KC // 2):
            wt, r_wt, ch = self.w13t.next()
            P.emit("pool", lambda e, wt=wt, n2=n2: e.dma_start(out=wt[:], in_=w_d[n2]), writes=[r_wt], chan=ch)
            for q in range(2):
                nn = n2 * 2 + q
                xt, r_xt, chx = self.xs.next()
                xo, r_xo, cho = self.xo.next()
                for bi, (c0, n, row) in enumerate(blocks):
                    o = offs[bi]
                    P.emit("sp", lambda e, xt=xt, nn=nn, c0=c0, n=n, o=o: e.dma_start(
                        out=xt[:, o:o + n], in_=Xv[:, nn, c0:c0 + n]), reads=[r_X], writes=[r_xt], chan=chx + "_%d" % bi)
                    po, r_po, _ = self.ps.next()
                    for kc in range(KC):
                        P.emit("pe", lambda e, po=po, wt=wt, kc=kc, q=q, o=o, n=n: e.matmul(
                            po[:, 0:n], lhsT=wt[:, kc, q * 128:(q + 1) * 128], rhs=self.h[:, kc, o:o + n],
                            start=(kc == 0), stop=(kc == KC - 1)), reads=[r_wt, self.r_h], writes=[r_po])
                    src_ap = po[:, 0:n]
                    rd = [r_po]
                    if bias_sb is not None:
                        sg, r_sg, _ = self.sg.next()
                        P.emit("act", lambda e, sg=sg, po=po, n=n, nn=nn: e.activation(
                            out=sg[:, 0:n], in_=po[:, 0:n], func=AF.Identity, bias=bias_sb[:, nn:nn + 1], scale=1.0),
                            reads=[r_po, self.r_mod], writes=[r_sg])
                        src_ap = sg[:, 0:n]
                        rd = [r_sg]
                    P.emit("dve", lambda e, src_ap=src_ap, xt=xt, xo=xo, nn=nn, o=o, n=n, row=row: e.scalar_tensor_tensor(
                        out=xo[:, o:o + n], in0=src_ap, scalar=self.hg[:, l, 1, nn, row:row + 1], in1=xt[:, o:o + n],
                        op0=ALU.mult, op1=ALU.add), reads=rd + [r_xt, self.r_mod], writes=[r_xo])
                    P.emit("sp", lambda e, xo=xo, nn=nn, c0=c0, n=n, o=o: e.dma_start(
                        out=Xov[:, nn, c0:c0 + n], in_=xo[:, o:o + n]), reads=[r_xo], writes=[r_Xo], chan=cho + "_o%d" % bi)

    def shift_ap(self, l, s, c, row):
        return self.mod[:, l, (3 * s) * 16 + c, row:row + 1]

    def norm_mod(self, blocks, X, r_X, l, s, out_dram=None, r_out=None, final_g=None):
        P = self.P
        T = sum(b[1] for b in blocks)
        offs = np.cumsum([0] + [b[1] for b in blocks])
        stat = [self.ps.next() for _ in blocks]
        Xv = X.rearrange("(c p) t -> p c t", p=128)

        def load_x(c):
            xt, r_xt, ch = self.xs.next()
            for bi, (c0, n, row) in enumerate(blocks):
                P.emit("sp", lambda e, xt=xt, c=c, c0=c0, n=n, o=offs[bi]: e.dma_start(
                    out=xt[:, o:o + n], in_=Xv[:, c, c0:c0 + n]), reads=[r_X], writes=[r_xt], chan=ch + "_%d" % bi)
            return xt, r_xt

        for c in range(KC):
            xt, r_xt = load_x(c)
            sq, r_sq, _ = self.sq.next()
            P.emit("act", lambda e, xt=xt, sq=sq: e.activation(out=sq[:, 0:T], in_=xt[:, 0:T], func=AF.Square),
                   reads=[r_xt], writes=[r_sq])
            for bi, (c0, n, row) in enumerate(blocks):
                pst, r_ps, _ = stat[bi]
                P.emit("pe", lambda e, pst=pst, sq=sq, o=offs[bi], n=n, c=c: e.matmul(
                    pst[:, 0:n], lhsT=self.ones[:], rhs=sq[:, o:o + n], start=(c == 0), stop=(c == KC - 1)),
                    reads=[r_sq, self.r_ones], writes=[r_ps])
        for bi, (c0, n, row) in enumerate(blocks):
            pst, r_ps, _ = stat[bi]
            P.emit("act", lambda e, pst=pst, o=offs[bi], n=n: e.activation(
                out=self.rstd[:, o:o + n], in_=pst[:, 0:n], func=AF.Sqrt, bias=self.epsb[:, 0:1], scale=1.0),
                reads=[r_ps, self.r_ones], writes=[self.r_rstd])
            P.emit("dve", lambda e, o=offs[bi], n=n: e.reciprocal(
                out=self.rstd[:, o:o + n], in_=self.rstd[:, o:o + n]),
                reads=[self.r_rstd], writes=[self.r_rstd])
        for c in range(KC):
            xt, r_xt = load_x(c)
            tmp, r_tmp, _ = self.tmp.next()
            P.emit("dve", lambda e, xt=xt, tmp=tmp: e.tensor_tensor(
                out=tmp[:, 0:T], in0=xt[:, 0:T], in1=self.rstd[:, 0:T], op=ALU.mult),
                reads=[r_xt, self.r_rstd], writes=[r_tmp])
            if out_dram is None:
                for bi, (c0, n, row) in enumerate(blocks):
                    P.emit("act", lambda e, tmp=tmp, o=offs[bi], n=n, c=c, row=row: e.activation(
                        out=self.h[:, c, o:o + n], in_=tmp[:, o:o + n], func=AF.Identity,
                        scale=self.gs[:, l, s, c, row:row + 1], bias=self.shift_ap(l, s, c, row)),
                        reads=[r_tmp, self.r_mod], writes=[self.r_h])
            else:
                xo, r_xo, ch = self.xo.next()
                odt = out_dram.dtype
                xov = xo if odt == F32 else xo[:].bitcast(BF16)
                for bi, (c0, n, row) in enumerate(blocks):
                    if final_g is not None:
                        P.emit("act", lambda e, tmp=tmp, xov=xov, o=offs[bi], n=n, c=c: e.activation(
                            out=xov[:, o:o + n], in_=tmp[:, o:o + n], func=AF.Identity, scale=final_g[:, c:c + 1], bias=0.0),
                            reads=[r_tmp, self.r_mod], writes=[r_xo])
                    else:
                        P.emit("act", lambda e, tmp=tmp, xov=xov, o=offs[bi], n=n, c=c, row=row: e.activation(
                            out=xov[:, o:o + n], in_=tmp[:, o:o + n], func=AF.Identity,
                            scale=self.gs[:, l, s, c, row:row + 1], bias=self.shift_ap(l, s, c, row)),
                            reads=[r_tmp, self.r_mod], writes=[r_xo])
                ov = out_dram.rearrange("(c p) t -> p c t", p=128)
                for bi, (c0, n, row) in enumerate(blocks):
                    P.emit("sp", lambda e, xov=xov, o=offs[bi], n=n, c=c, c0=c0: e.dma_start(
                        out=ov[:, c, c0:c0 + n], in_=xov[:, o:o + n]), reads=[r_xo], writes=[r_out], chan=ch + "_o%d" % bi)

    def ffn(self, blocks, w13_d, w2_d, X, r_X, Xo, r_Xo, l, s):
        P = self.P
        offs = np.cumsum([0] + [b[1] for b in blocks])
        for j in range(JC):
            wt, r_wt, ch = self.w13t.next()
            P.emit("pool", lambda e, wt=wt, j=j: e.dma_start(out=wt[:], in_=w13_d[j]), writes=[r_wt], chan=ch)
            for bi, (c0, n, row) in enumerate(blocks):
                o = offs[bi]
                pg, r_pg, _ = self.ps.next()
                pu, r_pu, _ = self.ps.next()
                for half, (pp, r_pp) in enumerate(((pg, r_pg), (pu, r_pu))):
                    for kc in range(KC):
                        P.emit("pe", lambda e, pp=pp, wt=wt, kc=kc, half=half, o=o, n=n: e.matmul(
                            pp[:, 0:n], lhsT=wt[:, kc, half * 128:(half + 1) * 128], rhs=self.h[:, kc, o:o + n],
                            start=(kc == 0), stop=(kc == KC - 1)), reads=[r_wt, self.r_h], writes=[r_pp])
                sg, r_sg, _ = self.sg.next()
                P.emit("act", lambda e, sg=sg, pg=pg, n=n: e.activation(out=sg[:, 0:n], in_=pg[:, 0:n], func=AF.Silu),
                       reads=[r_pg], writes=[r_sg])
                P.emit("dve", lambda e, sg=sg, pu=pu, j=j, o=o, n=n: e.tensor_tensor(
                    out=self.hid[:, j, o:o + n], in0=sg[:, 0:n], in1=pu[:, 0:n], op=ALU.mult),
                    reads=[r_sg, r_pu], writes=[self.r_hid])
        Xv = X.rearrange("(c p) t -> p c t", p=128)
        Xov = Xo.rearrange("(c p) t -> p c t", p=128)
        for nn in range(KC):
            wt, r_wt, ch = self.w2t.next()
            P.emit("pool", lambda e, wt=wt, nn=nn: e.dma_start(out=wt[:], in_=w2_d[nn]), writes=[r_wt], chan=ch)
            xt, r_xt, chx = self.xs.next()
            xo, r_xo, cho = self.xo.next()
            for bi, (c0, n, row) in enumerate(blocks):
                o = offs[bi]
                P.emit("sp", lambda e, xt=xt, nn=nn, c0=c0, n=n, o=o: e.dma_start(
                    out=xt[:, o:o + n], in_=Xv[:, nn, c0:c0 + n]), reads=[r_X], writes=[r_xt], chan=chx + "_%d" % bi)
                po, r_po, _ = self.ps.next()
                for jc in range(JC):
                    P.emit("pe", lambda e, po=po, wt=wt, jc=jc, o=o, n=n: e.matmul(
                        po[:, 0:n], lhsT=wt[:, jc, :], rhs=self.hid[:, jc, o:o + n],
                        start=(jc == 0), stop=(jc == JC - 1)), reads=[r_wt, self.r_hid], writes=[r_po])
                P.emit("dve", lambda e, po=po, xt=xt, xo=xo, nn=nn, o=o, n=n, row=row: e.scalar_tensor_tensor(
                    out=xo[:, o:o + n], in0=po[:, 0:n], scalar=self.hg[:, l, s, nn, row:row + 1], in1=xt[:, o:o + n],
                    op0=ALU.mult, op1=ALU.add), reads=[r_po, r_xt, self.r_mod], writes=[r_xo])
                P.emit("sp", lambda e, xo=xo, nn=nn, c0=c0, n=n, o=o: e.dma_start(
                    out=Xov[:, nn, c0:c0 + n], in_=xo[:, o:o + n]), reads=[r_xo], writes=[r_Xo], chan=cho + "_o%d" % bi)


def make_passes(nlat, nctx, brow):
    blks = []
    c = 0
    while c < nlat:
        n = min(512, nlat - c)
        blks.append((c, n, brow))
        c += n
    if nctx:
        blks.append((nlat, nctx, 2))
    fine = []
    for (c0, n, row) in blks:
        fine.append((c0, n, row))
    passes = []
    cur, tot = [], 0
    queue = list(fine)
    while queue:
        c0, n, row = queue.pop(0)
        if tot + n <= 768:
            cur.append((c0, n, row)); tot += n
        elif n == 512 and tot + 256 <= 768:
            cur.append((c0, 256, row)); tot += 256
            queue.insert(0, (c0 + 256, 256, row))
        else:
            passes.append(cur); cur, tot = [], 0
            queue.insert(0, (c0, n, row))
    if cur:
        passes.append(cur)
    return passes


def lay_w13(w):
    return np.ascontiguousarray(w.reshape(KC, 128, 2, JC, 128).transpose(3, 1, 0, 2, 4)).reshape(JC, 128, KC, 256)


def lay_w2(w):
    return np.ascontiguousarray(w.reshape(JC, 128, KC, 128).transpose(2, 1, 0, 3))


def lay_sq(w):
    return np.ascontiguousarray(w.reshape(KC, 128, -1).transpose(1, 0, 2))


def lay_vec(v):
    sh = v.shape[:-1]
    a = v.reshape(sh + (KC, 128))
    return np.ascontiguousarray(np.moveaxis(a, -1, 0))


def core_tokens(core):
    b = core // 4
    q = core % 4
    return b, q


def build_l0():
    nc = bass.Bass("TRN2", target_bir_lowering=False)
    dt = lambda name, shape, dty, kind: nc.dram_tensor(name, shape, dty, kind=kind).ap()
    sT = dt("sT", [128, KC, 3], F32, "ExternalInput")
    modw = dt("modw", [2, 128, KC, 18 * 128], F32, "ExternalInput")
    modb = dt("modb", [128, 2, 18], F32, "ExternalInput")
    modo = dt("modo", [128, 2 * 18 * 3], F32, "ExternalOutput")
    with contextlib.ExitStack() as st:
        P = Prog(nc)
        dn = Dense(nc, P, st, 64, 0)
        dn.mod_compute(sT, modw, modb, 2, 18)
        P.emit("sp", lambda e: e.dma_start(out=modo, in_=dn.mod[:].rearrange("p a b c -> p (a b c)")), reads=[dn.r_mod], chan="mo0")
        P.run(final_waits=_all_dma_tails(P))
    return nc


def build_l1(nlat, nctx):
    NT = nlat + nctx
    nc = bass.Bass("TRN2", target_bir_lowering=False)
    dt = lambda name, shape, dty, kind: nc.dram_tensor(name, shape, dty, kind=kind).ap()
    xT = dt("xT", [D, NT], F32, "ExternalInput")
    modi = dt("modi", [128, 2 * 144 * 3], F32, "ExternalInput")
    normw = dt("normw", [128, 2, 3, KC], F32, "ExternalInput")
    w13 = dt("w13", [JC, 128, KC, 256], F32, "ExternalInput")
    w2 = dt("w2", [KC, 128, JC, 128], F32, "ExternalInput")
    X1 = dt("X1", [D, NT], F32, "ExternalOutput")
    h0 = dt("h0", [D, NT], BF16, "ExternalOutput")
    gso = dt("gso", [128, 2 * 3 * KC * 3], F32, "ExternalOutput")
    hgo = dt("hgo", [128, 2 * 3 * KC * 3], F32, "ExternalOutput")
    outs = []
    with contextlib.ExitStack() as st:
        P = Prog(nc)
        dn = Dense(nc, P, st, 768, 0)
        dn.mod_derive(modi, normw)
        r_o = Res()
        outs.append(P.emit("sp", lambda e: e.dma_start(out=gso, in_=dn.gs[:].rearrange("p a b c d -> p (a b c d)")), reads=[dn.r_mod], writes=[r_o], chan="mo1"))
        outs.append(P.emit("sp", lambda e: e.dma_start(out=hgo, in_=dn.hg[:].rearrange("p a b c d -> p (a b c d)")), reads=[dn.r_mod], writes=[r_o], chan="mo2"))
        r_xin = Res()
        passes = make_passes(nlat, nctx, 0)
        for blocks in passes:
            r_X1 = Res()
            r_h0 = Res()
            dn.norm_mod(blocks, xT, r_xin, 0, 0)
            dn.ffn(blocks, w13, w2, xT, r_xin, X1, r_X1, 0, 0)
            dn.norm_mod(blocks, X1, r_X1, 0, 1, out_dram=h0, r_out=r_h0)
            outs.append(r_X1)
            outs.append(r_h0)
        fw = [o.lastw if isinstance(o, Res) else o for o in outs]
        P.run(final_waits=_all_dma_tails(P))
    return nc


def _all_dma_tails(P):
    return list(P.chan_last.values())


def fft_tables():
    bf = ml_dtypes.bfloat16
    ch = np.arange(256, dtype=np.float64)
    ang = 2 * np.pi * np.outer(ch, ch) / 256.0
    sc = 1.0 / 2048.0
    cs = np.zeros((128, 2, 4, 128), np.float64)
    for kc in range(2):
        for q in range(4):
            a = ang[kc * 128:(kc + 1) * 128, q * 64:(q + 1) * 64]
            cs[:, kc, q, 0:64] = np.cos(a) * sc
            cs[:, kc, q, 64:128] = -np.sin(a) * sc
    l = np.arange(128, dtype=np.float64)
    a1 = 2 * np.pi * np.outer(l, l) / 128.0
    f1 = np.zeros((128, 2, 256), np.float64)
    f1[:, 0, 0:128] = np.cos(a1); f1[:, 0, 128:256] = -np.sin(a1)
    f1[:, 1, 0:128] = np.sin(a1); f1[:, 1, 128:256] = np.cos(a1)
    k = np.arange(128)[None, :] * 128 + np.arange(128)[:, None]
    ae = 2 * np.pi * (l[:, None, None] * k[None]) / 16384.0
    E = np.concatenate([np.cos(ae), np.sin(ae)], axis=2)
    scc = 1.0 / 256.0
    csc = np.zeros((128, 2, 512), np.float64)
    for kc in range(2):
        a = ang[kc * 128:(kc + 1) * 128, :]
        csc[:, kc, 0:256] = np.cos(a) * scc
        csc[:, kc, 256:512] = -np.sin(a) * scc
    g = np.zeros((128, 2, 512), np.float64)
    for tc in range(2):
        a = ang[tc * 128:(tc + 1) * 128, :]
        g[:, tc, 0:256] = np.cos(a)
        g[:, tc, 256:512] = np.sin(a)
    return {"t_cs": cs.astype(bf), "t_f1": f1.astype(bf), "t_E": E.astype(bf), "t_csc": csc.astype(bf), "t_g": g.astype(bf)}


def emit_fft(nc, P, st, hT, r_hT, hcT, r_hcT, tabs, fo, r_fo, fco, r_fco, nb=2, pfx="ff"):
    sb = lambda name, shape, dt: st.enter_context(nc.sbuf_tensor("sb_" + pfx + name, shape, dt))
    hs = sb("hs", [128, 2, 16384], BF16); r_hs = Res()
    W = sb("W", [128, 128, 128], BF16); r_W = Res()
    Z = sb("Z", [128, 64, 256], BF16); r_Z = Res()
    fT = sb("fT", [64, 16384], BF16); r_fT = Res()
    Eb = Rot([sb("E%d" % i, [128, 16, 256], BF16) for i in range(2)], pfx + "E")
    cs = sb("cs", [128, 2, 4, 128], BF16)
    f1 = sb("f1", [128, 2, 256], BF16)
    csc = sb("csc", [128, 2, 512], BF16)
    gt = sb("gt", [128, 2, 512], BF16)
    hcs = sb("hcs", [128, 2, 256], BF16); r_hcs = Res()
    Wc = sb("Wc", [128, 2, 512], BF16); r_Wc = Res()
    fcs = sb("fcs", [128, 256], BF16); r_fcs = Res()
    r_tab = Res()
    ps = Rot([st.enter_context(nc.psum_tensor(pfx + "ps%d" % i, [128, 512], F32)) for i in range(8)], pfx + "ps")
    for i, (dst, src) in enumerate(((cs, tabs["t_cs"]), (f1, tabs["t_f1"]), (csc, tabs["t_csc"]), (gt, tabs["t_g"]))):
        P.emit("sp", lambda e, dst=dst, src=src: e.dma_start(out=dst[:], in_=src), writes=[r_tab], chan=pfx + "tab%d" % i)
    evac_i = [0]

    def evac(out_ap, in_ap, reads, writes):
        eng = "act" if evac_i[0] % 2 == 0 else "dve"
        evac_i[0] += 1
        if eng == "act":
            P.emit("act", lambda e: e.activation(out=out_ap, in_=in_ap, func=AF.Copy), reads=reads, writes=writes)
        else:
            P.emit("dve", lambda e: e.tensor_copy(out=out_ap, in_=in_ap), reads=reads, writes=writes)

    for b in range(nb):
        P.emit("sp", lambda e, b=b: e.dma_start(out=hcs[:], in_=hcT[b]), reads=[r_hcT], writes=[r_hcs], chan=pfx + "hc")
        for tc in range(2):
            pt, r_pt, _ = ps.next()
            for kc in range(2):
                P.emit("pe", lambda e, pt=pt, tc=tc, kc=kc: e.matmul(
                    pt[:, :], lhsT=hcs[:, kc, tc * 128:(tc + 1) * 128], rhs=csc[:, kc, :], start=(kc == 0), stop=(kc == 1)),
                    reads=[r_hcs, r_tab], writes=[r_pt])
            evac(Wc[:, tc, :], pt[:, :], [r_pt], [r_Wc])
        for half in range(2):
            pt, r_pt, _ = ps.next()
            k = 0
            for tc in range(2):
                for ri in range(2):
                    P.emit("pe", lambda e, pt=pt, tc=tc, ri=ri, half=half, k=k: e.matmul(
                        pt[:, 0:256], lhsT=Wc[:, tc, ri * 256 + half * 128: ri * 256 + (half + 1) * 128],
                        rhs=gt[:, tc, ri * 256:(ri + 1) * 256], start=(k == 0), stop=(k == 3)),
                        reads=[r_Wc, r_tab], writes=[r_pt])
                    k += 1
            evac(fcs[:, :], pt[:, 0:256], [r_pt], [r_fcs])
            P.emit("sp", lambda e, b=b, half=half: e.dma_start(out=fco[b, half * 128:(half + 1) * 128, :], in_=fcs[:, :]),
                   reads=[r_fcs], writes=[r_fco], chan=pfx + "fco")
        for kc in range(2):
            P.emit("sp" if kc == 0 else "act", lambda e, b=b, kc=kc: e.dma_start(out=hs[:, kc, :], in_=hT[b, :, kc, :]),
                   reads=[r_hT], writes=[r_hs], chan=pfx + "hs%d" % kc)
        hv = hs[:].rearrange("p k (a l) -> p k l a", l=128)
        for q in range(4):
            for g4 in range(32):
                pt, r_pt, _ = ps.next()
                for li in range(4):
                    l2 = g4 * 4 + li
                    for kc in range(2):
                        P.emit("pe", lambda e, pt=pt, li=li, l2=l2, kc=kc, q=q: e.matmul(
                            pt[:, li * 128:(li + 1) * 128], lhsT=hv[:, kc, l2, :], rhs=cs[:, kc, q, :],
                            start=(kc == 0), stop=(kc == 1)), reads=[r_hs, r_tab], writes=[r_pt])
                evac(W[:, g4 * 4:(g4 + 1) * 4, :].rearrange("p a b -> p (a b)"), pt[:, :], [r_pt], [r_W])
            for c2 in range(32):
                pt, r_pt, _ = ps.next()
                for ci in range(2):
                    c = c2 * 2 + ci
                    for ri in range(2):
                        P.emit("pe", lambda e, pt=pt, ci=ci, c=c, ri=ri: e.matmul(
                            pt[:, ci * 256:(ci + 1) * 256], lhsT=W[:, :, ri * 64 + c], rhs=f1[:, ri, :],
                            start=(ri == 0), stop=(ri == 1)), reads=[r_W, r_tab], writes=[r_pt])
                evac(Z[:, c2 * 2:(c2 + 1) * 2, :].rearrange("p a b -> p (a b)"), pt[:, :], [r_pt], [r_Z])
            fv = fT[:].rearrange("p (k2 k1) -> p k1 k2", k1=128)
            for eb in range(8):
                Et, r_Et, ch = Eb.next()
                P.emit("sp", lambda e, Et=Et, eb=eb: e.dma_start(out=Et[:], in_=tabs["t_E"][:, eb * 16:(eb + 1) * 16, :]),
                       writes=[r_Et], chan=ch)
                for k4 in range(4):
                    pt, r_pt, _ = ps.next()
                    for ki in range(4):
                        kl = k4 * 4 + ki
                        k1 = eb * 16 + kl
                        for ri in range(2):
                            P.emit("pe", lambda e, pt=pt, ki=ki, kl=kl, k1=k1, ri=ri, Et=Et: e.matmul(
                                pt[0:64, ki * 128:(ki + 1) * 128], lhsT=Z[:, :, ri * 128 + k1], rhs=Et[:, kl, ri * 128:(ri + 1) * 128],
                                start=(ri == 0), stop=(ri == 1)), reads=[r_Z, r_Et], writes=[r_pt])
                    k10 = eb * 16 + k4 * 4
                    evac(fv[:, k10:k10 + 4, :], pt[0:64, :].rearrange("p (a b) -> p a b", a=4), [r_pt], [r_fT])
            P.emit("sp", lambda e, b=b, q=q: e.dma_start(out=fo[b, q * 64:(q + 1) * 64, :], in_=fT[:, :]),
                   reads=[r_fT], writes=[r_fo], chan=pfx + "fo")


def build_l2(nb=2):
    nc = bass.Bass("TRN2", target_bir_lowering=False)
    dt = lambda name, shape, dty, kind: nc.dram_tensor(name, shape, dty, kind=kind).ap()
    hT = dt("hT", [nb, 128, 2, 16384], BF16, "ExternalInput")
    hcT = dt("hcT", [nb, 128, 2, 256], BF16, "ExternalInput")
    tabs = {"t_cs": dt("t_cs", [128, 2, 4, 128], BF16, "ExternalInput"), "t_f1": dt("t_f1", [128, 2, 256], BF16, "ExternalInput"),
            "t_E": dt("t_E", [128, 128, 256], BF16, "ExternalInput"), "t_csc": dt("t_csc", [128, 2, 512], BF16, "ExternalInput"),
            "t_g": dt("t_g", [128, 2, 512], BF16, "ExternalInput")}
    fo = dt("fo", [nb, 256, 16384], BF16, "ExternalOutput")
    fco = dt("fco", [nb, 256, 256], BF16, "ExternalOutput")
    with contextlib.ExitStack() as st:
        P = Prog(nc)
        emit_fft(nc, P, st, hT, Res(), hcT, Res(), tabs, fo, Res(), fco, Res(), nb=nb)
        P.run(final_waits=_all_dma_tails(P))
    return nc


LDC = -0.6065306597126334
GN_EPS = 64e-5


def rwkv_consts():
    bf = ml_dtypes.bfloat16
    idx = np.arange(128)
    cm = np.zeros((128, 2, 3, 256), np.float32)
    ct = np.zeros((128, 2, 3, 128), np.float32)
    for z in range(2):
        before = (idx[:, None] < idx[None, :]) if z == 0 else (idx[:, None] > idx[None, :])
        beq = before | np.eye(128, dtype=bool)
        cm[:, z, 0, 0:128] = before
        cm[:, z, 0, 128:256] = beq
        cm[:, z, 1, 0:128] = before.T
        cm[:, z, 1, 128:256] = before.T
        cm[:, z, 2, 0:128] = beq
        cm[:, z, 2, 128:256] = beq
        ct[:, z, 0, :] = LDC * beq
        ct[:, z, 1, :] = LDC * before
        ct[:, z, 2, :] = LDC
    ident = np.eye(128, dtype=np.float32)
    return {"c_mask": cm.astype(bf), "c_tri": ct, "c_ident": ident.astype(bf)}


def emit_rwkv(nc, P, st, hT, r_hT, wd, yscr, o_out, r_out, nlat=SEQ, nctx=CTX, nb=2, pfx="rw", dbg=None, dbg_at=(0, "ctx", 0, 0)):
    sbt = lambda name, shape, dt: st.enter_context(nc.sbuf_tensor("sb_" + pfx + name, shape, dt))
    ps = Rot([st.enter_context(nc.psum_tensor(pfx + "ps%d" % i, [128, 512], F32)) for i in range(8)], pfx + "ps")
    r_c = Res()

    def ld(dst, src, eng="sp", chan=None, **kw):
        return P.emit(eng, lambda e: e.dma_start(out=dst, in_=src, **kw), writes=[r_c], chan=chan)
    cmask = sbt("cmask", [128, 2, 3, 256], BF16); ld(cmask[:], wd["c_mask"], chan=pfx + "k0")
    ctri = sbt("ctri", [128, 2, 3, 128], F32); ld(ctri[:], wd["c_tri"], chan=pfx + "k1")
    ident = sbt("ident", [128, 128], BF16); ld(ident[:], wd["c_ident"], chan=pfx + "k2")
    vecs = sbt("vecs", [128, 9, 256], F32); ld(vecs[:], wd["vecs"], chan=pfx + "k3")
    mu = sbt("mu", [128, KC, 6], F32); ld(mu[:], wd["mu"], chan=pfx + "k4")
    om = sbt("om", [128, KC, 6], F32)
    W2s = sbt("W2s", [96, 2, 256], BF16); ld(W2s[:], wd["w2"], eng="pool", chan=pfx + "k5")
    A2s = sbt("A2s", [96, 2, 256], BF16); ld(A2s[:], wd["a2"], eng="pool", chan=pfx + "k6")
    G2s = sbt("G2s", [128, 2, 256], BF16); ld(G2s[:], wd["g2"], eng="pool", chan=pfx + "k7")
    negcol = sbt("negcol", [128, 1], F32)
    P.emit("pool", lambda e: e.memset(negcol[:], LDC), writes=[r_c])
    gneps = sbt("gneps", [128, 1], F32)
    P.emit("pool", lambda e: e.memset(gneps[:], GN_EPS), writes=[r_c])
    P.emit("dve", lambda e: e.tensor_scalar(out=om[:], in0=mu[:], scalar1=-1.0, scalar2=1.0, op0=ALU.mult, op1=ALU.add),
           reads=[r_c], writes=[r_c])
    RKa = sbt("RKa", [128, KC, 768], BF16); RKb = sbt("RKb", [128, KC, 768], BF16)
    LWa = sbt("LWa", [128, KC, 640], BF16); LWb = sbt("LWb", [128, KC, 640], BF16)
    stg = Rot([sbt("stg%d" % i, [128, 768], F32) for i in range(1)], pfx + "stg")
    blocks = [(0, 256, 0), (256, 512, 1), (512, 768, 2), (768, 960, 3), (960, 1152, 4), (1152, 1408, 5)]
    for kc in range(KC):
        for part in range(2):
            sg, r_sg, ch = stg.next()
            if part == 0:
                P.emit("sp", lambda e, sg=sg, kc=kc: e.dma_start(out=sg[:, 0:768], in_=wd["rkv"][:, kc, :]), writes=[r_sg], chan=ch)
            else:
                P.emit("sp", lambda e, sg=sg, kc=kc: e.dma_start(out=sg[:, 0:640], in_=wd["lw"][:, kc, :]), writes=[r_sg], chan=ch)
            for (c0, c1, p) in blocks:
                if (c0 < 768) != (part == 0):
                    continue
                o = 0 if part == 0 else 768
                da = RKa[:, kc, c0:c1] if c0 < 768 else LWa[:, kc, c0 - 768:c1 - 768]
                db = RKb[:, kc, c0:c1] if c0 < 768 else LWb[:, kc, c0 - 768:c1 - 768]
                P.emit("dve", lambda e, sg=sg, c0=c0 - o, c1=c1 - o, p=p, kc=kc, db=db: e.tensor_scalar(
                    out=db, in0=sg[:, c0:c1], scalar1=mu[:, kc, p:p + 1], scalar2=None, op0=ALU.mult), reads=[r_sg, r_c], writes=[r_c])
                P.emit("pool", lambda e, sg=sg, c0=c0 - o, c1=c1 - o, p=p, kc=kc, da=da: e.tensor_scalar(
                    out=da, in0=sg[:, c0:c1], scalar1=om[:, kc, p:p + 1], scalar2=None, op0=ALU.mult), reads=[r_sg, r_c], writes=[r_c])

    SC = 128
    hw0 = sbt("hw", [128, KC, 64 + SC + 64], BF16); r_hw0 = Res()
    hsb0 = sbt("hsb", [128, KC, SC], BF16); r_hs0 = Res()
    hw = [hw0 for b in range(nb)]; r_hw = [r_hw0 for _ in range(nb)]
    hsb = [hsb0 for b in range(nb)]; r_hs = [r_hs0 for _ in range(nb)]
    xw = [sbt("xw%d" % b, [96, SC], BF16) for b in range(nb)]
    xa = [sbt("xa%d" % b, [96, 2, SC], BF16) for b in range(nb)]
    xg = [sbt("xg%d" % b, [128, 2, SC], BF16) for b in range(nb)]
    r_x = [Res() for _ in range(nb)]
    rkv = [sbt("rkv%d" % b, [128, 768], F32) for b in range(nb)]; r_rkv = [Res() for _ in range(nb)]
    NSCR = 12
    scr0 = [sbt("scr_%d" % i, [128, 256], F32) for i in range(NSCR)]
    r_scr0 = [Res() for _ in range(NSCR)]
    scr = [scr0 for b in range(nb)]
    r_scr = [r_scr0 for b in range(nb)]
    small = [[sbt("sm%d_%d" % (b, i), [128, 4], F32) for i in range(4)] for b in range(nb)]
    r_small = [[Res() for _ in range(4)] for b in range(nb)]
    opn = ["At", "Rt", "Bt", "Kt", "Bh", "Kh", "Vt"]
    opt = [{n: sbt("%s%d" % (n, b), [128, 256], BF16) for n in opn} for b in range(nb)]
    r_opt = [{n: Res() for n in opn} for b in range(nb)]
    keep = [{n: sbt("kp%s%d" % (n, b), [128, 256], F32) for n in ("kb", "g")} for b in range(nb)]
    r_keep = [Res() for _ in range(nb)]
    gend = [sbt("gend%d" % b, [64, 4], F32) for b in range(nb)]; r_gend = [Res() for _ in range(nb)]
    Yt = [sbt("Y%d" % b, [128, 256], F32) for b in range(nb)]; r_Y = [Res() for _ in range(nb)]
    yfin = [scr0[4] for b in range(nb)]; r_yf = [r_scr0[4] for _ in range(nb)]
    ob = [sbt("ob%d" % b, [128, 256], BF16) for b in range(nb)]; r_ob = [Res() for _ in range(nb)]
    units = [(b, h) for b in range(nb) for h in range(4)]
    U = {}
    for (b, h) in units:
        u = {}
        n = "%d_%d" % (b, h)
        u["fm"] = sbt("fm" + n, [64, 4, 128], BF16); u["r_fm"] = Res()
        u["Mrk"] = sbt("Mrk" + n, [128, 256], BF16)
        u["MkaT"] = sbt("MkaT" + n, [128, 128], BF16)
        u["r_M"] = Res()
        u["T"] = [sbt("T%d" % i + n, [128, 128], F32) for i in range(2)]; u["r_T"] = [Res(), Res()]
        u["Tbf"] = sbt("Tbf" + n, [128, 128], BF16); u["r_Tbf"] = Res()
        u["PP"] = [sbt("PP%d" % i + n, [128, 256], F32) for i in range(2)]; u["r_PP"] = [Res(), Res()]
        u["X"] = sbt("X" + n, [128, 128], BF16); u["Ah"] = sbt("Ah" + n, [64, 128], BF16); u["r_XA"] = Res()
        u["Ut"] = sbt("Ut" + n, [128, 64], BF16); u["r_Ut"] = Res()
        u["S32"] = sbt("S32" + n, [64, 64], F32); u["Sbf"] = sbt("Sbf" + n, [64, 64], BF16); u["r_S"] = Res(); u["r_Sbf"] = Res()
        U[(b, h)] = u

    W0 = lambda z: vecs[:, 0 + z, :]
    A0 = lambda z: vecs[:, 2 + z, :]
    KK_, KA_, RK_, LNW_, LNB_ = vecs[:, 4, :], vecs[:, 5, :], vecs[:, 6, :], vecs[:, 7, :], vecs[:, 8, :]
    hv = lambda t: t.rearrange("p (h j) -> p h j", h=4)
    rr = [0]

    def ew(reads, writes, fn_dve, allow=("dve",)):
        eng = allow[rr[0] % len(allow)]
        rr[0] += 1
        P.emit(eng, fn_dve, reads=reads, writes=writes)

    def _pass(z):
        for (b, h) in units:
            u = U[(b, h)]
            P.emit("pool", lambda e, u=u: e.memset(u["S32"][:], 0.0), writes=[u["r_S"]])
            P.emit("pool", lambda e, u=u: e.memset(u["Sbf"][:], 0.0), writes=[u["r_Sbf"]])
        segs = [("ctx", 0, nctx), ("lat", nctx, nlat)]
        def _seg(sname, soff, slen):
            nsc = slen // SC
            sc_order = range(nsc) if z == 0 else range(nsc - 1, -1, -1)
            def _sc(sci):
                t0 = sci * SC
                for b in range(nb):
                    lo = max(0, t0 - 64); hi = min(slen, t0 + SC + 64)
                    if lo > t0 - 64:
                        P.emit("pool", lambda e, b=b: e.memset(hw[b][:, :, 0:64], 0.0), writes=[r_hw[b]])
                    if hi < t0 + SC + 64:
                        P.emit("pool", lambda e, b=b: e.memset(hw[b][:, :, 64 + SC:], 0.0), writes=[r_hw[b]])
                    for half in range(2):
                        P.emit("sp", lambda e, b=b, lo=lo, hi=hi, half=half: e.dma_start(
                            out=hw[b][:, half * 8:(half + 1) * 8, 64 + lo - t0: 64 + hi - t0],
                            in_=hT[b].rearrange("(c p) t -> p c t", p=128)[:, half * 8:(half + 1) * 8, soff + lo: soff + hi]),
                            reads=[r_hT], writes=[r_hw[b]], chan=pfx + "hw%d_%d" % (b, half))
                    shifts = (-1, 1, -64, 64) if sname == "lat" else (-1, 1, -1, 1)
                    for qd in range(4):
                        sh = shifts[qd]
                        P.emit("pool", lambda e, b=b, qd=qd, sh=sh: e.tensor_copy(
                            out=hsb[b][:, qd * 4:(qd + 1) * 4, :], in_=hw[b][:, qd * 4:(qd + 1) * 4, 64 + sh:64 + sh + SC]),
                            reads=[r_hw[b]], writes=[r_hs[b]])
                    if sname == "lat":
                        P.emit("pool", lambda e, b=b: e.memset(hsb[b][:, 0:4, 0:SC:64], 0.0), writes=[r_hs[b]])
                        P.emit("pool", lambda e, b=b: e.memset(hsb[b][:, 4:8, 63:SC:64], 0.0), writes=[r_hs[b]])
                    groups = [(0 + 96 * z, 96, "w", 0)]
                    if z == 0:
                        groups += [(192, 96, "a", 0)]
                    else:
                        groups += [(192, 96, "a", 0), (288, 96, "a", 1), (384, 128, "g", 0), (512, 128, "g", 1)]
                    for (c0, m, kind, gi) in groups:
                        pt, r_pt, _ = ps.next()
                        k = 0
                        for kc in range(KC):
                            for (wt, src) in ((LWa, hw[b][:, kc, 64:64 + SC]), (LWb, hsb[b][:, kc, :])):
                                P.emit("pe", lambda e, pt=pt, wt=wt, src=src, kc=kc, c0=c0, m=m, k=k: e.matmul(
                                    pt[0:m, 0:SC], lhsT=wt[:, kc, c0:c0 + m], rhs=src, start=(k == 0), stop=(k == 2 * KC - 1)),
                                    reads=[r_c, r_hw[b], r_hs[b]], writes=[r_pt])
                                k += 1
                        if kind == "w":
                            P.emit("act", lambda e, pt=pt, b=b: e.activation(out=xw[b][:, :], in_=pt[0:96, 0:SC], func=AF.Tanh),
                                   reads=[r_pt], writes=[r_x[b]])
                        elif kind == "a":
                            P.emit("act", lambda e, pt=pt, b=b, gi=gi: e.activation(out=xa[b][:, gi, :], in_=pt[0:96, 0:SC], func=AF.Copy),
                                   reads=[r_pt], writes=[r_x[b]])
                        else:
                            P.emit("act", lambda e, pt=pt, b=b, gi=gi: e.activation(out=xg[b][:, gi, :], in_=pt[0:128, 0:SC], func=AF.Sigmoid),
                                   reads=[r_pt], writes=[r_x[b]])
                    pa, r_pa, _ = ps.next()
                    pb_, r_pb, _ = ps.next()
                    k = 0
                    for kc in range(KC):
                        for (src, wt) in ((hw[b][:, kc, 64 + 0 * 128:64 + (0 + 1) * 128], RKa), (hsb[b][:, kc, 0 * 128:(0 + 1) * 128], RKb)):
                            P.emit("pe", lambda e, pa=pa, src=src, wt=wt, kc=kc, k=k: e.matmul(
                                pa[:, 0:512], lhsT=src, rhs=wt[:, kc, 0:512], start=(k == 0), stop=(k == 2 * KC - 1)),
                                reads=[r_c, r_hw[b], r_hs[b]], writes=[r_pa])
                            P.emit("pe", lambda e, pb_=pb_, src=src, wt=wt, kc=kc, k=k: e.matmul(
                                pb_[:, 0:256], lhsT=src, rhs=wt[:, kc, 512:768], start=(k == 0), stop=(k == 2 * KC - 1)),
                                reads=[r_c, r_hw[b], r_hs[b]], writes=[r_pb])
                            k += 1
                    P.emit("act", lambda e, b=b, pa=pa: e.activation(out=rkv[b][:, 0:512], in_=pa[:, 0:512], func=AF.Copy),
                           reads=[r_pa], writes=[r_rkv[b]])
                    P.emit("act", lambda e, b=b, pb_=pb_: e.activation(out=rkv[b][:, 512:768], in_=pb_[:, 0:256], func=AF.Copy),
                           reads=[r_pb], writes=[r_rkv[b]])
                ch_order = range(SC // 128) if z == 0 else range(SC // 128 - 1, -1, -1)
                def _ch(ci):
                    tok0 = t0 + ci * 128
                    for b in range(nb):
                        S_ = scr[b]; RS = r_scr[b]
                        r_t, k_t, v_t = rkv[b][:, 0:256], rkv[b][:, 256:512], rkv[b][:, 512:768]
                        cs = slice(ci * 128, (ci + 1) * 128)
                        pw, r_pw, _ = ps.next()
                        P.emit("pe", lambda e, pw=pw, b=b, cs=cs: e.matmul(pw[:, 0:256], lhsT=xw[b][:, cs], rhs=W2s[:, z, :], start=True, stop=True),
                               reads=[r_x[b], r_c], writes=[r_pw])
                        P.emit("dve", lambda e, pw=pw, b=b: e.tensor_tensor(out=S_[0][:], in0=pw[:, 0:256], in1=W0(z), op=ALU.add),
                               reads=[r_pw, r_c], writes=[RS[0]])
                        P.emit("act", lambda e, b=b: e.activation(out=S_[0][:], in_=S_[0][:], func=AF.Sigmoid), reads=[RS[0]], writes=[RS[0]])
                        zs = [z] if z == 0 else [1, 0]
                        for ai, za in enumerate(zs):
                            pw, r_pw, _ = ps.next()
                            P.emit("pe", lambda e, pw=pw, b=b, cs=cs, za=za: e.matmul(pw[:, 0:256], lhsT=xa[b][:, za, cs], rhs=A2s[:, za, :], start=True, stop=True),
                                   reads=[r_x[b], r_c], writes=[r_pw])
                            P.emit("dve", lambda e, pw=pw, b=b, ai=ai, za=za: e.tensor_tensor(out=S_[1 + ai][:], in0=pw[:, 0:256], in1=A0(za), op=ALU.add),
                                   reads=[r_pw, r_c], writes=[RS[1 + ai]])
                            P.emit("act", lambda e, b=b, ai=ai: e.activation(out=S_[1 + ai][:], in_=S_[1 + ai][:], func=AF.Sigmoid),
                                   reads=[RS[1 + ai]], writes=[RS[1 + ai]])
                        if z == 1:
                            pw, r_pw, _ = ps.next()
                            for kc2 in range(2):
                                P.emit("pe", lambda e, pw=pw, b=b, cs=cs, kc2=kc2: e.matmul(pw[:, 0:256], lhsT=xg[b][:, kc2, cs], rhs=G2s[:, kc2, :],
                                                                                           start=(kc2 == 0), stop=(kc2 == 1)), reads=[r_x[b], r_c], writes=[r_pw])
                            P.emit("act", lambda e, pw=pw, b=b: e.activation(out=keep[b]["g"][:], in_=pw[:, 0:256], func=AF.Copy),
                                   reads=[r_pw], writes=[r_keep[b]])
                        ew([r_rkv[b], r_c], [RS[3]], lambda e, b=b, k_t=k_t: e.tensor_tensor(out=S_[3][:], in0=k_t, in1=KK_, op=ALU.mult))
                        ew([RS[3]], [RS[4]], lambda e, b=b: e.tensor_tensor(out=S_[4][:], in0=S_[3][:], in1=S_[3][:], op=ALU.mult))
                        P.emit("dve", lambda e, b=b: e.tensor_reduce(out=small[b][0][:], in_=hv(S_[4][:]), axis=AX.X, op=ALU.add),
                               reads=[RS[4]], writes=[r_small[b][0]])
                        P.emit("dve", lambda e, b=b: e.tensor_scalar(out=small[b][0][:], in0=small[b][0][:], scalar1=1e-24, scalar2=None, op0=ALU.max),
                               reads=[r_small[b][0]], writes=[r_small[b][0]])
                        P.emit("act", lambda e, b=b: e.activation(out=small[b][0][:], in_=small[b][0][:], func=AF.Sqrt),
                               reads=[r_small[b][0]], writes=[r_small[b][0]])
                        P.emit("dve", lambda e, b=b: e.reciprocal(out=small[b][0][:], in_=small[b][0][:]),
                               reads=[r_small[b][0]], writes=[r_small[b][0]])
                        ew([RS[3], r_small[b][0]], [RS[3]], lambda e, b=b: e.tensor_tensor(
                            out=hv(S_[3][:]), in0=hv(S_[3][:]), in1=small[b][0][:].unsqueeze(2).to_broadcast([128, 4, 64]), op=ALU.mult))
                        pl, r_pl, _ = ps.next()
                        pe2, r_pe2, _ = ps.next()
                        P.emit("pe", lambda e, pl=pl, b=b: e.matmul(pl[:, 0:256], lhsT=ctri[:, z, 0, :], rhs=S_[0][:], start=True, stop=True),
                               reads=[RS[0], r_c], writes=[r_pl])
                        P.emit("pe", lambda e, pl=pl, b=b: e.matmul(pl[:, 256:512], lhsT=ctri[:, z, 1, :], rhs=S_[0][:], start=True, stop=True),
                               reads=[RS[0], r_c], writes=[r_pl])
                        P.emit("pe", lambda e, pe2=pe2, b=b: e.matmul(pe2[:, 0:256], lhsT=ctri[:, z, 2, :], rhs=S_[0][:], start=True, stop=True),
                               reads=[RS[0], r_c], writes=[r_pe2])
                        for hh in range(4):
                            P.emit("pe", lambda e, pe2=pe2, b=b, hh=hh: e.matmul(pe2[0:64, 256 + hh:257 + hh], lhsT=S_[0][:, hh * 64:(hh + 1) * 64], rhs=negcol[:, 0:1],
                                                                                 start=True, stop=True), reads=[RS[0], r_c], writes=[r_pe2])
                        P.emit("act", lambda e, pl=pl, b=b: e.activation(out=S_[5][:], in_=pl[:, 0:256], func=AF.Exp), reads=[r_pl], writes=[RS[5]])
                        P.emit("act", lambda e, pl=pl, b=b: e.activation(out=S_[6][:], in_=pl[:, 0:256], func=AF.Exp, scale=-1.0), reads=[r_pl], writes=[RS[6]])
                        P.emit("act", lambda e, pl=pl, b=b: e.activation(out=S_[7][:], in_=pl[:, 256:512], func=AF.Exp), reads=[r_pl], writes=[RS[7]])
                        P.emit("act", lambda e, pe2=pe2, b=b: e.activation(out=S_[8][:], in_=pe2[:, 0:256], func=AF.Exp), reads=[r_pe2], writes=[RS[8]])
                        P.emit("act", lambda e, pe2=pe2, b=b: e.activation(out=gend[b][:], in_=pe2[0:64, 256:260], func=AF.Exp), reads=[r_pe2], writes=[r_gend[b]])
                        O_ = opt[b]; RO = r_opt[b]
                        ew([RS[3], RS[7]], [RO["At"]], lambda e, b=b: e.scalar_tensor_tensor(
                            out=O_["At"][:], in0=S_[3][:], scalar=-1.0, in1=S_[7][:], op0=ALU.mult, op1=ALU.mult), allow=("dve",))
                        ew([r_rkv[b], RS[5]], [RO["Rt"]], lambda e, b=b, r_t=r_t: e.tensor_tensor(out=O_["Rt"][:], in0=r_t, in1=S_[5][:], op=ALU.mult))
                        ew([r_rkv[b]], [RO["Vt"]], lambda e, b=b, v_t=v_t: e.tensor_copy(out=O_["Vt"][:], in_=v_t))
                        ew([RS[3], RS[1]], [RS[9]], lambda e, b=b: e.tensor_tensor(out=S_[9][:], in0=S_[3][:], in1=S_[1][:], op=ALU.mult))
                        ew([RS[9], RS[6]], [RO["Bt"]], lambda e, b=b: e.tensor_tensor(out=O_["Bt"][:], in0=S_[9][:], in1=S_[6][:], op=ALU.mult))
                        ew([RS[1], r_c], [RS[10]], lambda e, b=b: e.scalar_tensor_tensor(
                            out=S_[10][:], in0=S_[1][:], scalar=-1.0, in1=KA_, op0=ALU.add, op1=ALU.mult), allow=("dve",))
                        ew([RS[10], r_rkv[b]], [RS[10]], lambda e, b=b, k_t=k_t: e.scalar_tensor_tensor(
                            out=S_[10][:], in0=S_[10][:], scalar=1.0, in1=k_t, op0=ALU.add, op1=ALU.mult), allow=("dve",))
                        ew([RS[10], RS[6]], [RO["Kt"]], lambda e, b=b: e.tensor_tensor(out=O_["Kt"][:], in0=S_[10][:], in1=S_[6][:], op=ALU.mult))
                        ew([RO["Bt"], RS[8]], [RO["Bh"]], lambda e, b=b: e.tensor_tensor(out=O_["Bh"][:], in0=O_["Bt"][:], in1=S_[8][:], op=ALU.mult))
                        ew([RO["Kt"], RS[8]], [RO["Kh"]], lambda e, b=b: e.tensor_tensor(out=O_["Kh"][:], in0=O_["Kt"][:], in1=S_[8][:], op=ALU.mult))
                        if z == 1 and sname == "lat":
                            ew([RS[2], r_c], [RS[11]], lambda e, b=b: e.scalar_tensor_tensor(
                                out=S_[11][:], in0=S_[2][:], scalar=-1.0, in1=KA_, op0=ALU.add, op1=ALU.mult), allow=("dve",))
                            ew([RS[11], r_rkv[b]], [RS[11]], lambda e, b=b, k_t=k_t: e.scalar_tensor_tensor(
                                out=S_[11][:], in0=S_[11][:], scalar=1.0, in1=k_t, op0=ALU.add, op1=ALU.mult), allow=("dve",))
                            ew([RS[11], RS[10]], [RS[11]], lambda e, b=b: e.tensor_tensor(out=S_[11][:], in0=S_[11][:], in1=S_[10][:], op=ALU.add))
                            ew([RS[11]], [r_keep[b]], lambda e, b=b: e.tensor_scalar(out=keep[b]["kb"][:], in0=S_[11][:], scalar1=0.5, scalar2=None, op0=ALU.mult))
                    if dbg is not None and (z, sname, sci, ci) == dbg_at:
                        b = 0
                        dbg("rkv", rkv[b][:], [r_rkv[b]])
                        for i in (0, 1, 3, 5, 6, 7, 8, 9, 10):
                            dbg("s%d" % i, scr[b][i][:], [r_scr[b][i]])
                        for nme in opn:
                            dbg(nme, opt[b][nme][:], [r_opt[b][nme]])
                        dbg("gend", gend[b][:], [r_gend[b]])
                        dbg("hw", hw[b][:], [r_hw[b]])
                        dbg("hsb", hsb[b][:], [r_hs[b]])
                        dbg("xw", xw[b][:], [r_x[b]])
                        dbg("xa", xa[b][:], [r_x[b]])
                    for (b, h) in units:
                        u = U[(b, h)]; O_ = opt[b]; RO = r_opt[b]
                        hs_ = slice(h * 64, (h + 1) * 64)
                        pt, r_pt, _ = ps.next()
                        ptb = pt[:].bitcast(BF16)
                        for i, nme in enumerate(("At", "Rt", "Bt", "Kt")):
                            P.emit("pe", lambda e, ptb=ptb, i=i, nme=nme, b=b, hs_=hs_: e.transpose(
                                ptb[0:64, i * 128:(i + 1) * 128], O_[nme][:, hs_], ident[:]), reads=[RO[nme], r_c], writes=[r_pt])
                        P.emit("act", lambda e, ptb=ptb, u=u: e.activation(out=u["fm"][:].rearrange("p a t -> p (a t)"), in_=ptb[0:64, 0:512], func=AF.Copy),
                               reads=[r_pt], writes=[u["r_fm"]])
                        fm = u["fm"]
                        p1, r_p1, _ = ps.next()
                        p3, r_p3, _ = ps.next()
                        P.emit("pe", lambda e, p1=p1, fm=fm: e.matmul(p1[:, 0:256], lhsT=fm[:, 2, :], rhs=fm[:, 0:2, :].rearrange("p a t -> p (a t)"), start=True, stop=True),
                               reads=[u["r_fm"]], writes=[r_p1])
                        P.emit("pe", lambda e, p1=p1, fm=fm: e.matmul(p1[:, 256:384], lhsT=fm[:, 3, :], rhs=fm[:, 1, :], start=True, stop=True),
                               reads=[u["r_fm"]], writes=[r_p1])
                        P.emit("pe", lambda e, p3=p3, fm=fm: e.matmul(p3[:, 0:256], lhsT=fm[:, 0, :], rhs=fm[:, 2:4, :].rearrange("p a t -> p (a t)"), start=True, stop=True),
                               reads=[u["r_fm"]], writes=[r_p3])
                        P.emit("dve", lambda e, p1=p1, u=u: e.tensor_tensor(out=u["Mrk"][:], in0=p1[:, 128:384], in1=cmask[:, z, 2, :], op=ALU.mult),
                               reads=[r_p1, r_c], writes=[u["r_M"]])
                        P.emit("dve", lambda e, p3=p3, u=u: e.tensor_tensor(out=u["MkaT"][:], in0=p3[:, 128:256], in1=cmask[:, z, 1, 128:256], op=ALU.mult),
                               reads=[r_p3, r_c], writes=[u["r_M"]])
                        P.emit("dve", lambda e, p1=p1, u=u: e.tensor_tensor(out=u["PP"][1][:, 0:128], in0=p1[:, 0:128], in1=cmask[:, z, 0, 0:128], op=ALU.mult),
                               reads=[r_p1, r_c], writes=[u["r_PP"][1]])
                        P.emit("dve", lambda e, p3=p3, u=u: e.tensor_tensor(out=u["PP"][1][:, 128:256], in0=p3[:, 0:128], in1=cmask[:, z, 1, 0:128], op=ALU.mult),
                               reads=[r_p3, r_c], writes=[u["r_PP"][1]])
                        P.emit("dve", lambda e, u=u: e.tensor_tensor(out=u["T"][0][:], in0=u["PP"][1][:, 0:128], in1=ident[:], op=ALU.add),
                               reads=[u["r_PP"][1], r_c], writes=[u["r_T"][0]])
                    for kk_ in range(1, 7):
                        for (b, h) in units:
                            u = U[(b, h)]
                            Pm, PTm, rd = u["PP"][kk_ % 2][:, 0:128], u["PP"][kk_ % 2][:, 128:256], u["r_PP"][kk_ % 2]
                            dst, r_dst = u["PP"][(kk_ + 1) % 2], u["r_PP"][(kk_ + 1) % 2]
                            pp, r_pp, _ = ps.next()
                            P.emit("pe", lambda e, pp=pp, Pm=Pm, PTm=PTm: e.matmul(pp[:, 128:256], lhsT=Pm, rhs=PTm, start=True, stop=True),
                                   reads=[rd], writes=[r_pp])
                            if kk_ < 6:
                                P.emit("pe", lambda e, pp=pp, Pm=Pm, PTm=PTm: e.matmul(pp[:, 0:128], lhsT=PTm, rhs=Pm, start=True, stop=True),
                                       reads=[rd], writes=[r_pp])
                                P.emit("act", lambda e, pp=pp, dst=dst: e.activation(out=dst[:], in_=pp[:, 0:256], func=AF.Copy), reads=[r_pp], writes=[r_dst])
                            else:
                                P.emit("act", lambda e, pp=pp, dst=dst: e.activation(out=dst[:, 128:256], in_=pp[:, 128:256], func=AF.Copy), reads=[r_pp], writes=[r_dst])
                        for (b, h) in units:
                            u = U[(b, h)]
                            PTk, r_ptk = u["PP"][(kk_ + 1) % 2][:, 128:256], u["r_PP"][(kk_ + 1) % 2]
                            Told, r_told = u["T"][(kk_ - 1) % 2], u["r_T"][(kk_ - 1) % 2]
                            pt, r_pt, _ = ps.next()
                            P.emit("pe", lambda e, pt=pt, PTk=PTk, Told=Told: e.matmul(pt[:, 0:128], lhsT=PTk, rhs=Told[:], start=True, stop=True),
                                   reads=[r_ptk, r_told], writes=[r_pt])
                            if kk_ < 6:
                                Tnew, r_tnew = u["T"][kk_ % 2], u["r_T"][kk_ % 2]
                            else:
                                Tnew, r_tnew = u["Tbf"], u["r_Tbf"]
                            P.emit("dve", lambda e, pt=pt, Tnew=Tnew, Told=Told: e.tensor_tensor(out=Tnew[:], in0=pt[:, 0:128], in1=Told[:], op=ALU.add),
                                   reads=[r_pt, r_told], writes=[r_tnew])
                    for (b, h) in units:
                        u = U[(b, h)]; O_ = opt[b]; RO = r_opt[b]
                        hs_ = slice(h * 64, (h + 1) * 64)
                        Tf, r_tf = u["Tbf"], u["r_Tbf"]
                        px, r_px, _ = ps.next()
                        P.emit("pe", lambda e, px=px, u=u, Tf=Tf: e.matmul(px[:, 0:128], lhsT=u["MkaT"][:], rhs=Tf[:], start=True, stop=True),
                               reads=[u["r_M"], r_tf], writes=[r_px])
                        P.emit("pe", lambda e, px=px, b=b, hs_=hs_, Tf=Tf: e.matmul(px[0:64, 128:256], lhsT=O_["At"][:, hs_], rhs=Tf[:], start=True, stop=True),
                               reads=[RO["At"], r_tf], writes=[r_px])
                        P.emit("act", lambda e, px=px, u=u: e.activation(out=u["X"][:], in_=px[:, 0:128], func=AF.Copy), reads=[r_px], writes=[u["r_XA"]])
                        P.emit("dve", lambda e, px=px, u=u: e.tensor_copy(out=u["Ah"][:], in_=px[0:64, 128:256]), reads=[r_px], writes=[u["r_XA"]])
                    for (b, h) in units:
                        u = U[(b, h)]; O_ = opt[b]; RO = r_opt[b]
                        hs_ = slice(h * 64, (h + 1) * 64)
                        pu, r_pu, _ = ps.next()
                        P.emit("pe", lambda e, pu=pu, u=u: e.matmul(pu[:, 0:64], lhsT=u["Ah"][:], rhs=u["Sbf"][:], start=True, stop=False),
                               reads=[u["r_XA"], u["r_Sbf"]], writes=[r_pu])
                        P.emit("pe", lambda e, pu=pu, u=u, b=b, hs_=hs_: e.matmul(pu[:, 0:64], lhsT=u["X"][:], rhs=O_["Vt"][:, hs_], start=False, stop=True),
                               reads=[u["r_XA"], RO["Vt"]], writes=[r_pu])
                        P.emit("act", lambda e, pu=pu, u=u: e.activation(out=u["Ut"][:], in_=pu[:, 0:64], func=AF.Copy), reads=[r_pu], writes=[u["r_Ut"]])
                    for (b, h) in units:
                        u = U[(b, h)]; O_ = opt[b]; RO = r_opt[b]
                        hs_ = slice(h * 64, (h + 1) * 64)
                        py, r_py, _ = ps.next()
                        P.emit("pe", lambda e, py=py, u=u: e.matmul(py[:, 0:64], lhsT=u["fm"][:, 1, :], rhs=u["Sbf"][:], start=True, stop=False),
                               reads=[u["r_fm"], u["r_Sbf"]], writes=[r_py])
                        P.emit("pe", lambda e, py=py, u=u: e.matmul(py[:, 0:64], lhsT=u["Mrk"][:, 0:128], rhs=u["Ut"][:], start=False, stop=False),
                               reads=[u["r_M"], u["r_Ut"]], writes=[r_py])
                        P.emit("pe", lambda e, py=py, u=u, b=b, hs_=hs_: e.matmul(py[:, 0:64], lhsT=u["Mrk"][:, 128:256], rhs=O_["Vt"][:, hs_], start=False, stop=True),
                               reads=[u["r_M"], RO["Vt"]], writes=[r_py])
                        P.emit("act", lambda e, py=py, b=b, hs_=hs_: e.activation(out=Yt[b][:, hs_], in_=py[:, 0:64], func=AF.Copy), reads=[r_py], writes=[r_Y[b]])
                        pss, r_pss, _ = ps.next()
                        P.emit("pe", lambda e, pss=pss, u=u, b=b, hs_=hs_: e.matmul(pss[0:64, 0:64], lhsT=O_["Bh"][:, hs_], rhs=u["Ut"][:], start=True, stop=False),
                               reads=[RO["Bh"], u["r_Ut"]], writes=[r_pss])
                        P.emit("pe", lambda e, pss=pss, u=u, b=b, hs_=hs_: e.matmul(pss[0:64, 0:64], lhsT=O_["Kh"][:, hs_], rhs=O_["Vt"][:, hs_], start=False, stop=True),
                               reads=[RO["Kh"], RO["Vt"]], writes=[r_pss])
                        P.emit("dve", lambda e, pss=pss, u=u, b=b, h=h: e.scalar_tensor_tensor(
                            out=u["S32"][:], in0=u["S32"][:], scalar=gend[b][:, h:h + 1], in1=pss[0:64, 0:64], op0=ALU.mult, op1=ALU.add),
                            reads=[r_pss, r_gend[b], u["r_S"]], writes=[u["r_S"]])
                        P.emit("pool", lambda e, u=u: e.tensor_copy(out=u["Sbf"][:], in_=u["S32"][:]), reads=[u["r_S"]], writes=[u["r_Sbf"]])
                    if dbg is not None and (z, sname, sci, ci) == dbg_at:
                        u = U[(0, 0)]
                        dbg("fm", u["fm"][:], [u["r_fm"]])
                        dbg("Mrk", u["Mrk"][:], [u["r_M"]])
                        dbg("T", u["Tbf"][:], [u["r_Tbf"]])
                        dbg("X", u["X"][:], [u["r_XA"]]); dbg("Ah", u["Ah"][:], [u["r_XA"]])
                        dbg("Ut", u["Ut"][:], [u["r_Ut"]])
                        dbg("Y", Yt[0][:], [r_Y[0]])
                        dbg("S32", u["S32"][:], [u["r_S"]])
                    if sname != "lat":
                        return
                    for b in range(nb):
                        if z == 0:
                            P.emit("sp", lambda e, b=b, tok0=tok0: e.dma_start(out=yscr[b, tok0:tok0 + 128, :], in_=Yt[b][:]),
                                   reads=[r_Y[b]], writes=[r_out], chan=pfx + "ys%d" % b)
                            continue
                        S_ = scr[b]; RS = r_scr[b]; K_ = keep[b]
                        P.emit("sp", lambda e, b=b, tok0=tok0: e.dma_start(out=yfin[b][:], in_=yscr[b, tok0:tok0 + 128, :]),
                               reads=[r_out], writes=[r_yf[b]], chan=pfx + "yl%d" % b)
                        ew([r_Y[b], r_yf[b]], [RS[0]], lambda e, b=b: e.tensor_tensor(out=S_[0][:], in0=Yt[b][:], in1=yfin[b][:], op=ALU.add))
                        P.emit("dve", lambda e, b=b: e.tensor_reduce(out=small[b][1][:], in_=hv(S_[0][:]), axis=AX.X, op=ALU.add),
                               reads=[RS[0]], writes=[r_small[b][1]])
                        P.emit("dve", lambda e, b=b: e.tensor_scalar(out=small[b][1][:], in0=small[b][1][:], scalar1=1.0 / 64, scalar2=None, op0=ALU.mult),
                               reads=[r_small[b][1]], writes=[r_small[b][1]])
                        ew([RS[0], r_small[b][1]], [RS[1]], lambda e, b=b: e.tensor_tensor(
                            out=hv(S_[1][:]), in0=hv(S_[0][:]), in1=small[b][1][:].unsqueeze(2).to_broadcast([128, 4, 64]), op=ALU.subtract))
                        ew([RS[1]], [RS[2]], lambda e, b=b: e.tensor_tensor(out=S_[2][:], in0=S_[1][:], in1=S_[1][:], op=ALU.mult))
                        P.emit("dve", lambda e, b=b: e.tensor_reduce(out=small[b][2][:], in_=hv(S_[2][:]), axis=AX.X, op=ALU.add),
                               reads=[RS[2]], writes=[r_small[b][2]])
                        P.emit("act", lambda e, b=b: e.activation(out=small[b][2][:], in_=small[b][2][:], func=AF.Sqrt, scale=1.0 / 64, bias=gneps[:, 0:1]),
                               reads=[r_small[b][2], r_c], writes=[r_small[b][2]])
                        P.emit("dve", lambda e, b=b: e.reciprocal(out=small[b][2][:], in_=small[b][2][:]), reads=[r_small[b][2]], writes=[r_small[b][2]])
                        ew([RS[1], r_small[b][2]], [RS[1]], lambda e, b=b: e.tensor_tensor(
                            out=hv(S_[1][:]), in0=hv(S_[1][:]), in1=small[b][2][:].unsqueeze(2).to_broadcast([128, 4, 64]), op=ALU.mult))
                        ew([RS[1], r_c], [RS[1]], lambda e, b=b: e.tensor_tensor(out=S_[1][:], in0=S_[1][:], in1=LNW_, op=ALU.mult))
                        ew([RS[1], r_c], [RS[1]], lambda e, b=b: e.tensor_tensor(out=S_[1][:], in0=S_[1][:], in1=LNB_, op=ALU.add))
                        ew([r_keep[b], r_rkv[b]], [RS[3]], lambda e, b=b: e.tensor_tensor(out=S_[3][:], in0=rkv[b][:, 0:256], in1=K_["kb"][:], op=ALU.mult))
                        ew([RS[3], r_c], [RS[3]], lambda e, b=b: e.tensor_tensor(out=S_[3][:], in0=S_[3][:], in1=RK_, op=ALU.mult))
                        P.emit("dve", lambda e, b=b: e.tensor_reduce(out=small[b][3][:], in_=hv(S_[3][:]), axis=AX.X, op=ALU.add),
                               reads=[RS[3]], writes=[r_small[b][3]])
                        ew([r_rkv[b], r_small[b][3]], [RS[3]], lambda e, b=b: e.tensor_tensor(
                            out=hv(S_[3][:]), in0=hv(rkv[b][:, 512:768]), in1=small[b][3][:].unsqueeze(2).to_broadcast([128, 4, 64]), op=ALU.mult))
                        ew([RS[1], RS[3]], [RS[1]], lambda e, b=b: e.tensor_tensor(out=S_[1][:], in0=S_[1][:], in1=S_[3][:], op=ALU.add))
                        ew([RS[1], r_keep[b]], [r_ob[b]], lambda e, b=b: e.tensor_tensor(out=ob[b][:], in0=S_[1][:], in1=K_["g"][:], op=ALU.mult))
                        P.emit("sp", lambda e, b=b, tok0=tok0: e.dma_start(out=o_out[b, tok0:tok0 + 128, :], in_=ob[b][:]),
                               reads=[r_ob[b]], writes=[r_out], chan=pfx + "oo%d" % b)
                for ci in ch_order:
                    _ch(ci)
            for sci in sc_order:
                _sc(sci)
        for seg in segs:
            _seg(*seg)
    for z in range(2):
        _pass(z)


def build_l45(nlat=SEQ, nctx=CTX, nb=2, debug=False):
    nc = bass.Bass("TRN2", target_bir_lowering=False)
    dt = lambda name, shape, dty, kind: nc.dram_tensor(name, shape, dty, kind=kind).ap()
    hT = dt("hT", [nb, D, nctx + nlat], BF16, "ExternalInput")
    wd = {"c_mask": dt("c_mask", [128, 2, 3, 256], BF16, "ExternalInput"), "c_tri": dt("c_tri", [128, 2, 3, 128], F32, "ExternalInput"),
          "c_ident": dt("c_ident", [128, 128], BF16, "ExternalInput"), "vecs": dt("vecs", [128, 9, 256], F32, "ExternalInput"),
          "mu": dt("mu", [128, KC, 6], F32, "ExternalInput"), "w2": dt("w2", [96, 2, 256], F32, "ExternalInput"),
          "a2": dt("a2", [96, 2, 256], F32, "ExternalInput"), "g2": dt("g2", [128, 2, 256], F32, "ExternalInput"),
          "rkv": dt("rkv", [128, KC, 768], F32, "ExternalInput"), "lw": dt("lw", [128, KC, 640], F32, "ExternalInput")}
    yscr = dt("yscr", [nb, nlat, 256], F32, "ExternalOutput")
    o_out = dt("o_out", [nb, nlat, 256], BF16, "ExternalOutput")
    with contextlib.ExitStack() as st:
        P = Prog(nc)
        dbg = None
        if debug:
            def dbg(name, ap, reads):
                t = nc.dram_tensor("dbg_" + name, list(ap.shape), ap.dtype, kind="ExternalOutput").ap()
                P.emit("sp", lambda e: e.dma_start(out=t, in_=ap), reads=reads, chan="dbg_" + name)
        emit_rwkv(nc, P, st, hT, Res(), wd, yscr, o_out, Res(), nlat=nlat, nctx=nctx, nb=nb, dbg=dbg)
        P.run(final_waits=_all_dma_tails(P))
    return nc


def rwkv_host_weights(inp, g):
    cs = slice(256 * g, 256 * (g + 1))
    rkv = np.concatenate([inp["rwkv_w_rkv"][0, i][:, cs] for i in range(3)], axis=1)
    lw = np.concatenate([inp["rwkv_w1"][0, 0], inp["rwkv_w1"][0, 1], inp["rwkv_a1"][0, 0], inp["rwkv_a1"][0, 1], inp["rwkv_g1"][0]], axis=1)
    vec = np.stack([inp["rwkv_w0"][0, 0][cs], inp["rwkv_w0"][0, 1][cs], inp["rwkv_a0"][0, 0][cs], inp["rwkv_a0"][0, 1][cs],
                    inp["rwkv_k_k"][0][cs], inp["rwkv_k_a"][0][cs], inp["rwkv_r_k"][0].reshape(-1)[cs], inp["rwkv_ln_w"][0][cs], inp["rwkv_ln_b"][0][cs]])
    return {"rkv": lay_sq(rkv), "lw": lay_sq(lw),
            "vecs": np.ascontiguousarray(np.broadcast_to(vec[None], (128, 9, 256))).astype(np.float32),
            "mu": np.ascontiguousarray(lay_vec(inp["rwkv_mu"][0]).transpose(0, 2, 1)),
            "w2": np.ascontiguousarray(inp["rwkv_w2"][0][:, :, cs].transpose(1, 0, 2)),
            "a2": np.ascontiguousarray(inp["rwkv_a2"][0][:, :, cs].transpose(1, 0, 2)),
            "g2": np.ascontiguousarray(inp["rwkv_g2"][0][:, cs].reshape(2, 128, 256).transpose(1, 0, 2))}


def lay_wo(w):
    return np.ascontiguousarray(w.reshape(KC, 128, 8, 256).transpose(2, 1, 0, 3))


def build_l3(nlat, nctx):
    NT = nlat + nctx
    nc = bass.Bass("TRN2", target_bir_lowering=False)
    dt = lambda name, shape, dty, kind="ExternalInput": nc.dram_tensor(name, shape, dty, kind=kind).ap()
    X1 = dt("X1", [D, NT], F32)
    fT = dt("fT", [D, NT], BF16)
    modo = dt("modo", [128, 2 * 144 * 3], F32); gso = dt("gso", [128, 2 * 3 * KC * 3], F32); hgo = dt("hgo", [128, 2 * 3 * KC * 3], F32)
    wo = dt("wo", [8, 128, KC, 256], F32)
    bo = dt("bo", [128, KC], F32)
    w13a = dt("w13a", [JC, 128, KC, 256], F32); w2a = dt("w2a", [KC, 128, JC, 128], F32)
    w13b = dt("w13b", [JC, 128, KC, 256], F32); w2b = dt("w2b", [KC, 128, JC, 128], F32)
    X2 = dt("X2", [D, NT], F32, "Internal"); X3 = dt("X3", [D, NT], F32, "Internal")
    X4 = dt("X4", [D, NT], F32, "ExternalOutput")
    h1 = dt("h1", [D, NT], BF16, "ExternalOutput")
    with contextlib.ExitStack() as st:
        P = Prog(nc)
        dn = Dense(nc, P, st, 768, 0)
        dn.mod_load(modo, gso, hgo)
        bos = st.enter_context(nc.sbuf_tensor("sb_bos", [128, KC], F32))
        P.emit("sp", lambda e: e.dma_start(out=bos[:], in_=bo), writes=[dn.r_mod], chan="bo")
        r_in = Res()
        for blocks in make_passes(nlat, nctx, 0):
            r2, r3, r4, rh = Res(), Res(), Res(), Res()
            dn.linear_res(blocks, fT, r_in, wo, bos, X1, r_in, X2, r2, 0)
            dn.norm_mod(blocks, X2, r2, 0, 2)
            dn.ffn(blocks, w13a, w2a, X2, r2, X3, r3, 0, 2)
            dn.norm_mod(blocks, X3, r3, 1, 0)
            dn.ffn(blocks, w13b, w2b, X3, r3, X4, r4, 1, 0)
            dn.norm_mod(blocks, X4, r4, 1, 1, out_dram=h1, r_out=rh)
        P.run(final_waits=_all_dma_tails(P))
    return nc


def build_l6(nlat):
    NT = nlat
    nc = bass.Bass("TRN2", target_bir_lowering=False)
    dt = lambda name, shape, dty, kind="ExternalInput": nc.dram_tensor(name, shape, dty, kind=kind).ap()
    X4 = dt("X4", [D, NT], F32)
    oT = dt("oT", [D, NT], BF16)
    modo = dt("modo", [128, 2 * 144 * 3], F32); gso = dt("gso", [128, 2 * 3 * KC * 3], F32); hgo = dt("hgo", [128, 2 * 3 * KC * 3], F32)
    wo = dt("wo", [8, 128, KC, 256], F32)
    fng = dt("fng", [128, KC], F32)
    w13 = dt("w13", [JC, 128, KC, 256], F32); w2 = dt("w2", [KC, 128, JC, 128], F32)
    X5 = dt("X5", [D, NT], F32, "Internal"); X6 = dt("X6", [D, NT], F32, "Internal")
    out = dt("out", [D, NT], F32, "ExternalOutput")
    with contextlib.ExitStack() as st:
        P = Prog(nc)
        dn = Dense(nc, P, st, 768, 0)
        dn.mod_load(modo, gso, hgo)
        fgs = st.enter_context(nc.sbuf_tensor("sb_fgs", [128, KC], F32))
        P.emit("sp", lambda e: e.dma_start(out=fgs[:], in_=fng), writes=[dn.r_mod], chan="fg")
        r_in = Res()
        for blocks in make_passes(nlat, 0, 0):
            r5, r6, ro = Res(), Res(), Res()
            dn.linear_res(blocks, oT, r_in, wo, None, X4, r_in, X5, r5, 1)
            dn.norm_mod(blocks, X5, r5, 1, 2)
            dn.ffn(blocks, w13, w2, X5, r5, X6, r6, 1, 2)
            dn.norm_mod(blocks, X6, r6, 0, 0, out_dram=out, r_out=ro, final_g=fgs)
        P.run(final_waits=_all_dma_tails(P))
    return nc


_DBG = {}


def _run(nc, maps):
    res = run_bass_kernel_spmd(nc, maps, core_ids=list(range(NCORES)))
    return res.results


def kernel(x, c, ctx, c_ctx, mod_w, mod_b, norm_w, ffn_w13, ffn_w2, fnet_w_o, fnet_b_o,
           rwkv_mu, rwkv_w_rkv, rwkv_w0, rwkv_w1, rwkv_w2, rwkv_a0, rwkv_a1, rwkv_a2,
           rwkv_g1, rwkv_g2, rwkv_k_k, rwkv_k_a, rwkv_r_k, rwkv_ln_w, rwkv_ln_b, rwkv_w_o,
           final_norm_w):
    f32 = np.float32
    A = lambda a: np.asarray(a, dtype=f32)
    x, c, ctx, c_ctx = A(x), A(c), A(ctx), A(c_ctx)
    B, L, _ = x.shape
    NL = L // 4
    NCX = CTX // 4
    NT = NL + NCX
    sT = np.ascontiguousarray(lay_vec(np.stack([c[0], c[1], c_ctx])).transpose(0, 2, 1))
    mod_w = A(mod_w); mod_b = A(mod_b)
    maps = []
    for core in range(NCORES):
        cs = slice(core * 2304, (core + 1) * 2304)
        maps.append({"sT": sT,
                     "modw": np.ascontiguousarray(mod_w[:, :, cs].reshape(2, KC, 128, 2304).transpose(0, 2, 1, 3)),
                     "modb": np.ascontiguousarray(mod_b[:, cs].reshape(2, 18, 128).transpose(2, 0, 1))})
    r0 = _run(build_l0(), maps)
    modfull = np.concatenate([r0[i]["modo"].reshape(128, 2, 18, 3) for i in range(NCORES)], axis=2)
    modsw = modfull.copy(); modsw[..., 0] = modfull[..., 1]; modsw[..., 1] = modfull[..., 0]
    modin = [np.ascontiguousarray((modfull if core // 4 == 0 else modsw).reshape(128, -1)) for core in range(NCORES)]
    normw = lay_vec(A(norm_w))
    ffn_w13 = A(ffn_w13); ffn_w2 = A(ffn_w2)
    maps = []
    w13_00, w2_00 = lay_w13(ffn_w13[0, 0]), lay_w2(ffn_w2[0, 0])
    for core in range(NCORES):
        b, q = core // 4, core % 4
        xt = np.concatenate([x[b, q * NL:(q + 1) * NL], ctx[b, q * NCX:(q + 1) * NCX]], 0).T
        maps.append({"xT": np.ascontiguousarray(xt), "modi": modin[core], "normw": normw, "w13": w13_00, "w2": w2_00})
    r1 = _run(build_l1(NL, NCX), maps)
    del maps
    tabs = fft_tables()
    maps = []
    for g in range(NCORES):
        rows = slice(256 * g, 256 * (g + 1))
        hT = np.stack([np.concatenate([r1[b * 4 + q]["h0"][rows, 0:NL] for q in range(4)], axis=1) for b in range(B)])
        hcT = np.stack([np.concatenate([r1[b * 4 + q]["h0"][rows, NL:NT] for q in range(4)], axis=1) for b in range(B)])
        maps.append({"hT": np.ascontiguousarray(hT.reshape(B, 2, 128, L).transpose(0, 2, 1, 3)),
                     "hcT": np.ascontiguousarray(hcT.reshape(B, 2, 128, CTX).transpose(0, 2, 1, 3)), **tabs})
    r2 = _run(build_l2(B), maps)
    maps = []
    wo_f = lay_wo(A(fnet_w_o)[0]); bo_f = lay_vec(A(fnet_b_o)[0])
    w13a, w2a = lay_w13(ffn_w13[0, 1]), lay_w2(ffn_w2[0, 1])
    w13b, w2b = lay_w13(ffn_w13[1, 0]), lay_w2(ffn_w2[1, 0])
    for core in range(NCORES):
        b, q = core // 4, core % 4
        fT = np.concatenate([np.concatenate([r2[g]["fo"][b][:, q * NL:(q + 1) * NL] for g in range(NCORES)], axis=0),
                             np.concatenate([r2[g]["fco"][b][:, q * NCX:(q + 1) * NCX] for g in range(NCORES)], axis=0)], axis=1)
        maps.append({"X1": r1[core]["X1"], "fT": np.ascontiguousarray(fT), "modo": modin[core], "gso": r1[core]["gso"], "hgo": r1[core]["hgo"],
                     "wo": wo_f, "bo": bo_f, "w13a": w13a, "w2a": w2a, "w13b": w13b, "w2b": w2b})
    r3 = _run(build_l3(NL, NCX), maps)
    del r2, maps
    inp = {"rwkv_mu": A(rwkv_mu), "rwkv_w_rkv": A(rwkv_w_rkv), "rwkv_w0": A(rwkv_w0), "rwkv_w1": A(rwkv_w1), "rwkv_w2": A(rwkv_w2),
           "rwkv_a0": A(rwkv_a0), "rwkv_a1": A(rwkv_a1), "rwkv_a2": A(rwkv_a2), "rwkv_g1": A(rwkv_g1), "rwkv_g2": A(rwkv_g2),
           "rwkv_k_k": A(rwkv_k_k), "rwkv_k_a": A(rwkv_k_a), "rwkv_r_k": A(rwkv_r_k), "rwkv_ln_w": A(rwkv_ln_w), "rwkv_ln_b": A(rwkv_ln_b)}
    hT = np.stack([np.concatenate([r3[b * 4 + q]["h1"][:, NL:NT] for q in range(4)] + [r3[b * 4 + q]["h1"][:, 0:NL] for q in range(4)], axis=1)
                   for b in range(B)])
    hT = np.ascontiguousarray(hT)
    cst = rwkv_consts()
    maps = [{"hT": hT, **cst, **rwkv_host_weights(inp, g)} for g in range(NCORES)]
    r45 = _run(build_l45(L, CTX, B), maps)
    del hT, maps
    maps = []
    wo_r = lay_wo(A(rwkv_w_o)[0]); fng = lay_vec(A(final_norm_w))
    w13c, w2c = lay_w13(ffn_w13[1, 1]), lay_w2(ffn_w2[1, 1])
    for core in range(NCORES):
        b, q = core // 4, core % 4
        oT = np.concatenate([r45[g]["o_out"][b][q * NL:(q + 1) * NL, :] for g in range(NCORES)], axis=1).T
        maps.append({"X4": np.ascontiguousarray(r3[core]["X4"][:, 0:NL]), "oT": np.ascontiguousarray(oT),
                     "modo": modin[core], "gso": r1[core]["gso"], "hgo": r1[core]["hgo"],
                     "wo": wo_r, "fng": fng, "w13": w13c, "w2": w2c})
    r6 = _run(build_l6(NL), maps)
    _DBG.update(r1=r1, r3=r3, r45=r45, modfull=modfull)
    out = np.empty((B, L, D), f32)
    for core in range(NCORES):
        b, q = core // 4, core % 4
        out[b, q * NL:(q + 1) * NL] = r6[core]["out"].T
    return out
```

```python
import contextlib
import types
import numpy as np
import ml_dtypes
import concourse.bass as bass
import concourse.mybir as mybir
from concourse.bass_utils import run_bass_kernel_spmd

F32 = mybir.dt.float32
BF16 = mybir.dt.bfloat16
ALU = mybir.AluOpType
AF = mybir.ActivationFunctionType
AX = mybir.AxisListType

D = 2048
KC = 16
DFF = 5632
JC = 44
NMOD = 9
SEQ = 16384
CTX = 256
EPS = 1e-6
NCORES = 8


class Res:
    __slots__ = ("lastw", "readers")

    def __init__(self):
        self.lastw = None
        self.readers = []


class Op:
    __slots__ = ("eng", "fn", "deps", "sig", "cnt", "chan")

    def __init__(self, eng, fn, chan=None):
        self.eng = eng
        self.fn = fn
        self.deps = []
        self.sig = False
        self.cnt = 0
        self.chan = chan


ENGS = ("pe", "act", "dve", "pool", "sp")


def _freeze(fn):
    cl = fn.__closure__
    if cl is None:
        return fn
    cells = []
    for c in cl:
        try:
            cells.append(types.CellType(c.cell_contents))
        except ValueError:
            cells.append(c)
    return types.FunctionType(fn.__code__, fn.__globals__, fn.__name__, fn.__defaults__, tuple(cells))


class Prog:
    def __init__(self, nc, same_engine_sync=True):
        self.nc = nc
        self.ops = {e: [] for e in ENGS}
        self.chan_last = {}
        self.same = same_engine_sync

    def emit(self, eng, fn, reads=(), writes=(), chan=None):
        op = Op(eng, _freeze(fn), chan)
        deps = []
        for r in reads:
            if r.lastw is not None:
                deps.append(r.lastw)
        for w in writes:
            if w.lastw is not None:
                deps.append(w.lastw)
            deps.extend(w.readers)
        if chan is not None:
            prev = self.chan_last.get(chan)
            if prev is not None:
                deps.append(prev)
                op.cnt = prev.cnt + 16
            else:
                op.cnt = 16
            self.chan_last[chan] = op
        seen = set()
        for d in deps:
            if id(d) in seen or d is op:
                continue
            seen.add(id(d))
            if d.chan is None and d.eng == eng and (eng == "pe" or not self.same):
                continue
            op.deps.append(d)
            d.sig = True
        for r in reads:
            r.readers.append(op)
        for w in writes:
            w.lastw = op
            w.readers = []
        self.ops[eng].append(op)
        return op

    def run(self, final_waits=()):
        nc = self.nc
        chans = dict(self.chan_last)
        for e in ENGS:
            c = 0
            for op in self.ops[e]:
                if op.chan is None and op.sig:
                    c += 1
                    op.cnt = c
        with contextlib.ExitStack() as st:
            esem = {e: st.enter_context(nc.semaphore("s_" + e)) for e in ENGS}
            csem = {c: st.enter_context(nc.semaphore("c_%s" % (str(c),))) for c in chans}
            block = st.enter_context(nc.Block())

            def mk(ename):
                oplist = self.ops[ename]
                fw = list(final_waits) if ename == "sp" else []

                def body(eng):
                    known = {}
                    for op in oplist:
                        need = {}
                        for d in op.deps:
                            if d.chan is not None:
                                key = ("c", d.chan)
                                sem = csem[d.chan]
                            else:
                                key = ("e", d.eng)
                                sem = esem[d.eng]
                            if known.get(key, 0) >= d.cnt:
                                continue
                            if key not in need or need[key][1] < d.cnt:
                                need[key] = (sem, d.cnt)
                        for key, (sem, cnt) in need.items():
                            known[key] = cnt
                            eng.wait_ge(sem, cnt)
                        ins = op.fn(eng)
                        if op.chan is not None:
                            ins.then_inc(csem[op.chan], 16)
                        elif op.sig:
                            ins.then_inc(esem[op.eng], 1)
                    for d in fw:
                        eng.wait_ge(csem[d.chan], d.cnt)
                return body

            block.tensor(mk("pe"))
            block.scalar(mk("act"))
            block.vector(mk("dve"))
            block.gpsimd(mk("pool"))
            block.sync(mk("sp"))


class Rot:
    def __init__(self, bufs, name):
        self.bufs = bufs
        self.res = [Res() for _ in bufs]
        self.name = name
        self.i = 0

    def next(self):
        k = self.i % len(self.bufs)
        self.i += 1
        return self.bufs[k], self.res[k], "%s%d" % (self.name, k)


class Dense:
    def __init__(self, nc, P, st, tmax, batch_row):
        self.nc, self.P, self.st = nc, P, st
        self.tmax = tmax
        self.brow = batch_row
        sb = lambda name, shape, dt: st.enter_context(nc.sbuf_tensor("sb_" + name, shape, dt))
        self.h = sb("h", [128, KC, tmax], BF16)
        self.r_h = Res()
        self.hid = sb("hid", [128, JC, tmax], BF16)
        self.r_hid = Res()
        self.xs = Rot([sb("xs%d" % i, [128, tmax], F32) for i in range(3)], "xs")
        self.sq = Rot([sb("sq%d" % i, [128, tmax], BF16) for i in range(2)], "sq")
        self.rstd = sb("rstd", [128, tmax], F32)
        self.r_rstd = Res()
        self.tmp = Rot([sb("tmp%d" % i, [128, tmax], F32) for i in range(2)], "tmp")
        self.sg = Rot([sb("sg%d" % i, [128, 512], F32) for i in range(2)], "sg")
        self.w13t = Rot([sb("w13t%d" % i, [128, KC, 256], BF16) for i in range(3)], "w13t")
        self.w2t = Rot([sb("w2t%d" % i, [128, JC, 128], BF16) for i in range(2)], "w2t")
        self.xo = Rot([sb("xo%d" % i, [128, tmax], F32) for i in range(2)], "xo")
        self.ones = sb("ones", [128, 128], BF16)
        self.r_ones = Res()
        P.emit("pool", lambda e: e.memset(self.ones[:], 1.0 / D), writes=[self.r_ones])
        self.epsb = sb("epsb", [128, 1], F32)
        P.emit("pool", lambda e: e.memset(self.epsb[:], EPS), writes=[self.r_ones])
        self.ps = Rot([st.enter_context(nc.psum_tensor("ps%d" % i, [128, 512], F32)) for i in range(8)], "ps")

    def mod_compute(self, sT_d, modw_d, modb_d, nlayers=2, nchunks=144):
        nc, P, st = self.nc, self.P, self.st
        sb = lambda name, shape, dt: st.enter_context(nc.sbuf_tensor("sb_" + name, shape, dt))
        sT = sb("sT", [128, KC, 3], F32)
        r_sT = Res()
        self.mod = sb("mod", [128, nlayers, nchunks, 3], F32)
        self.r_mod = Res()
        modb = sb("modb", [128, nlayers, nchunks], F32)
        r_modb = Res()
        wm = Rot([sb("wm%d" % i, [128, KC, 256], F32) for i in range(2)], "wm")
        P.emit("sp", lambda e: e.dma_start(out=sT[:], in_=sT_d), writes=[r_sT], chan="msc0")
        P.emit("sp", lambda e: e.dma_start(out=modb[:], in_=modb_d), writes=[r_modb], chan="msc1")
        P.emit("act", lambda e: e.activation(out=sT[:], in_=sT[:], func=AF.Silu), reads=[r_sT], writes=[r_sT])
        for l in range(nlayers):
            pst, r_ps, _ = self.ps.next()
            psv = pst[:, 0:nchunks * 3].rearrange("p (n r) -> p n r", r=3)
            for nb in range(nchunks // 2):
                wt, r_wt, ch = wm.next()
                P.emit("sp" if nb % 2 else "act",
                       lambda e, wt=wt, l=l, nb=nb: e.dma_start(out=wt[:], in_=modw_d[l, :, :, nb * 256:(nb + 1) * 256]),
                       writes=[r_wt], chan=ch)
                for q in range(2):
                    n = nb * 2 + q
                    for kc in range(KC):
                        P.emit("pe", lambda e, wt=wt, q=q, kc=kc, n=n, psv=psv: e.matmul(
                            psv[:, n, :], lhsT=wt[:, kc, q * 128:(q + 1) * 128], rhs=sT[:, kc, :],
                            start=(kc == 0), stop=(kc == KC - 1)), reads=[r_wt, r_sT], writes=[r_ps])
            P.emit("dve", lambda e, l=l, psv=psv: e.tensor_tensor(
                out=self.mod[:, l], in0=psv, in1=modb[:, l, :].unsqueeze(2).to_broadcast([128, nchunks, 3]), op=ALU.add),
                reads=[r_ps, r_modb], writes=[self.r_mod])

    def mod_derive(self, mod_d, normw_d, nlayers=2):
        nc, P, st = self.nc, self.P, self.st
        sb = lambda name, shape, dt: st.enter_context(nc.sbuf_tensor("sb_" + name, shape, dt))
        self.mod = sb("mod", [128, nlayers, 144, 3], F32)
        self.r_mod = Res()
        self.normw = sb("normw", [128, nlayers, 3, KC], F32)
        r_nw = Res()
        self.gs = sb("gs", [128, nlayers, 3, KC, 3], F32)
        self.hg = sb("hg", [128, nlayers, 3, KC, 3], F32)
        P.emit("sp", lambda e: e.dma_start(out=self.mod[:].rearrange("p a b c -> p (a b c)"), in_=mod_d), writes=[self.r_mod], chan="md0")
        P.emit("sp", lambda e: e.dma_start(out=self.normw[:], in_=normw_d), writes=[r_nw], chan="md1")
        for l in range(nlayers):
            for s in range(3):
                sc = self.mod[:, l, (3 * s + 1) * 16:(3 * s + 2) * 16, :]
                gt = self.mod[:, l, (3 * s + 2) * 16:(3 * s + 3) * 16, :]
                P.emit("dve", lambda e, l=l, s=s, sc=sc: e.scalar_tensor_tensor(
                    out=self.gs[:, l, s], in0=sc, scalar=1.0, in1=self.normw[:, l, s, :].unsqueeze(2).to_broadcast([128, KC, 3]),
                    op0=ALU.add, op1=ALU.mult), reads=[self.r_mod, r_nw], writes=[self.r_mod])
                P.emit("dve", lambda e, l=l, s=s, gt=gt: e.tensor_scalar(
                    out=self.hg[:, l, s], in0=gt, scalar1=(1.0 if s == 1 else 0.5), scalar2=None, op0=ALU.mult),
                    reads=[self.r_mod], writes=[self.r_mod])

    def mod_load(self, modo, gso, hgo, nlayers=2):
        nc, P, st = self.nc, self.P, self.st
        sb = lambda name, shape, dt: st.enter_context(nc.sbuf_tensor("sb_" + name, shape, dt))
        self.mod = sb("mod", [128, nlayers, 144, 3], F32)
        self.gs = sb("gs", [128, nlayers, 3, KC, 3], F32)
        self.hg = sb("hg", [128, nlayers, 3, KC, 3], F32)
        self.r_mod = Res()
        P.emit("sp", lambda e: e.dma_start(out=self.mod[:].rearrange("p a b c -> p (a b c)"), in_=modo), writes=[self.r_mod], chan="ml0")
        P.emit("sp", lambda e: e.dma_start(out=self.gs[:].rearrange("p a b c d -> p (a b c d)"), in_=gso), writes=[self.r_mod], chan="ml1")
        P.emit("sp", lambda e: e.dma_start(out=self.hg[:].rearrange("p a b c d -> p (a b c d)"), in_=hgo), writes=[self.r_mod], chan="ml2")

    def linear_res(self, blocks, src_d, r_src, w_d, bias_sb, X, r_X, Xo, r_Xo, l):
        P = self.P
        offs = np.cumsum([0] + [b[1] for b in blocks])
        sv = src_d.rearrange("(c p) t -> p c t", p=128)
        for bi, (c0, n, row) in enumerate(blocks):
            for half in range(2):
                P.emit("sp", lambda e, c0=c0, n=n, o=offs[bi], half=half: e.dma_start(
                    out=self.h[:, half * 8:(half + 1) * 8, o:o + n], in_=sv[:, half * 8:(half + 1) * 8, c0:c0 + n]),
                    reads=[r_src], writes=[self.r_h], chan="lrh%d_%d" % (bi, half))
        Xv = X.rearrange("(c p) t -> p c t", p=128)
        Xov = Xo.rearrange("(c p) t -> p c t", p=128)
        for n2 in range(KC // 2):
            wt, r_wt, ch = self.w13t.next()
            P.emit("pool", lambda e, wt=wt, n2=n2: e.dma_start(out=wt[:], in_=w_d[n2]), writes=[r_wt], chan=ch)
            for q in range(2):
                nn = n2 * 2 + q
                xt, r_xt, chx = self.xs.next()
                xo, r_xo, cho = self.xo.next()
                for bi, (c0, n, row) in enumerate(blocks):
                    o = offs[bi]
                    P.emit("sp", lambda e, xt=xt, nn=nn, c0=c0, n=n, o=o: e.dma_start(
                        out=xt[:, o:o + n], in_=Xv[:, nn, c0:c0 + n]), reads=[r_X], writes=[r_xt], chan=chx + "_%d" % bi)
                    po, r_po, _ = self.ps.next()
                    for kc in range(KC):
                        P.emit("pe", lambda e, po=po, wt=wt, kc=kc, q=q, o=o, n=n: e.matmul(
                            po[:, 0:n], lhsT=wt[:, kc, q * 128:(q + 1) * 128], rhs=self.h[:, kc, o:o + n],
                            start=(kc == 0), stop=(kc == KC - 1)), reads=[r_wt, self.r_h], writes=[r_po])
                    src_ap = po[:, 0:n]
                    rd = [r_po]
                    if bias_sb is not None:
                        sg, r_sg, _ = self.sg.next()
                        P.emit("act", lambda e, sg=sg, po=po, n=n, nn=nn: e.activation(
                            out=sg[:, 0:n], in_=po[:, 0:n], func=AF.Identity, bias=bias_sb[:, nn:nn + 1], scale=1.0),
                            reads=[r_po, self.r_mod], writes=[r_sg])
                        src_ap = sg[:, 0:n]
                        rd = [r_sg]
                    P.emit("dve", lambda e, src_ap=src_ap, xt=xt, xo=xo, nn=nn, o=o, n=n, row=row: e.scalar_tensor_tensor(
                        out=xo[:, o:o + n], in0=src_ap, scalar=self.hg[:, l, 1, nn, row:row + 1], in1=xt[:, o:o + n],
                        op0=ALU.mult, op1=ALU.add), reads=rd + [r_xt, self.r_mod], writes=[r_xo])
                    P.emit("sp", lambda e, xo=xo, nn=nn, c0=c0, n=n, o=o: e.dma_start(
                        out=Xov[:, nn, c0:c0 + n], in_=xo[:, o:o + n]), reads=[r_xo], writes=[r_Xo], chan=cho + "_o%d" % bi)

    def shift_ap(self, l, s, c, row):
        return self.mod[:, l, (3 * s) * 16 + c, row:row + 1]

    def norm_mod(self, blocks, X, r_X, l, s, out_dram=None, r_out=None, final_g=None):
        P = self.P
        T = sum(b[1] for b in blocks)
        offs = np.cumsum([0] + [b[1] for b in blocks])
        stat = [self.ps.next() for _ in blocks]
        Xv = X.rearrange("(c p) t -> p c t", p=128)

        def load_x(c):
            xt, r_xt, ch = self.xs.next()
            for bi, (c0, n, row) in enumerate(blocks):
                P.emit("sp", lambda e, xt=xt, c=c, c0=c0, n=n, o=offs[bi]: e.dma_start(
                    out=xt[:, o:o + n], in_=Xv[:, c, c0:c0 + n]), reads=[r_X], writes=[r_xt], chan=ch + "_%d" % bi)
            return xt, r_xt

        for c in range(KC):
            xt, r_xt = load_x(c)
            sq, r_sq, _ = self.sq.next()
            P.emit("act", lambda e, xt=xt, sq=sq: e.activation(out=sq[:, 0:T], in_=xt[:, 0:T], func=AF.Square),
                   reads=[r_xt], writes=[r_sq])
            for bi, (c0, n, row) in enumerate(blocks):
                pst, r_ps, _ = stat[bi]
                P.emit("pe", lambda e, pst=pst, sq=sq, o=offs[bi], n=n, c=c: e.matmul(
                    pst[:, 0:n], lhsT=self.ones[:], rhs=sq[:, o:o + n], start=(c == 0), stop=(c == KC - 1)),
                    reads=[r_sq, self.r_ones], writes=[r_ps])
        for bi, (c0, n, row) in enumerate(blocks):
            pst, r_ps, _ = stat[bi]
            P.emit("act", lambda e, pst=pst, o=offs[bi], n=n: e.activation(
                out=self.rstd[:, o:o + n], in_=pst[:, 0:n], func=AF.Sqrt, bias=self.epsb[:, 0:1], scale=1.0),
                reads=[r_ps, self.r_ones], writes=[self.r_rstd])
            P.emit("dve", lambda e, o=offs[bi], n=n: e.reciprocal(
                out=self.rstd[:, o:o + n], in_=self.rstd[:, o:o + n]),
                reads=[self.r_rstd], writes=[self.r_rstd])
        for c in range(KC):
            xt, r_xt = load_x(c)
            tmp, r_tmp, _ = self.tmp.next()
            P.emit("dve", lambda e, xt=xt, tmp=tmp: e.tensor_tensor(
                out=tmp[:, 0:T], in0=xt[:, 0:T], in1=self.rstd[:, 0:T], op=ALU.mult),
                reads=[r_xt, self.r_rstd], writes=[r_tmp])
            if out_dram is None:
                for bi, (c0, n, row) in enumerate(blocks):
                    P.emit("act", lambda e, tmp=tmp, o=offs[bi], n=n, c=c, row=row: e.activation(
                        out=self.h[:, c, o:o + n], in_=tmp[:, o:o + n], func=AF.Identity,
                        scale=self.gs[:, l, s, c, row:row + 1], bias=self.shift_ap(l, s, c, row)),
                        reads=[r_tmp, self.r_mod], writes=[self.r_h])
            else:
                xo, r_xo, ch = self.xo.next()
                odt = out_dram.dtype
                xov = xo if odt == F32 else xo[:].bitcast(BF16)
                for bi, (c0, n, row) in enumerate(blocks):
                    if final_g is not None:
                        P.emit("act", lambda e, tmp=tmp, xov=xov, o=offs[bi], n=n, c=c: e.activation(
                            out=xov[:, o:o + n], in_=tmp[:, o:o + n], func=AF.Identity, scale=final_g[:, c:c + 1], bias=0.0),
                            reads=[r_tmp, self.r_mod], writes=[r_xo])
                    else:
                        P.emit("act", lambda e, tmp=tmp, xov=xov, o=offs[bi], n=n, c=c, row=row: e.activation(
                            out=xov[:, o:o + n], in_=tmp[:, o:o + n], func=AF.Identity,
                            scale=self.gs[:, l, s, c, row:row + 1], bias=self.shift_ap(l, s, c, row)),
                            reads=[r_tmp, self.r_mod], writes=[r_xo])
                ov = out_dram.rearrange("(c p) t -> p c t", p=128)
                for bi, (c0, n, row) in enumerate(blocks):
                    P.emit("sp", lambda e, xov=xov, o=offs[bi], n=n, c=c, c0=c0: e.dma_start(
                        out=ov[:, c, c0:c0 + n], in_=xov[:, o:o + n]), reads=[r_xo], writes=[r_out], chan=ch + "_o%d" % bi)

    def ffn(self, blocks, w13_d, w2_d, X, r_X, Xo, r_Xo, l, s):
        P = self.P
        offs = np.cumsum([0] + [b[1] for b in blocks])
        for j in range(JC):
            wt, r_wt, ch = self.w13t.next()
            P.emit("pool", lambda e, wt=wt, j=j: e.dma_start(out=wt[:], in_=w13_d[j]), writes=[r_wt], chan=ch)
            for bi, (c0, n, row) in enumerate(blocks):
                o = offs[bi]
                pg, r_pg, _ = self.ps.next()
                pu, r_pu, _ = self.ps.next()
                for half, (pp, r_pp) in enumerate(((pg, r_pg), (pu, r_pu))):
                    for kc in range(KC):
                        P.emit("pe", lambda e, pp=pp, wt=wt, kc=kc, half=half, o=o, n=n: e.matmul(
                            pp[:, 0:n], lhsT=wt[:, kc, half * 128:(half + 1) * 128], rhs=self.h[:, kc, o:o + n],
                            start=(kc == 0), stop=(kc == KC - 1)), reads=[r_wt, self.r_h], writes=[r_pp])
                sg, r_sg, _ = self.sg.next()
                P.emit("act", lambda e, sg=sg, pg=pg, n=n: e.activation(out=sg[:, 0:n], in_=pg[:, 0:n], func=AF.Silu),
                       reads=[r_pg], writes=[r_sg])
                P.emit("dve", lambda e, sg=sg, pu=pu, j=j, o=o, n=n: e.tensor_tensor(
                    out=self.hid[:, j, o:o + n], in0=sg[:, 0:n], in1=pu[:, 0:n], op=ALU.mult),
                    reads=[r_sg, r_pu], writes=[self.r_hid])
        Xv = X.rearrange("(c p) t -> p c t", p=128)
        Xov = Xo.rearrange("(c p) t -> p c t", p=128)
        for nn in range(KC):
            wt, r_wt, ch = self.w2t.next()
            P.emit("pool", lambda e, wt=wt, nn=nn: e.dma_start(out=wt[:], in_=w2_d[nn]), writes=[r_wt], chan=ch)
            xt, r_xt, chx = self.xs.next()
            xo, r_xo, cho = self.xo.next()
            for bi, (c0, n, row) in enumerate(blocks):
                o = offs[bi]
                P.emit("sp", lambda e, xt=xt, nn=nn, c0=c0, n=n, o=o: e.dma_start(
                    out=xt[:, o:o + n], in_=Xv[:, nn, c0:c0 + n]), reads=[r_X], writes=[r_xt], chan=chx + "_%d" % bi)
                po, r_po, _ = self.ps.next()
                for jc in range(JC):
                    P.emit("pe", lambda e, po=po, wt=wt, jc=jc, o=o, n=n: e.matmul(
                        po[:, 0:n], lhsT=wt[:, jc, :], rhs=self.hid[:, jc, o:o + n],
                        start=(jc == 0), stop=(jc == JC - 1)), reads=[r_wt, self.r_hid], writes=[r_po])
                P.emit("dve", lambda e, po=po, xt=xt, xo=xo, nn=nn, o=o, n=n, row=row: e.scalar_tensor_tensor(
                    out=xo[:, o:o + n], in0=po[:, 0:n], scalar=self.hg[:, l, s, nn, row:row + 1], in1=xt[:, o:o + n],
                    op0=ALU.mult, op1=ALU.add), reads=[r_po, r_xt, self.r_mod], writes=[r_xo])
                P.emit("sp", lambda e, xo=xo, nn=nn, c0=c0, n=n, o=o: e.dma_start(
                    out=Xov[:, nn, c0:c0 + n], in_=xo[:, o:o + n]), reads=[r_xo], writes=[r_Xo], chan=cho + "_o%d" % bi)


def make_passes(nlat, nctx, brow):
    blks = []
    c = 0
    while c < nlat:
        n = min(512, nlat - c)
        blks.append((c, n, brow))
        c += n
    if nctx:
        blks.append((nlat, nctx, 2))
    fine = []
    for (c0, n, row) in blks:
        fine.append((c0, n, row))
    passes = []
    cur, tot = [], 0
    queue = list(fine)
    while queue:
        c0, n, row = queue.pop(0)
        if tot + n <= 768:
            cur.append((c0, n, row)); tot += n
        elif n == 512 and tot + 256 <= 768:
            cur.append((c0, 256, row)); tot += 256
            queue.insert(0, (c0 + 256, 256, row))
        else:
            passes.append(cur); cur, tot = [], 0
            queue.insert(0, (c0, n, row))
    if cur:
        passes.append(cur)
    return passes


def lay_w13(w):
    return np.ascontiguousarray(w.reshape(KC, 128, 2, JC, 128).transpose(3, 1, 0, 2, 4)).reshape(JC, 128, KC, 256)


def lay_w2(w):
    return np.ascontiguousarray(w.reshape(JC, 128, KC, 128).transpose(2, 1, 0, 3))


def lay_sq(w):
    return np.ascontiguousarray(w.reshape(KC, 128, -1).transpose(1, 0, 2))


def lay_vec(v):
    sh = v.shape[:-1]
    a = v.reshape(sh + (KC, 128))
    return np.ascontiguousarray(np.moveaxis(a, -1, 0))


def core_tokens(core):
    b = core // 4
    q = core % 4
    return b, q


def build_l0():
    nc = bass.Bass("TRN2", target_bir_lowering=False)
    dt = lambda name, shape, dty, kind: nc.dram_tensor(name, shape, dty, kind=kind).ap()
    sT = dt("sT", [128, KC, 3], F32, "ExternalInput")
    modw = dt("modw", [2, 128, KC, 18 * 128], F32, "ExternalInput")
    modb = dt("modb", [128, 2, 18], F32, "ExternalInput")
    modo = dt("modo", [128, 2 * 18 * 3], F32, "ExternalOutput")
    with contextlib.ExitStack() as st:
        P = Prog(nc)
        dn = Dense(nc, P, st, 64, 0)
        dn.mod_compute(sT, modw, modb, 2, 18)
        P.emit("sp", lambda e: e.dma_start(out=modo, in_=dn.mod[:].rearrange("p a b c -> p (a b c)")), reads=[dn.r_mod], chan="mo0")
        P.run(final_waits=_all_dma_tails(P))
    return nc


def build_l1(nlat, nctx):
    NT = nlat + nctx
    nc = bass.Bass("TRN2", target_bir_lowering=False)
    dt = lambda name, shape, dty, kind: nc.dram_tensor(name, shape, dty, kind=kind).ap()
    xT = dt("xT", [D, NT], F32, "ExternalInput")
    modi = dt("modi", [128, 2 * 144 * 3], F32, "ExternalInput")
    normw = dt("normw", [128, 2, 3, KC], F32, "ExternalInput")
    w13 = dt("w13", [JC, 128, KC, 256], F32, "ExternalInput")
    w2 = dt("w2", [KC, 128, JC, 128], F32, "ExternalInput")
    X1 = dt("X1", [D, NT], F32, "ExternalOutput")
    h0 = dt("h0", [D, NT], BF16, "ExternalOutput")
    gso = dt("gso", [128, 2 * 3 * KC * 3], F32, "ExternalOutput")
    hgo = dt("hgo", [128, 2 * 3 * KC * 3], F32, "ExternalOutput")
    outs = []
    with contextlib.ExitStack() as st:
        P = Prog(nc)
        dn = Dense(nc, P, st, 768, 0)
        dn.mod_derive(modi, normw)
        r_o = Res()
        outs.append(P.emit("sp", lambda e: e.dma_start(out=gso, in_=dn.gs[:].rearrange("p a b c d -> p (a b c d)")), reads=[dn.r_mod], writes=[r_o], chan="mo1"))
        outs.append(P.emit("sp", lambda e: e.dma_start(out=hgo, in_=dn.hg[:].rearrange("p a b c d -> p (a b c d)")), reads=[dn.r_mod], writes=[r_o], chan="mo2"))
        r_xin = Res()
        passes = make_passes(nlat, nctx, 0)
        for blocks in passes:
            r_X1 = Res()
            r_h0 = Res()
            dn.norm_mod(blocks, xT, r_xin, 0, 0)
            dn.ffn(blocks, w13, w2, xT, r_xin, X1, r_X1, 0, 0)
            dn.norm_mod(blocks, X1, r_X1, 0, 1, out_dram=h0, r_out=r_h0)
            outs.append(r_X1)
            outs.append(r_h0)
        fw = [o.lastw if isinstance(o, Res) else o for o in outs]
        P.run(final_waits=_all_dma_tails(P))
    return nc


def _all_dma_tails(P):
    return list(P.chan_last.values())


def fft_tables():
    bf = ml_dtypes.bfloat16
    ch = np.arange(256, dtype=np.float64)
    ang = 2 * np.pi * np.outer(ch, ch) / 256.0
    sc = 1.0 / 2048.0
    cs = np.zeros((128, 2, 4, 128), np.float64)
    for kc in range(2):
        for q in range(4):
            a = ang[kc * 128:(kc + 1) * 128, q * 64:(q + 1) * 64]
            cs[:, kc, q, 0:64] = np.cos(a) * sc
            cs[:, kc, q, 64:128] = -np.sin(a) * sc
    l = np.arange(128, dtype=np.float64)
    a1 = 2 * np.pi * np.outer(l, l) / 128.0
    f1 = np.zeros((128, 2, 256), np.float64)
    f1[:, 0, 0:128] = np.cos(a1); f1[:, 0, 128:256] = -np.sin(a1)
    f1[:, 1, 0:128] = np.sin(a1); f1[:, 1, 128:256] = np.cos(a1)
    k = np.arange(128)[None, :] * 128 + np.arange(128)[:, None]
    ae = 2 * np.pi * (l[:, None, None] * k[None]) / 16384.0
    E = np.concatenate([np.cos(ae), np.sin(ae)], axis=2)
    scc = 1.0 / 256.0
    csc = np.zeros((128, 2, 512), np.float64)
    for kc in range(2):
        a = ang[kc * 128:(kc + 1) * 128, :]
        csc[:, kc, 0:256] = np.cos(a) * scc
        csc[:, kc, 256:512] = -np.sin(a) * scc
    g = np.zeros((128, 2, 512), np.float64)
    for tc in range(2):
        a = ang[tc * 128:(tc + 1) * 128, :]
        g[:, tc, 0:256] = np.cos(a)
        g[:, tc, 256:512] = np.sin(a)
    return {"t_cs": cs.astype(bf), "t_f1": f1.astype(bf), "t_E": E.astype(bf), "t_csc": csc.astype(bf), "t_g": g.astype(bf)}


def emit_fft(nc, P, st, hT, r_hT, hcT, r_hcT, tabs, fo, r_fo, fco, r_fco, nb=2, pfx="ff"):
    sb = lambda name, shape, dt: st.enter_context(nc.sbuf_tensor("sb_" + pfx + name, shape, dt))
    hs = sb("hs", [128, 2, 16384], BF16); r_hs = Res()
    W = sb("W", [128, 128, 128], BF16); r_W = Res()
    Z = sb("Z", [128, 64, 256], BF16); r_Z = Res()
    fT = sb("fT", [64, 16384], BF16); r_fT = Res()
    Eb = Rot([sb("E%d" % i, [128, 16, 256], BF16) for i in range(2)], pfx + "E")
    cs = sb("cs", [128, 2, 4, 128], BF16)
    f1 = sb("f1", [128, 2, 256], BF16)
    csc = sb("csc", [128, 2, 512], BF16)
    gt = sb("gt", [128, 2, 512], BF16)
    hcs = sb("hcs", [128, 2, 256], BF16); r_hcs = Res()
    Wc = sb("Wc", [128, 2, 512], BF16); r_Wc = Res()
    fcs = sb("fcs", [128, 256], BF16); r_fcs = Res()
    r_tab = Res()
    ps = Rot([st.enter_context(nc.psum_tensor(pfx + "ps%d" % i, [128, 512], F32)) for i in range(8)], pfx + "ps")
    for i, (dst, src) in enumerate(((cs, tabs["t_cs"]), (f1, tabs["t_f1"]), (csc, tabs["t_csc"]), (gt, tabs["t_g"]))):
        P.emit("sp", lambda e, dst=dst, src=src: e.dma_start(out=dst[:], in_=src), writes=[r_tab], chan=pfx + "tab%d" % i)
    evac_i = [0]

    def evac(out_ap, in_ap, reads, writes):
        eng = "act" if evac_i[0] % 2 == 0 else "dve"
        evac_i[0] += 1
        if eng == "act":
            P.emit("act", lambda e: e.activation(out=out_ap, in_=in_ap, func=AF.Copy), reads=reads, writes=writes)
        else:
            P.emit("dve", lambda e: e.tensor_copy(out=out_ap, in_=in_ap), reads=reads, writes=writes)

    for b in range(nb):
        P.emit("sp", lambda e, b=b: e.dma_start(out=hcs[:], in_=hcT[b]), reads=[r_hcT], writes=[r_hcs], chan=pfx + "hc")
        for tc in range(2):
            pt, r_pt, _ = ps.next()
            for kc in range(2):
                P.emit("pe", lambda e, pt=pt, tc=tc, kc=kc: e.matmul(
                    pt[:, :], lhsT=hcs[:, kc, tc * 128:(tc + 1) * 128], rhs=csc[:, kc, :], start=(kc == 0), stop=(kc == 1)),
                    reads=[r_hcs, r_tab], writes=[r_pt])
            evac(Wc[:, tc, :], pt[:, :], [r_pt], [r_Wc])
        for half in range(2):
            pt, r_pt, _ = ps.next()
            k = 0
            for tc in range(2):
                for ri in range(2):
                    P.emit("pe", lambda e, pt=pt, tc=tc, ri=ri, half=half, k=k: e.matmul(
                        pt[:, 0:256], lhsT=Wc[:, tc, ri * 256 + half * 128: ri * 256 + (half + 1) * 128],
                        rhs=gt[:, tc, ri * 256:(ri + 1) * 256], start=(k == 0), stop=(k == 3)),
                        reads=[r_Wc, r_tab], writes=[r_pt])
                    k += 1
            evac(fcs[:, :], pt[:, 0:256], [r_pt], [r_fcs])
            P.emit("sp", lambda e, b=b, half=half: e.dma_start(out=fco[b, half * 128:(half + 1) * 128, :], in_=fcs[:, :]),
                   reads=[r_fcs], writes=[r_fco], chan=pfx + "fco")
        for kc in range(2):
            P.emit("sp" if kc == 0 else "act", lambda e, b=b, kc=kc: e.dma_start(out=hs[:, kc, :], in_=hT[b, :, kc, :]),
                   reads=[r_hT], writes=[r_hs], chan=pfx + "hs%d" % kc)
        hv = hs[:].rearrange("p k (a l) -> p k l a", l=128)
        for q in range(4):
            for g4 in range(32):
                pt, r_pt, _ = ps.next()
                for li in range(4):
                    l2 = g4 * 4 + li
                    for kc in range(2):
                        P.emit("pe", lambda e, pt=pt, li=li, l2=l2, kc=kc, q=q: e.matmul(
                            pt[:, li * 128:(li + 1) * 128], lhsT=hv[:, kc, l2, :], rhs=cs[:, kc, q, :],
                            start=(kc == 0), stop=(kc == 1)), reads=[r_hs, r_tab], writes=[r_pt])
                evac(W[:, g4 * 4:(g4 + 1) * 4, :].rearrange("p a b -> p (a b)"), pt[:, :], [r_pt], [r_W])
            for c2 in range(32):
                pt, r_pt, _ = ps.next()
                for ci in range(2):
                    c = c2 * 2 + ci
                    for ri in range(2):
                        P.emit("pe", lambda e, pt=pt, ci=ci, c=c, ri=ri: e.matmul(
                            pt[:, ci * 256:(ci + 1) * 256], lhsT=W[:, :, ri * 64 + c], rhs=f1[:, ri, :],
                            start=(ri == 0), stop=(ri == 1)), reads=[r_W, r_tab], writes=[r_pt])
                evac(Z[:, c2 * 2:(c2 + 1) * 2, :].rearrange("p a b -> p (a b)"), pt[:, :], [r_pt], [r_Z])
            fv = fT[:].rearrange("p (k2 k1) -> p k1 k2", k1=128)
            for eb in range(8):
                Et, r_Et, ch = Eb.next()
                P.emit("sp", lambda e, Et=Et, eb=eb: e.dma_start(out=Et[:], in_=tabs["t_E"][:, eb * 16:(eb + 1) * 16, :]),
                       writes=[r_Et], chan=ch)
                for k4 in range(4):
                    pt, r_pt, _ = ps.next()
                    for ki in range(4):
                        kl = k4 * 4 + ki
                        k1 = eb * 16 + kl
                        for ri in range(2):
                            P.emit("pe", lambda e, pt=pt, ki=ki, kl=kl, k1=k1, ri=ri, Et=Et: e.matmul(
                                pt[0:64, ki * 128:(ki + 1) * 128], lhsT=Z[:, :, ri * 128 + k1], rhs=Et[:, kl, ri * 128:(ri + 1) * 128],
                                start=(ri == 0), stop=(ri == 1)), reads=[r_Z, r_Et], writes=[r_pt])
                    k10 = eb * 16 + k4 * 4
                    evac(fv[:, k10:k10 + 4, :], pt[0:64, :].rearrange("p (a b) -> p a b", a=4), [r_pt], [r_fT])
            P.emit("sp", lambda e, b=b, q=q: e.dma_start(out=fo[b, q * 64:(q + 1) * 64, :], in_=fT[:, :]),
                   reads=[r_fT], writes=[r_fo], chan=pfx + "fo")


def build_l2(nb=2):
    nc = bass.Bass("TRN2", target_bir_lowering=False)
    dt = lambda name, shape, dty, kind: nc.dram_tensor(name, shape, dty, kind=kind).ap()
    hT = dt("hT", [nb, 128, 2, 16384], BF16, "ExternalInput")
    hcT = dt("hcT", [nb, 128, 2, 256], BF16, "ExternalInput")
    tabs = {"t_cs": dt("t_cs", [128, 2, 4, 128], BF16, "ExternalInput"), "t_f1": dt("t_f1", [128, 2, 256], BF16, "ExternalInput"),
            "t_E": dt("t_E", [128, 128, 256], BF16, "ExternalInput"), "t_csc": dt("t_csc", [128, 2, 512], BF16, "ExternalInput"),
            "t_g": dt("t_g", [128, 2, 512], BF16, "ExternalInput")}
    fo = dt("fo", [nb, 256, 16384], BF16, "ExternalOutput")
    fco = dt("fco", [nb, 256, 256], BF16, "ExternalOutput")
    with contextlib.ExitStack() as st:
        P = Prog(nc)
        emit_fft(nc, P, st, hT, Res(), hcT, Res(), tabs, fo, Res(), fco, Res(), nb=nb)
        P.run(final_waits=_all_dma_tails(P))
    return nc


LDC = -0.6065306597126334
GN_EPS = 64e-5


def rwkv_consts():
    bf = ml_dtypes.bfloat16
    idx = np.arange(128)
    cm = np.zeros((128, 2, 3, 256), np.float32)
    ct = np.zeros((128, 2, 3, 128), np.float32)
    for z in range(2):
        before = (idx[:, None] < idx[None, :]) if z == 0 else (idx[:, None] > idx[None, :])
        beq = before | np.eye(128, dtype=bool)
        cm[:, z, 0, 0:128] = before
        cm[:, z, 0, 128:256] = beq
        cm[:, z, 1, 0:128] = before.T
        cm[:, z, 1, 128:256] = before.T
        cm[:, z, 2, 0:128] = beq
        cm[:, z, 2, 128:256] = beq
        ct[:, z, 0, :] = LDC * beq
        ct[:, z, 1, :] = LDC * before
        ct[:, z, 2, :] = LDC
    ident = np.eye(128, dtype=np.float32)
    return {"c_mask": cm.astype(bf), "c_tri": ct, "c_ident": ident.astype(bf)}


def emit_rwkv(nc, P, st, hT, r_hT, wd, yscr, o_out, r_out, nlat=SEQ, nctx=CTX, nb=2, pfx="rw", dbg=None, dbg_at=(0, "ctx", 0, 0), ew_engs=("dve",), prefetch=True):
    sbt = lambda name, shape, dt: st.enter_context(nc.sbuf_tensor("sb_" + pfx + name, shape, dt))
    ps = Rot([st.enter_context(nc.psum_tensor(pfx + "ps%d" % i, [128, 512], F32)) for i in range(8)], pfx + "ps")
    r_c = Res()

    def ld(dst, src, eng="sp", chan=None, **kw):
        return P.emit(eng, lambda e: e.dma_start(out=dst, in_=src, **kw), writes=[r_c], chan=chan)
    cmask = sbt("cmask", [128, 2, 3, 256], BF16); ld(cmask[:], wd["c_mask"], chan=pfx + "k0")
    ctri = sbt("ctri", [128, 2, 3, 128], F32); ld(ctri[:], wd["c_tri"], chan=pfx + "k1")
    ident = sbt("ident", [128, 128], BF16); ld(ident[:], wd["c_ident"], chan=pfx + "k2")
    vecs = sbt("vecs", [128, 9, 256], F32); ld(vecs[:], wd["vecs"], chan=pfx + "k3")
    mu = sbt("mu", [128, KC, 6], F32); ld(mu[:], wd["mu"], chan=pfx + "k4")
    om = sbt("om", [128, KC, 6], F32)
    W2s = sbt("W2s", [96, 2, 256], BF16); ld(W2s[:], wd["w2"], eng="pool", chan=pfx + "k5")
    A2s = sbt("A2s", [96, 2, 256], BF16); ld(A2s[:], wd["a2"], eng="pool", chan=pfx + "k6")
    G2s = sbt("G2s", [128, 2, 256], BF16); ld(G2s[:], wd["g2"], eng="pool", chan=pfx + "k7")
    negcol = sbt("negcol", [128, 1], F32)
    P.emit("pool", lambda e: e.memset(negcol[:], LDC), writes=[r_c])
    gneps = sbt("gneps", [128, 1], F32)
    P.emit("pool", lambda e: e.memset(gneps[:], GN_EPS), writes=[r_c])
    P.emit("dve", lambda e: e.tensor_scalar(out=om[:], in0=mu[:], scalar1=-1.0, scalar2=1.0, op0=ALU.mult, op1=ALU.add),
           reads=[r_c], writes=[r_c])
    RKa = sbt("RKa", [128, KC, 768], BF16); RKb = sbt("RKb", [128, KC, 768], BF16)
    LWa = sbt("LWa", [128, KC, 640], BF16); LWb = sbt("LWb", [128, KC, 640], BF16)
    stg = Rot([sbt("stg%d" % i, [128, 768], F32) for i in range(1)], pfx + "stg")
    blocks = [(0, 256, 0), (256, 512, 1), (512, 768, 2), (768, 960, 3), (960, 1152, 4), (1152, 1408, 5)]
    for kc in range(KC):
        for part in range(2):
            sg, r_sg, ch = stg.next()
            if part == 0:
                P.emit("sp", lambda e, sg=sg, kc=kc: e.dma_start(out=sg[:, 0:768], in_=wd["rkv"][:, kc, :]), writes=[r_sg], chan=ch)
            else:
                P.emit("sp", lambda e, sg=sg, kc=kc: e.dma_start(out=sg[:, 0:640], in_=wd["lw"][:, kc, :]), writes=[r_sg], chan=ch)
            for (c0, c1, p) in blocks:
                if (c0 < 768) != (part == 0):
                    continue
                o = 0 if part == 0 else 768
                da = RKa[:, kc, c0:c1] if c0 < 768 else LWa[:, kc, c0 - 768:c1 - 768]
                db = RKb[:, kc, c0:c1] if c0 < 768 else LWb[:, kc, c0 - 768:c1 - 768]
                P.emit("dve", lambda e, sg=sg, c0=c0 - o, c1=c1 - o, p=p, kc=kc, db=db: e.tensor_scalar(
                    out=db, in0=sg[:, c0:c1], scalar1=mu[:, kc, p:p + 1], scalar2=None, op0=ALU.mult), reads=[r_sg, r_c], writes=[r_c])
                P.emit("pool", lambda e, sg=sg, c0=c0 - o, c1=c1 - o, p=p, kc=kc, da=da: e.tensor_scalar(
                    out=da, in0=sg[:, c0:c1], scalar1=om[:, kc, p:p + 1], scalar2=None, op0=ALU.mult), reads=[r_sg, r_c], writes=[r_c])

    SC = 128
    hw0 = sbt("hw", [128, KC, 64 + SC + 64], BF16); r_hw0 = Res()
    hsb0 = sbt("hsb", [128, KC, SC], BF16); r_hs0 = Res()
    hw = [hw0 for b in range(nb)]; r_hw = [r_hw0 for _ in range(nb)]
    hsb = [hsb0 for b in range(nb)]; r_hs = [r_hs0 for _ in range(nb)]
    xw = [sbt("xw%d" % b, [96, SC], BF16) for b in range(nb)]
    xa = [sbt("xa%d" % b, [96, 2, SC], BF16) for b in range(nb)]
    xg = [sbt("xg%d" % b, [128, 2, SC], BF16) for b in range(nb)]
    r_x = [Res() for _ in range(nb)]
    rkv = [sbt("rkv%d" % b, [128, 768], F32) for b in range(nb)]; r_rkv = [Res() for _ in range(nb)]
    NSCR = 12
    scr0 = [sbt("scr_%d" % i, [128, 256], F32) for i in range(NSCR)]
    r_scr0 = [Res() for _ in range(NSCR)]
    scr = [scr0 for b in range(nb)]
    r_scr = [r_scr0 for b in range(nb)]
    small = [[sbt("sm%d_%d" % (b, i), [128, 4], F32) for i in range(4)] for b in range(nb)]
    r_small = [[Res() for _ in range(4)] for b in range(nb)]
    opn = ["At", "Rt", "Bt", "Kt", "Bh", "Kh", "Vt"]
    opt = [{n: sbt("%s%d" % (n, b), [128, 256], BF16) for n in opn} for b in range(nb)]
    r_opt = [{n: Res() for n in opn} for b in range(nb)]
    keep = [{n: sbt("kp%s%d" % (n, b), [128, 256], F32) for n in ("kb", "g")} for b in range(nb)]
    r_keep = [Res() for _ in range(nb)]
    gend = [sbt("gend%d" % b, [64, 4], F32) for b in range(nb)]; r_gend = [Res() for _ in range(nb)]
    Yt = [sbt("Y%d" % b, [128, 256], F32) for b in range(nb)]; r_Y = [Res() for _ in range(nb)]
    yfin = [scr0[4] for b in range(nb)]; r_yf = [r_scr0[4] for _ in range(nb)]
    ob = [sbt("ob%d" % b, [128, 256], BF16) for b in range(nb)]; r_ob = [Res() for _ in range(nb)]
    units = [(b, h) for b in range(nb) for h in range(4)]
    U = {}
    for (b, h) in units:
        u = {}
        n = "%d_%d" % (b, h)
        u["fm"] = sbt("fm" + n, [64, 4, 128], BF16); u["r_fm"] = Res()
        u["Mrk"] = sbt("Mrk" + n, [128, 256], BF16)
        u["MkaT"] = sbt("MkaT" + n, [128, 128], BF16)
        u["r_M"] = Res()
        u["T"] = [sbt("T%d" % i + n, [128, 128], F32) for i in range(2)]; u["r_T"] = [Res(), Res()]
        u["Tbf"] = sbt("Tbf" + n, [128, 128], BF16); u["r_Tbf"] = Res()
        u["PP"] = [sbt("PP%d" % i + n, [128, 256], F32) for i in range(2)]; u["r_PP"] = [Res(), Res()]
        u["X"] = sbt("X" + n, [128, 128], BF16); u["Ah"] = sbt("Ah" + n, [64, 128], BF16); u["r_XA"] = Res()
        u["Ut"] = sbt("Ut" + n, [128, 64], BF16); u["r_Ut"] = Res()
        u["S32"] = sbt("S32" + n, [64, 64], F32); u["Sbf"] = sbt("Sbf" + n, [64, 64], BF16); u["r_S"] = Res(); u["r_Sbf"] = Res()
        U[(b, h)] = u

    W0 = lambda z: vecs[:, 0 + z, :]
    A0 = lambda z: vecs[:, 2 + z, :]
    KK_, KA_, RK_, LNW_, LNB_ = vecs[:, 4, :], vecs[:, 5, :], vecs[:, 6, :], vecs[:, 7, :], vecs[:, 8, :]
    hv = lambda t: t.rearrange("p (h j) -> p h j", h=4)
    rr = [0]

    def ew(reads, writes, fn_dve, allow=None):
        allow = allow or ew_engs
        eng = allow[rr[0] % len(allow)]
        rr[0] += 1
        P.emit(eng, fn_dve, reads=reads, writes=writes)

    def _pass(z):
        for (b, h) in units:
            u = U[(b, h)]
            P.emit("pool", lambda e, u=u: e.memset(u["S32"][:], 0.0), writes=[u["r_S"]])
            P.emit("pool", lambda e, u=u: e.memset(u["Sbf"][:], 0.0), writes=[u["r_Sbf"]])
        segs = [("ctx", 0, nctx), ("lat", nctx, nlat)]
        def _seg(sname, soff, slen):
            nsc = slen // SC
            sc_order = range(nsc) if z == 0 else range(nsc - 1, -1, -1)
            def _proj(sci):
                t0 = sci * SC
                for b in range(nb):
                    lo = max(0, t0 - 64); hi = min(slen, t0 + SC + 64)
                    if lo > t0 - 64:
                        P.emit("pool", lambda e, b=b: e.memset(hw[b][:, :, 0:64], 0.0), writes=[r_hw[b]])
                    if hi < t0 + SC + 64:
                        P.emit("pool", lambda e, b=b: e.memset(hw[b][:, :, 64 + SC:], 0.0), writes=[r_hw[b]])
                    for half in range(2):
                        P.emit("sp", lambda e, b=b, lo=lo, hi=hi, half=half: e.dma_start(
                            out=hw[b][:, half * 8:(half + 1) * 8, 64 + lo - t0: 64 + hi - t0],
                            in_=hT[b].rearrange("(c p) t -> p c t", p=128)[:, half * 8:(half + 1) * 8, soff + lo: soff + hi]),
                            reads=[r_hT], writes=[r_hw[b]], chan=pfx + "hw%d_%d" % (b, half))
                    shifts = (-1, 1, -64, 64) if sname == "lat" else (-1, 1, -1, 1)
                    for qd in range(4):
                        sh = shifts[qd]
                        P.emit("pool", lambda e, b=b, qd=qd, sh=sh: e.tensor_copy(
                            out=hsb[b][:, qd * 4:(qd + 1) * 4, :], in_=hw[b][:, qd * 4:(qd + 1) * 4, 64 + sh:64 + sh + SC]),
                            reads=[r_hw[b]], writes=[r_hs[b]])
                    if sname == "lat":
                        P.emit("pool", lambda e, b=b: e.memset(hsb[b][:, 0:4, 0:SC:64], 0.0), writes=[r_hs[b]])
                        P.emit("pool", lambda e, b=b: e.memset(hsb[b][:, 4:8, 63:SC:64], 0.0), writes=[r_hs[b]])
                    groups = [(0 + 96 * z, 96, "w", 0)]
                    if z == 0:
                        groups += [(192, 96, "a", 0)]
                    else:
                        groups += [(192, 96, "a", 0), (288, 96, "a", 1), (384, 128, "g", 0), (512, 128, "g", 1)]
                    for (c0, m, kind, gi) in groups:
                        pt, r_pt, _ = ps.next()
                        k = 0
                        for kc in range(KC):
                            for (wt, src) in ((LWa, hw[b][:, kc, 64:64 + SC]), (LWb, hsb[b][:, kc, :])):
                                P.emit("pe", lambda e, pt=pt, wt=wt, src=src, kc=kc, c0=c0, m=m, k=k: e.matmul(
                                    pt[0:m, 0:SC], lhsT=wt[:, kc, c0:c0 + m], rhs=src, start=(k == 0), stop=(k == 2 * KC - 1)),
                                    reads=[r_c, r_hw[b], r_hs[b]], writes=[r_pt])
                                k += 1
                        if kind == "w":
                            P.emit("act", lambda e, pt=pt, b=b: e.activation(out=xw[b][:, :], in_=pt[0:96, 0:SC], func=AF.Tanh),
                                   reads=[r_pt], writes=[r_x[b]])
                        elif kind == "a":
                            P.emit("act", lambda e, pt=pt, b=b, gi=gi: e.activation(out=xa[b][:, gi, :], in_=pt[0:96, 0:SC], func=AF.Copy),
                                   reads=[r_pt], writes=[r_x[b]])
                        else:
                            P.emit("act", lambda e, pt=pt, b=b, gi=gi: e.activation(out=xg[b][:, gi, :], in_=pt[0:128, 0:SC], func=AF.Sigmoid),
                                   reads=[r_pt], writes=[r_x[b]])
                    pa, r_pa, _ = ps.next()
                    pb_, r_pb, _ = ps.next()
                    k = 0
                    for kc in range(KC):
                        for (src, wt) in ((hw[b][:, kc, 64 + 0 * 128:64 + (0 + 1) * 128], RKa), (hsb[b][:, kc, 0 * 128:(0 + 1) * 128], RKb)):
                            P.emit("pe", lambda e, pa=pa, src=src, wt=wt, kc=kc, k=k: e.matmul(
                                pa[:, 0:512], lhsT=src, rhs=wt[:, kc, 0:512], start=(k == 0), stop=(k == 2 * KC - 1)),
                                reads=[r_c, r_hw[b], r_hs[b]], writes=[r_pa])
                            P.emit("pe", lambda e, pb_=pb_, src=src, wt=wt, kc=kc, k=k: e.matmul(
                                pb_[:, 0:256], lhsT=src, rhs=wt[:, kc, 512:768], start=(k == 0), stop=(k == 2 * KC - 1)),
                                reads=[r_c, r_hw[b], r_hs[b]], writes=[r_pb])
                            k += 1
                    P.emit("act", lambda e, b=b, pa=pa: e.activation(out=rkv[b][:, 0:512], in_=pa[:, 0:512], func=AF.Copy),
                           reads=[r_pa], writes=[r_rkv[b]])
                    P.emit("act", lambda e, b=b, pb_=pb_: e.activation(out=rkv[b][:, 512:768], in_=pb_[:, 0:256], func=AF.Copy),
                           reads=[r_pb], writes=[r_rkv[b]])
            def _sc(sci, pre_done, nxt):
                t0 = sci * SC
                if not pre_done:
                    _proj(sci)
                ch_order = range(SC // 128) if z == 0 else range(SC // 128 - 1, -1, -1)
                def _ch(ci):
                    tok0 = t0 + ci * 128
                    for b in range(nb):
                        S_ = scr[b]; RS = r_scr[b]
                        r_t, k_t, v_t = rkv[b][:, 0:256], rkv[b][:, 256:512], rkv[b][:, 512:768]
                        cs = slice(ci * 128, (ci + 1) * 128)
                        pw, r_pw, _ = ps.next()
                        P.emit("pe", lambda e, pw=pw, b=b, cs=cs: e.matmul(pw[:, 0:256], lhsT=xw[b][:, cs], rhs=W2s[:, z, :], start=True, stop=True),
                               reads=[r_x[b], r_c], writes=[r_pw])
                        P.emit("dve", lambda e, pw=pw, b=b: e.tensor_tensor(out=S_[0][:], in0=pw[:, 0:256], in1=W0(z), op=ALU.add),
                               reads=[r_pw, r_c], writes=[RS[0]])
                        P.emit("act", lambda e, b=b: e.activation(out=S_[0][:], in_=S_[0][:], func=AF.Sigmoid), reads=[RS[0]], writes=[RS[0]])
                        zs = [z] if z == 0 else [1, 0]
                        for ai, za in enumerate(zs):
                            pw, r_pw, _ = ps.next()
                            P.emit("pe", lambda e, pw=pw, b=b, cs=cs, za=za: e.matmul(pw[:, 0:256], lhsT=xa[b][:, za, cs], rhs=A2s[:, za, :], start=True, stop=True),
                                   reads=[r_x[b], r_c], writes=[r_pw])
                            P.emit("dve", lambda e, pw=pw, b=b, ai=ai, za=za: e.tensor_tensor(out=S_[1 + ai][:], in0=pw[:, 0:256], in1=A0(za), op=ALU.add),
                                   reads=[r_pw, r_c], writes=[RS[1 + ai]])
                            P.emit("act", lambda e, b=b, ai=ai: e.activation(out=S_[1 + ai][:], in_=S_[1 + ai][:], func=AF.Sigmoid),
                                   reads=[RS[1 + ai]], writes=[RS[1 + ai]])
                        if z == 1:
                            pw, r_pw, _ = ps.next()
                            for kc2 in range(2):
                                P.emit("pe", lambda e, pw=pw, b=b, cs=cs, kc2=kc2: e.matmul(pw[:, 0:256], lhsT=xg[b][:, kc2, cs], rhs=G2s[:, kc2, :],
                                                                                           start=(kc2 == 0), stop=(kc2 == 1)), reads=[r_x[b], r_c], writes=[r_pw])
                            P.emit("act", lambda e, pw=pw, b=b: e.activation(out=keep[b]["g"][:], in_=pw[:, 0:256], func=AF.Copy),
                                   reads=[r_pw], writes=[r_keep[b]])
                        ew([r_rkv[b], r_c], [RS[3]], lambda e, b=b, k_t=k_t: e.tensor_tensor(out=S_[3][:], in0=k_t, in1=KK_, op=ALU.mult))
                        ew([RS[3]], [RS[4]], lambda e, b=b: e.tensor_tensor(out=S_[4][:], in0=S_[3][:], in1=S_[3][:], op=ALU.mult))
                        P.emit("dve", lambda e, b=b: e.tensor_reduce(out=small[b][0][:], in_=hv(S_[4][:]), axis=AX.X, op=ALU.add),
                               reads=[RS[4]], writes=[r_small[b][0]])
                        P.emit("dve", lambda e, b=b: e.tensor_scalar(out=small[b][0][:], in0=small[b][0][:], scalar1=1e-24, scalar2=None, op0=ALU.max),
                               reads=[r_small[b][0]], writes=[r_small[b][0]])
                        P.emit("act", lambda e, b=b: e.activation(out=small[b][0][:], in_=small[b][0][:], func=AF.Sqrt),
                               reads=[r_small[b][0]], writes=[r_small[b][0]])
                        P.emit("dve", lambda e, b=b: e.reciprocal(out=small[b][0][:], in_=small[b][0][:]),
                               reads=[r_small[b][0]], writes=[r_small[b][0]])
                        ew([RS[3], r_small[b][0]], [RS[3]], lambda e, b=b: e.tensor_tensor(
                            out=hv(S_[3][:]), in0=hv(S_[3][:]), in1=small[b][0][:].unsqueeze(2).to_broadcast([128, 4, 64]), op=ALU.mult))
                        pl, r_pl, _ = ps.next()
                        pe2, r_pe2, _ = ps.next()
                        P.emit("pe", lambda e, pl=pl, b=b: e.matmul(pl[:, 0:256], lhsT=ctri[:, z, 0, :], rhs=S_[0][:], start=True, stop=True),
                               reads=[RS[0], r_c], writes=[r_pl])
                        P.emit("pe", lambda e, pl=pl, b=b: e.matmul(pl[:, 256:512], lhsT=ctri[:, z, 1, :], rhs=S_[0][:], start=True, stop=True),
                               reads=[RS[0], r_c], writes=[r_pl])
                        P.emit("pe", lambda e, pe2=pe2, b=b: e.matmul(pe2[:, 0:256], lhsT=ctri[:, z, 2, :], rhs=S_[0][:], start=True, stop=True),
                               reads=[RS[0], r_c], writes=[r_pe2])
                        for hh in range(4):
                            P.emit("pe", lambda e, pe2=pe2, b=b, hh=hh: e.matmul(pe2[0:64, 256 + hh:257 + hh], lhsT=S_[0][:, hh * 64:(hh + 1) * 64], rhs=negcol[:, 0:1],
                                                                                 start=True, stop=True), reads=[RS[0], r_c], writes=[r_pe2])
                        P.emit("act", lambda e, pl=pl, b=b: e.activation(out=S_[5][:], in_=pl[:, 0:256], func=AF.Exp), reads=[r_pl], writes=[RS[5]])
                        P.emit("act", lambda e, pl=pl, b=b: e.activation(out=S_[6][:], in_=pl[:, 0:256], func=AF.Exp, scale=-1.0), reads=[r_pl], writes=[RS[6]])
                        P.emit("act", lambda e, pl=pl, b=b: e.activation(out=S_[7][:], in_=pl[:, 256:512], func=AF.Exp), reads=[r_pl], writes=[RS[7]])
                        P.emit("act", lambda e, pe2=pe2, b=b: e.activation(out=S_[8][:], in_=pe2[:, 0:256], func=AF.Exp), reads=[r_pe2], writes=[RS[8]])
                        P.emit("act", lambda e, pe2=pe2, b=b: e.activation(out=gend[b][:], in_=pe2[0:64, 256:260], func=AF.Exp), reads=[r_pe2], writes=[r_gend[b]])
                        O_ = opt[b]; RO = r_opt[b]
                        ew([RS[3], RS[7]], [RO["At"]], lambda e, b=b: e.scalar_tensor_tensor(
                            out=O_["At"][:], in0=S_[3][:], scalar=-1.0, in1=S_[7][:], op0=ALU.mult, op1=ALU.mult), allow=("dve",))
                        ew([r_rkv[b], RS[5]], [RO["Rt"]], lambda e, b=b, r_t=r_t: e.tensor_tensor(out=O_["Rt"][:], in0=r_t, in1=S_[5][:], op=ALU.mult))
                        ew([r_rkv[b]], [RO["Vt"]], lambda e, b=b, v_t=v_t: e.tensor_copy(out=O_["Vt"][:], in_=v_t))
                        ew([RS[3], RS[1]], [RS[9]], lambda e, b=b: e.tensor_tensor(out=S_[9][:], in0=S_[3][:], in1=S_[1][:], op=ALU.mult))
                        ew([RS[9], RS[6]], [RO["Bt"]], lambda e, b=b: e.tensor_tensor(out=O_["Bt"][:], in0=S_[9][:], in1=S_[6][:], op=ALU.mult))
                        ew([RS[1], r_c], [RS[10]], lambda e, b=b: e.scalar_tensor_tensor(
                            out=S_[10][:], in0=S_[1][:], scalar=-1.0, in1=KA_, op0=ALU.add, op1=ALU.mult), allow=("dve",))
                        ew([RS[10], r_rkv[b]], [RS[10]], lambda e, b=b, k_t=k_t: e.scalar_tensor_tensor(
                            out=S_[10][:], in0=S_[10][:], scalar=1.0, in1=k_t, op0=ALU.add, op1=ALU.mult), allow=("dve",))
                        ew([RS[10], RS[6]], [RO["Kt"]], lambda e, b=b: e.tensor_tensor(out=O_["Kt"][:], in0=S_[10][:], in1=S_[6][:], op=ALU.mult))
                        ew([RO["Bt"], RS[8]], [RO["Bh"]], lambda e, b=b: e.tensor_tensor(out=O_["Bh"][:], in0=O_["Bt"][:], in1=S_[8][:], op=ALU.mult))
                        ew([RO["Kt"], RS[8]], [RO["Kh"]], lambda e, b=b: e.tensor_tensor(out=O_["Kh"][:], in0=O_["Kt"][:], in1=S_[8][:], op=ALU.mult))
                        if z == 1 and sname == "lat":
                            ew([RS[2], r_c], [RS[11]], lambda e, b=b: e.scalar_tensor_tensor(
                                out=S_[11][:], in0=S_[2][:], scalar=-1.0, in1=KA_, op0=ALU.add, op1=ALU.mult), allow=("dve",))
                            ew([RS[11], r_rkv[b]], [RS[11]], lambda e, b=b, k_t=k_t: e.scalar_tensor_tensor(
                                out=S_[11][:], in0=S_[11][:], scalar=1.0, in1=k_t, op0=ALU.add, op1=ALU.mult), allow=("dve",))
                            ew([RS[11], RS[10]], [RS[11]], lambda e, b=b: e.tensor_tensor(out=S_[11][:], in0=S_[11][:], in1=S_[10][:], op=ALU.add))
                            ew([RS[11]], [r_keep[b]], lambda e, b=b: e.tensor_scalar(out=keep[b]["kb"][:], in0=S_[11][:], scalar1=0.5, scalar2=None, op0=ALU.mult))
                    if dbg is not None and (z, sname, sci, ci) == dbg_at:
                        b = 0
                        dbg("rkv", rkv[b][:], [r_rkv[b]])
                        for i in (0, 1, 3, 5, 6, 7, 8, 9, 10):
                            dbg("s%d" % i, scr[b][i][:], [r_scr[b][i]])
                        for nme in opn:
                            dbg(nme, opt[b][nme][:], [r_opt[b][nme]])
                        dbg("gend", gend[b][:], [r_gend[b]])
                        dbg("hw", hw[b][:], [r_hw[b]])
                        dbg("hsb", hsb[b][:], [r_hs[b]])
                        dbg("xw", xw[b][:], [r_x[b]])
                        dbg("xa", xa[b][:], [r_x[b]])
                    if nxt is not None:
                        _proj(nxt)
                    for (b, h) in units:
                        u = U[(b, h)]; O_ = opt[b]; RO = r_opt[b]
                        hs_ = slice(h * 64, (h + 1) * 64)
                        pt, r_pt, _ = ps.next()
                        ptb = pt[:].bitcast(BF16)
                        for i, nme in enumerate(("At", "Rt", "Bt", "Kt")):
                            P.emit("pe", lambda e, ptb=ptb, i=i, nme=nme, b=b, hs_=hs_: e.transpose(
                                ptb[0:64, i * 128:(i + 1) * 128], O_[nme][:, hs_], ident[:]), reads=[RO[nme], r_c], writes=[r_pt])
                        P.emit("act", lambda e, ptb=ptb, u=u: e.activation(out=u["fm"][:].rearrange("p a t -> p (a t)"), in_=ptb[0:64, 0:512], func=AF.Copy),
                               reads=[r_pt], writes=[u["r_fm"]])
                        fm = u["fm"]
                        p1, r_p1, _ = ps.next()
                        p3, r_p3, _ = ps.next()
                        P.emit("pe", lambda e, p1=p1, fm=fm: e.matmul(p1[:, 0:256], lhsT=fm[:, 2, :], rhs=fm[:, 0:2, :].rearrange("p a t -> p (a t)"), start=True, stop=True),
                               reads=[u["r_fm"]], writes=[r_p1])
                        P.emit("pe", lambda e, p1=p1, fm=fm: e.matmul(p1[:, 256:384], lhsT=fm[:, 3, :], rhs=fm[:, 1, :], start=True, stop=True),
                               reads=[u["r_fm"]], writes=[r_p1])
                        P.emit("pe", lambda e, p3=p3, fm=fm: e.matmul(p3[:, 0:256], lhsT=fm[:, 0, :], rhs=fm[:, 2:4, :].rearrange("p a t -> p (a t)"), start=True, stop=True),
                               reads=[u["r_fm"]], writes=[r_p3])
                        P.emit("dve", lambda e, p1=p1, u=u: e.tensor_tensor(out=u["Mrk"][:], in0=p1[:, 128:384], in1=cmask[:, z, 2, :], op=ALU.mult),
                               reads=[r_p1, r_c], writes=[u["r_M"]])
                        P.emit("dve", lambda e, p3=p3, u=u: e.tensor_tensor(out=u["MkaT"][:], in0=p3[:, 128:256], in1=cmask[:, z, 1, 128:256], op=ALU.mult),
                               reads=[r_p3, r_c], writes=[u["r_M"]])
                        P.emit("dve", lambda e, p1=p1, u=u: e.tensor_tensor(out=u["PP"][1][:, 0:128], in0=p1[:, 0:128], in1=cmask[:, z, 0, 0:128], op=ALU.mult),
                               reads=[r_p1, r_c], writes=[u["r_PP"][1]])
                        P.emit("dve", lambda e, p3=p3, u=u: e.tensor_tensor(out=u["PP"][1][:, 128:256], in0=p3[:, 0:128], in1=cmask[:, z, 1, 0:128], op=ALU.mult),
                               reads=[r_p3, r_c], writes=[u["r_PP"][1]])
                        P.emit("dve", lambda e, u=u: e.tensor_tensor(out=u["T"][0][:], in0=u["PP"][1][:, 0:128], in1=ident[:], op=ALU.add),
                               reads=[u["r_PP"][1], r_c], writes=[u["r_T"][0]])
                    for kk_ in range(1, 7):
                        for (b, h) in units:
                            u = U[(b, h)]
                            Pm, PTm, rd = u["PP"][kk_ % 2][:, 0:128], u["PP"][kk_ % 2][:, 128:256], u["r_PP"][kk_ % 2]
                            dst, r_dst = u["PP"][(kk_ + 1) % 2], u["r_PP"][(kk_ + 1) % 2]
                            pp, r_pp, _ = ps.next()
                            P.emit("pe", lambda e, pp=pp, Pm=Pm, PTm=PTm: e.matmul(pp[:, 128:256], lhsT=Pm, rhs=PTm, start=True, stop=True),
                                   reads=[rd], writes=[r_pp])
                            if kk_ < 6:
                                P.emit("pe", lambda e, pp=pp, Pm=Pm, PTm=PTm: e.matmul(pp[:, 0:128], lhsT=PTm, rhs=Pm, start=True, stop=True),
                                       reads=[rd], writes=[r_pp])
                                P.emit("act", lambda e, pp=pp, dst=dst: e.activation(out=dst[:], in_=pp[:, 0:256], func=AF.Copy), reads=[r_pp], writes=[r_dst])
                            else:
                                P.emit("act", lambda e, pp=pp, dst=dst: e.activation(out=dst[:, 128:256], in_=pp[:, 128:256], func=AF.Copy), reads=[r_pp], writes=[r_dst])
                        for (b, h) in units:
                            u = U[(b, h)]
                            PTk, r_ptk = u["PP"][(kk_ + 1) % 2][:, 128:256], u["r_PP"][(kk_ + 1) % 2]
                            Told, r_told = u["T"][(kk_ - 1) % 2], u["r_T"][(kk_ - 1) % 2]
                            pt, r_pt, _ = ps.next()
                            P.emit("pe", lambda e, pt=pt, PTk=PTk, Told=Told: e.matmul(pt[:, 0:128], lhsT=PTk, rhs=Told[:], start=True, stop=True),
                                   reads=[r_ptk, r_told], writes=[r_pt])
                            if kk_ < 6:
                                Tnew, r_tnew = u["T"][kk_ % 2], u["r_T"][kk_ % 2]
                            else:
                                Tnew, r_tnew = u["Tbf"], u["r_Tbf"]
                            P.emit("dve", lambda e, pt=pt, Tnew=Tnew, Told=Told: e.tensor_tensor(out=Tnew[:], in0=pt[:, 0:128], in1=Told[:], op=ALU.add),
                                   reads=[r_pt, r_told], writes=[r_tnew])
                    for (b, h) in units:
                        u = U[(b, h)]; O_ = opt[b]; RO = r_opt[b]
                        hs_ = slice(h * 64, (h + 1) * 64)
                        Tf, r_tf = u["Tbf"], u["r_Tbf"]
                        px, r_px, _ = ps.next()
                        P.emit("pe", lambda e, px=px, u=u, Tf=Tf: e.matmul(px[:, 0:128], lhsT=u["MkaT"][:], rhs=Tf[:], start=True, stop=True),
                               reads=[u["r_M"], r_tf], writes=[r_px])
                        P.emit("pe", lambda e, px=px, b=b, hs_=hs_, Tf=Tf: e.matmul(px[0:64, 128:256], lhsT=O_["At"][:, hs_], rhs=Tf[:], start=True, stop=True),
                               reads=[RO["At"], r_tf], writes=[r_px])
                        P.emit("act", lambda e, px=px, u=u: e.activation(out=u["X"][:], in_=px[:, 0:128], func=AF.Copy), reads=[r_px], writes=[u["r_XA"]])
                        P.emit("dve", lambda e, px=px, u=u: e.tensor_copy(out=u["Ah"][:], in_=px[0:64, 128:256]), reads=[r_px], writes=[u["r_XA"]])
                    for (b, h) in units:
                        u = U[(b, h)]; O_ = opt[b]; RO = r_opt[b]
                        hs_ = slice(h * 64, (h + 1) * 64)
                        pu, r_pu, _ = ps.next()
                        P.emit("pe", lambda e, pu=pu, u=u: e.matmul(pu[:, 0:64], lhsT=u["Ah"][:], rhs=u["Sbf"][:], start=True, stop=False),
                               reads=[u["r_XA"], u["r_Sbf"]], writes=[r_pu])
                        P.emit("pe", lambda e, pu=pu, u=u, b=b, hs_=hs_: e.matmul(pu[:, 0:64], lhsT=u["X"][:], rhs=O_["Vt"][:, hs_], start=False, stop=True),
                               reads=[u["r_XA"], RO["Vt"]], writes=[r_pu])
                        P.emit("act", lambda e, pu=pu, u=u: e.activation(out=u["Ut"][:], in_=pu[:, 0:64], func=AF.Copy), reads=[r_pu], writes=[u["r_Ut"]])
                    for (b, h) in units:
                        u = U[(b, h)]; O_ = opt[b]; RO = r_opt[b]
                        hs_ = slice(h * 64, (h + 1) * 64)
                        py, r_py, _ = ps.next()
                        P.emit("pe", lambda e, py=py, u=u: e.matmul(py[:, 0:64], lhsT=u["fm"][:, 1, :], rhs=u["Sbf"][:], start=True, stop=False),
                               reads=[u["r_fm"], u["r_Sbf"]], writes=[r_py])
                        P.emit("pe", lambda e, py=py, u=u: e.matmul(py[:, 0:64], lhsT=u["Mrk"][:, 0:128], rhs=u["Ut"][:], start=False, stop=False),
                               reads=[u["r_M"], u["r_Ut"]], writes=[r_py])
                        P.emit("pe", lambda e, py=py, u=u, b=b, hs_=hs_: e.matmul(py[:, 0:64], lhsT=u["Mrk"][:, 128:256], rhs=O_["Vt"][:, hs_], start=False, stop=True),
                               reads=[u["r_M"], RO["Vt"]], writes=[r_py])
                        P.emit("act", lambda e, py=py, b=b, hs_=hs_: e.activation(out=Yt[b][:, hs_], in_=py[:, 0:64], func=AF.Copy), reads=[r_py], writes=[r_Y[b]])
                        pss, r_pss, _ = ps.next()
                        P.emit("pe", lambda e, pss=pss, u=u, b=b, hs_=hs_: e.matmul(pss[0:64, 0:64], lhsT=O_["Bh"][:, hs_], rhs=u["Ut"][:], start=True, stop=False),
                               reads=[RO["Bh"], u["r_Ut"]], writes=[r_pss])
                        P.emit("pe", lambda e, pss=pss, u=u, b=b, hs_=hs_: e.matmul(pss[0:64, 0:64], lhsT=O_["Kh"][:, hs_], rhs=O_["Vt"][:, hs_], start=False, stop=True),
                               reads=[RO["Kh"], RO["Vt"]], writes=[r_pss])
                        P.emit("dve", lambda e, pss=pss, u=u, b=b, h=h: e.scalar_tensor_tensor(
                            out=u["S32"][:], in0=u["S32"][:], scalar=gend[b][:, h:h + 1], in1=pss[0:64, 0:64], op0=ALU.mult, op1=ALU.add),
                            reads=[r_pss, r_gend[b], u["r_S"]], writes=[u["r_S"]])
                        P.emit("pool", lambda e, u=u: e.tensor_copy(out=u["Sbf"][:], in_=u["S32"][:]), reads=[u["r_S"]], writes=[u["r_Sbf"]])
                    if dbg is not None and (z, sname, sci, ci) == dbg_at:
                        u = U[(0, 0)]
                        dbg("fm", u["fm"][:], [u["r_fm"]])
                        dbg("Mrk", u["Mrk"][:], [u["r_M"]])
                        dbg("T", u["Tbf"][:], [u["r_Tbf"]])
                        dbg("X", u["X"][:], [u["r_XA"]]); dbg("Ah", u["Ah"][:], [u["r_XA"]])
                        dbg("Ut", u["Ut"][:], [u["r_Ut"]])
                        dbg("Y", Yt[0][:], [r_Y[0]])
                        dbg("S32", u["S32"][:], [u["r_S"]])
                    if sname != "lat":
                        return
                    for b in range(nb):
                        if z == 0:
                            P.emit("sp", lambda e, b=b, tok0=tok0: e.dma_start(out=yscr[b, tok0:tok0 + 128, :], in_=Yt[b][:]),
                                   reads=[r_Y[b]], writes=[r_out], chan=pfx + "ys%d" % b)
                            continue
                        S_ = scr[b]; RS = r_scr[b]; K_ = keep[b]
                        P.emit("sp", lambda e, b=b, tok0=tok0: e.dma_start(out=yfin[b][:], in_=yscr[b, tok0:tok0 + 128, :]),
                               reads=[r_out], writes=[r_yf[b]], chan=pfx + "yl%d" % b)
                        ew([r_Y[b], r_yf[b]], [RS[0]], lambda e, b=b: e.tensor_tensor(out=S_[0][:], in0=Yt[b][:], in1=yfin[b][:], op=ALU.add))
                        P.emit("dve", lambda e, b=b: e.tensor_reduce(out=small[b][1][:], in_=hv(S_[0][:]), axis=AX.X, op=ALU.add),
                               reads=[RS[0]], writes=[r_small[b][1]])
                        P.emit("dve", lambda e, b=b: e.tensor_scalar(out=small[b][1][:], in0=small[b][1][:], scalar1=1.0 / 64, scalar2=None, op0=ALU.mult),
                               reads=[r_small[b][1]], writes=[r_small[b][1]])
                        ew([RS[0], r_small[b][1]], [RS[1]], lambda e, b=b: e.tensor_tensor(
                            out=hv(S_[1][:]), in0=hv(S_[0][:]), in1=small[b][1][:].unsqueeze(2).to_broadcast([128, 4, 64]), op=ALU.subtract))
                        ew([RS[1]], [RS[2]], lambda e, b=b: e.tensor_tensor(out=S_[2][:], in0=S_[1][:], in1=S_[1][:], op=ALU.mult))
                        P.emit("dve", lambda e, b=b: e.tensor_reduce(out=small[b][2][:], in_=hv(S_[2][:]), axis=AX.X, op=ALU.add),
                               reads=[RS[2]], writes=[r_small[b][2]])
                        P.emit("act", lambda e, b=b: e.activation(out=small[b][2][:], in_=small[b][2][:], func=AF.Sqrt, scale=1.0 / 64, bias=gneps[:, 0:1]),
                               reads=[r_small[b][2], r_c], writes=[r_small[b][2]])
                        P.emit("dve", lambda e, b=b: e.reciprocal(out=small[b][2][:], in_=small[b][2][:]), reads=[r_small[b][2]], writes=[r_small[b][2]])
                        ew([RS[1], r_small[b][2]], [RS[1]], lambda e, b=b: e.tensor_tensor(
                            out=hv(S_[1][:]), in0=hv(S_[1][:]), in1=small[b][2][:].unsqueeze(2).to_broadcast([128, 4, 64]), op=ALU.mult))
                        ew([RS[1], r_c], [RS[1]], lambda e, b=b: e.tensor_tensor(out=S_[1][:], in0=S_[1][:], in1=LNW_, op=ALU.mult))
                        ew([RS[1], r_c], [RS[1]], lambda e, b=b: e.tensor_tensor(out=S_[1][:], in0=S_[1][:], in1=LNB_, op=ALU.add))
                        ew([r_keep[b], r_rkv[b]], [RS[3]], lambda e, b=b: e.tensor_tensor(out=S_[3][:], in0=rkv[b][:, 0:256], in1=K_["kb"][:], op=ALU.mult))
                        ew([RS[3], r_c], [RS[3]], lambda e, b=b: e.tensor_tensor(out=S_[3][:], in0=S_[3][:], in1=RK_, op=ALU.mult))
                        P.emit("dve", lambda e, b=b: e.tensor_reduce(out=small[b][3][:], in_=hv(S_[3][:]), axis=AX.X, op=ALU.add),
                               reads=[RS[3]], writes=[r_small[b][3]])
                        ew([r_rkv[b], r_small[b][3]], [RS[3]], lambda e, b=b: e.tensor_tensor(
                            out=hv(S_[3][:]), in0=hv(rkv[b][:, 512:768]), in1=small[b][3][:].unsqueeze(2).to_broadcast([128, 4, 64]), op=ALU.mult))
                        ew([RS[1], RS[3]], [RS[1]], lambda e, b=b: e.tensor_tensor(out=S_[1][:], in0=S_[1][:], in1=S_[3][:], op=ALU.add))
                        ew([RS[1], r_keep[b]], [r_ob[b]], lambda e, b=b: e.tensor_tensor(out=ob[b][:], in0=S_[1][:], in1=K_["g"][:], op=ALU.mult))
                        P.emit("sp", lambda e, b=b, tok0=tok0: e.dma_start(out=o_out[b, tok0:tok0 + 128, :], in_=ob[b][:]),
                               reads=[r_ob[b]], writes=[r_out], chan=pfx + "oo%d" % b)
                for ci in ch_order:
                    _ch(ci)
            order = list(sc_order)
            for i, sci in enumerate(order):
                if prefetch and z == 0:
                    _sc(sci, i > 0, order[i + 1] if i + 1 < len(order) else None)
                else:
                    _sc(sci, False, None)
        for seg in segs:
            _seg(*seg)
    for z in range(2):
        _pass(z)


def build_l45(nlat=SEQ, nctx=CTX, nb=2, debug=False, same_sync=True, ew_engs=("dve",)):
    nc = bass.Bass("TRN2", target_bir_lowering=False)
    dt = lambda name, shape, dty, kind: nc.dram_tensor(name, shape, dty, kind=kind).ap()
    hT = dt("hT", [nb, D, nctx + nlat], BF16, "ExternalInput")
    wd = {"c_mask": dt("c_mask", [128, 2, 3, 256], BF16, "ExternalInput"), "c_tri": dt("c_tri", [128, 2, 3, 128], F32, "ExternalInput"),
          "c_ident": dt("c_ident", [128, 128], BF16, "ExternalInput"), "vecs": dt("vecs", [128, 9, 256], F32, "ExternalInput"),
          "mu": dt("mu", [128, KC, 6], F32, "ExternalInput"), "w2": dt("w2", [96, 2, 256], F32, "ExternalInput"),
          "a2": dt("a2", [96, 2, 256], F32, "ExternalInput"), "g2": dt("g2", [128, 2, 256], F32, "ExternalInput"),
          "rkv": dt("rkv", [128, KC, 768], F32, "ExternalInput"), "lw": dt("lw", [128, KC, 640], F32, "ExternalInput")}
    yscr = dt("yscr", [nb, nlat, 256], F32, "ExternalOutput")
    o_out = dt("o_out", [nb, nlat, 256], BF16, "ExternalOutput")
    with contextlib.ExitStack() as st:
        P = Prog(nc, same_engine_sync=same_sync)
        dbg = None
        if debug:
            def dbg(name, ap, reads):
                t = nc.dram_tensor("dbg_" + name, list(ap.shape), ap.dtype, kind="ExternalOutput").ap()
                P.emit("sp", lambda e: e.dma_start(out=t, in_=ap), reads=reads, chan="dbg_" + name)
        emit_rwkv(nc, P, st, hT, Res(), wd, yscr, o_out, Res(), nlat=nlat, nctx=nctx, nb=nb, dbg=dbg, ew_engs=ew_engs)
        P.run(final_waits=_all_dma_tails(P))
    return nc


def rwkv_host_weights(inp, g):
    cs = slice(256 * g, 256 * (g + 1))
    rkv = np.concatenate([inp["rwkv_w_rkv"][0, i][:, cs] for i in range(3)], axis=1)
    lw = np.concatenate([inp["rwkv_w1"][0, 0], inp["rwkv_w1"][0, 1], inp["rwkv_a1"][0, 0], inp["rwkv_a1"][0, 1], inp["rwkv_g1"][0]], axis=1)
    vec = np.stack([inp["rwkv_w0"][0, 0][cs], inp["rwkv_w0"][0, 1][cs], inp["rwkv_a0"][0, 0][cs], inp["rwkv_a0"][0, 1][cs],
                    inp["rwkv_k_k"][0][cs], inp["rwkv_k_a"][0][cs], inp["rwkv_r_k"][0].reshape(-1)[cs], inp["rwkv_ln_w"][0][cs], inp["rwkv_ln_b"][0][cs]])
    return {"rkv": lay_sq(rkv), "lw": lay_sq(lw),
            "vecs": np.ascontiguousarray(np.broadcast_to(vec[None], (128, 9, 256))).astype(np.float32),
            "mu": np.ascontiguousarray(lay_vec(inp["rwkv_mu"][0]).transpose(0, 2, 1)),
            "w2": np.ascontiguousarray(inp["rwkv_w2"][0][:, :, cs].transpose(1, 0, 2)),
            "a2": np.ascontiguousarray(inp["rwkv_a2"][0][:, :, cs].transpose(1, 0, 2)),
            "g2": np.ascontiguousarray(inp["rwkv_g2"][0][:, cs].reshape(2, 128, 256).transpose(1, 0, 2))}


def lay_wo(w):
    return np.ascontiguousarray(w.reshape(KC, 128, 8, 256).transpose(2, 1, 0, 3))


def build_l3(nlat, nctx):
    NT = nlat + nctx
    nc = bass.Bass("TRN2", target_bir_lowering=False)
    dt = lambda name, shape, dty, kind="ExternalInput": nc.dram_tensor(name, shape, dty, kind=kind).ap()
    X1 = dt("X1", [D, NT], F32)
    fT = dt("fT", [D, NT], BF16)
    modo = dt("modo", [128, 2 * 144 * 3], F32); gso = dt("gso", [128, 2 * 3 * KC * 3], F32); hgo = dt("hgo", [128, 2 * 3 * KC * 3], F32)
    wo = dt("wo", [8, 128, KC, 256], F32)
    bo = dt("bo", [128, KC], F32)
    w13a = dt("w13a", [JC, 128, KC, 256], F32); w2a = dt("w2a", [KC, 128, JC, 128], F32)
    w13b = dt("w13b", [JC, 128, KC, 256], F32); w2b = dt("w2b", [KC, 128, JC, 128], F32)
    X2 = dt("X2", [D, NT], F32, "Internal"); X3 = dt("X3", [D, NT], F32, "Internal")
    X4 = dt("X4", [D, NT], F32, "ExternalOutput")
    h1 = dt("h1", [D, NT], BF16, "ExternalOutput")
    with contextlib.ExitStack() as st:
        P = Prog(nc)
        dn = Dense(nc, P, st, 768, 0)
        dn.mod_load(modo, gso, hgo)
        bos = st.enter_context(nc.sbuf_tensor("sb_bos", [128, KC], F32))
        P.emit("sp", lambda e: e.dma_start(out=bos[:], in_=bo), writes=[dn.r_mod], chan="bo")
        r_in = Res()
        for blocks in make_passes(nlat, nctx, 0):
            r2, r3, r4, rh = Res(), Res(), Res(), Res()
            dn.linear_res(blocks, fT, r_in, wo, bos, X1, r_in, X2, r2, 0)
            dn.norm_mod(blocks, X2, r2, 0, 2)
            dn.ffn(blocks, w13a, w2a, X2, r2, X3, r3, 0, 2)
            dn.norm_mod(blocks, X3, r3, 1, 0)
            dn.ffn(blocks, w13b, w2b, X3, r3, X4, r4, 1, 0)
            dn.norm_mod(blocks, X4, r4, 1, 1, out_dram=h1, r_out=rh)
        P.run(final_waits=_all_dma_tails(P))
    return nc


def build_l6(nlat):
    NT = nlat
    nc = bass.Bass("TRN2", target_bir_lowering=False)
    dt = lambda name, shape, dty, kind="ExternalInput": nc.dram_tensor(name, shape, dty, kind=kind).ap()
    X4 = dt("X4", [D, NT], F32)
    oT = dt("oT", [D, NT], BF16)
    modo = dt("modo", [128, 2 * 144 * 3], F32); gso = dt("gso", [128, 2 * 3 * KC * 3], F32); hgo = dt("hgo", [128, 2 * 3 * KC * 3], F32)
    wo = dt("wo", [8, 128, KC, 256], F32)
    fng = dt("fng", [128, KC], F32)
    w13 = dt("w13", [JC, 128, KC, 256], F32); w2 = dt("w2", [KC, 128, JC, 128], F32)
    X5 = dt("X5", [D, NT], F32, "Internal"); X6 = dt("X6", [D, NT], F32, "Internal")
    out = dt("out", [D, NT], F32, "ExternalOutput")
    with contextlib.ExitStack() as st:
        P = Prog(nc)
        dn = Dense(nc, P, st, 768, 0)
        dn.mod_load(modo, gso, hgo)
        fgs = st.enter_context(nc.sbuf_tensor("sb_fgs", [128, KC], F32))
        P.emit("sp", lambda e: e.dma_start(out=fgs[:], in_=fng), writes=[dn.r_mod], chan="fg")
        r_in = Res()
        for blocks in make_passes(nlat, 0, 0):
            r5, r6, ro = Res(), Res(), Res()
            dn.linear_res(blocks, oT, r_in, wo, None, X4, r_in, X5, r5, 1)
            dn.norm_mod(blocks, X5, r5, 1, 2)
            dn.ffn(blocks, w13, w2, X5, r5, X6, r6, 1, 2)
            dn.norm_mod(blocks, X6, r6, 0, 0, out_dram=out, r_out=ro, final_g=fgs)
        P.run(final_waits=_all_dma_tails(P))
    return nc


_DBG = {}


def _run(nc, maps):
    res = run_bass_kernel_spmd(nc, maps, core_ids=list(range(NCORES)))
    return res.results


def kernel(x, c, ctx, c_ctx, mod_w, mod_b, norm_w, ffn_w13, ffn_w2, fnet_w_o, fnet_b_o,
           rwkv_mu, rwkv_w_rkv, rwkv_w0, rwkv_w1, rwkv_w2, rwkv_a0, rwkv_a1, rwkv_a2,
           rwkv_g1, rwkv_g2, rwkv_k_k, rwkv_k_a, rwkv_r_k, rwkv_ln_w, rwkv_ln_b, rwkv_w_o,
           final_norm_w):
    f32 = np.float32
    A = lambda a: np.asarray(a, dtype=f32)
    x, c, ctx, c_ctx = A(x), A(c), A(ctx), A(c_ctx)
    B, L, _ = x.shape
    NL = L // 4
    NCX = CTX // 4
    NT = NL + NCX
    sT = np.ascontiguousarray(lay_vec(np.stack([c[0], c[1], c_ctx])).transpose(0, 2, 1))
    mod_w = A(mod_w); mod_b = A(mod_b)
    maps = []
    for core in range(NCORES):
        cs = slice(core * 2304, (core + 1) * 2304)
        maps.append({"sT": sT,
                     "modw": np.ascontiguousarray(mod_w[:, :, cs].reshape(2, KC, 128, 2304).transpose(0, 2, 1, 3)),
                     "modb": np.ascontiguousarray(mod_b[:, cs].reshape(2, 18, 128).transpose(2, 0, 1))})
    r0 = _run(build_l0(), maps)
    modfull = np.concatenate([r0[i]["modo"].reshape(128, 2, 18, 3) for i in range(NCORES)], axis=2)
    modsw = modfull.copy(); modsw[..., 0] = modfull[..., 1]; modsw[..., 1] = modfull[..., 0]
    modin = [np.ascontiguousarray((modfull if core // 4 == 0 else modsw).reshape(128, -1)) for core in range(NCORES)]
    normw = lay_vec(A(norm_w))
    ffn_w13 = A(ffn_w13); ffn_w2 = A(ffn_w2)
    maps = []
    w13_00, w2_00 = lay_w13(ffn_w13[0, 0]), lay_w2(ffn_w2[0, 0])
    for core in range(NCORES):
        b, q = core // 4, core % 4
        xt = np.concatenate([x[b, q * NL:(q + 1) * NL], ctx[b, q * NCX:(q + 1) * NCX]], 0).T
        maps.append({"xT": np.ascontiguousarray(xt), "modi": modin[core], "normw": normw, "w13": w13_00, "w2": w2_00})
    r1 = _run(build_l1(NL, NCX), maps)
    del maps
    tabs = fft_tables()
    maps = []
    for g in range(NCORES):
        rows = slice(256 * g, 256 * (g + 1))
        hT = np.stack([np.concatenate([r1[b * 4 + q]["h0"][rows, 0:NL] for q in range(4)], axis=1) for b in range(B)])
        hcT = np.stack([np.concatenate([r1[b * 4 + q]["h0"][rows, NL:NT] for q in range(4)], axis=1) for b in range(B)])
        maps.append({"hT": np.ascontiguousarray(hT.reshape(B, 2, 128, L).transpose(0, 2, 1, 3)),
                     "hcT": np.ascontiguousarray(hcT.reshape(B, 2, 128, CTX).transpose(0, 2, 1, 3)), **tabs})
    r2 = _run(build_l2(B), maps)
    maps = []
    wo_f = lay_wo(A(fnet_w_o)[0]); bo_f = lay_vec(A(fnet_b_o)[0])
    w13a, w2a = lay_w13(ffn_w13[0, 1]), lay_w2(ffn_w2[0, 1])
    w13b, w2b = lay_w13(ffn_w13[1, 0]), lay_w2(ffn_w2[1, 0])
    for core in range(NCORES):
        b, q = core // 4, core % 4
        fT = np.concatenate([np.concatenate([r2[g]["fo"][b][:, q * NL:(q + 1) * NL] for g in range(NCORES)], axis=0),
                             np.concatenate([r2[g]["fco"][b][:, q * NCX:(q + 1) * NCX] for g in range(NCORES)], axis=0)], axis=1)
        maps.append({"X1": r1[core]["X1"], "fT": np.ascontiguousarray(fT), "modo": modin[core], "gso": r1[core]["gso"], "hgo": r1[core]["hgo"],
                     "wo": wo_f, "bo": bo_f, "w13a": w13a, "w2a": w2a, "w13b": w13b, "w2b": w2b})
    r3 = _run(build_l3(NL, NCX), maps)
    del r2, maps
    inp = {"rwkv_mu": A(rwkv_mu), "rwkv_w_rkv": A(rwkv_w_rkv), "rwkv_w0": A(rwkv_w0), "rwkv_w1": A(rwkv_w1), "rwkv_w2": A(rwkv_w2),
           "rwkv_a0": A(rwkv_a0), "rwkv_a1": A(rwkv_a1), "rwkv_a2": A(rwkv_a2), "rwkv_g1": A(rwkv_g1), "rwkv_g2": A(rwkv_g2),
           "rwkv_k_k": A(rwkv_k_k), "rwkv_k_a": A(rwkv_k_a), "rwkv_r_k": A(rwkv_r_k), "rwkv_ln_w": A(rwkv_ln_w), "rwkv_ln_b": A(rwkv_ln_b)}
    hT = np.stack([np.concatenate([r3[b * 4 + q]["h1"][:, NL:NT] for q in range(4)] + [r3[b * 4 + q]["h1"][:, 0:NL] for q in range(4)], axis=1)
                   for b in range(B)])
    hT = np.ascontiguousarray(hT)
    cst = rwkv_consts()
    maps = [{"hT": hT, **cst, **rwkv_host_weights(inp, g)} for g in range(NCORES)]
    r45 = _run(build_l45(L, CTX, B), maps)
    del hT, maps
    maps = []
    wo_r = lay_wo(A(rwkv_w_o)[0]); fng = lay_vec(A(final_norm_w))
    w13c, w2c = lay_w13(ffn_w13[1, 1]), lay_w2(ffn_w2[1, 1])
    for core in range(NCORES):
        b, q = core // 4, core % 4
        oT = np.concatenate([r45[g]["o_out"][b][q * NL:(q + 1) * NL, :] for g in range(NCORES)], axis=1).T
        maps.append({"X4": np.ascontiguousarray(r3[core]["X4"][:, 0:NL]), "oT": np.ascontiguousarray(oT),
                     "modo": modin[core], "gso": r1[core]["gso"], "hgo": r1[core]["hgo"],
                     "wo": wo_r, "fng": fng, "w13": w13c, "w2": w2c})
    r6 = _run(build_l6(NL), maps)
    _DBG.update(r1=r1, r3=r3, r45=r45, modfull=modfull)
    out = np.empty((B, L, D), f32)
    for core in range(NCORES):
        b, q = core // 4, core % 4
        out[b, q * NL:(q + 1) * NL] = r6[core]["out"].T
    return out
```

```python
import contextlib
import types
import numpy as np
import ml_dtypes
import concourse.bass as bass
import concourse.mybir as mybir
from concourse.bass_utils import run_bass_kernel_spmd

F32 = mybir.dt.float32
BF16 = mybir.dt.bfloat16
ALU = mybir.AluOpType
AF = mybir.ActivationFunctionType
AX = mybir.AxisListType

D = 2048
KC = 16
DFF = 5632
JC = 44
NMOD = 9
SEQ = 16384
CTX = 256
EPS = 1e-6
NCORES = 8


class Res:
    __slots__ = ("lastw", "readers")

    def __init__(self):
        self.lastw = None
        self.readers = []


class Op:
    __slots__ = ("eng", "fn", "deps", "sig", "cnt", "chan")

    def __init__(self, eng, fn, chan=None):
        self.eng = eng
        self.fn = fn
        self.deps = []
        self.sig = False
        self.cnt = 0
        self.chan = chan


ENGS = ("pe", "act", "dve", "pool", "sp")


def _freeze(fn):
    cl = fn.__closure__
    if cl is None:
        return fn
    cells = []
    for c in cl:
        try:
            cells.append(types.CellType(c.cell_contents))
        except ValueError:
            cells.append(c)
    return types.FunctionType(fn.__code__, fn.__globals__, fn.__name__, fn.__defaults__, tuple(cells))


class Prog:
    def __init__(self, nc, same_engine_sync=True):
        self.nc = nc
        self.ops = {e: [] for e in ENGS}
        self.chan_last = {}
        self.same = same_engine_sync

    def emit(self, eng, fn, reads=(), writes=(), chan=None):
        op = Op(eng, _freeze(fn), chan)
        deps = []
        for r in reads:
            if r.lastw is not None:
                deps.append(r.lastw)
        for w in writes:
            if w.lastw is not None:
                deps.append(w.lastw)
            deps.extend(w.readers)
        if chan is not None:
            prev = self.chan_last.get(chan)
            if prev is not None:
                deps.append(prev)
                op.cnt = prev.cnt + 16
            else:
                op.cnt = 16
            self.chan_last[chan] = op
        seen = set()
        for d in deps:
            if id(d) in seen or d is op:
                continue
            seen.add(id(d))
            if d.chan is None and d.eng == eng and (eng == "pe" or not self.same):
                continue
            op.deps.append(d)
            d.sig = True
        for r in reads:
            r.readers.append(op)
        for w in writes:
            w.lastw = op
            w.readers = []
        self.ops[eng].append(op)
        return op

    def run(self, final_waits=()):
        nc = self.nc
        chans = dict(self.chan_last)
        for e in ENGS:
            c = 0
            for op in self.ops[e]:
                if op.chan is None and op.sig:
                    c += 1
                    op.cnt = c
        with contextlib.ExitStack() as st:
            esem = {e: st.enter_context(nc.semaphore("s_" + e)) for e in ENGS}
            csem = {c: st.enter_context(nc.semaphore("c_%s" % (str(c),))) for c in chans}
            block = st.enter_context(nc.Block())

            def mk(ename):
                oplist = self.ops[ename]
                fw = list(final_waits) if ename == "sp" else []

                def body(eng):
                    known = {}
                    for op in oplist:
                        need = {}
                        for d in op.deps:
                            if d.chan is not None:
                                key = ("c", d.chan)
                                sem = csem[d.chan]
                            else:
                                key = ("e", d.eng)
                                sem = esem[d.eng]
                            if known.get(key, 0) >= d.cnt:
                                continue
                            if key not in need or need[key][1] < d.cnt:
                                need[key] = (sem, d.cnt)
                        for key, (sem, cnt) in need.items():
                            known[key] = cnt
                            eng.wait_ge(sem, cnt)
                        ins = op.fn(eng)
                        if op.chan is not None:
                            ins.then_inc(csem[op.chan], 16)
                        elif op.sig:
                            ins.then_inc(esem[op.eng], 1)
                    for d in fw:
                        eng.wait_ge(csem[d.chan], d.cnt)
                return body

            block.tensor(mk("pe"))
            block.scalar(mk("act"))
            block.vector(mk("dve"))
            block.gpsimd(mk("pool"))
            block.sync(mk("sp"))


class Rot:
    def __init__(self, bufs, name):
        self.bufs = bufs
        self.res = [Res() for _ in bufs]
        self.name = name
        self.i = 0

    def next(self):
        k = self.i % len(self.bufs)
        self.i += 1
        return self.bufs[k], self.res[k], "%s%d" % (self.name, k)


class Dense:
    def __init__(self, nc, P, st, tmax, batch_row):
        self.nc, self.P, self.st = nc, P, st
        self.tmax = tmax
        self.brow = batch_row
        sb = lambda name, shape, dt: st.enter_context(nc.sbuf_tensor("sb_" + name, shape, dt))
        self.h = sb("h", [128, KC, tmax], BF16)
        self.r_h = Res()
        self.hid = sb("hid", [128, JC, tmax], BF16)
        self.r_hid = Res()
        self.xs = Rot([sb("xs%d" % i, [128, tmax], F32) for i in range(3)], "xs")
        self.sq = Rot([sb("sq%d" % i, [128, tmax], BF16) for i in range(2)], "sq")
        self.rstd = sb("rstd", [128, tmax], F32)
        self.r_rstd = Res()
        self.tmp = Rot([sb("tmp%d" % i, [128, tmax], F32) for i in range(2)], "tmp")
        self.sg = Rot([sb("sg%d" % i, [128, 512], F32) for i in range(2)], "sg")
        self.w13t = Rot([sb("w13t%d" % i, [128, KC, 256], BF16) for i in range(3)], "w13t")
        self.w2t = Rot([sb("w2t%d" % i, [128, JC, 128], BF16) for i in range(2)], "w2t")
        self.xo = Rot([sb("xo%d" % i, [128, tmax], F32) for i in range(2)], "xo")
        self.ones = sb("ones", [128, 128], BF16)
        self.r_ones = Res()
        P.emit("pool", lambda e: e.memset(self.ones[:], 1.0 / D), writes=[self.r_ones])
        self.epsb = sb("epsb", [128, 1], F32)
        P.emit("pool", lambda e: e.memset(self.epsb[:], EPS), writes=[self.r_ones])
        self.ps = Rot([st.enter_context(nc.psum_tensor("ps%d" % i, [128, 512], F32)) for i in range(8)], "ps")

    def mod_compute(self, sT_d, modw_d, modb_d, nlayers=2, nchunks=144):
        nc, P, st = self.nc, self.P, self.st
        sb = lambda name, shape, dt: st.enter_context(nc.sbuf_tensor("sb_" + name, shape, dt))
        sT = sb("sT", [128, KC, 3], F32)
        r_sT = Res()
        self.mod = sb("mod", [128, nlayers, nchunks, 3], F32)
        self.r_mod = Res()
        modb = sb("modb", [128, nlayers, nchunks], F32)
        r_modb = Res()
        wm = Rot([sb("wm%d" % i, [128, KC, 256], F32) for i in range(2)], "wm")
        P.emit("sp", lambda e: e.dma_start(out=sT[:], in_=sT_d), writes=[r_sT], chan="msc0")
        P.emit("sp", lambda e: e.dma_start(out=modb[:], in_=modb_d), writes=[r_modb], chan="msc1")
        P.emit("act", lambda e: e.activation(out=sT[:], in_=sT[:], func=AF.Silu), reads=[r_sT], writes=[r_sT])
        for l in range(nlayers):
            pst, r_ps, _ = self.ps.next()
            psv = pst[:, 0:nchunks * 3].rearrange("p (n r) -> p n r", r=3)
            for nb in range(nchunks // 2):
                wt, r_wt, ch = wm.next()
                P.emit("sp" if nb % 2 else "act",
                       lambda e, wt=wt, l=l, nb=nb: e.dma_start(out=wt[:], in_=modw_d[l, :, :, nb * 256:(nb + 1) * 256]),
                       writes=[r_wt], chan=ch)
                for q in range(2):
                    n = nb * 2 + q
                    for kc in range(KC):
                        P.emit("pe", lambda e, wt=wt, q=q, kc=kc, n=n, psv=psv: e.matmul(
                            psv[:, n, :], lhsT=wt[:, kc, q * 128:(q + 1) * 128], rhs=sT[:, kc, :],
                            start=(kc == 0), stop=(kc == KC - 1)), reads=[r_wt, r_sT], writes=[r_ps])
            P.emit("dve", lambda e, l=l, psv=psv: e.tensor_tensor(
                out=self.mod[:, l], in0=psv, in1=modb[:, l, :].unsqueeze(2).to_broadcast([128, nchunks, 3]), op=ALU.add),
                reads=[r_ps, r_modb], writes=[self.r_mod])

    def mod_derive(self, mod_d, normw_d, nlayers=2):
        nc, P, st = self.nc, self.P, self.st
        sb = lambda name, shape, dt: st.enter_context(nc.sbuf_tensor("sb_" + name, shape, dt))
        self.mod = sb("mod", [128, nlayers, 144, 3], F32)
        self.r_mod = Res()
        self.normw = sb("normw", [128, nlayers, 3, KC], F32)
        r_nw = Res()
        self.gs = sb("gs", [128, nlayers, 3, KC, 3], F32)
        self.hg = sb("hg", [128, nlayers, 3, KC, 3], F32)
        P.emit("sp", lambda e: e.dma_start(out=self.mod[:].rearrange("p a b c -> p (a b c)"), in_=mod_d), writes=[self.r_mod], chan="md0")
        P.emit("sp", lambda e: e.dma_start(out=self.normw[:], in_=normw_d), writes=[r_nw], chan="md1")
        for l in range(nlayers):
            for s in range(3):
                sc = self.mod[:, l, (3 * s + 1) * 16:(3 * s + 2) * 16, :]
                gt = self.mod[:, l, (3 * s + 2) * 16:(3 * s + 3) * 16, :]
                P.emit("dve", lambda e, l=l, s=s, sc=sc: e.scalar_tensor_tensor(
                    out=self.gs[:, l, s], in0=sc, scalar=1.0, in1=self.normw[:, l, s, :].unsqueeze(2).to_broadcast([128, KC, 3]),
                    op0=ALU.add, op1=ALU.mult), reads=[self.r_mod, r_nw], writes=[self.r_mod])
                P.emit("dve", lambda e, l=l, s=s, gt=gt: e.tensor_scalar(
                    out=self.hg[:, l, s], in0=gt, scalar1=(1.0 if s == 1 else 0.5), scalar2=None, op0=ALU.mult),
                    reads=[self.r_mod], writes=[self.r_mod])

    def mod_load(self, modo, gso, hgo, nlayers=2):
        nc, P, st = self.nc, self.P, self.st
        sb = lambda name, shape, dt: st.enter_context(nc.sbuf_tensor("sb_" + name, shape, dt))
        self.mod = sb("mod", [128, nlayers, 144, 3], F32)
        self.gs = sb("gs", [128, nlayers, 3, KC, 3], F32)
        self.hg = sb("hg", [128, nlayers, 3, KC, 3], F32)
        self.r_mod = Res()
        P.emit("sp", lambda e: e.dma_start(out=self.mod[:].rearrange("p a b c -> p (a b c)"), in_=modo), writes=[self.r_mod], chan="ml0")
        P.emit("sp", lambda e: e.dma_start(out=self.gs[:].rearrange("p a b c d -> p (a b c d)"), in_=gso), writes=[self.r_mod], chan="ml1")
        P.emit("sp", lambda e: e.dma_start(out=self.hg[:].rearrange("p a b c d -> p (a b c d)"), in_=hgo), writes=[self.r_mod], chan="ml2")

    def linear_res(self, blocks, src_d, r_src, w_d, bias_sb, X, r_X, Xo, r_Xo, l):
        P = self.P
        offs = np.cumsum([0] + [b[1] for b in blocks])
        sv = src_d.rearrange("(c p) t -> p c t", p=128)
        for bi, (c0, n, row) in enumerate(blocks):
            for half in range(2):
                P.emit("sp", lambda e, c0=c0, n=n, o=offs[bi], half=half: e.dma_start(
                    out=self.h[:, half * 8:(half + 1) * 8, o:o + n], in_=sv[:, half * 8:(half + 1) * 8, c0:c0 + n]),
                    reads=[r_src], writes=[self.r_h], chan="lrh%d_%d" % (bi, half))
        Xv = X.rearrange("(c p) t -> p c t", p=128)
        Xov = Xo.rearrange("(c p) t -> p c t", p=128)
        for n2 in range(KC // 2):
            wt, r_wt, ch = self.w13t.next()
            P.emit("pool", lambda e, wt=wt, n2=n2: e.dma_start(out=wt[:], in_=w_d[n2]), writes=[r_wt], chan=ch)
            for q in range(2):
                nn = n2 * 2 + q
                xt, r_xt, chx = self.xs.next()
                xo, r_xo, cho = self.xo.next()
                for bi, (c0, n, row) in enumerate(blocks):
                    o = offs[bi]
                    P.emit("sp", lambda e, xt=xt, nn=nn, c0=c0, n=n, o=o: e.dma_start(
                        out=xt[:, o:o + n], in_=Xv[:, nn, c0:c0 + n]), reads=[r_X], writes=[r_xt], chan=chx + "_%d" % bi)
                    po, r_po, _ = self.ps.next()
                    for kc in range(KC):
                        P.emit("pe", lambda e, po=po, wt=wt, kc=kc, q=q, o=o, n=n: e.matmul(
                            po[:, 0:n], lhsT=wt[:, kc, q * 128:(q + 1) * 128], rhs=self.h[:, kc, o:o + n],
                            start=(kc == 0), stop=(kc == KC - 1)), reads=[r_wt, self.r_h], writes=[r_po])
                    src_ap = po[:, 0:n]
                    rd = [r_po]
                    if bias_sb is not None:
                        sg, r_sg, _ = self.sg.next()
                        P.emit("act", lambda e, sg=sg, po=po, n=n, nn=nn: e.activation(
                            out=sg[:, 0:n], in_=po[:, 0:n], func=AF.Identity, bias=bias_sb[:, nn:nn + 1], scale=1.0),
                            reads=[r_po, self.r_mod], writes=[r_sg])
                        src_ap = sg[:, 0:n]
                        rd = [r_sg]
                    P.emit("dve", lambda e, src_ap=src_ap, xt=xt, xo=xo, nn=nn, o=o, n=n, row=row: e.scalar_tensor_tensor(
                        out=xo[:, o:o + n], in0=src_ap, scalar=self.hg[:, l, 1, nn, row:row + 1], in1=xt[:, o:o + n],
                        op0=ALU.mult, op1=ALU.add), reads=rd + [r_xt, self.r_mod], writes=[r_xo])
                    P.emit("sp", lambda e, xo=xo, nn=nn, c0=c0, n=n, o=o: e.dma_start(
                        out=Xov[:, nn, c0:c0 + n], in_=xo[:, o:o + n]), reads=[r_xo], writes=[r_Xo], chan=cho + "_o%d" % bi)

    def shift_ap(self, l, s, c, row):
        return self.mod[:, l, (3 * s) * 16 + c, row:row + 1]

    def norm_mod(self, blocks, X, r_X, l, s, out_dram=None, r_out=None, final_g=None):
        P = self.P
        T = sum(b[1] for b in blocks)
        offs = np.cumsum([0] + [b[1] for b in blocks])
        stat = [self.ps.next() for _ in blocks]
        Xv = X.rearrange("(c p) t -> p c t", p=128)

        def load_x(c):
            xt, r_xt, ch = self.xs.next()
            for bi, (c0, n, row) in enumerate(blocks):
                P.emit("sp", lambda e, xt=xt, c=c, c0=c0, n=n, o=offs[bi]: e.dma_start(
                    out=xt[:, o:o + n], in_=Xv[:, c, c0:c0 + n]), reads=[r_X], writes=[r_xt], chan=ch + "_%d" % bi)
            return xt, r_xt

        for c in range(KC):
            xt, r_xt = load_x(c)
            sq, r_sq, _ = self.sq.next()
            P.emit("act", lambda e, xt=xt, sq=sq: e.activation(out=sq[:, 0:T], in_=xt[:, 0:T], func=AF.Square),
                   reads=[r_xt], writes=[r_sq])
            for bi, (c0, n, row) in enumerate(blocks):
                pst, r_ps, _ = stat[bi]
                P.emit("pe", lambda e, pst=pst, sq=sq, o=offs[bi], n=n, c=c: e.matmul(
                    pst[:, 0:n], lhsT=self.ones[:], rhs=sq[:, o:o + n], start=(c == 0), stop=(c == KC - 1)),
                    reads=[r_sq, self.r_ones], writes=[r_ps])
        for bi, (c0, n, row) in enumerate(blocks):
            pst, r_ps, _ = stat[bi]
            P.emit("act", lambda e, pst=pst, o=offs[bi], n=n: e.activation(
                out=self.rstd[:, o:o + n], in_=pst[:, 0:n], func=AF.Sqrt, bias=self.epsb[:, 0:1], scale=1.0),
                reads=[r_ps, self.r_ones], writes=[self.r_rstd])
            P.emit("dve", lambda e, o=offs[bi], n=n: e.reciprocal(
                out=self.rstd[:, o:o + n], in_=self.rstd[:, o:o + n]),
                reads=[self.r_rstd], writes=[self.r_rstd])
        for c in range(KC):
            xt, r_xt = load_x(c)
            tmp, r_tmp, _ = self.tmp.next()
            P.emit("dve", lambda e, xt=xt, tmp=tmp: e.tensor_tensor(
                out=tmp[:, 0:T], in0=xt[:, 0:T], in1=self.rstd[:, 0:T], op=ALU.mult),
                reads=[r_xt, self.r_rstd], writes=[r_tmp])
            if out_dram is None:
                for bi, (c0, n, row) in enumerate(blocks):
                    P.emit("act", lambda e, tmp=tmp, o=offs[bi], n=n, c=c, row=row: e.activation(
                        out=self.h[:, c, o:o + n], in_=tmp[:, o:o + n], func=AF.Identity,
                        scale=self.gs[:, l, s, c, row:row + 1], bias=self.shift_ap(l, s, c, row)),
                        reads=[r_tmp, self.r_mod], writes=[self.r_h])
            else:
                xo, r_xo, ch = self.xo.next()
                odt = out_dram.dtype
                xov = xo if odt == F32 else xo[:].bitcast(BF16)
                for bi, (c0, n, row) in enumerate(blocks):
                    if final_g is not None:
                        P.emit("act", lambda e, tmp=tmp, xov=xov, o=offs[bi], n=n, c=c: e.activation(
                            out=xov[:, o:o + n], in_=tmp[:, o:o + n], func=AF.Identity, scale=final_g[:, c:c + 1], bias=0.0),
                            reads=[r_tmp, self.r_mod], writes=[r_xo])
                    else:
                        P.emit("act", lambda e, tmp=tmp, xov=xov, o=offs[bi], n=n, c=c, row=row: e.activation(
                            out=xov[:, o:o + n], in_=tmp[:, o:o + n], func=AF.Identity,
                            scale=self.gs[:, l, s, c, row:row + 1], bias=self.shift_ap(l, s, c, row)),
                            reads=[r_tmp, self.r_mod], writes=[r_xo])
                ov = out_dram.rearrange("(c p) t -> p c t", p=128)
                for bi, (c0, n, row) in enumerate(blocks):
                    P.emit("sp", lambda e, xov=xov, o=offs[bi], n=n, c=c, c0=c0: e.dma_start(
                        out=ov[:, c, c0:c0 + n], in_=xov[:, o:o + n]), reads=[r_xo], writes=[r_out], chan=ch + "_o%d" % bi)

    def ffn(self, blocks, w13_d, w2_d, X, r_X, Xo, r_Xo, l, s):
        P = self.P
        offs = np.cumsum([0] + [b[1] for b in blocks])
        for j in range(JC):
            wt, r_wt, ch = self.w13t.next()
            P.emit("pool", lambda e, wt=wt, j=j: e.dma_start(out=wt[:], in_=w13_d[j]), writes=[r_wt], chan=ch)
            for bi, (c0, n, row) in enumerate(blocks):
                o = offs[bi]
                pg, r_pg, _ = self.ps.next()
                pu, r_pu, _ = self.ps.next()
                for half, (pp, r_pp) in enumerate(((pg, r_pg), (pu, r_pu))):
                    for kc in range(KC):
                        P.emit("pe", lambda e, pp=pp, wt=wt, kc=kc, half=half, o=o, n=n: e.matmul(
                            pp[:, 0:n], lhsT=wt[:, kc, half * 128:(half + 1) * 128], rhs=self.h[:, kc, o:o + n],
                            start=(kc == 0), stop=(kc == KC - 1)), reads=[r_wt, self.r_h], writes=[r_pp])
                sg, r_sg, _ = self.sg.next()
                P.emit("act", lambda e, sg=sg, pg=pg, n=n: e.activation(out=sg[:, 0:n], in_=pg[:, 0:n], func=AF.Silu),
                       reads=[r_pg], writes=[r_sg])
                P.emit("dve", lambda e, sg=sg, pu=pu, j=j, o=o, n=n: e.tensor_tensor(
                    out=self.hid[:, j, o:o + n], in0=sg[:, 0:n], in1=pu[:, 0:n], op=ALU.mult),
                    reads=[r_sg, r_pu], writes=[self.r_hid])
        Xv = X.rearrange("(c p) t -> p c t", p=128)
        Xov = Xo.rearrange("(c p) t -> p c t", p=128)
        for nn in range(KC):
            wt, r_wt, ch = self.w2t.next()
            P.emit("pool", lambda e, wt=wt, nn=nn: e.dma_start(out=wt[:], in_=w2_d[nn]), writes=[r_wt], chan=ch)
            xt, r_xt, chx = self.xs.next()
            xo, r_xo, cho = self.xo.next()
            for bi, (c0, n, row) in enumerate(blocks):
                o = offs[bi]
                P.emit("sp", lambda e, xt=xt, nn=nn, c0=c0, n=n, o=o: e.dma_start(
                    out=xt[:, o:o + n], in_=Xv[:, nn, c0:c0 + n]), reads=[r_X], writes=[r_xt], chan=chx + "_%d" % bi)
                po, r_po, _ = self.ps.next()
                for jc in range(JC):
                    P.emit("pe", lambda e, po=po, wt=wt, jc=jc, o=o, n=n: e.matmul(
                        po[:, 0:n], lhsT=wt[:, jc, :], rhs=self.hid[:, jc, o:o + n],
                        start=(jc == 0), stop=(jc == JC - 1)), reads=[r_wt, self.r_hid], writes=[r_po])
                P.emit("dve", lambda e, po=po, xt=xt, xo=xo, nn=nn, o=o, n=n, row=row: e.scalar_tensor_tensor(
                    out=xo[:, o:o + n], in0=po[:, 0:n], scalar=self.hg[:, l, s, nn, row:row + 1], in1=xt[:, o:o + n],
                    op0=ALU.mult, op1=ALU.add), reads=[r_po, r_xt, self.r_mod], writes=[r_xo])
                P.emit("sp", lambda e, xo=xo, nn=nn, c0=c0, n=n, o=o: e.dma_start(
                    out=Xov[:, nn, c0:c0 + n], in_=xo[:, o:o + n]), reads=[r_xo], writes=[r_Xo], chan=cho + "_o%d" % bi)


def make_passes(nlat, nctx, brow):
    blks = []
    c = 0
    while c < nlat:
        n = min(512, nlat - c)
        blks.append((c, n, brow))
        c += n
    if nctx:
        blks.append((nlat, nctx, 2))
    fine = []
    for (c0, n, row) in blks:
        fine.append((c0, n, row))
    passes = []
    cur, tot = [], 0
    queue = list(fine)
    while queue:
        c0, n, row = queue.pop(0)
        if tot + n <= 768:
            cur.append((c0, n, row)); tot += n
        elif n == 512 and tot + 256 <= 768:
            cur.append((c0, 256, row)); tot += 256
            queue.insert(0, (c0 + 256, 256, row))
        else:
            passes.append(cur); cur, tot = [], 0
            queue.insert(0, (c0, n, row))
    if cur:
        passes.append(cur)
    return passes


def lay_w13(w):
    return np.ascontiguousarray(w.reshape(KC, 128, 2, JC, 128).transpose(3, 1, 0, 2, 4)).reshape(JC, 128, KC, 256)


def lay_w2(w):
    return np.ascontiguousarray(w.reshape(JC, 128, KC, 128).transpose(2, 1, 0, 3))


def lay_sq(w):
    return np.ascontiguousarray(w.reshape(KC, 128, -1).transpose(1, 0, 2))


def lay_vec(v):
    sh = v.shape[:-1]
    a = v.reshape(sh + (KC, 128))
    return np.ascontiguousarray(np.moveaxis(a, -1, 0))


def core_tokens(core):
    b = core // 4
    q = core % 4
    return b, q


def build_l0():
    nc = bass.Bass("TRN2", target_bir_lowering=False)
    dt = lambda name, shape, dty, kind: nc.dram_tensor(name, shape, dty, kind=kind).ap()
    sT = dt("sT", [128, KC, 3], F32, "ExternalInput")
    modw = dt("modw", [2, 128, KC, 18 * 128], F32, "ExternalInput")
    modb = dt("modb", [128, 2, 18], F32, "ExternalInput")
    modo = dt("modo", [128, 2 * 18 * 3], F32, "ExternalOutput")
    with contextlib.ExitStack() as st:
        P = Prog(nc)
        dn = Dense(nc, P, st, 64, 0)
        dn.mod_compute(sT, modw, modb, 2, 18)
        P.emit("sp", lambda e: e.dma_start(out=modo, in_=dn.mod[:].rearrange("p a b c -> p (a b c)")), reads=[dn.r_mod], chan="mo0")
        P.run(final_waits=_all_dma_tails(P))
    return nc


def build_l1(nlat, nctx):
    NT = nlat + nctx
    nc = bass.Bass("TRN2", target_bir_lowering=False)
    dt = lambda name, shape, dty, kind: nc.dram_tensor(name, shape, dty, kind=kind).ap()
    xT = dt("xT", [D, NT], F32, "ExternalInput")
    modi = dt("modi", [128, 2 * 144 * 3], F32, "ExternalInput")
    normw = dt("normw", [128, 2, 3, KC], F32, "ExternalInput")
    w13 = dt("w13", [JC, 128, KC, 256], F32, "ExternalInput")
    w2 = dt("w2", [KC, 128, JC, 128], F32, "ExternalInput")
    X1 = dt("X1", [D, NT], F32, "ExternalOutput")
    h0 = dt("h0", [D, NT], BF16, "ExternalOutput")
    gso = dt("gso", [128, 2 * 3 * KC * 3], F32, "ExternalOutput")
    hgo = dt("hgo", [128, 2 * 3 * KC * 3], F32, "ExternalOutput")
    outs = []
    with contextlib.ExitStack() as st:
        P = Prog(nc)
        dn = Dense(nc, P, st, 768, 0)
        dn.mod_derive(modi, normw)
        r_o = Res()
        outs.append(P.emit("sp", lambda e: e.dma_start(out=gso, in_=dn.gs[:].rearrange("p a b c d -> p (a b c d)")), reads=[dn.r_mod], writes=[r_o], chan="mo1"))
        outs.append(P.emit("sp", lambda e: e.dma_start(out=hgo, in_=dn.hg[:].rearrange("p a b c d -> p (a b c d)")), reads=[dn.r_mod], writes=[r_o], chan="mo2"))
        r_xin = Res()
        passes = make_passes(nlat, nctx, 0)
        for blocks in passes:
            r_X1 = Res()
            r_h0 = Res()
            dn.norm_mod(blocks, xT, r_xin, 0, 0)
            dn.ffn(blocks, w13, w2, xT, r_xin, X1, r_X1, 0, 0)
            dn.norm_mod(blocks, X1, r_X1, 0, 1, out_dram=h0, r_out=r_h0)
            outs.append(r_X1)
            outs.append(r_h0)
        fw = [o.lastw if isinstance(o, Res) else o for o in outs]
        P.run(final_waits=_all_dma_tails(P))
    return nc


def _all_dma_tails(P):
    return list(P.chan_last.values())


def fft_tables():
    bf = ml_dtypes.bfloat16
    ch = np.arange(256, dtype=np.float64)
    ang = 2 * np.pi * np.outer(ch, ch) / 256.0
    sc = 1.0 / 2048.0
    cs = np.zeros((128, 2, 4, 128), np.float64)
    for kc in range(2):
        for q in range(4):
            a = ang[kc * 128:(kc + 1) * 128, q * 64:(q + 1) * 64]
            cs[:, kc, q, 0:64] = np.cos(a) * sc
            cs[:, kc, q, 64:128] = -np.sin(a) * sc
    l = np.arange(128, dtype=np.float64)
    a1 = 2 * np.pi * np.outer(l, l) / 128.0
    f1 = np.zeros((128, 2, 256), np.float64)
    f1[:, 0, 0:128] = np.cos(a1); f1[:, 0, 128:256] = -np.sin(a1)
    f1[:, 1, 0:128] = np.sin(a1); f1[:, 1, 128:256] = np.cos(a1)
    k = np.arange(128)[None, :] * 128 + np.arange(128)[:, None]
    ae = 2 * np.pi * (l[:, None, None] * k[None]) / 16384.0
    E = np.concatenate([np.cos(ae), np.sin(ae)], axis=2)
    scc = 1.0 / 256.0
    csc = np.zeros((128, 2, 512), np.float64)
    for kc in range(2):
        a = ang[kc * 128:(kc + 1) * 128, :]
        csc[:, kc, 0:256] = np.cos(a) * scc
        csc[:, kc, 256:512] = -np.sin(a) * scc
    g = np.zeros((128, 2, 512), np.float64)
    for tc in range(2):
        a = ang[tc * 128:(tc + 1) * 128, :]
        g[:, tc, 0:256] = np.cos(a)
        g[:, tc, 256:512] = np.sin(a)
    return {"t_cs": cs.astype(bf), "t_f1": f1.astype(bf), "t_E": E.astype(bf), "t_csc": csc.astype(bf), "t_g": g.astype(bf)}


def emit_fft(nc, P, st, hT, r_hT, hcT, r_hcT, tabs, fo, r_fo, fco, r_fco, nb=2, pfx="ff"):
    sb = lambda name, shape, dt: st.enter_context(nc.sbuf_tensor("sb_" + pfx + name, shape, dt))
    hs = sb("hs", [128, 2, 16384], BF16); r_hs = Res()
    W = sb("W", [128, 128, 128], BF16); r_W = Res()
    Z = sb("Z", [128, 64, 256], BF16); r_Z = Res()
    fT = sb("fT", [64, 16384], BF16); r_fT = Res()
    Eb = Rot([sb("E%d" % i, [128, 16, 256], BF16) for i in range(2)], pfx + "E")
    cs = sb("cs", [128, 2, 4, 128], BF16)
    f1 = sb("f1", [128, 2, 256], BF16)
    csc = sb("csc", [128, 2, 512], BF16)
    gt = sb("gt", [128, 2, 512], BF16)
    hcs = sb("hcs", [128, 2, 256], BF16); r_hcs = Res()
    Wc = sb("Wc", [128, 2, 512], BF16); r_Wc = Res()
    fcs = sb("fcs", [128, 256], BF16); r_fcs = Res()
    r_tab = Res()
    ps = Rot([st.enter_context(nc.psum_tensor(pfx + "ps%d" % i, [128, 512], F32)) for i in range(8)], pfx + "ps")
    for i, (dst, src) in enumerate(((cs, tabs["t_cs"]), (f1, tabs["t_f1"]), (csc, tabs["t_csc"]), (gt, tabs["t_g"]))):
        P.emit("sp", lambda e, dst=dst, src=src: e.dma_start(out=dst[:], in_=src), writes=[r_tab], chan=pfx + "tab%d" % i)
    evac_i = [0]

    def evac(out_ap, in_ap, reads, writes):
        eng = "act" if evac_i[0] % 2 == 0 else "dve"
        evac_i[0] += 1
        if eng == "act":
            P.emit("act", lambda e: e.activation(out=out_ap, in_=in_ap, func=AF.Copy), reads=reads, writes=writes)
        else:
            P.emit("dve", lambda e: e.tensor_copy(out=out_ap, in_=in_ap), reads=reads, writes=writes)

    for b in range(nb):
        P.emit("sp", lambda e, b=b: e.dma_start(out=hcs[:], in_=hcT[b]), reads=[r_hcT], writes=[r_hcs], chan=pfx + "hc")
        for tc in range(2):
            pt, r_pt, _ = ps.next()
            for kc in range(2):
                P.emit("pe", lambda e, pt=pt, tc=tc, kc=kc: e.matmul(
                    pt[:, :], lhsT=hcs[:, kc, tc * 128:(tc + 1) * 128], rhs=csc[:, kc, :], start=(kc == 0), stop=(kc == 1)),
                    reads=[r_hcs, r_tab], writes=[r_pt])
            evac(Wc[:, tc, :], pt[:, :], [r_pt], [r_Wc])
        for half in range(2):
            pt, r_pt, _ = ps.next()
            k = 0
            for tc in range(2):
                for ri in range(2):
                    P.emit("pe", lambda e, pt=pt, tc=tc, ri=ri, half=half, k=k: e.matmul(
                        pt[:, 0:256], lhsT=Wc[:, tc, ri * 256 + half * 128: ri * 256 + (half + 1) * 128],
                        rhs=gt[:, tc, ri * 256:(ri + 1) * 256], start=(k == 0), stop=(k == 3)),
                        reads=[r_Wc, r_tab], writes=[r_pt])
                    k += 1
            evac(fcs[:, :], pt[:, 0:256], [r_pt], [r_fcs])
            P.emit("sp", lambda e, b=b, half=half: e.dma_start(out=fco[b, half * 128:(half + 1) * 128, :], in_=fcs[:, :]),
                   reads=[r_fcs], writes=[r_fco], chan=pfx + "fco")
        for kc in range(2):
            P.emit("sp" if kc == 0 else "act", lambda e, b=b, kc=kc: e.dma_start(out=hs[:, kc, :], in_=hT[b, :, kc, :]),
                   reads=[r_hT], writes=[r_hs], chan=pfx + "hs%d" % kc)
        hv = hs[:].rearrange("p k (a l) -> p k l a", l=128)
        for q in range(4):
            for g4 in range(32):
                pt, r_pt, _ = ps.next()
                for li in range(4):
                    l2 = g4 * 4 + li
                    for kc in range(2):
                        P.emit("pe", lambda e, pt=pt, li=li, l2=l2, kc=kc, q=q: e.matmul(
                            pt[:, li * 128:(li + 1) * 128], lhsT=hv[:, kc, l2, :], rhs=cs[:, kc, q, :],
                            start=(kc == 0), stop=(kc == 1)), reads=[r_hs, r_tab], writes=[r_pt])
                evac(W[:, g4 * 4:(g4 + 1) * 4, :].rearrange("p a b -> p (a b)"), pt[:, :], [r_pt], [r_W])
            for c2 in range(32):
                pt, r_pt, _ = ps.next()
                for ci in range(2):
                    c = c2 * 2 + ci
                    for ri in range(2):
                        P.emit("pe", lambda e, pt=pt, ci=ci, c=c, ri=ri: e.matmul(
                            pt[:, ci * 256:(ci + 1) * 256], lhsT=W[:, :, ri * 64 + c], rhs=f1[:, ri, :],
                            start=(ri == 0), stop=(ri == 1)), reads=[r_W, r_tab], writes=[r_pt])
                evac(Z[:, c2 * 2:(c2 + 1) * 2, :].rearrange("p a b -> p (a b)"), pt[:, :], [r_pt], [r_Z])
            fv = fT[:].rearrange("p (k2 k1) -> p k1 k2", k1=128)
            for eb in range(8):
                Et, r_Et, ch = Eb.next()
                P.emit("sp", lambda e, Et=Et, eb=eb: e.dma_start(out=Et[:], in_=tabs["t_E"][:, eb * 16:(eb + 1) * 16, :]),
                       writes=[r_Et], chan=ch)
                for k4 in range(4):
                    pt, r_pt, _ = ps.next()
                    for ki in range(4):
                        kl = k4 * 4 + ki
                        k1 = eb * 16 + kl
                        for ri in range(2):
                            P.emit("pe", lambda e, pt=pt, ki=ki, kl=kl, k1=k1, ri=ri, Et=Et: e.matmul(
                                pt[0:64, ki * 128:(ki + 1) * 128], lhsT=Z[:, :, ri * 128 + k1], rhs=Et[:, kl, ri * 128:(ri + 1) * 128],
                                start=(ri == 0), stop=(ri == 1)), reads=[r_Z, r_Et], writes=[r_pt])
                    k10 = eb * 16 + k4 * 4
                    evac(fv[:, k10:k10 + 4, :], pt[0:64, :].rearrange("p (a b) -> p a b", a=4), [r_pt], [r_fT])
            P.emit("sp", lambda e, b=b, q=q: e.dma_start(out=fo[b, q * 64:(q + 1) * 64, :], in_=fT[:, :]),
                   reads=[r_fT], writes=[r_fo], chan=pfx + "fo")


def build_l2(nb=2):
    nc = bass.Bass("TRN2", target_bir_lowering=False)
    dt = lambda name, shape, dty, kind: nc.dram_tensor(name, shape, dty, kind=kind).ap()
    hT = dt("hT", [nb, 128, 2, 16384], BF16, "ExternalInput")
    hcT = dt("hcT", [nb, 128, 2, 256], BF16, "ExternalInput")
    tabs = {"t_cs": dt("t_cs", [128, 2, 4, 128], BF16, "ExternalInput"), "t_f1": dt("t_f1", [128, 2, 256], BF16, "ExternalInput"),
            "t_E": dt("t_E", [128, 128, 256], BF16, "ExternalInput"), "t_csc": dt("t_csc", [128, 2, 512], BF16, "ExternalInput"),
            "t_g": dt("t_g", [128, 2, 512], BF16, "ExternalInput")}
    fo = dt("fo", [nb, 256, 16384], BF16, "ExternalOutput")
    fco = dt("fco", [nb, 256, 256], BF16, "ExternalOutput")
    with contextlib.ExitStack() as st:
        P = Prog(nc)
        emit_fft(nc, P, st, hT, Res(), hcT, Res(), tabs, fo, Res(), fco, Res(), nb=nb)
        P.run(final_waits=_all_dma_tails(P))
    return nc


LDC = -0.6065306597126334
GN_EPS = 64e-5


def rwkv_consts():
    bf = ml_dtypes.bfloat16
    idx = np.arange(128)
    cm = np.zeros((128, 2, 3, 256), np.float32)
    ct = np.zeros((128, 2, 3, 128), np.float32)
    for z in range(2):
        before = (idx[:, None] < idx[None, :]) if z == 0 else (idx[:, None] > idx[None, :])
        beq = before | np.eye(128, dtype=bool)
        cm[:, z, 0, 0:128] = before
        cm[:, z, 0, 128:256] = beq
        cm[:, z, 1, 0:128] = before.T
        cm[:, z, 1, 128:256] = before.T
        cm[:, z, 2, 0:128] = beq
        cm[:, z, 2, 128:256] = beq
        ct[:, z, 0, :] = LDC * beq
        ct[:, z, 1, :] = LDC * before
        ct[:, z, 2, :] = LDC
    ident = np.eye(128, dtype=np.float32)
    return {"c_mask": cm.astype(bf), "c_tri": ct, "c_ident": ident.astype(bf)}


def emit_rwkv(nc, P, st, hT, r_hT, wd, yscr, o_out, r_out, rkvs=None, nlat=SEQ, nctx=CTX, nb=2, pfx="rw", dbg=None, dbg_at=(0, "ctx", 0, 0), ew_engs=("dve",), prefetch=True):
    sbt = lambda name, shape, dt: st.enter_context(nc.sbuf_tensor("sb_" + pfx + name, shape, dt))
    ps = Rot([st.enter_context(nc.psum_tensor(pfx + "ps%d" % i, [128, 512], F32)) for i in range(8)], pfx + "ps")
    r_c = Res()

    def ld(dst, src, eng="sp", chan=None, **kw):
        return P.emit(eng, lambda e: e.dma_start(out=dst, in_=src, **kw), writes=[r_c], chan=chan)
    cmask = sbt("cmask", [128, 2, 3, 256], BF16); ld(cmask[:], wd["c_mask"], chan=pfx + "k0")
    ctri = sbt("ctri", [128, 2, 3, 128], F32); ld(ctri[:], wd["c_tri"], chan=pfx + "k1")
    ident = sbt("ident", [128, 128], BF16); ld(ident[:], wd["c_ident"], chan=pfx + "k2")
    vecs = sbt("vecs", [128, 9, 256], F32); ld(vecs[:], wd["vecs"], chan=pfx + "k3")
    mu = sbt("mu", [128, KC, 6], F32); ld(mu[:], wd["mu"], chan=pfx + "k4")
    om = sbt("om", [128, KC, 6], F32)
    W2s = sbt("W2s", [96, 2, 256], BF16); ld(W2s[:], wd["w2"], eng="pool", chan=pfx + "k5")
    A2s = sbt("A2s", [96, 2, 256], BF16); ld(A2s[:], wd["a2"], eng="pool", chan=pfx + "k6")
    G2s = sbt("G2s", [128, 2, 256], BF16); ld(G2s[:], wd["g2"], eng="pool", chan=pfx + "k7")
    negcol = sbt("negcol", [128, 1], F32)
    P.emit("pool", lambda e: e.memset(negcol[:], LDC), writes=[r_c])
    gneps = sbt("gneps", [128, 1], F32)
    P.emit("pool", lambda e: e.memset(gneps[:], GN_EPS), writes=[r_c])
    P.emit("dve", lambda e: e.tensor_scalar(out=om[:], in0=mu[:], scalar1=-1.0, scalar2=1.0, op0=ALU.mult, op1=ALU.add),
           reads=[r_c], writes=[r_c])
    RKa = sbt("RKa", [128, KC, 768], BF16); RKb = sbt("RKb", [128, KC, 768], BF16)
    LWa = sbt("LWa", [128, KC, 640], BF16); LWb = sbt("LWb", [128, KC, 640], BF16)
    stg = Rot([sbt("stg%d" % i, [128, 768], F32) for i in range(1)], pfx + "stg")
    blocks = [(0, 256, 0), (256, 512, 1), (512, 768, 2), (768, 960, 3), (960, 1152, 4), (1152, 1408, 5)]
    for kc in range(KC):
        for part in range(2):
            sg, r_sg, ch = stg.next()
            if part == 0:
                P.emit("sp", lambda e, sg=sg, kc=kc: e.dma_start(out=sg[:, 0:768], in_=wd["rkv"][:, kc, :]), writes=[r_sg], chan=ch)
            else:
                P.emit("sp", lambda e, sg=sg, kc=kc: e.dma_start(out=sg[:, 0:640], in_=wd["lw"][:, kc, :]), writes=[r_sg], chan=ch)
            for (c0, c1, p) in blocks:
                if (c0 < 768) != (part == 0):
                    continue
                o = 0 if part == 0 else 768
                da = RKa[:, kc, c0:c1] if c0 < 768 else LWa[:, kc, c0 - 768:c1 - 768]
                db = RKb[:, kc, c0:c1] if c0 < 768 else LWb[:, kc, c0 - 768:c1 - 768]
                P.emit("dve", lambda e, sg=sg, c0=c0 - o, c1=c1 - o, p=p, kc=kc, db=db: e.tensor_scalar(
                    out=db, in0=sg[:, c0:c1], scalar1=mu[:, kc, p:p + 1], scalar2=None, op0=ALU.mult), reads=[r_sg, r_c], writes=[r_c])
                P.emit("pool", lambda e, sg=sg, c0=c0 - o, c1=c1 - o, p=p, kc=kc, da=da: e.tensor_scalar(
                    out=da, in0=sg[:, c0:c1], scalar1=om[:, kc, p:p + 1], scalar2=None, op0=ALU.mult), reads=[r_sg, r_c], writes=[r_c])

    SC = 128
    hw0 = sbt("hw", [128, KC, 64 + SC + 64], BF16); r_hw0 = Res()
    hsb0 = sbt("hsb", [128, KC, SC], BF16); r_hs0 = Res()
    hw = [hw0 for b in range(nb)]; r_hw = [r_hw0 for _ in range(nb)]
    hsb = [hsb0 for b in range(nb)]; r_hs = [r_hs0 for _ in range(nb)]
    xw = [sbt("xw%d" % b, [96, SC], BF16) for b in range(nb)]
    xa = [sbt("xa%d" % b, [96, 2, SC], BF16) for b in range(nb)]
    xg = [sbt("xg%d" % b, [128, 2, SC], BF16) for b in range(nb)]
    r_x = [Res() for _ in range(nb)]
    rkv = [sbt("rkv%d" % b, [128, 768], F32) for b in range(nb)]; r_rkv = [Res() for _ in range(nb)]
    NSCR = 12
    scr0 = [sbt("scr_%d" % i, [128, 256], F32) for i in range(NSCR)]
    r_scr0 = [Res() for _ in range(NSCR)]
    scr = [scr0 for b in range(nb)]
    r_scr = [r_scr0 for b in range(nb)]
    small = [[sbt("sm%d_%d" % (b, i), [128, 4], F32) for i in range(4)] for b in range(nb)]
    r_small = [[Res() for _ in range(4)] for b in range(nb)]
    opn = ["At", "Rt", "Bt", "Kt", "Bh", "Kh", "Vt"]
    opt = [{n: sbt("%s%d" % (n, b), [128, 256], BF16) for n in opn} for b in range(nb)]
    r_opt = [{n: Res() for n in opn} for b in range(nb)]
    keep = [{n: sbt("kp%s%d" % (n, b), [128, 256], F32) for n in ("kb", "g")} for b in range(nb)]
    r_keep = [Res() for _ in range(nb)]
    gend = [sbt("gend%d" % b, [64, 4], F32) for b in range(nb)]; r_gend = [Res() for _ in range(nb)]
    Yt = [sbt("Y%d" % b, [128, 256], F32) for b in range(nb)]; r_Y = [Res() for _ in range(nb)]
    yfin = [scr0[4] for b in range(nb)]; r_yf = [r_scr0[4] for _ in range(nb)]
    ob = [sbt("ob%d" % b, [128, 256], BF16) for b in range(nb)]; r_ob = [Res() for _ in range(nb)]
    units = [(b, h) for b in range(nb) for h in range(4)]
    U = {}
    for (b, h) in units:
        u = {}
        n = "%d_%d" % (b, h)
        u["fm"] = sbt("fm" + n, [64, 4, 128], BF16); u["r_fm"] = Res()
        u["Mrk"] = sbt("Mrk" + n, [128, 256], BF16)
        u["MkaT"] = sbt("MkaT" + n, [128, 128], BF16)
        u["r_M"] = Res()
        u["T"] = [sbt("T%d" % i + n, [128, 128], F32) for i in range(2)]; u["r_T"] = [Res(), Res()]
        u["Tbf"] = sbt("Tbf" + n, [128, 128], BF16); u["r_Tbf"] = Res()
        u["PP"] = [sbt("PP%d" % i + n, [128, 256], F32) for i in range(2)]; u["r_PP"] = [Res(), Res()]
        u["X"] = sbt("X" + n, [128, 128], BF16); u["Ah"] = sbt("Ah" + n, [64, 128], BF16); u["r_XA"] = Res()
        u["Ut"] = sbt("Ut" + n, [128, 64], BF16); u["r_Ut"] = Res()
        u["S32"] = sbt("S32" + n, [64, 64], F32); u["Sbf"] = sbt("Sbf" + n, [64, 64], BF16); u["r_S"] = Res(); u["r_Sbf"] = Res()
        U[(b, h)] = u

    W0 = lambda z: vecs[:, 0 + z, :]
    A0 = lambda z: vecs[:, 2 + z, :]
    KK_, KA_, RK_, LNW_, LNB_ = vecs[:, 4, :], vecs[:, 5, :], vecs[:, 6, :], vecs[:, 7, :], vecs[:, 8, :]
    hv = lambda t: t.rearrange("p (h j) -> p h j", h=4)
    rr = [0]
    r_rkvs = Res()

    def ew(reads, writes, fn_dve, allow=None):
        allow = allow or ew_engs
        eng = allow[rr[0] % len(allow)]
        rr[0] += 1
        P.emit(eng, fn_dve, reads=reads, writes=writes)

    def _pass(z):
        for (b, h) in units:
            u = U[(b, h)]
            P.emit("pool", lambda e, u=u: e.memset(u["S32"][:], 0.0), writes=[u["r_S"]])
            P.emit("pool", lambda e, u=u: e.memset(u["Sbf"][:], 0.0), writes=[u["r_Sbf"]])
        segs = [("ctx", 0, nctx), ("lat", nctx, nlat)]
        def _seg(sname, soff, slen):
            nsc = slen // SC
            sc_order = range(nsc) if z == 0 else range(nsc - 1, -1, -1)
            def _proj(sci):
                t0 = sci * SC
                for b in range(nb):
                    lo = max(0, t0 - 64); hi = min(slen, t0 + SC + 64)
                    if lo > t0 - 64:
                        P.emit("pool", lambda e, b=b: e.memset(hw[b][:, :, 0:64], 0.0), writes=[r_hw[b]])
                    if hi < t0 + SC + 64:
                        P.emit("pool", lambda e, b=b: e.memset(hw[b][:, :, 64 + SC:], 0.0), writes=[r_hw[b]])
                    for half in range(2):
                        P.emit("sp", lambda e, b=b, lo=lo, hi=hi, half=half: e.dma_start(
                            out=hw[b][:, half * 8:(half + 1) * 8, 64 + lo - t0: 64 + hi - t0],
                            in_=hT[b].rearrange("(c p) t -> p c t", p=128)[:, half * 8:(half + 1) * 8, soff + lo: soff + hi]),
                            reads=[r_hT], writes=[r_hw[b]], chan=pfx + "hw%d_%d" % (b, half))
                    shifts = (-1, 1, -64, 64) if sname == "lat" else (-1, 1, -1, 1)
                    for qd in range(4):
                        sh = shifts[qd]
                        P.emit("pool", lambda e, b=b, qd=qd, sh=sh: e.tensor_copy(
                            out=hsb[b][:, qd * 4:(qd + 1) * 4, :], in_=hw[b][:, qd * 4:(qd + 1) * 4, 64 + sh:64 + sh + SC]),
                            reads=[r_hw[b]], writes=[r_hs[b]])
                    if sname == "lat":
                        P.emit("pool", lambda e, b=b: e.memset(hsb[b][:, 0:4, 0:SC:64], 0.0), writes=[r_hs[b]])
                        P.emit("pool", lambda e, b=b: e.memset(hsb[b][:, 4:8, 63:SC:64], 0.0), writes=[r_hs[b]])
                    groups = [(0 + 96 * z, 96, "w", 0)]
                    if z == 0:
                        groups += [(192, 96, "a", 0)]
                    else:
                        if rkvs is None:
                            groups += [(192, 96, "a", 0)]
                        groups += [(288, 96, "a", 1), (384, 128, "g", 0), (512, 128, "g", 1)]
                    for (c0, m, kind, gi) in groups:
                        pt, r_pt, _ = ps.next()
                        k = 0
                        for kc in range(KC):
                            for (wt, src) in ((LWa, hw[b][:, kc, 64:64 + SC]), (LWb, hsb[b][:, kc, :])):
                                P.emit("pe", lambda e, pt=pt, wt=wt, src=src, kc=kc, c0=c0, m=m, k=k: e.matmul(
                                    pt[0:m, 0:SC], lhsT=wt[:, kc, c0:c0 + m], rhs=src, start=(k == 0), stop=(k == 2 * KC - 1)),
                                    reads=[r_c, r_hw[b], r_hs[b]], writes=[r_pt])
                                k += 1
                        if kind == "w":
                            P.emit("act", lambda e, pt=pt, b=b: e.activation(out=xw[b][:, :], in_=pt[0:96, 0:SC], func=AF.Tanh),
                                   reads=[r_pt], writes=[r_x[b]])
                        elif kind == "a":
                            P.emit("act", lambda e, pt=pt, b=b, gi=gi: e.activation(out=xa[b][:, gi, :], in_=pt[0:96, 0:SC], func=AF.Copy),
                                   reads=[r_pt], writes=[r_x[b]])
                        else:
                            P.emit("act", lambda e, pt=pt, b=b, gi=gi: e.activation(out=xg[b][:, gi, :], in_=pt[0:128, 0:SC], func=AF.Sigmoid),
                                   reads=[r_pt], writes=[r_x[b]])
                    gidx = (soff + t0) // 128
                    if rkvs is not None and z == 1:
                        P.emit("sp", lambda e, b=b, gidx=gidx: e.dma_start(out=rkv[b][:, :], in_=rkvs[b, gidx, :, 0:768]),
                               reads=[r_rkvs], writes=[r_rkv[b]], chan=pfx + "rkl%d" % b)
                    else:
                        pa, r_pa, _ = ps.next()
                        pb_, r_pb, _ = ps.next()
                        k = 0
                        for kc in range(KC):
                            for (src, wt) in ((hw[b][:, kc, 64 + 0 * 128:64 + (0 + 1) * 128], RKa), (hsb[b][:, kc, 0 * 128:(0 + 1) * 128], RKb)):
                                P.emit("pe", lambda e, pa=pa, src=src, wt=wt, kc=kc, k=k: e.matmul(
                                    pa[:, 0:512], lhsT=src, rhs=wt[:, kc, 0:512], start=(k == 0), stop=(k == 2 * KC - 1)),
                                    reads=[r_c, r_hw[b], r_hs[b]], writes=[r_pa])
                                P.emit("pe", lambda e, pb_=pb_, src=src, wt=wt, kc=kc, k=k: e.matmul(
                                    pb_[:, 0:256], lhsT=src, rhs=wt[:, kc, 512:768], start=(k == 0), stop=(k == 2 * KC - 1)),
                                    reads=[r_c, r_hw[b], r_hs[b]], writes=[r_pb])
                                k += 1
                        P.emit("act", lambda e, b=b, pa=pa: e.activation(out=rkv[b][:, 0:512], in_=pa[:, 0:512], func=AF.Copy),
                               reads=[r_pa], writes=[r_rkv[b]])
                        P.emit("act", lambda e, b=b, pb_=pb_: e.activation(out=rkv[b][:, 512:768], in_=pb_[:, 0:256], func=AF.Copy),
                               reads=[r_pb], writes=[r_rkv[b]])
                        if rkvs is not None:
                            P.emit("sp", lambda e, b=b, gidx=gidx: e.dma_start(out=rkvs[b, gidx, :, 0:768], in_=rkv[b][:, :]),
                                   reads=[r_rkv[b]], writes=[r_rkvs], chan=pfx + "rks%d" % b)
            def _sc(sci, pre_done, nxt):
                t0 = sci * SC
                if not pre_done:
                    _proj(sci)
                ch_order = range(SC // 128) if z == 0 else range(SC // 128 - 1, -1, -1)
                def _ch(ci):
                    tok0 = t0 + ci * 128
                    for b in range(nb):
                        S_ = scr[b]; RS = r_scr[b]
                        r_t, k_t, v_t = rkv[b][:, 0:256], rkv[b][:, 256:512], rkv[b][:, 512:768]
                        cs = slice(ci * 128, (ci + 1) * 128)
                        pw, r_pw, _ = ps.next()
                        P.emit("pe", lambda e, pw=pw, b=b, cs=cs: e.matmul(pw[:, 0:256], lhsT=xw[b][:, cs], rhs=W2s[:, z, :], start=True, stop=True),
                               reads=[r_x[b], r_c], writes=[r_pw])
                        P.emit("dve", lambda e, pw=pw, b=b: e.tensor_tensor(out=S_[0][:], in0=pw[:, 0:256], in1=W0(z), op=ALU.add),
                               reads=[r_pw, r_c], writes=[RS[0]])
                        P.emit("act", lambda e, b=b: e.activation(out=S_[0][:], in_=S_[0][:], func=AF.Sigmoid), reads=[RS[0]], writes=[RS[0]])
                        gidx = (soff + tok0) // 128
                        zs = [z] if (z == 0 or rkvs is not None) else [1, 0]
                        if rkvs is not None and z == 1:
                            P.emit("sp", lambda e, b=b, gidx=gidx: e.dma_start(out=S_[2][:], in_=rkvs[b, gidx, :, 768:1024]),
                                   reads=[r_rkvs], writes=[RS[2]], chan=pfx + "agl")
                        for ai, za in enumerate(zs):
                            pw, r_pw, _ = ps.next()
                            P.emit("pe", lambda e, pw=pw, b=b, cs=cs, za=za: e.matmul(pw[:, 0:256], lhsT=xa[b][:, za, cs], rhs=A2s[:, za, :], start=True, stop=True),
                                   reads=[r_x[b], r_c], writes=[r_pw])
                            P.emit("dve", lambda e, pw=pw, b=b, ai=ai, za=za: e.tensor_tensor(out=S_[1 + ai][:], in0=pw[:, 0:256], in1=A0(za), op=ALU.add),
                                   reads=[r_pw, r_c], writes=[RS[1 + ai]])
                            P.emit("act", lambda e, b=b, ai=ai: e.activation(out=S_[1 + ai][:], in_=S_[1 + ai][:], func=AF.Sigmoid),
                                   reads=[RS[1 + ai]], writes=[RS[1 + ai]])
                        if rkvs is not None and z == 0:
                            P.emit("sp", lambda e, b=b, gidx=gidx: e.dma_start(out=rkvs[b, gidx, :, 768:1024], in_=S_[1][:]),
                                   reads=[RS[1]], writes=[r_rkvs], chan=pfx + "ags")
                        if z == 1:
                            pw, r_pw, _ = ps.next()
                            for kc2 in range(2):
                                P.emit("pe", lambda e, pw=pw, b=b, cs=cs, kc2=kc2: e.matmul(pw[:, 0:256], lhsT=xg[b][:, kc2, cs], rhs=G2s[:, kc2, :],
                                                                                           start=(kc2 == 0), stop=(kc2 == 1)), reads=[r_x[b], r_c], writes=[r_pw])
                            P.emit("act", lambda e, pw=pw, b=b: e.activation(out=keep[b]["g"][:], in_=pw[:, 0:256], func=AF.Copy),
                                   reads=[r_pw], writes=[r_keep[b]])
                        ew([r_rkv[b], r_c], [RS[3]], lambda e, b=b, k_t=k_t: e.tensor_tensor(out=S_[3][:], in0=k_t, in1=KK_, op=ALU.mult))
                        ew([RS[3]], [RS[4]], lambda e, b=b: e.tensor_tensor(out=S_[4][:], in0=S_[3][:], in1=S_[3][:], op=ALU.mult))
                        P.emit("dve", lambda e, b=b: e.tensor_reduce(out=small[b][0][:], in_=hv(S_[4][:]), axis=AX.X, op=ALU.add),
                               reads=[RS[4]], writes=[r_small[b][0]])
                        P.emit("dve", lambda e, b=b: e.tensor_scalar(out=small[b][0][:], in0=small[b][0][:], scalar1=1e-24, scalar2=None, op0=ALU.max),
                               reads=[r_small[b][0]], writes=[r_small[b][0]])
                        P.emit("act", lambda e, b=b: e.activation(out=small[b][0][:], in_=small[b][0][:], func=AF.Sqrt),
                               reads=[r_small[b][0]], writes=[r_small[b][0]])
                        P.emit("dve", lambda e, b=b: e.reciprocal(out=small[b][0][:], in_=small[b][0][:]),
                               reads=[r_small[b][0]], writes=[r_small[b][0]])
                        ew([RS[3], r_small[b][0]], [RS[3]], lambda e, b=b: e.tensor_tensor(
                            out=hv(S_[3][:]), in0=hv(S_[3][:]), in1=small[b][0][:].unsqueeze(2).to_broadcast([128, 4, 64]), op=ALU.mult))
                        pl, r_pl, _ = ps.next()
                        pe2, r_pe2, _ = ps.next()
                        P.emit("pe", lambda e, pl=pl, b=b: e.matmul(pl[:, 0:256], lhsT=ctri[:, z, 0, :], rhs=S_[0][:], start=True, stop=True),
                               reads=[RS[0], r_c], writes=[r_pl])
                        P.emit("pe", lambda e, pl=pl, b=b: e.matmul(pl[:, 256:512], lhsT=ctri[:, z, 1, :], rhs=S_[0][:], start=True, stop=True),
                               reads=[RS[0], r_c], writes=[r_pl])
                        P.emit("pe", lambda e, pe2=pe2, b=b: e.matmul(pe2[:, 0:256], lhsT=ctri[:, z, 2, :], rhs=S_[0][:], start=True, stop=True),
                               reads=[RS[0], r_c], writes=[r_pe2])
                        for hh in range(4):
                            P.emit("pe", lambda e, pe2=pe2, b=b, hh=hh: e.matmul(pe2[0:64, 256 + hh:257 + hh], lhsT=S_[0][:, hh * 64:(hh + 1) * 64], rhs=negcol[:, 0:1],
                                                                                 start=True, stop=True), reads=[RS[0], r_c], writes=[r_pe2])
                        P.emit("act", lambda e, pl=pl, b=b: e.activation(out=S_[5][:], in_=pl[:, 0:256], func=AF.Exp), reads=[r_pl], writes=[RS[5]])
                        P.emit("act", lambda e, pl=pl, b=b: e.activation(out=S_[6][:], in_=pl[:, 0:256], func=AF.Exp, scale=-1.0), reads=[r_pl], writes=[RS[6]])
                        P.emit("act", lambda e, pl=pl, b=b: e.activation(out=S_[7][:], in_=pl[:, 256:512], func=AF.Exp), reads=[r_pl], writes=[RS[7]])
                        P.emit("act", lambda e, pe2=pe2, b=b: e.activation(out=S_[8][:], in_=pe2[:, 0:256], func=AF.Exp), reads=[r_pe2], writes=[RS[8]])
                        P.emit("act", lambda e, pe2=pe2, b=b: e.activation(out=gend[b][:], in_=pe2[0:64, 256:260], func=AF.Exp), reads=[r_pe2], writes=[r_gend[b]])
                        O_ = opt[b]; RO = r_opt[b]
                        ew([RS[3], RS[7]], [RO["At"]], lambda e, b=b: e.scalar_tensor_tensor(
                            out=O_["At"][:], in0=S_[3][:], scalar=-1.0, in1=S_[7][:], op0=ALU.mult, op1=ALU.mult), allow=("dve",))
                        ew([r_rkv[b], RS[5]], [RO["Rt"]], lambda e, b=b, r_t=r_t: e.tensor_tensor(out=O_["Rt"][:], in0=r_t, in1=S_[5][:], op=ALU.mult))
                        ew([r_rkv[b]], [RO["Vt"]], lambda e, b=b, v_t=v_t: e.tensor_copy(out=O_["Vt"][:], in_=v_t))
                        ew([RS[3], RS[1]], [RS[9]], lambda e, b=b: e.tensor_tensor(out=S_[9][:], in0=S_[3][:], in1=S_[1][:], op=ALU.mult))
                        ew([RS[9], RS[6]], [RO["Bt"]], lambda e, b=b: e.tensor_tensor(out=O_["Bt"][:], in0=S_[9][:], in1=S_[6][:], op=ALU.mult))
                        ew([RS[1], r_c], [RS[10]], lambda e, b=b: e.scalar_tensor_tensor(
                            out=S_[10][:], in0=S_[1][:], scalar=-1.0, in1=KA_, op0=ALU.add, op1=ALU.mult), allow=("dve",))
                        ew([RS[10], r_rkv[b]], [RS[10]], lambda e, b=b, k_t=k_t: e.scalar_tensor_tensor(
                            out=S_[10][:], in0=S_[10][:], scalar=1.0, in1=k_t, op0=ALU.add, op1=ALU.mult), allow=("dve",))
                        ew([RS[10], RS[6]], [RO["Kt"]], lambda e, b=b: e.tensor_tensor(out=O_["Kt"][:], in0=S_[10][:], in1=S_[6][:], op=ALU.mult))
                        ew([RO["Bt"], RS[8]], [RO["Bh"]], lambda e, b=b: e.tensor_tensor(out=O_["Bh"][:], in0=O_["Bt"][:], in1=S_[8][:], op=ALU.mult))
                        ew([RO["Kt"], RS[8]], [RO["Kh"]], lambda e, b=b: e.tensor_tensor(out=O_["Kh"][:], in0=O_["Kt"][:], in1=S_[8][:], op=ALU.mult))
                        if z == 1 and sname == "lat":
                            ew([RS[2], r_c], [RS[11]], lambda e, b=b: e.scalar_tensor_tensor(
                                out=S_[11][:], in0=S_[2][:], scalar=-1.0, in1=KA_, op0=ALU.add, op1=ALU.mult), allow=("dve",))
                            ew([RS[11], r_rkv[b]], [RS[11]], lambda e, b=b, k_t=k_t: e.scalar_tensor_tensor(
                                out=S_[11][:], in0=S_[11][:], scalar=1.0, in1=k_t, op0=ALU.add, op1=ALU.mult), allow=("dve",))
                            ew([RS[11], RS[10]], [RS[11]], lambda e, b=b: e.tensor_tensor(out=S_[11][:], in0=S_[11][:], in1=S_[10][:], op=ALU.add))
                            ew([RS[11]], [r_keep[b]], lambda e, b=b: e.tensor_scalar(out=keep[b]["kb"][:], in0=S_[11][:], scalar1=0.5, scalar2=None, op0=ALU.mult))
                    if dbg is not None and (z, sname, sci, ci) == dbg_at:
                        b = 0
                        dbg("rkv", rkv[b][:], [r_rkv[b]])
                        for i in (0, 1, 3, 5, 6, 7, 8, 9, 10):
                            dbg("s%d" % i, scr[b][i][:], [r_scr[b][i]])
                        for nme in opn:
                            dbg(nme, opt[b][nme][:], [r_opt[b][nme]])
                        dbg("gend", gend[b][:], [r_gend[b]])
                        dbg("hw", hw[b][:], [r_hw[b]])
                        dbg("hsb", hsb[b][:], [r_hs[b]])
                        dbg("xw", xw[b][:], [r_x[b]])
                        dbg("xa", xa[b][:], [r_x[b]])
                    if nxt is not None:
                        _proj(nxt)
                    for (b, h) in units:
                        u = U[(b, h)]; O_ = opt[b]; RO = r_opt[b]
                        hs_ = slice(h * 64, (h + 1) * 64)
                        pt, r_pt, _ = ps.next()
                        ptb = pt[:].bitcast(BF16)
                        for i, nme in enumerate(("At", "Rt", "Bt", "Kt")):
                            P.emit("pe", lambda e, ptb=ptb, i=i, nme=nme, b=b, hs_=hs_: e.transpose(
                                ptb[0:64, i * 128:(i + 1) * 128], O_[nme][:, hs_], ident[:]), reads=[RO[nme], r_c], writes=[r_pt])
                        P.emit("act", lambda e, ptb=ptb, u=u: e.activation(out=u["fm"][:].rearrange("p a t -> p (a t)"), in_=ptb[0:64, 0:512], func=AF.Copy),
                               reads=[r_pt], writes=[u["r_fm"]])
                        fm = u["fm"]
                        p1, r_p1, _ = ps.next()
                        p3, r_p3, _ = ps.next()
                        P.emit("pe", lambda e, p1=p1, fm=fm: e.matmul(p1[:, 0:256], lhsT=fm[:, 2, :], rhs=fm[:, 0:2, :].rearrange("p a t -> p (a t)"), start=True, stop=True),
                               reads=[u["r_fm"]], writes=[r_p1])
                        P.emit("pe", lambda e, p1=p1, fm=fm: e.matmul(p1[:, 256:384], lhsT=fm[:, 3, :], rhs=fm[:, 1, :], start=True, stop=True),
                               reads=[u["r_fm"]], writes=[r_p1])
                        P.emit("pe", lambda e, p3=p3, fm=fm: e.matmul(p3[:, 0:256], lhsT=fm[:, 0, :], rhs=fm[:, 2:4, :].rearrange("p a t -> p (a t)"), start=True, stop=True),
                               reads=[u["r_fm"]], writes=[r_p3])
                        P.emit("dve", lambda e, p1=p1, u=u: e.tensor_tensor(out=u["Mrk"][:], in0=p1[:, 128:384], in1=cmask[:, z, 2, :], op=ALU.mult),
                               reads=[r_p1, r_c], writes=[u["r_M"]])
                        P.emit("dve", lambda e, p3=p3, u=u: e.tensor_tensor(out=u["MkaT"][:], in0=p3[:, 128:256], in1=cmask[:, z, 1, 128:256], op=ALU.mult),
                               reads=[r_p3, r_c], writes=[u["r_M"]])
                        P.emit("dve", lambda e, p1=p1, u=u: e.tensor_tensor(out=u["PP"][1][:, 0:128], in0=p1[:, 0:128], in1=cmask[:, z, 0, 0:128], op=ALU.mult),
                               reads=[r_p1, r_c], writes=[u["r_PP"][1]])
                        P.emit("dve", lambda e, p3=p3, u=u: e.tensor_tensor(out=u["PP"][1][:, 128:256], in0=p3[:, 0:128], in1=cmask[:, z, 1, 0:128], op=ALU.mult),
                               reads=[r_p3, r_c], writes=[u["r_PP"][1]])
                        P.emit("dve", lambda e, u=u: e.tensor_tensor(out=u["T"][0][:], in0=u["PP"][1][:, 0:128], in1=ident[:], op=ALU.add),
                               reads=[u["r_PP"][1], r_c], writes=[u["r_T"][0]])
                    for kk_ in range(1, 7):
                        for (b, h) in units:
                            u = U[(b, h)]
                            Pm, PTm, rd = u["PP"][kk_ % 2][:, 0:128], u["PP"][kk_ % 2][:, 128:256], u["r_PP"][kk_ % 2]
                            dst, r_dst = u["PP"][(kk_ + 1) % 2], u["r_PP"][(kk_ + 1) % 2]
                            pp, r_pp, _ = ps.next()
                            P.emit("pe", lambda e, pp=pp, Pm=Pm, PTm=PTm: e.matmul(pp[:, 128:256], lhsT=Pm, rhs=PTm, start=True, stop=True),
                                   reads=[rd], writes=[r_pp])
                            if kk_ < 6:
                                P.emit("pe", lambda e, pp=pp, Pm=Pm, PTm=PTm: e.matmul(pp[:, 0:128], lhsT=PTm, rhs=Pm, start=True, stop=True),
                                       reads=[rd], writes=[r_pp])
                                P.emit("act", lambda e, pp=pp, dst=dst: e.activation(out=dst[:], in_=pp[:, 0:256], func=AF.Copy), reads=[r_pp], writes=[r_dst])
                            else:
                                P.emit("act", lambda e, pp=pp, dst=dst: e.activation(out=dst[:, 128:256], in_=pp[:, 128:256], func=AF.Copy), reads=[r_pp], writes=[r_dst])
                        for (b, h) in units:
                            u = U[(b, h)]
                            PTk, r_ptk = u["PP"][(kk_ + 1) % 2][:, 128:256], u["r_PP"][(kk_ + 1) % 2]
                            Told, r_told = u["T"][(kk_ - 1) % 2], u["r_T"][(kk_ - 1) % 2]
                            pt, r_pt, _ = ps.next()
                            P.emit("pe", lambda e, pt=pt, PTk=PTk, Told=Told: e.matmul(pt[:, 0:128], lhsT=PTk, rhs=Told[:], start=True, stop=True),
                                   reads=[r_ptk, r_told], writes=[r_pt])
                            if kk_ < 6:
                                Tnew, r_tnew = u["T"][kk_ % 2], u["r_T"][kk_ % 2]
                            else:
                                Tnew, r_tnew = u["Tbf"], u["r_Tbf"]
                            P.emit("dve", lambda e, pt=pt, Tnew=Tnew, Told=Told: e.tensor_tensor(out=Tnew[:], in0=pt[:, 0:128], in1=Told[:], op=ALU.add),
                                   reads=[r_pt, r_told], writes=[r_tnew])
                    for (b, h) in units:
                        u = U[(b, h)]; O_ = opt[b]; RO = r_opt[b]
                        hs_ = slice(h * 64, (h + 1) * 64)
                        Tf, r_tf = u["Tbf"], u["r_Tbf"]
                        px, r_px, _ = ps.next()
                        P.emit("pe", lambda e, px=px, u=u, Tf=Tf: e.matmul(px[:, 0:128], lhsT=u["MkaT"][:], rhs=Tf[:], start=True, stop=True),
                               reads=[u["r_M"], r_tf], writes=[r_px])
                        P.emit("pe", lambda e, px=px, b=b, hs_=hs_, Tf=Tf: e.matmul(px[0:64, 128:256], lhsT=O_["At"][:, hs_], rhs=Tf[:], start=True, stop=True),
                               reads=[RO["At"], r_tf], writes=[r_px])
                        P.emit("act", lambda e, px=px, u=u: e.activation(out=u["X"][:], in_=px[:, 0:128], func=AF.Copy), reads=[r_px], writes=[u["r_XA"]])
                        P.emit("dve", lambda e, px=px, u=u: e.tensor_copy(out=u["Ah"][:], in_=px[0:64, 128:256]), reads=[r_px], writes=[u["r_XA"]])
                    for (b, h) in units:
                        u = U[(b, h)]; O_ = opt[b]; RO = r_opt[b]
                        hs_ = slice(h * 64, (h + 1) * 64)
                        pu, r_pu, _ = ps.next()
                        P.emit("pe", lambda e, pu=pu, u=u: e.matmul(pu[:, 0:64], lhsT=u["Ah"][:], rhs=u["Sbf"][:], start=True, stop=False),
                               reads=[u["r_XA"], u["r_Sbf"]], writes=[r_pu])
                        P.emit("pe", lambda e, pu=pu, u=u, b=b, hs_=hs_: e.matmul(pu[:, 0:64], lhsT=u["X"][:], rhs=O_["Vt"][:, hs_], start=False, stop=True),
                               reads=[u["r_XA"], RO["Vt"]], writes=[r_pu])
                        P.emit("act", lambda e, pu=pu, u=u: e.activation(out=u["Ut"][:], in_=pu[:, 0:64], func=AF.Copy), reads=[r_pu], writes=[u["r_Ut"]])
                    for (b, h) in units:
                        u = U[(b, h)]; O_ = opt[b]; RO = r_opt[b]
                        hs_ = slice(h * 64, (h + 1) * 64)
                        py, r_py, _ = ps.next()
                        P.emit("pe", lambda e, py=py, u=u: e.matmul(py[:, 0:64], lhsT=u["fm"][:, 1, :], rhs=u["Sbf"][:], start=True, stop=False),
                               reads=[u["r_fm"], u["r_Sbf"]], writes=[r_py])
                        P.emit("pe", lambda e, py=py, u=u: e.matmul(py[:, 0:64], lhsT=u["Mrk"][:, 0:128], rhs=u["Ut"][:], start=False, stop=False),
                               reads=[u["r_M"], u["r_Ut"]], writes=[r_py])
                        P.emit("pe", lambda e, py=py, u=u, b=b, hs_=hs_: e.matmul(py[:, 0:64], lhsT=u["Mrk"][:, 128:256], rhs=O_["Vt"][:, hs_], start=False, stop=True),
                               reads=[u["r_M"], RO["Vt"]], writes=[r_py])
                        P.emit("act", lambda e, py=py, b=b, hs_=hs_: e.activation(out=Yt[b][:, hs_], in_=py[:, 0:64], func=AF.Copy), reads=[r_py], writes=[r_Y[b]])
                        pss, r_pss, _ = ps.next()
                        P.emit("pe", lambda e, pss=pss, u=u, b=b, hs_=hs_: e.matmul(pss[0:64, 0:64], lhsT=O_["Bh"][:, hs_], rhs=u["Ut"][:], start=True, stop=False),
                               reads=[RO["Bh"], u["r_Ut"]], writes=[r_pss])
                        P.emit("pe", lambda e, pss=pss, u=u, b=b, hs_=hs_: e.matmul(pss[0:64, 0:64], lhsT=O_["Kh"][:, hs_], rhs=O_["Vt"][:, hs_], start=False, stop=True),
                               reads=[RO["Kh"], RO["Vt"]], writes=[r_pss])
                        P.emit("dve", lambda e, pss=pss, u=u, b=b, h=h: e.scalar_tensor_tensor(
                            out=u["S32"][:], in0=u["S32"][:], scalar=gend[b][:, h:h + 1], in1=pss[0:64, 0:64], op0=ALU.mult, op1=ALU.add),
                            reads=[r_pss, r_gend[b], u["r_S"]], writes=[u["r_S"]])
                        P.emit("pool", lambda e, u=u: e.tensor_copy(out=u["Sbf"][:], in_=u["S32"][:]), reads=[u["r_S"]], writes=[u["r_Sbf"]])
                    if dbg is not None and (z, sname, sci, ci) == dbg_at:
                        u = U[(0, 0)]
                        dbg("fm", u["fm"][:], [u["r_fm"]])
                        dbg("Mrk", u["Mrk"][:], [u["r_M"]])
                        dbg("T", u["Tbf"][:], [u["r_Tbf"]])
                        dbg("X", u["X"][:], [u["r_XA"]]); dbg("Ah", u["Ah"][:], [u["r_XA"]])
                        dbg("Ut", u["Ut"][:], [u["r_Ut"]])
                        dbg("Y", Yt[0][:], [r_Y[0]])
                        dbg("S32", u["S32"][:], [u["r_S"]])
                    if sname != "lat":
                        return
                    for b in range(nb):
                        if z == 0:
                            P.emit("sp", lambda e, b=b, tok0=tok0: e.dma_start(out=yscr[b, tok0:tok0 + 128, :], in_=Yt[b][:]),
                                   reads=[r_Y[b]], writes=[r_out], chan=pfx + "ys%d" % b)
                            continue
                        S_ = scr[b]; RS = r_scr[b]; K_ = keep[b]
                        P.emit("sp", lambda e, b=b, tok0=tok0: e.dma_start(out=yfin[b][:], in_=yscr[b, tok0:tok0 + 128, :]),
                               reads=[r_out], writes=[r_yf[b]], chan=pfx + "yl%d" % b)
                        ew([r_Y[b], r_yf[b]], [RS[0]], lambda e, b=b: e.tensor_tensor(out=S_[0][:], in0=Yt[b][:], in1=yfin[b][:], op=ALU.add))
                        P.emit("dve", lambda e, b=b: e.tensor_reduce(out=small[b][1][:], in_=hv(S_[0][:]), axis=AX.X, op=ALU.add),
                               reads=[RS[0]], writes=[r_small[b][1]])
                        P.emit("dve", lambda e, b=b: e.tensor_scalar(out=small[b][1][:], in0=small[b][1][:], scalar1=1.0 / 64, scalar2=None, op0=ALU.mult),
                               reads=[r_small[b][1]], writes=[r_small[b][1]])
                        ew([RS[0], r_small[b][1]], [RS[1]], lambda e, b=b: e.tensor_tensor(
                            out=hv(S_[1][:]), in0=hv(S_[0][:]), in1=small[b][1][:].unsqueeze(2).to_broadcast([128, 4, 64]), op=ALU.subtract))
                        ew([RS[1]], [RS[2]], lambda e, b=b: e.tensor_tensor(out=S_[2][:], in0=S_[1][:], in1=S_[1][:], op=ALU.mult))
                        P.emit("dve", lambda e, b=b: e.tensor_reduce(out=small[b][2][:], in_=hv(S_[2][:]), axis=AX.X, op=ALU.add),
                               reads=[RS[2]], writes=[r_small[b][2]])
                        P.emit("act", lambda e, b=b: e.activation(out=small[b][2][:], in_=small[b][2][:], func=AF.Sqrt, scale=1.0 / 64, bias=gneps[:, 0:1]),
                               reads=[r_small[b][2], r_c], writes=[r_small[b][2]])
                        P.emit("dve", lambda e, b=b: e.reciprocal(out=small[b][2][:], in_=small[b][2][:]), reads=[r_small[b][2]], writes=[r_small[b][2]])
                        ew([RS[1], r_small[b][2]], [RS[1]], lambda e, b=b: e.tensor_tensor(
                            out=hv(S_[1][:]), in0=hv(S_[1][:]), in1=small[b][2][:].unsqueeze(2).to_broadcast([128, 4, 64]), op=ALU.mult))
                        ew([RS[1], r_c], [RS[1]], lambda e, b=b: e.tensor_tensor(out=S_[1][:], in0=S_[1][:], in1=LNW_, op=ALU.mult))
                        ew([RS[1], r_c], [RS[1]], lambda e, b=b: e.tensor_tensor(out=S_[1][:], in0=S_[1][:], in1=LNB_, op=ALU.add))
                        ew([r_keep[b], r_rkv[b]], [RS[3]], lambda e, b=b: e.tensor_tensor(out=S_[3][:], in0=rkv[b][:, 0:256], in1=K_["kb"][:], op=ALU.mult))
                        ew([RS[3], r_c], [RS[3]], lambda e, b=b: e.tensor_tensor(out=S_[3][:], in0=S_[3][:], in1=RK_, op=ALU.mult))
                        P.emit("dve", lambda e, b=b: e.tensor_reduce(out=small[b][3][:], in_=hv(S_[3][:]), axis=AX.X, op=ALU.add),
                               reads=[RS[3]], writes=[r_small[b][3]])
                        ew([r_rkv[b], r_small[b][3]], [RS[3]], lambda e, b=b: e.tensor_tensor(
                            out=hv(S_[3][:]), in0=hv(rkv[b][:, 512:768]), in1=small[b][3][:].unsqueeze(2).to_broadcast([128, 4, 64]), op=ALU.mult))
                        ew([RS[1], RS[3]], [RS[1]], lambda e, b=b: e.tensor_tensor(out=S_[1][:], in0=S_[1][:], in1=S_[3][:], op=ALU.add))
                        ew([RS[1], r_keep[b]], [r_ob[b]], lambda e, b=b: e.tensor_tensor(out=ob[b][:], in0=S_[1][:], in1=K_["g"][:], op=ALU.mult))
                        P.emit("sp", lambda e, b=b, tok0=tok0: e.dma_start(out=o_out[b, tok0:tok0 + 128, :], in_=ob[b][:]),
                               reads=[r_ob[b]], writes=[r_out], chan=pfx + "oo%d" % b)
                for ci in ch_order:
                    _ch(ci)
            order = list(sc_order)
            for i, sci in enumerate(order):
                if prefetch and z == 0:
                    _sc(sci, i > 0, order[i + 1] if i + 1 < len(order) else None)
                else:
                    _sc(sci, False, None)
        for seg in segs:
            _seg(*seg)
    for z in range(2):
        _pass(z)


def build_l45(nlat=SEQ, nctx=CTX, nb=2, debug=False, same_sync=True, ew_engs=("dve",), reuse=True):
    nc = bass.Bass("TRN2", target_bir_lowering=False)
    dt = lambda name, shape, dty, kind: nc.dram_tensor(name, shape, dty, kind=kind).ap()
    hT = dt("hT", [nb, D, nctx + nlat], BF16, "ExternalInput")
    wd = {"c_mask": dt("c_mask", [128, 2, 3, 256], BF16, "ExternalInput"), "c_tri": dt("c_tri", [128, 2, 3, 128], F32, "ExternalInput"),
          "c_ident": dt("c_ident", [128, 128], BF16, "ExternalInput"), "vecs": dt("vecs", [128, 9, 256], F32, "ExternalInput"),
          "mu": dt("mu", [128, KC, 6], F32, "ExternalInput"), "w2": dt("w2", [96, 2, 256], F32, "ExternalInput"),
          "a2": dt("a2", [96, 2, 256], F32, "ExternalInput"), "g2": dt("g2", [128, 2, 256], F32, "ExternalInput"),
          "rkv": dt("rkv", [128, KC, 768], F32, "ExternalInput"), "lw": dt("lw", [128, KC, 640], F32, "ExternalInput")}
    rkvs = dt("rkvs", [nb, (nctx + nlat) // 128, 128, 1024], F32, "Internal") if reuse else None
    yscr = dt("yscr", [nb, nlat, 256], F32, "ExternalOutput")
    o_out = dt("o_out", [nb, nlat, 256], BF16, "ExternalOutput")
    with contextlib.ExitStack() as st:
        P = Prog(nc, same_engine_sync=same_sync)
        dbg = None
        if debug:
            def dbg(name, ap, reads):
                t = nc.dram_tensor("dbg_" + name, list(ap.shape), ap.dtype, kind="ExternalOutput").ap()
                P.emit("sp", lambda e: e.dma_start(out=t, in_=ap), reads=reads, chan="dbg_" + name)
        emit_rwkv(nc, P, st, hT, Res(), wd, yscr, o_out, Res(), rkvs=rkvs, nlat=nlat, nctx=nctx, nb=nb, dbg=dbg, ew_engs=ew_engs)
        P.run(final_waits=_all_dma_tails(P))
    return nc


def rwkv_host_weights(inp, g):
    cs = slice(256 * g, 256 * (g + 1))
    rkv = np.concatenate([inp["rwkv_w_rkv"][0, i][:, cs] for i in range(3)], axis=1)
    lw = np.concatenate([inp["rwkv_w1"][0, 0], inp["rwkv_w1"][0, 1], inp["rwkv_a1"][0, 0], inp["rwkv_a1"][0, 1], inp["rwkv_g1"][0]], axis=1)
    vec = np.stack([inp["rwkv_w0"][0, 0][cs], inp["rwkv_w0"][0, 1][cs], inp["rwkv_a0"][0, 0][cs], inp["rwkv_a0"][0, 1][cs],
                    inp["rwkv_k_k"][0][cs], inp["rwkv_k_a"][0][cs], inp["rwkv_r_k"][0].reshape(-1)[cs], inp["rwkv_ln_w"][0][cs], inp["rwkv_ln_b"][0][cs]])
    return {"rkv": lay_sq(rkv), "lw": lay_sq(lw),
            "vecs": np.ascontiguousarray(np.broadcast_to(vec[None], (128, 9, 256))).astype(np.float32),
            "mu": np.ascontiguousarray(lay_vec(inp["rwkv_mu"][0]).transpose(0, 2, 1)),
            "w2": np.ascontiguousarray(inp["rwkv_w2"][0][:, :, cs].transpose(1, 0, 2)),
            "a2": np.ascontiguousarray(inp["rwkv_a2"][0][:, :, cs].transpose(1, 0, 2)),
            "g2": np.ascontiguousarray(inp["rwkv_g2"][0][:, cs].reshape(2, 128, 256).transpose(1, 0, 2))}


def lay_wo(w):
    return np.ascontiguousarray(w.reshape(KC, 128, 8, 256).transpose(2, 1, 0, 3))


def build_l3(nlat, nctx):
    NT = nlat + nctx
    nc = bass.Bass("TRN2", target_bir_lowering=False)
    dt = lambda name, shape, dty, kind="ExternalInput": nc.dram_tensor(name, shape, dty, kind=kind).ap()
    X1 = dt("X1", [D, NT], F32)
    fT = dt("fT", [D, NT], BF16)
    modo = dt("modo", [128, 2 * 144 * 3], F32); gso = dt("gso", [128, 2 * 3 * KC * 3], F32); hgo = dt("hgo", [128, 2 * 3 * KC * 3], F32)
    wo = dt("wo", [8, 128, KC, 256], F32)
    bo = dt("bo", [128, KC], F32)
    w13a = dt("w13a", [JC, 128, KC, 256], F32); w2a = dt("w2a", [KC, 128, JC, 128], F32)
    w13b = dt("w13b", [JC, 128, KC, 256], F32); w2b = dt("w2b", [KC, 128, JC, 128], F32)
    X2 = dt("X2", [D, NT], F32, "Internal"); X3 = dt("X3", [D, NT], F32, "Internal")
    X4 = dt("X4", [D, NT], F32, "ExternalOutput")
    h1 = dt("h1", [D, NT], BF16, "ExternalOutput")
    with contextlib.ExitStack() as st:
        P = Prog(nc)
        dn = Dense(nc, P, st, 768, 0)
        dn.mod_load(modo, gso, hgo)
        bos = st.enter_context(nc.sbuf_tensor("sb_bos", [128, KC], F32))
        P.emit("sp", lambda e: e.dma_start(out=bos[:], in_=bo), writes=[dn.r_mod], chan="bo")
        r_in = Res()
        for blocks in make_passes(nlat, nctx, 0):
            r2, r3, r4, rh = Res(), Res(), Res(), Res()
            dn.linear_res(blocks, fT, r_in, wo, bos, X1, r_in, X2, r2, 0)
            dn.norm_mod(blocks, X2, r2, 0, 2)
            dn.ffn(blocks, w13a, w2a, X2, r2, X3, r3, 0, 2)
            dn.norm_mod(blocks, X3, r3, 1, 0)
            dn.ffn(blocks, w13b, w2b, X3, r3, X4, r4, 1, 0)
            dn.norm_mod(blocks, X4, r4, 1, 1, out_dram=h1, r_out=rh)
        P.run(final_waits=_all_dma_tails(P))
    return nc


def build_l6(nlat):
    NT = nlat
    nc = bass.Bass("TRN2", target_bir_lowering=False)
    dt = lambda name, shape, dty, kind="ExternalInput": nc.dram_tensor(name, shape, dty, kind=kind).ap()
    X4 = dt("X4", [D, NT], F32)
    oT = dt("oT", [D, NT], BF16)
    modo = dt("modo", [128, 2 * 144 * 3], F32); gso = dt("gso", [128, 2 * 3 * KC * 3], F32); hgo = dt("hgo", [128, 2 * 3 * KC * 3], F32)
    wo = dt("wo", [8, 128, KC, 256], F32)
    fng = dt("fng", [128, KC], F32)
    w13 = dt("w13", [JC, 128, KC, 256], F32); w2 = dt("w2", [KC, 128, JC, 128], F32)
    X5 = dt("X5", [D, NT], F32, "Internal"); X6 = dt("X6", [D, NT], F32, "Internal")
    out = dt("out", [D, NT], F32, "ExternalOutput")
    with contextlib.ExitStack() as st:
        P = Prog(nc)
        dn = Dense(nc, P, st, 768, 0)
        dn.mod_load(modo, gso, hgo)
        fgs = st.enter_context(nc.sbuf_tensor("sb_fgs", [128, KC], F32))
        P.emit("sp", lambda e: e.dma_start(out=fgs[:], in_=fng), writes=[dn.r_mod], chan="fg")
        r_in = Res()
        for blocks in make_passes(nlat, 0, 0):
            r5, r6, ro = Res(), Res(), Res()
            dn.linear_res(blocks, oT, r_in, wo, None, X4, r_in, X5, r5, 1)
            dn.norm_mod(blocks, X5, r5, 1, 2)
            dn.ffn(blocks, w13, w2, X5, r5, X6, r6, 1, 2)
            dn.norm_mod(blocks, X6, r6, 0, 0, out_dram=out, r_out=ro, final_g=fgs)
        P.run(final_waits=_all_dma_tails(P))
    return nc


_DBG = {}


def _run(nc, maps):
    res = run_bass_kernel_spmd(nc, maps, core_ids=list(range(NCORES)))
    return res.results


def kernel(x, c, ctx, c_ctx, mod_w, mod_b, norm_w, ffn_w13, ffn_w2, fnet_w_o, fnet_b_o,
           rwkv_mu, rwkv_w_rkv, rwkv_w0, rwkv_w1, rwkv_w2, rwkv_a0, rwkv_a1, rwkv_a2,
           rwkv_g1, rwkv_g2, rwkv_k_k, rwkv_k_a, rwkv_r_k, rwkv_ln_w, rwkv_ln_b, rwkv_w_o,
           final_norm_w):
    f32 = np.float32
    A = lambda a: np.asarray(a, dtype=f32)
    x, c, ctx, c_ctx = A(x), A(c), A(ctx), A(c_ctx)
    B, L, _ = x.shape
    NL = L // 4
    NCX = CTX // 4
    NT = NL + NCX
    sT = np.ascontiguousarray(lay_vec(np.stack([c[0], c[1], c_ctx])).transpose(0, 2, 1))
    mod_w = A(mod_w); mod_b = A(mod_b)
    maps = []
    for core in range(NCORES):
        cs = slice(core * 2304, (core + 1) * 2304)
        maps.append({"sT": sT,
                     "modw": np.ascontiguousarray(mod_w[:, :, cs].reshape(2, KC, 128, 2304).transpose(0, 2, 1, 3)),
                     "modb": np.ascontiguousarray(mod_b[:, cs].reshape(2, 18, 128).transpose(2, 0, 1))})
    r0 = _run(build_l0(), maps)
    modfull = np.concatenate([r0[i]["modo"].reshape(128, 2, 18, 3) for i in range(NCORES)], axis=2)
    modsw = modfull.copy(); modsw[..., 0] = modfull[..., 1]; modsw[..., 1] = modfull[..., 0]
    modin = [np.ascontiguousarray((modfull if core // 4 == 0 else modsw).reshape(128, -1)) for core in range(NCORES)]
    normw = lay_vec(A(norm_w))
    ffn_w13 = A(ffn_w13); ffn_w2 = A(ffn_w2)
    maps = []
    w13_00, w2_00 = lay_w13(ffn_w13[0, 0]), lay_w2(ffn_w2[0, 0])
    for core in range(NCORES):
        b, q = core // 4, core % 4
        xt = np.concatenate([x[b, q * NL:(q + 1) * NL], ctx[b, q * NCX:(q + 1) * NCX]], 0).T
        maps.append({"xT": np.ascontiguousarray(xt), "modi": modin[core], "normw": normw, "w13": w13_00, "w2": w2_00})
    r1 = _run(build_l1(NL, NCX), maps)
    del maps
    tabs = fft_tables()
    maps = []
    for g in range(NCORES):
        rows = slice(256 * g, 256 * (g + 1))
        hT = np.stack([np.concatenate([r1[b * 4 + q]["h0"][rows, 0:NL] for q in range(4)], axis=1) for b in range(B)])
        hcT = np.stack([np.concatenate([r1[b * 4 + q]["h0"][rows, NL:NT] for q in range(4)], axis=1) for b in range(B)])
        maps.append({"hT": np.ascontiguousarray(hT.reshape(B, 2, 128, L).transpose(0, 2, 1, 3)),
                     "hcT": np.ascontiguousarray(hcT.reshape(B, 2, 128, CTX).transpose(0, 2, 1, 3)), **tabs})
    r2 = _run(build_l2(B), maps)
    maps = []
    wo_f = lay_wo(A(fnet_w_o)[0]); bo_f = lay_vec(A(fnet_b_o)[0])
    w13a, w2a = lay_w13(ffn_w13[0, 1]), lay_w2(ffn_w2[0, 1])
    w13b, w2b = lay_w13(ffn_w13[1, 0]), lay_w2(ffn_w2[1, 0])
    for core in range(NCORES):
        b, q = core // 4, core % 4
        fT = np.concatenate([np.concatenate([r2[g]["fo"][b][:, q * NL:(q + 1) * NL] for g in range(NCORES)], axis=0),
                             np.concatenate([r2[g]["fco"][b][:, q * NCX:(q + 1) * NCX] for g in range(NCORES)], axis=0)], axis=1)
        maps.append({"X1": r1[core]["X1"], "fT": np.ascontiguousarray(fT), "modo": modin[core], "gso": r1[core]["gso"], "hgo": r1[core]["hgo"],
                     "wo": wo_f, "bo": bo_f, "w13a": w13a, "w2a": w2a, "w13b": w13b, "w2b": w2b})
    r3 = _run(build_l3(NL, NCX), maps)
    del r2, maps
    inp = {"rwkv_mu": A(rwkv_mu), "rwkv_w_rkv": A(rwkv_w_rkv), "rwkv_w0": A(rwkv_w0), "rwkv_w1": A(rwkv_w1), "rwkv_w2": A(rwkv_w2),
           "rwkv_a0": A(rwkv_a0), "rwkv_a1": A(rwkv_a1), "rwkv_a2": A(rwkv_a2), "rwkv_g1": A(rwkv_g1), "rwkv_g2": A(rwkv_g2),
           "rwkv_k_k": A(rwkv_k_k), "rwkv_k_a": A(rwkv_k_a), "rwkv_r_k": A(rwkv_r_k), "rwkv_ln_w": A(rwkv_ln_w), "rwkv_ln_b": A(rwkv_ln_b)}
    hT = np.stack([np.concatenate([r3[b * 4 + q]["h1"][:, NL:NT] for q in range(4)] + [r3[b * 4 + q]["h1"][:, 0:NL] for q in range(4)], axis=1)
                   for b in range(B)])
    hT = np.ascontiguousarray(hT)
    cst = rwkv_consts()
    maps = [{"hT": hT, **cst, **rwkv_host_weights(inp, g)} for g in range(NCORES)]
    r45 = _run(build_l45(L, CTX, B), maps)
    del hT, maps
    maps = []
    wo_r = lay_wo(A(rwkv_w_o)[0]); fng = lay_vec(A(final_norm_w))
    w13c, w2c = lay_w13(ffn_w13[1, 1]), lay_w2(ffn_w2[1, 1])
    for core in range(NCORES):
        b, q = core // 4, core % 4
        oT = np.concatenate([r45[g]["o_out"][b][q * NL:(q + 1) * NL, :] for g in range(NCORES)], axis=1).T
        maps.append({"X4": np.ascontiguousarray(r3[core]["X4"][:, 0:NL]), "oT": np.ascontiguousarray(oT),
                     "modo": modin[core], "gso": r1[core]["gso"], "hgo": r1[core]["hgo"],
                     "wo": wo_r, "fng": fng, "w13": w13c, "w2": w2c})
    r6 = _run(build_l6(NL), maps)
    _DBG.update(r1=r1, r3=r3, r45=r45, modfull=modfull)
    out = np.empty((B, L, D), f32)
    for core in range(NCORES):
        b, q = core // 4, core % 4
        out[b, q * NL:(q + 1) * NL] = r6[core]["out"].T
    return out
```

```python
import contextlib
import types
import numpy as np
import ml_dtypes
import concourse.bass as bass
import concourse.mybir as mybir
from concourse.bass_utils import run_bass_kernel_spmd

F32 = mybir.dt.float32
BF16 = mybir.dt.bfloat16
ALU = mybir.AluOpType
AF = mybir.ActivationFunctionType
AX = mybir.AxisListType

D = 2048
KC = 16
DFF = 5632
JC = 44
NMOD = 9
SEQ = 16384
CTX = 256
EPS = 1e-6
NCORES = 8


class Res:
    __slots__ = ("lastw", "readers")

    def __init__(self):
        self.lastw = None
        self.readers = []


class Op:
    __slots__ = ("eng", "fn", "deps", "sig", "cnt", "chan")

    def __init__(self, eng, fn, chan=None):
        self.eng = eng
        self.fn = fn
        self.deps = []
        self.sig = False
        self.cnt = 0
        self.chan = chan


ENGS = ("pe", "act", "dve", "pool", "sp")


def _freeze(fn):
    cl = fn.__closure__
    if cl is None:
        return fn
    cells = []
    for c in cl:
        try:
            cells.append(types.CellType(c.cell_contents))
        except ValueError:
            cells.append(c)
    return types.FunctionType(fn.__code__, fn.__globals__, fn.__name__, fn.__defaults__, tuple(cells))


class Prog:
    def __init__(self, nc, same_engine_sync=True):
        self.nc = nc
        self.ops = {e: [] for e in ENGS}
        self.chan_last = {}
        self.same = same_engine_sync

    def emit(self, eng, fn, reads=(), writes=(), chan=None):
        op = Op(eng, _freeze(fn), chan)
        deps = []
        for r in reads:
            if r.lastw is not None:
                deps.append(r.lastw)
        for w in writes:
            if w.lastw is not None:
                deps.append(w.lastw)
            deps.extend(w.readers)
        if chan is not None:
            prev = self.chan_last.get(chan)
            if prev is not None:
                deps.append(prev)
                op.cnt = prev.cnt + 16
            else:
                op.cnt = 16
            self.chan_last[chan] = op
        seen = set()
        for d in deps:
            if id(d) in seen or d is op:
                continue
            seen.add(id(d))
            if d.chan is None and d.eng == eng and (eng == "pe" or not self.same):
                continue
            op.deps.append(d)
            d.sig = True
        for r in reads:
            r.readers.append(op)
        for w in writes:
            w.lastw = op
            w.readers = []
        self.ops[eng].append(op)
        return op

    def run(self, final_waits=()):
        nc = self.nc
        chans = dict(self.chan_last)
        for e in ENGS:
            c = 0
            for op in self.ops[e]:
                if op.chan is None and op.sig:
                    c += 1
                    op.cnt = c
        with contextlib.ExitStack() as st:
            esem = {e: st.enter_context(nc.semaphore("s_" + e)) for e in ENGS}
            csem = {c: st.enter_context(nc.semaphore("c_%s" % (str(c),))) for c in chans}
            block = st.enter_context(nc.Block())

            def mk(ename):
                oplist = self.ops[ename]
                fw = list(final_waits) if ename == "sp" else []

                def body(eng):
                    known = {}
                    for op in oplist:
                        need = {}
                        for d in op.deps:
                            if d.chan is not None:
                                key = ("c", d.chan)
                                sem = csem[d.chan]
                            else:
                                key = ("e", d.eng)
                                sem = esem[d.eng]
                            if known.get(key, 0) >= d.cnt:
                                continue
                            if key not in need or need[key][1] < d.cnt:
                                need[key] = (sem, d.cnt)
                        for key, (sem, cnt) in need.items():
                            known[key] = cnt
                            eng.wait_ge(sem, cnt)
                        ins = op.fn(eng)
                        if op.chan is not None:
                            ins.then_inc(csem[op.chan], 16)
                        elif op.sig:
                            ins.then_inc(esem[op.eng], 1)
                    for d in fw:
                        eng.wait_ge(csem[d.chan], d.cnt)
                return body

            block.tensor(mk("pe"))
            block.scalar(mk("act"))
            block.vector(mk("dve"))
            block.gpsimd(mk("pool"))
            block.sync(mk("sp"))


class Rot:
    def __init__(self, bufs, name):
        self.bufs = bufs
        self.res = [Res() for _ in bufs]
        self.name = name
        self.i = 0

    def next(self):
        k = self.i % len(self.bufs)
        self.i += 1
        return self.bufs[k], self.res[k], "%s%d" % (self.name, k)


class Dense:
    def __init__(self, nc, P, st, tmax, batch_row):
        self.nc, self.P, self.st = nc, P, st
        self.tmax = tmax
        self.brow = batch_row
        sb = lambda name, shape, dt: st.enter_context(nc.sbuf_tensor("sb_" + name, shape, dt))
        self.h = sb("h", [128, KC, tmax], BF16)
        self.r_h = Res()
        self.hid = sb("hid", [128, JC, tmax], BF16)
        self.r_hid = Res()
        self.xs = Rot([sb("xs%d" % i, [128, tmax], F32) for i in range(3)], "xs")
        self.sq = Rot([sb("sq%d" % i, [128, tmax], BF16) for i in range(2)], "sq")
        self.rstd = sb("rstd", [128, tmax], F32)
        self.r_rstd = Res()
        self.tmp = Rot([sb("tmp%d" % i, [128, tmax], F32) for i in range(2)], "tmp")
        self.sg = Rot([sb("sg%d" % i, [128, 512], F32) for i in range(2)], "sg")
        self.w13t = Rot([sb("w13t%d" % i, [128, KC, 256], BF16) for i in range(3)], "w13t")
        self.w2t = Rot([sb("w2t%d" % i, [128, JC, 128], BF16) for i in range(2)], "w2t")
        self.xo = Rot([sb("xo%d" % i, [128, tmax], F32) for i in range(2)], "xo")
        self.ones = sb("ones", [128, 128], BF16)
        self.r_ones = Res()
        P.emit("pool", lambda e: e.memset(self.ones[:], 1.0 / D), writes=[self.r_ones])
        self.epsb = sb("epsb", [128, 1], F32)
        P.emit("pool", lambda e: e.memset(self.epsb[:], EPS), writes=[self.r_ones])
        self.ps = Rot([st.enter_context(nc.psum_tensor("ps%d" % i, [128, 512], F32)) for i in range(8)], "ps")

    def mod_compute(self, sT_d, modw_d, modb_d, nlayers=2, nchunks=144):
        nc, P, st = self.nc, self.P, self.st
        sb = lambda name, shape, dt: st.enter_context(nc.sbuf_tensor("sb_" + name, shape, dt))
        sT = sb("sT", [128, KC, 3], F32)
        r_sT = Res()
        self.mod = sb("mod", [128, nlayers, nchunks, 3], F32)
        self.r_mod = Res()
        modb = sb("modb", [128, nlayers, nchunks], F32)
        r_modb = Res()
        wm = Rot([sb("wm%d" % i, [128, KC, 256], F32) for i in range(2)], "wm")
        P.emit("sp", lambda e: e.dma_start(out=sT[:], in_=sT_d), writes=[r_sT], chan="msc0")
        P.emit("sp", lambda e: e.dma_start(out=modb[:], in_=modb_d), writes=[r_modb], chan="msc1")
        P.emit("act", lambda e: e.activation(out=sT[:], in_=sT[:], func=AF.Silu), reads=[r_sT], writes=[r_sT])
        for l in range(nlayers):
            pst, r_ps, _ = self.ps.next()
            psv = pst[:, 0:nchunks * 3].rearrange("p (n r) -> p n r", r=3)
            for nb in range(nchunks // 2):
                wt, r_wt, ch = wm.next()
                P.emit("sp" if nb % 2 else "act",
                       lambda e, wt=wt, l=l, nb=nb: e.dma_start(out=wt[:], in_=modw_d[l, :, :, nb * 256:(nb + 1) * 256]),
                       writes=[r_wt], chan=ch)
                for q in range(2):
                    n = nb * 2 + q
                    for kc in range(KC):
                        P.emit("pe", lambda e, wt=wt, q=q, kc=kc, n=n, psv=psv: e.matmul(
                            psv[:, n, :], lhsT=wt[:, kc, q * 128:(q + 1) * 128], rhs=sT[:, kc, :],
                            start=(kc == 0), stop=(kc == KC - 1)), reads=[r_wt, r_sT], writes=[r_ps])
            P.emit("dve", lambda e, l=l, psv=psv: e.tensor_tensor(
                out=self.mod[:, l], in0=psv, in1=modb[:, l, :].unsqueeze(2).to_broadcast([128, nchunks, 3]), op=ALU.add),
                reads=[r_ps, r_modb], writes=[self.r_mod])

    def mod_derive(self, mod_d, normw_d, nlayers=2):
        nc, P, st = self.nc, self.P, self.st
        sb = lambda name, shape, dt: st.enter_context(nc.sbuf_tensor("sb_" + name, shape, dt))
        self.mod = sb("mod", [128, nlayers, 144, 3], F32)
        self.r_mod = Res()
        self.normw = sb("normw", [128, nlayers, 3, KC], F32)
        r_nw = Res()
        self.gs = sb("gs", [128, nlayers, 3, KC, 3], F32)
        self.hg = sb("hg", [128, nlayers, 3, KC, 3], F32)
        P.emit("sp", lambda e: e.dma_start(out=self.mod[:].rearrange("p a b c -> p (a b c)"), in_=mod_d), writes=[self.r_mod], chan="md0")
        P.emit("sp", lambda e: e.dma_start(out=self.normw[:], in_=normw_d), writes=[r_nw], chan="md1")
        for l in range(nlayers):
            for s in range(3):
                sc = self.mod[:, l, (3 * s + 1) * 16:(3 * s + 2) * 16, :]
                gt = self.mod[:, l, (3 * s + 2) * 16:(3 * s + 3) * 16, :]
                P.emit("dve", lambda e, l=l, s=s, sc=sc: e.scalar_tensor_tensor(
                    out=self.gs[:, l, s], in0=sc, scalar=1.0, in1=self.normw[:, l, s, :].unsqueeze(2).to_broadcast([128, KC, 3]),
                    op0=ALU.add, op1=ALU.mult), reads=[self.r_mod, r_nw], writes=[self.r_mod])
                P.emit("dve", lambda e, l=l, s=s, gt=gt: e.tensor_scalar(
                    out=self.hg[:, l, s], in0=gt, scalar1=(1.0 if s == 1 else 0.5), scalar2=None, op0=ALU.mult),
                    reads=[self.r_mod], writes=[self.r_mod])

    def mod_load(self, modo, gso, hgo, nlayers=2):
        nc, P, st = self.nc, self.P, self.st
        sb = lambda name, shape, dt: st.enter_context(nc.sbuf_tensor("sb_" + name, shape, dt))
        self.mod = sb("mod", [128, nlayers, 144, 3], F32)
        self.gs = sb("gs", [128, nlayers, 3, KC, 3], F32)
        self.hg = sb("hg", [128, nlayers, 3, KC, 3], F32)
        self.r_mod = Res()
        P.emit("sp", lambda e: e.dma_start(out=self.mod[:].rearrange("p a b c -> p (a b c)"), in_=modo), writes=[self.r_mod], chan="ml0")
        P.emit("sp", lambda e: e.dma_start(out=self.gs[:].rearrange("p a b c d -> p (a b c d)"), in_=gso), writes=[self.r_mod], chan="ml1")
        P.emit("sp", lambda e: e.dma_start(out=self.hg[:].rearrange("p a b c d -> p (a b c d)"), in_=hgo), writes=[self.r_mod], chan="ml2")

    def linear_res(self, blocks, src_d, r_src, w_d, bias_sb, X, r_X, Xo, r_Xo, l):
        P = self.P
        offs = np.cumsum([0] + [b[1] for b in blocks])
        sv = src_d.rearrange("(c p) t -> p c t", p=128)
        for bi, (c0, n, row) in enumerate(blocks):
            for half in range(2):
                P.emit("sp", lambda e, c0=c0, n=n, o=offs[bi], half=half: e.dma_start(
                    out=self.h[:, half * 8:(half + 1) * 8, o:o + n], in_=sv[:, half * 8:(half + 1) * 8, c0:c0 + n]),
                    reads=[r_src], writes=[self.r_h], chan="lrh%d_%d" % (bi, half))
        Xv = X.rearrange("(c p) t -> p c t", p=128)
        Xov = Xo.rearrange("(c p) t -> p c t", p=128)
        for n2 in range(KC // 2):
            wt, r_wt, ch = self.w13t.next()
            P.emit("pool", lambda e, wt=wt, n2=n2: e.dma_start(out=wt[:], in_=w_d[n2]), writes=[r_wt], chan=ch)
            for q in range(2):
                nn = n2 * 2 + q
                xt, r_xt, chx = self.xs.next()
                xo, r_xo, cho = self.xo.next()
                for bi, (c0, n, row) in enumerate(blocks):
                    o = offs[bi]
                    P.emit("sp", lambda e, xt=xt, nn=nn, c0=c0, n=n, o=o: e.dma_start(
                        out=xt[:, o:o + n], in_=Xv[:, nn, c0:c0 + n]), reads=[r_X], writes=[r_xt], chan=chx + "_%d" % bi)
                    po, r_po, _ = self.ps.next()
                    for kc in range(KC):
                        P.emit("pe", lambda e, po=po, wt=wt, kc=kc, q=q, o=o, n=n: e.matmul(
                            po[:, 0:n], lhsT=wt[:, kc, q * 128:(q + 1) * 128], rhs=self.h[:, kc, o:o + n],
                            start=(kc == 0), stop=(kc == KC - 1)), reads=[r_wt, self.r_h], writes=[r_po])
                    src_ap = po[:, 0:n]
                    rd = [r_po]
                    if bias_sb is not None:
                        sg, r_sg, _ = self.sg.next()
                        P.emit("act", lambda e, sg=sg, po=po, n=n, nn=nn: e.activation(
                            out=sg[:, 0:n], in_=po[:, 0:n], func=AF.Identity, bias=bias_sb[:, nn:nn + 1], scale=1.0),
                            reads=[r_po, self.r_mod], writes=[r_sg])
                        src_ap = sg[:, 0:n]
                        rd = [r_sg]
                    P.emit("dve", lambda e, src_ap=src_ap, xt=xt, xo=xo, nn=nn, o=o, n=n, row=row: e.scalar_tensor_tensor(
                        out=xo[:, o:o + n], in0=src_ap, scalar=self.hg[:, l, 1, nn, row:row + 1], in1=xt[:, o:o + n],
                        op0=ALU.mult, op1=ALU.add), reads=rd + [r_xt, self.r_mod], writes=[r_xo])
                    P.emit("sp", lambda e, xo=xo, nn=nn, c0=c0, n=n, o=o: e.dma_start(
                        out=Xov[:, nn, c0:c0 + n], in_=xo[:, o:o + n]), reads=[r_xo], writes=[r_Xo], chan=cho + "_o%d" % bi)

    def shift_ap(self, l, s, c, row):
        return self.mod[:, l, (3 * s) * 16 + c, row:row + 1]

    def norm_mod(self, blocks, X, r_X, l, s, out_dram=None, r_out=None, final_g=None):
        P = self.P
        T = sum(b[1] for b in blocks)
        offs = np.cumsum([0] + [b[1] for b in blocks])
        stat = [self.ps.next() for _ in blocks]
        Xv = X.rearrange("(c p) t -> p c t", p=128)

        def load_x(c):
            xt, r_xt, ch = self.xs.next()
            for bi, (c0, n, row) in enumerate(blocks):
                P.emit("sp", lambda e, xt=xt, c=c, c0=c0, n=n, o=offs[bi]: e.dma_start(
                    out=xt[:, o:o + n], in_=Xv[:, c, c0:c0 + n]), reads=[r_X], writes=[r_xt], chan=ch + "_%d" % bi)
            return xt, r_xt

        for c in range(KC):
            xt, r_xt = load_x(c)
            sq, r_sq, _ = self.sq.next()
            P.emit("act", lambda e, xt=xt, sq=sq: e.activation(out=sq[:, 0:T], in_=xt[:, 0:T], func=AF.Square),
                   reads=[r_xt], writes=[r_sq])
            for bi, (c0, n, row) in enumerate(blocks):
                pst, r_ps, _ = stat[bi]
                P.emit("pe", lambda e, pst=pst, sq=sq, o=offs[bi], n=n, c=c: e.matmul(
                    pst[:, 0:n], lhsT=self.ones[:], rhs=sq[:, o:o + n], start=(c == 0), stop=(c == KC - 1)),
                    reads=[r_sq, self.r_ones], writes=[r_ps])
        for bi, (c0, n, row) in enumerate(blocks):
            pst, r_ps, _ = stat[bi]
            P.emit("act", lambda e, pst=pst, o=offs[bi], n=n: e.activation(
                out=self.rstd[:, o:o + n], in_=pst[:, 0:n], func=AF.Sqrt, bias=self.epsb[:, 0:1], scale=1.0),
                reads=[r_ps, self.r_ones], writes=[self.r_rstd])
            P.emit("dve", lambda e, o=offs[bi], n=n: e.reciprocal(
                out=self.rstd[:, o:o + n], in_=self.rstd[:, o:o + n]),
                reads=[self.r_rstd], writes=[self.r_rstd])
        for c in range(KC):
            xt, r_xt = load_x(c)
            tmp, r_tmp, _ = self.tmp.next()
            P.emit("dve", lambda e, xt=xt, tmp=tmp: e.tensor_tensor(
                out=tmp[:, 0:T], in0=xt[:, 0:T], in1=self.rstd[:, 0:T], op=ALU.mult),
                reads=[r_xt, self.r_rstd], writes=[r_tmp])
            if out_dram is None:
                for bi, (c0, n, row) in enumerate(blocks):
                    P.emit("act", lambda e, tmp=tmp, o=offs[bi], n=n, c=c, row=row: e.activation(
                        out=self.h[:, c, o:o + n], in_=tmp[:, o:o + n], func=AF.Identity,
                        scale=self.gs[:, l, s, c, row:row + 1], bias=self.shift_ap(l, s, c, row)),
                        reads=[r_tmp, self.r_mod], writes=[self.r_h])
            else:
                xo, r_xo, ch = self.xo.next()
                odt = out_dram.dtype
                xov = xo if odt == F32 else xo[:].bitcast(BF16)
                for bi, (c0, n, row) in enumerate(blocks):
                    if final_g is not None:
                        P.emit("act", lambda e, tmp=tmp, xov=xov, o=offs[bi], n=n, c=c: e.activation(
                            out=xov[:, o:o + n], in_=tmp[:, o:o + n], func=AF.Identity, scale=final_g[:, c:c + 1], bias=0.0),
                            reads=[r_tmp, self.r_mod], writes=[r_xo])
                    else:
                        P.emit("act", lambda e, tmp=tmp, xov=xov, o=offs[bi], n=n, c=c, row=row: e.activation(
                            out=xov[:, o:o + n], in_=tmp[:, o:o + n], func=AF.Identity,
                            scale=self.gs[:, l, s, c, row:row + 1], bias=self.shift_ap(l, s, c, row)),
                            reads=[r_tmp, self.r_mod], writes=[r_xo])
                ov = out_dram.rearrange("(c p) t -> p c t", p=128)
                for bi, (c0, n, row) in enumerate(blocks):
                    P.emit("sp", lambda e, xov=xov, o=offs[bi], n=n, c=c, c0=c0: e.dma_start(
                        out=ov[:, c, c0:c0 + n], in_=xov[:, o:o + n]), reads=[r_xo], writes=[r_out], chan=ch + "_o%d" % bi)

    def ffn(self, blocks, w13_d, w2_d, X, r_X, Xo, r_Xo, l, s):
        P = self.P
        offs = np.cumsum([0] + [b[1] for b in blocks])
        for j in range(JC):
            wt, r_wt, ch = self.w13t.next()
            P.emit("pool", lambda e, wt=wt, j=j: e.dma_start(out=wt[:], in_=w13_d[j]), writes=[r_wt], chan=ch)
            for bi, (c0, n, row) in enumerate(blocks):
                o = offs[bi]
                pg, r_pg, _ = self.ps.next()
                pu, r_pu, _ = self.ps.next()
                for half, (pp, r_pp) in enumerate(((pg, r_pg), (pu, r_pu))):
                    for kc in range(KC):
                        P.emit("pe", lambda e, pp=pp, wt=wt, kc=kc, half=half, o=o, n=n: e.matmul(
                            pp[:, 0:n], lhsT=wt[:, kc, half * 128:(half + 1) * 128], rhs=self.h[:, kc, o:o + n],
                            start=(kc == 0), stop=(kc == KC - 1)), reads=[r_wt, self.r_h], writes=[r_pp])
                sg, r_sg, _ = self.sg.next()
                P.emit("act", lambda e, sg=sg, pg=pg, n=n: e.activation(out=sg[:, 0:n], in_=pg[:, 0:n], func=AF.Silu),
                       reads=[r_pg], writes=[r_sg])
                P.emit("dve", lambda e, sg=sg, pu=pu, j=j, o=o, n=n: e.tensor_tensor(
                    out=self.hid[:, j, o:o + n], in0=sg[:, 0:n], in1=pu[:, 0:n], op=ALU.mult),
                    reads=[r_sg, r_pu], writes=[self.r_hid])
        Xv = X.rearrange("(c p) t -> p c t", p=128)
        Xov = Xo.rearrange("(c p) t -> p c t", p=128)
        for nn in range(KC):
            wt, r_wt, ch = self.w2t.next()
            P.emit("pool", lambda e, wt=wt, nn=nn: e.dma_start(out=wt[:], in_=w2_d[nn]), writes=[r_wt], chan=ch)
            xt, r_xt, chx = self.xs.next()
            xo, r_xo, cho = self.xo.next()
            for bi, (c0, n, row) in enumerate(blocks):
                o = offs[bi]
                P.emit("sp", lambda e, xt=xt, nn=nn, c0=c0, n=n, o=o: e.dma_start(
                    out=xt[:, o:o + n], in_=Xv[:, nn, c0:c0 + n]), reads=[r_X], writes=[r_xt], chan=chx + "_%d" % bi)
                po, r_po, _ = self.ps.next()
                for jc in range(JC):
                    P.emit("pe", lambda e, po=po, wt=wt, jc=jc, o=o, n=n: e.matmul(
                        po[:, 0:n], lhsT=wt[:, jc, :], rhs=self.hid[:, jc, o:o + n],
                        start=(jc == 0), stop=(jc == JC - 1)), reads=[r_wt, self.r_hid], writes=[r_po])
                P.emit("dve", lambda e, po=po, xt=xt, xo=xo, nn=nn, o=o, n=n, row=row: e.scalar_tensor_tensor(
                    out=xo[:, o:o + n], in0=po[:, 0:n], scalar=self.hg[:, l, s, nn, row:row + 1], in1=xt[:, o:o + n],
                    op0=ALU.mult, op1=ALU.add), reads=[r_po, r_xt, self.r_mod], writes=[r_xo])
                P.emit("sp", lambda e, xo=xo, nn=nn, c0=c0, n=n, o=o: e.dma_start(
                    out=Xov[:, nn, c0:c0 + n], in_=xo[:, o:o + n]), reads=[r_xo], writes=[r_Xo], chan=cho + "_o%d" % bi)


def make_passes(nlat, nctx, brow):
    blks = []
    c = 0
    while c < nlat:
        n = min(512, nlat - c)
        blks.append((c, n, brow))
        c += n
    if nctx:
        blks.append((nlat, nctx, 2))
    fine = []
    for (c0, n, row) in blks:
        fine.append((c0, n, row))
    passes = []
    cur, tot = [], 0
    queue = list(fine)
    while queue:
        c0, n, row = queue.pop(0)
        if tot + n <= 768:
            cur.append((c0, n, row)); tot += n
        elif n == 512 and tot + 256 <= 768:
            cur.append((c0, 256, row)); tot += 256
            queue.insert(0, (c0 + 256, 256, row))
        else:
            passes.append(cur); cur, tot = [], 0
            queue.insert(0, (c0, n, row))
    if cur:
        passes.append(cur)
    return passes


def lay_w13(w):
    return np.ascontiguousarray(w.reshape(KC, 128, 2, JC, 128).transpose(3, 1, 0, 2, 4)).reshape(JC, 128, KC, 256)


def lay_w2(w):
    return np.ascontiguousarray(w.reshape(JC, 128, KC, 128).transpose(2, 1, 0, 3))


def lay_sq(w):
    return np.ascontiguousarray(w.reshape(KC, 128, -1).transpose(1, 0, 2))


def lay_vec(v):
    sh = v.shape[:-1]
    a = v.reshape(sh + (KC, 128))
    return np.ascontiguousarray(np.moveaxis(a, -1, 0))


def core_tokens(core):
    b = core // 4
    q = core % 4
    return b, q


def build_l0():
    nc = bass.Bass("TRN2", target_bir_lowering=False)
    dt = lambda name, shape, dty, kind: nc.dram_tensor(name, shape, dty, kind=kind).ap()
    sT = dt("sT", [128, KC, 3], F32, "ExternalInput")
    modw = dt("modw", [2, 128, KC, 18 * 128], F32, "ExternalInput")
    modb = dt("modb", [128, 2, 18], F32, "ExternalInput")
    modo = dt("modo", [128, 2 * 18 * 3], F32, "ExternalOutput")
    with contextlib.ExitStack() as st:
        P = Prog(nc)
        dn = Dense(nc, P, st, 64, 0)
        dn.mod_compute(sT, modw, modb, 2, 18)
        P.emit("sp", lambda e: e.dma_start(out=modo, in_=dn.mod[:].rearrange("p a b c -> p (a b c)")), reads=[dn.r_mod], chan="mo0")
        P.run(final_waits=_all_dma_tails(P))
    return nc


def build_l1(nlat, nctx):
    NT = nlat + nctx
    nc = bass.Bass("TRN2", target_bir_lowering=False)
    dt = lambda name, shape, dty, kind: nc.dram_tensor(name, shape, dty, kind=kind).ap()
    xT = dt("xT", [D, NT], F32, "ExternalInput")
    modi = dt("modi", [128, 2 * 144 * 3], F32, "ExternalInput")
    normw = dt("normw", [128, 2, 3, KC], F32, "ExternalInput")
    w13 = dt("w13", [JC, 128, KC, 256], F32, "ExternalInput")
    w2 = dt("w2", [KC, 128, JC, 128], F32, "ExternalInput")
    X1 = dt("X1", [D, NT], F32, "ExternalOutput")
    h0 = dt("h0", [D, NT], BF16, "ExternalOutput")
    gso = dt("gso", [128, 2 * 3 * KC * 3], F32, "ExternalOutput")
    hgo = dt("hgo", [128, 2 * 3 * KC * 3], F32, "ExternalOutput")
    outs = []
    with contextlib.ExitStack() as st:
        P = Prog(nc)
        dn = Dense(nc, P, st, 768, 0)
        dn.mod_derive(modi, normw)
        r_o = Res()
        outs.append(P.emit("sp", lambda e: e.dma_start(out=gso, in_=dn.gs[:].rearrange("p a b c d -> p (a b c d)")), reads=[dn.r_mod], writes=[r_o], chan="mo1"))
        outs.append(P.emit("sp", lambda e: e.dma_start(out=hgo, in_=dn.hg[:].rearrange("p a b c d -> p (a b c d)")), reads=[dn.r_mod], writes=[r_o], chan="mo2"))
        r_xin = Res()
        passes = make_passes(nlat, nctx, 0)
        for blocks in passes:
            r_X1 = Res()
            r_h0 = Res()
            dn.norm_mod(blocks, xT, r_xin, 0, 0)
            dn.ffn(blocks, w13, w2, xT, r_xin, X1, r_X1, 0, 0)
            dn.norm_mod(blocks, X1, r_X1, 0, 1, out_dram=h0, r_out=r_h0)
            outs.append(r_X1)
            outs.append(r_h0)
        fw = [o.lastw if isinstance(o, Res) else o for o in outs]
        P.run(final_waits=_all_dma_tails(P))
    return nc


def _all_dma_tails(P):
    return list(P.chan_last.values())


def fft_tables():
    bf = ml_dtypes.bfloat16
    ch = np.arange(256, dtype=np.float64)
    ang = 2 * np.pi * np.outer(ch, ch) / 256.0
    sc = 1.0 / 2048.0
    cs = np.zeros((128, 2, 4, 128), np.float64)
    for kc in range(2):
        for q in range(4):
            a = ang[kc * 128:(kc + 1) * 128, q * 64:(q + 1) * 64]
            cs[:, kc, q, 0:64] = np.cos(a) * sc
            cs[:, kc, q, 64:128] = -np.sin(a) * sc
    l = np.arange(128, dtype=np.float64)
    a1 = 2 * np.pi * np.outer(l, l) / 128.0
    f1 = np.zeros((128, 2, 256), np.float64)
    f1[:, 0, 0:128] = np.cos(a1); f1[:, 0, 128:256] = -np.sin(a1)
    f1[:, 1, 0:128] = np.sin(a1); f1[:, 1, 128:256] = np.cos(a1)
    k = np.arange(128)[None, :] * 128 + np.arange(128)[:, None]
    ae = 2 * np.pi * (l[:, None, None] * k[None]) / 16384.0
    E = np.concatenate([np.cos(ae), np.sin(ae)], axis=2)
    scc = 1.0 / 256.0
    csc = np.zeros((128, 2, 512), np.float64)
    for kc in range(2):
        a = ang[kc * 128:(kc + 1) * 128, :]
        csc[:, kc, 0:256] = np.cos(a) * scc
        csc[:, kc, 256:512] = -np.sin(a) * scc
    g = np.zeros((128, 2, 512), np.float64)
    for tc in range(2):
        a = ang[tc * 128:(tc + 1) * 128, :]
        g[:, tc, 0:256] = np.cos(a)
        g[:, tc, 256:512] = np.sin(a)
    return {"t_cs": cs.astype(bf), "t_f1": f1.astype(bf), "t_E": E.astype(bf), "t_csc": csc.astype(bf), "t_g": g.astype(bf)}


def emit_fft(nc, P, st, hT, r_hT, hcT, r_hcT, tabs, fo, r_fo, fco, r_fco, nb=2, pfx="ff"):
    sb = lambda name, shape, dt: st.enter_context(nc.sbuf_tensor("sb_" + pfx + name, shape, dt))
    hs = sb("hs", [128, 2, 16384], BF16); r_hs = Res()
    W = sb("W", [128, 128, 128], BF16); r_W = Res()
    Z = sb("Z", [128, 64, 256], BF16); r_Z = Res()
    fT = sb("fT", [64, 16384], BF16); r_fT = Res()
    Eb = Rot([sb("E%d" % i, [128, 16, 256], BF16) for i in range(2)], pfx + "E")
    cs = sb("cs", [128, 2, 4, 128], BF16)
    f1 = sb("f1", [128, 2, 256], BF16)
    csc = sb("csc", [128, 2, 512], BF16)
    gt = sb("gt", [128, 2, 512], BF16)
    hcs = sb("hcs", [128, 2, 256], BF16); r_hcs = Res()
    Wc = sb("Wc", [128, 2, 512], BF16); r_Wc = Res()
    fcs = sb("fcs", [128, 256], BF16); r_fcs = Res()
    r_tab = Res()
    ps = Rot([st.enter_context(nc.psum_tensor(pfx + "ps%d" % i, [128, 512], F32)) for i in range(8)], pfx + "ps")
    for i, (dst, src) in enumerate(((cs, tabs["t_cs"]), (f1, tabs["t_f1"]), (csc, tabs["t_csc"]), (gt, tabs["t_g"]))):
        P.emit("sp", lambda e, dst=dst, src=src: e.dma_start(out=dst[:], in_=src), writes=[r_tab], chan=pfx + "tab%d" % i)
    evac_i = [0]

    def evac(out_ap, in_ap, reads, writes):
        eng = "act" if evac_i[0] % 2 == 0 else "dve"
        evac_i[0] += 1
        if eng == "act":
            P.emit("act", lambda e: e.activation(out=out_ap, in_=in_ap, func=AF.Copy), reads=reads, writes=writes)
        else:
            P.emit("dve", lambda e: e.tensor_copy(out=out_ap, in_=in_ap), reads=reads, writes=writes)

    for b in range(nb):
        P.emit("sp", lambda e, b=b: e.dma_start(out=hcs[:], in_=hcT[b]), reads=[r_hcT], writes=[r_hcs], chan=pfx + "hc")
        for tc in range(2):
            pt, r_pt, _ = ps.next()
            for kc in range(2):
                P.emit("pe", lambda e, pt=pt, tc=tc, kc=kc: e.matmul(
                    pt[:, :], lhsT=hcs[:, kc, tc * 128:(tc + 1) * 128], rhs=csc[:, kc, :], start=(kc == 0), stop=(kc == 1)),
                    reads=[r_hcs, r_tab], writes=[r_pt])
            evac(Wc[:, tc, :], pt[:, :], [r_pt], [r_Wc])
        for half in range(2):
            pt, r_pt, _ = ps.next()
            k = 0
            for tc in range(2):
                for ri in range(2):
                    P.emit("pe", lambda e, pt=pt, tc=tc, ri=ri, half=half, k=k: e.matmul(
                        pt[:, 0:256], lhsT=Wc[:, tc, ri * 256 + half * 128: ri * 256 + (half + 1) * 128],
                        rhs=gt[:, tc, ri * 256:(ri + 1) * 256], start=(k == 0), stop=(k == 3)),
                        reads=[r_Wc, r_tab], writes=[r_pt])
                    k += 1
            evac(fcs[:, :], pt[:, 0:256], [r_pt], [r_fcs])
            P.emit("sp", lambda e, b=b, half=half: e.dma_start(out=fco[b, half * 128:(half + 1) * 128, :], in_=fcs[:, :]),
                   reads=[r_fcs], writes=[r_fco], chan=pfx + "fco")
        for kc in range(2):
            P.emit("sp" if kc == 0 else "act", lambda e, b=b, kc=kc: e.dma_start(out=hs[:, kc, :], in_=hT[b, :, kc, :]),
                   reads=[r_hT], writes=[r_hs], chan=pfx + "hs%d" % kc)
        hv = hs[:].rearrange("p k (a l) -> p k l a", l=128)
        for q in range(4):
            for g4 in range(32):
                pt, r_pt, _ = ps.next()
                for li in range(4):
                    l2 = g4 * 4 + li
                    for kc in range(2):
                        P.emit("pe", lambda e, pt=pt, li=li, l2=l2, kc=kc, q=q: e.matmul(
                            pt[:, li * 128:(li + 1) * 128], lhsT=hv[:, kc, l2, :], rhs=cs[:, kc, q, :],
                            start=(kc == 0), stop=(kc == 1)), reads=[r_hs, r_tab], writes=[r_pt])
                evac(W[:, g4 * 4:(g4 + 1) * 4, :].rearrange("p a b -> p (a b)"), pt[:, :], [r_pt], [r_W])
            for c2 in range(32):
                pt, r_pt, _ = ps.next()
                for ci in range(2):
                    c = c2 * 2 + ci
                    for ri in range(2):
                        P.emit("pe", lambda e, pt=pt, ci=ci, c=c, ri=ri: e.matmul(
                            pt[:, ci * 256:(ci + 1) * 256], lhsT=W[:, :, ri * 64 + c], rhs=f1[:, ri, :],
                            start=(ri == 0), stop=(ri == 1)), reads=[r_W, r_tab], writes=[r_pt])
                evac(Z[:, c2 * 2:(c2 + 1) * 2, :].rearrange("p a b -> p (a b)"), pt[:, :], [r_pt], [r_Z])
            fv = fT[:].rearrange("p (k2 k1) -> p k1 k2", k1=128)
            for eb in range(8):
                Et, r_Et, ch = Eb.next()
                P.emit("sp", lambda e, Et=Et, eb=eb: e.dma_start(out=Et[:], in_=tabs["t_E"][:, eb * 16:(eb + 1) * 16, :]),
                       writes=[r_Et], chan=ch)
                for k4 in range(4):
                    pt, r_pt, _ = ps.next()
                    for ki in range(4):
                        kl = k4 * 4 + ki
                        k1 = eb * 16 + kl
                        for ri in range(2):
                            P.emit("pe", lambda e, pt=pt, ki=ki, kl=kl, k1=k1, ri=ri, Et=Et: e.matmul(
                                pt[0:64, ki * 128:(ki + 1) * 128], lhsT=Z[:, :, ri * 128 + k1], rhs=Et[:, kl, ri * 128:(ri + 1) * 128],
                                start=(ri == 0), stop=(ri == 1)), reads=[r_Z, r_Et], writes=[r_pt])
                    k10 = eb * 16 + k4 * 4
                    evac(fv[:, k10:k10 + 4, :], pt[0:64, :].rearrange("p (a b) -> p a b", a=4), [r_pt], [r_fT])
            P.emit("sp", lambda e, b=b, q=q: e.dma_start(out=fo[b, q * 64:(q + 1) * 64, :], in_=fT[:, :]),
                   reads=[r_fT], writes=[r_fo], chan=pfx + "fo")


def build_l2(nb=2):
    nc = bass.Bass("TRN2", target_bir_lowering=False)
    dt = lambda name, shape, dty, kind: nc.dram_tensor(name, shape, dty, kind=kind).ap()
    hT = dt("hT", [nb, 128, 2, 16384], BF16, "ExternalInput")
    hcT = dt("hcT", [nb, 128, 2, 256], BF16, "ExternalInput")
    tabs = {"t_cs": dt("t_cs", [128, 2, 4, 128], BF16, "ExternalInput"), "t_f1": dt("t_f1", [128, 2, 256], BF16, "ExternalInput"),
            "t_E": dt("t_E", [128, 128, 256], BF16, "ExternalInput"), "t_csc": dt("t_csc", [128, 2, 512], BF16, "ExternalInput"),
            "t_g": dt("t_g", [128, 2, 512], BF16, "ExternalInput")}
    fo = dt("fo", [nb, 256, 16384], BF16, "ExternalOutput")
    fco = dt("fco", [nb, 256, 256], BF16, "ExternalOutput")
    with contextlib.ExitStack() as st:
        P = Prog(nc)
        emit_fft(nc, P, st, hT, Res(), hcT, Res(), tabs, fo, Res(), fco, Res(), nb=nb)
        P.run(final_waits=_all_dma_tails(P))
    return nc


LDC = -0.6065306597126334
GN_EPS = 64e-5


def rwkv_consts():
    bf = ml_dtypes.bfloat16
    idx = np.arange(128)
    cm = np.zeros((128, 2, 3, 256), np.float32)
    ct = np.zeros((128, 2, 3, 128), np.float32)
    for z in range(2):
        before = (idx[:, None] < idx[None, :]) if z == 0 else (idx[:, None] > idx[None, :])
        beq = before | np.eye(128, dtype=bool)
        cm[:, z, 0, 0:128] = before
        cm[:, z, 0, 128:256] = beq
        cm[:, z, 1, 0:128] = before.T
        cm[:, z, 1, 128:256] = before.T
        cm[:, z, 2, 0:128] = beq
        cm[:, z, 2, 128:256] = beq
        ct[:, z, 0, :] = LDC * beq
        ct[:, z, 1, :] = LDC * before
        ct[:, z, 2, :] = LDC
    ident = np.eye(128, dtype=np.float32)
    return {"c_mask": cm.astype(bf), "c_tri": ct, "c_ident": ident.astype(bf)}


def emit_rwkv(nc, P, st, hT, r_hT, wd, yscr, o_out, r_out, rkvs=None, nlat=SEQ, nctx=CTX, nb=2, pfx="rw", dbg=None, dbg_at=(0, "ctx", 0, 0), ew_engs=("dve",), prefetch=True):
    sbt = lambda name, shape, dt: st.enter_context(nc.sbuf_tensor("sb_" + pfx + name, shape, dt))
    ps = Rot([st.enter_context(nc.psum_tensor(pfx + "ps%d" % i, [128, 512], F32)) for i in range(8)], pfx + "ps")
    r_c = Res()

    def ld(dst, src, eng="sp", chan=None, **kw):
        return P.emit(eng, lambda e: e.dma_start(out=dst, in_=src, **kw), writes=[r_c], chan=chan)
    cmask = sbt("cmask", [128, 1, 3, 256], BF16)
    ctri = sbt("ctri", [128, 1, 3, 128], F32)
    ident = sbt("ident", [128, 128], BF16); ld(ident[:], wd["c_ident"], chan=pfx + "k2")
    vecs = sbt("vecs", [128, 9, 256], F32); ld(vecs[:], wd["vecs"], chan=pfx + "k3")
    mu = sbt("mu", [128, KC, 6], F32); ld(mu[:], wd["mu"], chan=pfx + "k4")
    om = sbt("om", [128, KC, 6], F32)
    W2s = sbt("W2s", [96, 2, 256], BF16); ld(W2s[:], wd["w2"], eng="pool", chan=pfx + "k5")
    A2s = sbt("A2s", [96, 2, 256], BF16); ld(A2s[:], wd["a2"], eng="pool", chan=pfx + "k6")
    G2s = sbt("G2s", [128, 2, 256], BF16); ld(G2s[:], wd["g2"], eng="pool", chan=pfx + "k7")
    negcol = sbt("negcol", [128, 1], F32)
    P.emit("pool", lambda e: e.memset(negcol[:], LDC), writes=[r_c])
    gneps = sbt("gneps", [128, 1], F32)
    P.emit("pool", lambda e: e.memset(gneps[:], GN_EPS), writes=[r_c])
    P.emit("dve", lambda e: e.tensor_scalar(out=om[:], in0=mu[:], scalar1=-1.0, scalar2=1.0, op0=ALU.mult, op1=ALU.add),
           reads=[r_c], writes=[r_c])
    RKa = sbt("RKa", [128, KC, 768], BF16); RKb = sbt("RKb", [128, KC, 768], BF16)
    LWa = sbt("LWa", [128, KC, 640], BF16); LWb = sbt("LWb", [128, KC, 640], BF16)
    stg = Rot([sbt("stg%d" % i, [128, 256], F32) for i in range(1)], pfx + "stg")
    blocks = [(0, 256, 0), (256, 512, 1), (512, 768, 2), (768, 960, 3), (960, 1152, 4), (1152, 1408, 5)]
    for kc in range(KC):
        for (c0, c1, p) in blocks:
            sg, r_sg, ch = stg.next()
            n_ = c1 - c0
            if c0 < 768:
                P.emit("sp", lambda e, sg=sg, kc=kc, c0=c0, c1=c1, n_=n_: e.dma_start(out=sg[:, 0:n_], in_=wd["rkv"][:, kc, c0:c1]), writes=[r_sg], chan=ch)
            else:
                P.emit("sp", lambda e, sg=sg, kc=kc, c0=c0, c1=c1, n_=n_: e.dma_start(out=sg[:, 0:n_], in_=wd["lw"][:, kc, c0 - 768:c1 - 768]), writes=[r_sg], chan=ch)
            da = RKa[:, kc, c0:c1] if c0 < 768 else LWa[:, kc, c0 - 768:c1 - 768]
            db = RKb[:, kc, c0:c1] if c0 < 768 else LWb[:, kc, c0 - 768:c1 - 768]
            P.emit("dve", lambda e, sg=sg, n_=n_, p=p, kc=kc, db=db: e.tensor_scalar(
                out=db, in0=sg[:, 0:n_], scalar1=mu[:, kc, p:p + 1], scalar2=None, op0=ALU.mult), reads=[r_sg, r_c], writes=[r_c])
            P.emit("pool", lambda e, sg=sg, n_=n_, p=p, kc=kc, da=da: e.tensor_scalar(
                out=da, in0=sg[:, 0:n_], scalar1=om[:, kc, p:p + 1], scalar2=None, op0=ALU.mult), reads=[r_sg, r_c], writes=[r_c])

    SC = 128
    hw0 = sbt("hw", [128, KC, 64 + SC + 64], BF16); r_hw0 = Res()
    hsb0 = sbt("hsb", [128, KC, SC], BF16); r_hs0 = Res()
    hw = [hw0 for b in range(nb)]; r_hw = [r_hw0 for _ in range(nb)]
    hsb = [hsb0 for b in range(nb)]; r_hs = [r_hs0 for _ in range(nb)]
    xw = [sbt("xw%d" % b, [96, SC], BF16) for b in range(nb)]
    xa = [sbt("xa%d" % b, [96, 2, SC], BF16) for b in range(nb)]
    xg = [sbt("xg%d" % b, [128, 2, SC], BF16) for b in range(nb)]
    r_x = [Res() for _ in range(nb)]
    rkv = [sbt("rkv%d" % b, [128, 768], F32) for b in range(nb)]; r_rkv = [Res() for _ in range(nb)]
    NSCR = 11
    scr = [[sbt("scr%d_%d" % (b, i), [128, 256], F32) for i in range(NSCR)] for b in range(nb)]
    r_scr = [[Res() for _ in range(NSCR)] for b in range(nb)]
    scr0, r_scr0 = scr[0], r_scr[0]
    small = [[sbt("sm%d_%d" % (b, i), [128, 4], F32) for i in range(4)] for b in range(nb)]
    r_small = [[Res() for _ in range(4)] for b in range(nb)]
    opn = ["At", "Rt", "Bt", "Kt", "Bh", "Kh", "Vt"]
    opt = [{n: sbt("%s%d" % (n, b), [128, 256], BF16) for n in opn} for b in range(nb)]
    r_opt = [{n: Res() for n in opn} for b in range(nb)]
    keep = [{n: sbt("kp%s%d" % (n, b), [128, 256], F32) for n in ("kb", "g")} for b in range(nb)]
    r_keep = [Res() for _ in range(nb)]
    gend = [sbt("gend%d" % b, [64, 4], F32) for b in range(nb)]; r_gend = [Res() for _ in range(nb)]
    Yt = [sbt("Y%d" % b, [128, 256], F32) for b in range(nb)]; r_Y = [Res() for _ in range(nb)]
    yfin = [scr[b][4] for b in range(nb)]; r_yf = [r_scr[b][4] for b in range(nb)]
    ob = [sbt("ob%d" % b, [128, 256], BF16) for b in range(nb)]; r_ob = [Res() for _ in range(nb)]
    units = [(b, h) for b in range(nb) for h in range(4)]
    U = {}
    for (b, h) in units:
        u = {}
        n = "%d_%d" % (b, h)
        u["fm"] = sbt("fm" + n, [64, 4, 128], BF16); u["r_fm"] = Res()
        u["Mrk"] = sbt("Mrk" + n, [128, 256], BF16)
        u["MkaT"] = sbt("MkaT" + n, [128, 128], BF16)
        u["r_M"] = Res()
        _T = sbt("T0" + n, [128, 128], F32); _rT = Res()
        u["T"] = [_T, _T]; u["r_T"] = [_rT, _rT]
        u["Tbf"] = sbt("Tbf" + n, [128, 128], BF16); u["r_Tbf"] = Res()
        u["PP"] = [sbt("PP%d" % i + n, [128, 256], F32) for i in range(2)]; u["r_PP"] = [Res(), Res()]
        u["X"] = sbt("X" + n, [128, 128], BF16); u["Ah"] = sbt("Ah" + n, [64, 128], BF16); u["r_XA"] = Res()
        u["Ut"] = sbt("Ut" + n, [128, 64], BF16); u["r_Ut"] = Res()
        u["S32"] = sbt("S32" + n, [64, 64], F32); u["Sbf"] = sbt("Sbf" + n, [64, 64], BF16); u["r_S"] = Res(); u["r_Sbf"] = Res()
        U[(b, h)] = u

    W0 = lambda z: vecs[:, 0 + z, :]
    A0 = lambda z: vecs[:, 2 + z, :]
    KK_, KA_, RK_, LNW_, LNB_ = vecs[:, 4, :], vecs[:, 5, :], vecs[:, 6, :], vecs[:, 7, :], vecs[:, 8, :]
    hv = lambda t: t.rearrange("p (h j) -> p h j", h=4)
    rr = [0]
    r_rkvs = Res()

    def ew(reads, writes, fn_dve, allow=None):
        allow = allow or ew_engs
        eng = allow[rr[0] % len(allow)]
        rr[0] += 1
        P.emit(eng, fn_dve, reads=reads, writes=writes)

    def _pass(z):
        P.emit("sp", lambda e: e.dma_start(out=cmask[:, 0], in_=wd["c_mask"][:, z]), reads=[r_c], writes=[r_c], chan=pfx + "k0")
        P.emit("sp", lambda e: e.dma_start(out=ctri[:, 0], in_=wd["c_tri"][:, z]), reads=[r_c], writes=[r_c], chan=pfx + "k1")
        for (b, h) in units:
            u = U[(b, h)]
            P.emit("pool", lambda e, u=u: e.memset(u["S32"][:], 0.0), writes=[u["r_S"]])
            P.emit("pool", lambda e, u=u: e.memset(u["Sbf"][:], 0.0), writes=[u["r_Sbf"]])
        segs = [("ctx", 0, nctx), ("lat", nctx, nlat)]
        def _seg(sname, soff, slen):
            nsc = slen // SC
            sc_order = range(nsc) if z == 0 else range(nsc - 1, -1, -1)
            def _proj(sci):
                t0 = sci * SC
                for b in range(nb):
                    hTv = hT[b].rearrange("(c p) t -> p c t", p=128)
                    for half in range(2):
                        P.emit("sp", lambda e, b=b, half=half, hTv=hTv: e.dma_start(
                            out=hw[b][:, half * 8:(half + 1) * 8, 64:64 + SC],
                            in_=hTv[:, half * 8:(half + 1) * 8, soff + t0: soff + t0 + SC]),
                            reads=[r_hT], writes=[r_hw[b]], chan=pfx + "hw%d_%d" % (b, half))
                    shifts = (-1, 1, -64, 64) if sname == "lat" else (-1, 1, -1, 1)
                    for qd in range(4):
                        sh = shifts[qd]
                        lo = t0 + sh; hi = t0 + sh + SC
                        clo, chi = max(lo, 0), min(hi, slen)
                        if clo > lo or chi < hi:
                            P.emit("dve", lambda e, b=b, qd=qd: e.memset(hsb[b][:, qd * 4:(qd + 1) * 4, :], 0.0), writes=[r_hs[b]])
                        P.emit("sp", lambda e, b=b, qd=qd, lo=lo, clo=clo, chi=chi, hTv=hTv: e.dma_start(
                            out=hsb[b][:, qd * 4:(qd + 1) * 4, clo - lo:chi - lo],
                            in_=hTv[:, qd * 4:(qd + 1) * 4, soff + clo: soff + chi]),
                            reads=[r_hT], writes=[r_hs[b]], chan=pfx + "hs%d_%d" % (b, qd))
                    if sname == "lat":
                        P.emit("dve", lambda e, b=b: e.memset(hsb[b][:, 0:4, 0:SC:64], 0.0), writes=[r_hs[b]])
                        P.emit("dve", lambda e, b=b: e.memset(hsb[b][:, 4:8, 63:SC:64], 0.0), writes=[r_hs[b]])
                    groups = [(0 + 96 * z, 96, "w", 0)]
                    if z == 0:
                        groups += [(192, 96, "a", 0)]
                    else:
                        if rkvs is None:
                            groups += [(192, 96, "a", 0)]
                        groups += [(288, 96, "a", 1), (384, 128, "g", 0), (512, 128, "g", 1)]
                    for (c0, m, kind, gi) in groups:
                        pt, r_pt, _ = ps.next()
                        k = 0
                        for kc in range(KC):
                            for (wt, src) in ((LWa, hw[b][:, kc, 64:64 + SC]), (LWb, hsb[b][:, kc, :])):
                                P.emit("pe", lambda e, pt=pt, wt=wt, src=src, kc=kc, c0=c0, m=m, k=k: e.matmul(
                                    pt[0:m, 0:SC], lhsT=wt[:, kc, c0:c0 + m], rhs=src, start=(k == 0), stop=(k == 2 * KC - 1)),
                                    reads=[r_c, r_hw[b], r_hs[b]], writes=[r_pt])
                                k += 1
                        if kind == "w":
                            P.emit("act", lambda e, pt=pt, b=b: e.activation(out=xw[b][:, :], in_=pt[0:96, 0:SC], func=AF.Tanh),
                                   reads=[r_pt], writes=[r_x[b]])
                        elif kind == "a":
                            P.emit("act", lambda e, pt=pt, b=b, gi=gi: e.activation(out=xa[b][:, gi, :], in_=pt[0:96, 0:SC], func=AF.Copy),
                                   reads=[r_pt], writes=[r_x[b]])
                        else:
                            P.emit("act", lambda e, pt=pt, b=b, gi=gi: e.activation(out=xg[b][:, gi, :], in_=pt[0:128, 0:SC], func=AF.Sigmoid),
                                   reads=[r_pt], writes=[r_x[b]])
                    gidx = (soff + t0) // 128
                    if rkvs is not None and z == 1:
                        P.emit("sp", lambda e, b=b, gidx=gidx: e.dma_start(out=rkv[b][:, :], in_=rkvs[b, gidx, :, 0:768]),
                               reads=[r_rkvs], writes=[r_rkv[b]], chan=pfx + "rkl%d" % b)
                    else:
                        pa, r_pa, _ = ps.next()
                        pb_, r_pb, _ = ps.next()
                        k = 0
                        for kc in range(KC):
                            for (src, wt) in ((hw[b][:, kc, 64 + 0 * 128:64 + (0 + 1) * 128], RKa), (hsb[b][:, kc, 0 * 128:(0 + 1) * 128], RKb)):
                                P.emit("pe", lambda e, pa=pa, src=src, wt=wt, kc=kc, k=k: e.matmul(
                                    pa[:, 0:512], lhsT=src, rhs=wt[:, kc, 0:512], start=(k == 0), stop=(k == 2 * KC - 1)),
                                    reads=[r_c, r_hw[b], r_hs[b]], writes=[r_pa])
                                P.emit("pe", lambda e, pb_=pb_, src=src, wt=wt, kc=kc, k=k: e.matmul(
                                    pb_[:, 0:256], lhsT=src, rhs=wt[:, kc, 512:768], start=(k == 0), stop=(k == 2 * KC - 1)),
                                    reads=[r_c, r_hw[b], r_hs[b]], writes=[r_pb])
                                k += 1
                        P.emit("act", lambda e, b=b, pa=pa: e.activation(out=rkv[b][:, 0:512], in_=pa[:, 0:512], func=AF.Copy),
                               reads=[r_pa], writes=[r_rkv[b]])
                        P.emit("act", lambda e, b=b, pb_=pb_: e.activation(out=rkv[b][:, 512:768], in_=pb_[:, 0:256], func=AF.Copy),
                               reads=[r_pb], writes=[r_rkv[b]])
                        if rkvs is not None:
                            P.emit("sp", lambda e, b=b, gidx=gidx: e.dma_start(out=rkvs[b, gidx, :, 0:768], in_=rkv[b][:, :]),
                                   reads=[r_rkv[b]], writes=[r_rkvs], chan=pfx + "rks%d" % b)
            def _sc(sci, pre_done, nxt):
                t0 = sci * SC
                if not pre_done:
                    _proj(sci)
                ch_order = range(SC // 128) if z == 0 else range(SC // 128 - 1, -1, -1)
                def _ch(ci):
                    tok0 = t0 + ci * 128
                    recs = []
                    real_emit = P.emit
                    for b in range(nb):
                        rec = []
                        P.emit = (lambda rec: (lambda eng, fn, reads=(), writes=(), chan=None: rec.append((eng, _freeze(fn), reads, writes, chan))))(rec)
                        S_ = scr[b]; RS = r_scr[b]
                        r_t, k_t, v_t = rkv[b][:, 0:256], rkv[b][:, 256:512], rkv[b][:, 512:768]
                        cs = slice(ci * 128, (ci + 1) * 128)
                        pw, r_pw, _ = ps.next()
                        P.emit("pe", lambda e, pw=pw, b=b, cs=cs: e.matmul(pw[:, 0:256], lhsT=xw[b][:, cs], rhs=W2s[:, z, :], start=True, stop=True),
                               reads=[r_x[b], r_c], writes=[r_pw])
                        P.emit("dve", lambda e, pw=pw, b=b: e.tensor_tensor(out=S_[0][:], in0=pw[:, 0:256], in1=W0(z), op=ALU.add),
                               reads=[r_pw, r_c], writes=[RS[0]])
                        P.emit("act", lambda e, b=b: e.activation(out=S_[0][:], in_=S_[0][:], func=AF.Sigmoid), reads=[RS[0]], writes=[RS[0]])
                        gidx = (soff + tok0) // 128
                        zs = [z] if (z == 0 or rkvs is not None) else [1, 0]
                        if rkvs is not None and z == 1:
                            P.emit("sp", lambda e, b=b, gidx=gidx: e.dma_start(out=S_[2][:], in_=rkvs[b, gidx, :, 768:1024]),
                                   reads=[r_rkvs], writes=[RS[2]], chan=pfx + "agl")
                        for ai, za in enumerate(zs):
                            pw, r_pw, _ = ps.next()
                            P.emit("pe", lambda e, pw=pw, b=b, cs=cs, za=za: e.matmul(pw[:, 0:256], lhsT=xa[b][:, za, cs], rhs=A2s[:, za, :], start=True, stop=True),
                                   reads=[r_x[b], r_c], writes=[r_pw])
                            P.emit("dve", lambda e, pw=pw, b=b, ai=ai, za=za: e.tensor_tensor(out=S_[1 + ai][:], in0=pw[:, 0:256], in1=A0(za), op=ALU.add),
                                   reads=[r_pw, r_c], writes=[RS[1 + ai]])
                            P.emit("act", lambda e, b=b, ai=ai: e.activation(out=S_[1 + ai][:], in_=S_[1 + ai][:], func=AF.Sigmoid),
                                   reads=[RS[1 + ai]], writes=[RS[1 + ai]])
                        if rkvs is not None and z == 0:
                            P.emit("sp", lambda e, b=b, gidx=gidx: e.dma_start(out=rkvs[b, gidx, :, 768:1024], in_=S_[1][:]),
                                   reads=[RS[1]], writes=[r_rkvs], chan=pfx + "ags")
                        if z == 1:
                            pw, r_pw, _ = ps.next()
                            for kc2 in range(2):
                                P.emit("pe", lambda e, pw=pw, b=b, cs=cs, kc2=kc2: e.matmul(pw[:, 0:256], lhsT=xg[b][:, kc2, cs], rhs=G2s[:, kc2, :],
                                                                                           start=(kc2 == 0), stop=(kc2 == 1)), reads=[r_x[b], r_c], writes=[r_pw])
                            P.emit("act", lambda e, pw=pw, b=b: e.activation(out=keep[b]["g"][:], in_=pw[:, 0:256], func=AF.Copy),
                                   reads=[r_pw], writes=[r_keep[b]])
                        ew([r_rkv[b], r_c], [RS[3]], lambda e, b=b, k_t=k_t: e.tensor_tensor(out=S_[3][:], in0=k_t, in1=KK_, op=ALU.mult))
                        ew([RS[3]], [RS[4]], lambda e, b=b: e.tensor_tensor(out=S_[4][:], in0=S_[3][:], in1=S_[3][:], op=ALU.mult))
                        P.emit("dve", lambda e, b=b: e.tensor_reduce(out=small[b][0][:], in_=hv(S_[4][:]), axis=AX.X, op=ALU.add),
                               reads=[RS[4]], writes=[r_small[b][0]])
                        P.emit("dve", lambda e, b=b: e.tensor_scalar(out=small[b][0][:], in0=small[b][0][:], scalar1=1e-24, scalar2=None, op0=ALU.max),
                               reads=[r_small[b][0]], writes=[r_small[b][0]])
                        P.emit("act", lambda e, b=b: e.activation(out=small[b][0][:], in_=small[b][0][:], func=AF.Sqrt),
                               reads=[r_small[b][0]], writes=[r_small[b][0]])
                        P.emit("dve", lambda e, b=b: e.reciprocal(out=small[b][0][:], in_=small[b][0][:]),
                               reads=[r_small[b][0]], writes=[r_small[b][0]])
                        ew([RS[3], r_small[b][0]], [RS[3]], lambda e, b=b: e.tensor_tensor(
                            out=hv(S_[3][:]), in0=hv(S_[3][:]), in1=small[b][0][:].unsqueeze(2).to_broadcast([128, 4, 64]), op=ALU.mult))
                        pl, r_pl, _ = ps.next()
                        pe2, r_pe2, _ = ps.next()
                        P.emit("pe", lambda e, pl=pl, b=b: e.matmul(pl[:, 0:256], lhsT=ctri[:, 0, 0, :], rhs=S_[0][:], start=True, stop=True),
                               reads=[RS[0], r_c], writes=[r_pl])
                        P.emit("pe", lambda e, pl=pl, b=b: e.matmul(pl[:, 256:512], lhsT=ctri[:, 0, 1, :], rhs=S_[0][:], start=True, stop=True),
                               reads=[RS[0], r_c], writes=[r_pl])
                        P.emit("pe", lambda e, pe2=pe2, b=b: e.matmul(pe2[:, 0:256], lhsT=ctri[:, 0, 2, :], rhs=S_[0][:], start=True, stop=True),
                               reads=[RS[0], r_c], writes=[r_pe2])
                        for hh in range(4):
                            P.emit("pe", lambda e, pe2=pe2, b=b, hh=hh: e.matmul(pe2[0:64, 256 + hh:257 + hh], lhsT=S_[0][:, hh * 64:(hh + 1) * 64], rhs=negcol[:, 0:1],
                                                                                 start=True, stop=True), reads=[RS[0], r_c], writes=[r_pe2])
                        P.emit("act", lambda e, pl=pl, b=b: e.activation(out=S_[5][:], in_=pl[:, 0:256], func=AF.Exp), reads=[r_pl], writes=[RS[5]])
                        P.emit("act", lambda e, pl=pl, b=b: e.activation(out=S_[6][:], in_=pl[:, 0:256], func=AF.Exp, scale=-1.0), reads=[r_pl], writes=[RS[6]])
                        P.emit("act", lambda e, pl=pl, b=b: e.activation(out=S_[7][:], in_=pl[:, 256:512], func=AF.Exp), reads=[r_pl], writes=[RS[7]])
                        P.emit("act", lambda e, pe2=pe2, b=b: e.activation(out=S_[8][:], in_=pe2[:, 0:256], func=AF.Exp), reads=[r_pe2], writes=[RS[8]])
                        P.emit("act", lambda e, pe2=pe2, b=b: e.activation(out=gend[b][:], in_=pe2[0:64, 256:260], func=AF.Exp), reads=[r_pe2], writes=[r_gend[b]])
                        O_ = opt[b]; RO = r_opt[b]
                        ew([RS[3], RS[7]], [RO["At"]], lambda e, b=b: e.scalar_tensor_tensor(
                            out=O_["At"][:], in0=S_[3][:], scalar=-1.0, in1=S_[7][:], op0=ALU.mult, op1=ALU.mult), allow=("dve",))
                        ew([r_rkv[b], RS[5]], [RO["Rt"]], lambda e, b=b, r_t=r_t: e.tensor_tensor(out=O_["Rt"][:], in0=r_t, in1=S_[5][:], op=ALU.mult))
                        ew([r_rkv[b]], [RO["Vt"]], lambda e, b=b, v_t=v_t: e.tensor_copy(out=O_["Vt"][:], in_=v_t))
                        ew([RS[3], RS[1]], [RS[9]], lambda e, b=b: e.tensor_tensor(out=S_[9][:], in0=S_[3][:], in1=S_[1][:], op=ALU.mult))
                        ew([RS[9], RS[6]], [RO["Bt"]], lambda e, b=b: e.tensor_tensor(out=O_["Bt"][:], in0=S_[9][:], in1=S_[6][:], op=ALU.mult))
                        ew([RS[1], r_c], [RS[10]], lambda e, b=b: e.scalar_tensor_tensor(
                            out=S_[10][:], in0=S_[1][:], scalar=-1.0, in1=KA_, op0=ALU.add, op1=ALU.mult), allow=("dve",))
                        ew([RS[10], r_rkv[b]], [RS[10]], lambda e, b=b, k_t=k_t: e.scalar_tensor_tensor(
                            out=S_[10][:], in0=S_[10][:], scalar=1.0, in1=k_t, op0=ALU.add, op1=ALU.mult), allow=("dve",))
                        ew([RS[10], RS[6]], [RO["Kt"]], lambda e, b=b: e.tensor_tensor(out=O_["Kt"][:], in0=S_[10][:], in1=S_[6][:], op=ALU.mult))
                        ew([RO["Bt"], RS[8]], [RO["Bh"]], lambda e, b=b: e.tensor_tensor(out=O_["Bh"][:], in0=O_["Bt"][:], in1=S_[8][:], op=ALU.mult))
                        ew([RO["Kt"], RS[8]], [RO["Kh"]], lambda e, b=b: e.tensor_tensor(out=O_["Kh"][:], in0=O_["Kt"][:], in1=S_[8][:], op=ALU.mult))
                        if z == 1 and sname == "lat":
                            ew([RS[2], r_c], [RS[9]], lambda e, b=b: e.scalar_tensor_tensor(
                                out=S_[9][:], in0=S_[2][:], scalar=-1.0, in1=KA_, op0=ALU.add, op1=ALU.mult), allow=("dve",))
                            ew([RS[9], r_rkv[b]], [RS[9]], lambda e, b=b, k_t=k_t: e.scalar_tensor_tensor(
                                out=S_[9][:], in0=S_[9][:], scalar=1.0, in1=k_t, op0=ALU.add, op1=ALU.mult), allow=("dve",))
                            ew([RS[9], RS[10]], [RS[9]], lambda e, b=b: e.tensor_tensor(out=S_[9][:], in0=S_[9][:], in1=S_[10][:], op=ALU.add))
                            ew([RS[9], r_rkv[b]], [RS[9]], lambda e, b=b, r_t=r_t: e.tensor_tensor(out=S_[9][:], in0=S_[9][:], in1=r_t, op=ALU.mult))
                            ew([RS[9], r_c], [RS[9]], lambda e, b=b: e.scalar_tensor_tensor(
                                out=S_[9][:], in0=S_[9][:], scalar=0.5, in1=RK_, op0=ALU.mult, op1=ALU.mult), allow=("dve",))
                            P.emit("dve", lambda e, b=b: e.tensor_reduce(out=small[b][3][:], in_=hv(S_[9][:]), axis=AX.X, op=ALU.add),
                                   reads=[RS[9]], writes=[r_small[b][3]])
                        recs.append(rec)
                    P.emit = real_emit
                    for i_ in range(max(len(r_) for r_ in recs)):
                        for rec in recs:
                            if i_ < len(rec):
                                eng_, fn_, rd_, wr_, ch_ = rec[i_]
                                real_emit(eng_, fn_, reads=rd_, writes=wr_, chan=ch_)
                    if dbg is not None and (z, sname, sci, ci) == dbg_at:
                        b = 0
                        dbg("rkv", rkv[b][:], [r_rkv[b]])
                        for i in (0, 1, 3, 5, 6, 7, 8, 9, 10):
                            dbg("s%d" % i, scr[b][i][:], [r_scr[b][i]])
                        for nme in opn:
                            dbg(nme, opt[b][nme][:], [r_opt[b][nme]])
                        dbg("gend", gend[b][:], [r_gend[b]])
                        dbg("hw", hw[b][:], [r_hw[b]])
                        dbg("hsb", hsb[b][:], [r_hs[b]])
                        dbg("xw", xw[b][:], [r_x[b]])
                        dbg("xa", xa[b][:], [r_x[b]])
                    if nxt is not None:
                        _proj(nxt)
                    for (b, h) in units:
                        u = U[(b, h)]; O_ = opt[b]; RO = r_opt[b]
                        hs_ = slice(h * 64, (h + 1) * 64)
                        pt, r_pt, _ = ps.next()
                        ptb = pt[:].bitcast(BF16)
                        for i, nme in enumerate(("At", "Rt", "Bt", "Kt")):
                            P.emit("pe", lambda e, ptb=ptb, i=i, nme=nme, b=b, hs_=hs_: e.transpose(
                                ptb[0:64, i * 128:(i + 1) * 128], O_[nme][:, hs_], ident[:]), reads=[RO[nme], r_c], writes=[r_pt])
                        P.emit("act", lambda e, ptb=ptb, u=u: e.activation(out=u["fm"][:].rearrange("p a t -> p (a t)"), in_=ptb[0:64, 0:512], func=AF.Copy),
                               reads=[r_pt], writes=[u["r_fm"]])
                        fm = u["fm"]
                        p1, r_p1, _ = ps.next()
                        p3, r_p3, _ = ps.next()
                        P.emit("pe", lambda e, p1=p1, fm=fm: e.matmul(p1[:, 0:256], lhsT=fm[:, 2, :], rhs=fm[:, 0:2, :].rearrange("p a t -> p (a t)"), start=True, stop=True),
                               reads=[u["r_fm"]], writes=[r_p1])
                        P.emit("pe", lambda e, p1=p1, fm=fm: e.matmul(p1[:, 256:384], lhsT=fm[:, 3, :], rhs=fm[:, 1, :], start=True, stop=True),
                               reads=[u["r_fm"]], writes=[r_p1])
                        P.emit("pe", lambda e, p3=p3, fm=fm: e.matmul(p3[:, 0:256], lhsT=fm[:, 0, :], rhs=fm[:, 2:4, :].rearrange("p a t -> p (a t)"), start=True, stop=True),
                               reads=[u["r_fm"]], writes=[r_p3])
                        P.emit("dve", lambda e, p1=p1, u=u: e.tensor_tensor(out=u["Mrk"][:], in0=p1[:, 128:384], in1=cmask[:, 0, 2, :], op=ALU.mult),
                               reads=[r_p1, r_c], writes=[u["r_M"]])
                        P.emit("dve", lambda e, p3=p3, u=u: e.tensor_tensor(out=u["MkaT"][:], in0=p3[:, 128:256], in1=cmask[:, 0, 1, 128:256], op=ALU.mult),
                               reads=[r_p3, r_c], writes=[u["r_M"]])
                        P.emit("dve", lambda e, p1=p1, u=u: e.tensor_tensor(out=u["PP"][1][:, 0:128], in0=p1[:, 0:128], in1=cmask[:, 0, 0, 0:128], op=ALU.mult),
                               reads=[r_p1, r_c], writes=[u["r_PP"][1]])
                        P.emit("dve", lambda e, p3=p3, u=u: e.tensor_tensor(out=u["PP"][1][:, 128:256], in0=p3[:, 0:128], in1=cmask[:, 0, 1, 0:128], op=ALU.mult),
                               reads=[r_p3, r_c], writes=[u["r_PP"][1]])
                        P.emit("dve", lambda e, u=u: e.tensor_tensor(out=u["T"][0][:], in0=u["PP"][1][:, 0:128], in1=ident[:], op=ALU.add),
                               reads=[u["r_PP"][1], r_c], writes=[u["r_T"][0]])
                    for kk_ in range(1, 7):
                        for (b, h) in units:
                            u = U[(b, h)]
                            Pm, PTm, rd = u["PP"][kk_ % 2][:, 0:128], u["PP"][kk_ % 2][:, 128:256], u["r_PP"][kk_ % 2]
                            dst, r_dst = u["PP"][(kk_ + 1) % 2], u["r_PP"][(kk_ + 1) % 2]
                            pp, r_pp, _ = ps.next()
                            P.emit("pe", lambda e, pp=pp, Pm=Pm, PTm=PTm: e.matmul(pp[:, 128:256], lhsT=Pm, rhs=PTm, start=True, stop=True),
                                   reads=[rd], writes=[r_pp])
                            if kk_ < 6:
                                P.emit("pe", lambda e, pp=pp, Pm=Pm, PTm=PTm: e.matmul(pp[:, 0:128], lhsT=PTm, rhs=Pm, start=True, stop=True),
                                       reads=[rd], writes=[r_pp])
                                P.emit("act", lambda e, pp=pp, dst=dst: e.activation(out=dst[:], in_=pp[:, 0:256], func=AF.Copy), reads=[r_pp], writes=[r_dst])
                            else:
                                P.emit("act", lambda e, pp=pp, dst=dst: e.activation(out=dst[:, 128:256], in_=pp[:, 128:256], func=AF.Copy), reads=[r_pp], writes=[r_dst])
                        for (b, h) in units:
                            u = U[(b, h)]
                            PTk, r_ptk = u["PP"][(kk_ + 1) % 2][:, 128:256], u["r_PP"][(kk_ + 1) % 2]
                            Told, r_told = u["T"][(kk_ - 1) % 2], u["r_T"][(kk_ - 1) % 2]
                            pt, r_pt, _ = ps.next()
                            P.emit("pe", lambda e, pt=pt, PTk=PTk, Told=Told: e.matmul(pt[:, 0:128], lhsT=PTk, rhs=Told[:], start=True, stop=True),
                                   reads=[r_ptk, r_told], writes=[r_pt])
                            if kk_ < 6:
                                Tnew, r_tnew = u["T"][kk_ % 2], u["r_T"][kk_ % 2]
                            else:
                                Tnew, r_tnew = u["Tbf"], u["r_Tbf"]
                            P.emit("dve", lambda e, pt=pt, Tnew=Tnew, Told=Told: e.tensor_tensor(out=Tnew[:], in0=pt[:, 0:128], in1=Told[:], op=ALU.add),
                                   reads=[r_pt, r_told], writes=[r_tnew])
                    for (b, h) in units:
                        u = U[(b, h)]; O_ = opt[b]; RO = r_opt[b]
                        hs_ = slice(h * 64, (h + 1) * 64)
                        Tf, r_tf = u["Tbf"], u["r_Tbf"]
                        px, r_px, _ = ps.next()
                        P.emit("pe", lambda e, px=px, u=u, Tf=Tf: e.matmul(px[:, 0:128], lhsT=u["MkaT"][:], rhs=Tf[:], start=True, stop=True),
                               reads=[u["r_M"], r_tf], writes=[r_px])
                        P.emit("pe", lambda e, px=px, b=b, hs_=hs_, Tf=Tf: e.matmul(px[0:64, 128:256], lhsT=O_["At"][:, hs_], rhs=Tf[:], start=True, stop=True),
                               reads=[RO["At"], r_tf], writes=[r_px])
                        P.emit("act", lambda e, px=px, u=u: e.activation(out=u["X"][:], in_=px[:, 0:128], func=AF.Copy), reads=[r_px], writes=[u["r_XA"]])
                        P.emit("dve", lambda e, px=px, u=u: e.tensor_copy(out=u["Ah"][:], in_=px[0:64, 128:256]), reads=[r_px], writes=[u["r_XA"]])
                    for (b, h) in units:
                        u = U[(b, h)]; O_ = opt[b]; RO = r_opt[b]
                        hs_ = slice(h * 64, (h + 1) * 64)
                        pu, r_pu, _ = ps.next()
                        P.emit("pe", lambda e, pu=pu, u=u: e.matmul(pu[:, 0:64], lhsT=u["Ah"][:], rhs=u["Sbf"][:], start=True, stop=False),
                               reads=[u["r_XA"], u["r_Sbf"]], writes=[r_pu])
                        P.emit("pe", lambda e, pu=pu, u=u, b=b, hs_=hs_: e.matmul(pu[:, 0:64], lhsT=u["X"][:], rhs=O_["Vt"][:, hs_], start=False, stop=True),
                               reads=[u["r_XA"], RO["Vt"]], writes=[r_pu])
                        P.emit("act", lambda e, pu=pu, u=u: e.activation(out=u["Ut"][:], in_=pu[:, 0:64], func=AF.Copy), reads=[r_pu], writes=[u["r_Ut"]])
                    for (b, h) in units:
                        u = U[(b, h)]; O_ = opt[b]; RO = r_opt[b]
                        hs_ = slice(h * 64, (h + 1) * 64)
                        py, r_py, _ = ps.next()
                        P.emit("pe", lambda e, py=py, u=u: e.matmul(py[:, 0:64], lhsT=u["fm"][:, 1, :], rhs=u["Sbf"][:], start=True, stop=False),
                               reads=[u["r_fm"], u["r_Sbf"]], writes=[r_py])
                        P.emit("pe", lambda e, py=py, u=u: e.matmul(py[:, 0:64], lhsT=u["Mrk"][:, 0:128], rhs=u["Ut"][:], start=False, stop=False),
                               reads=[u["r_M"], u["r_Ut"]], writes=[r_py])
                        P.emit("pe", lambda e, py=py, u=u, b=b, hs_=hs_: e.matmul(py[:, 0:64], lhsT=u["Mrk"][:, 128:256], rhs=O_["Vt"][:, hs_], start=False, stop=True),
                               reads=[u["r_M"], RO["Vt"]], writes=[r_py])
                        P.emit("act", lambda e, py=py, b=b, hs_=hs_: e.activation(out=Yt[b][:, hs_], in_=py[:, 0:64], func=AF.Copy), reads=[r_py], writes=[r_Y[b]])
                        pss, r_pss, _ = ps.next()
                        P.emit("pe", lambda e, pss=pss, u=u, b=b, hs_=hs_: e.matmul(pss[0:64, 0:64], lhsT=O_["Bh"][:, hs_], rhs=u["Ut"][:], start=True, stop=False),
                               reads=[RO["Bh"], u["r_Ut"]], writes=[r_pss])
                        P.emit("pe", lambda e, pss=pss, u=u, b=b, hs_=hs_: e.matmul(pss[0:64, 0:64], lhsT=O_["Kh"][:, hs_], rhs=O_["Vt"][:, hs_], start=False, stop=True),
                               reads=[RO["Kh"], RO["Vt"]], writes=[r_pss])
                        P.emit("dve", lambda e, pss=pss, u=u, b=b, h=h: e.scalar_tensor_tensor(
                            out=u["S32"][:], in0=u["S32"][:], scalar=gend[b][:, h:h + 1], in1=pss[0:64, 0:64], op0=ALU.mult, op1=ALU.add),
                            reads=[r_pss, r_gend[b], u["r_S"]], writes=[u["r_S"]])
                        P.emit("pool", lambda e, u=u: e.tensor_copy(out=u["Sbf"][:], in_=u["S32"][:]), reads=[u["r_S"]], writes=[u["r_Sbf"]])
                    if dbg is not None and (z, sname, sci, ci) == dbg_at:
                        u = U[(0, 0)]
                        dbg("fm", u["fm"][:], [u["r_fm"]])
                        dbg("Mrk", u["Mrk"][:], [u["r_M"]])
                        dbg("T", u["Tbf"][:], [u["r_Tbf"]])
                        dbg("X", u["X"][:], [u["r_XA"]]); dbg("Ah", u["Ah"][:], [u["r_XA"]])
                        dbg("Ut", u["Ut"][:], [u["r_Ut"]])
                        dbg("Y", Yt[0][:], [r_Y[0]])
                        dbg("S32", u["S32"][:], [u["r_S"]])
                    if sname != "lat":
                        return
                    for b in range(nb):
                        if z == 0:
                            P.emit("sp", lambda e, b=b, tok0=tok0: e.dma_start(out=yscr[b, tok0:tok0 + 128, :], in_=Yt[b][:]),
                                   reads=[r_Y[b]], writes=[r_out], chan=pfx + "ys%d" % b)
                            continue
                        S_ = scr[b]; RS = r_scr[b]; K_ = keep[b]
                        P.emit("sp", lambda e, b=b, tok0=tok0: e.dma_start(out=yfin[b][:], in_=yscr[b, tok0:tok0 + 128, :]),
                               reads=[r_out], writes=[r_yf[b]], chan=pfx + "yl%d" % b)
                        ew([r_Y[b], r_yf[b]], [RS[0]], lambda e, b=b: e.tensor_tensor(out=S_[0][:], in0=Yt[b][:], in1=yfin[b][:], op=ALU.add))
                        P.emit("dve", lambda e, b=b: e.tensor_reduce(out=small[b][1][:], in_=hv(S_[0][:]), axis=AX.X, op=ALU.add),
                               reads=[RS[0]], writes=[r_small[b][1]])
                        P.emit("dve", lambda e, b=b: e.tensor_scalar(out=small[b][1][:], in0=small[b][1][:], scalar1=1.0 / 64, scalar2=None, op0=ALU.mult),
                               reads=[r_small[b][1]], writes=[r_small[b][1]])
                        ew([RS[0], r_small[b][1]], [RS[1]], lambda e, b=b: e.tensor_tensor(
                            out=hv(S_[1][:]), in0=hv(S_[0][:]), in1=small[b][1][:].unsqueeze(2).to_broadcast([128, 4, 64]), op=ALU.subtract))
                        ew([RS[1]], [RS[2]], lambda e, b=b: e.tensor_tensor(out=S_[2][:], in0=S_[1][:], in1=S_[1][:], op=ALU.mult))
                        P.emit("dve", lambda e, b=b: e.tensor_reduce(out=small[b][2][:], in_=hv(S_[2][:]), axis=AX.X, op=ALU.add),
                               reads=[RS[2]], writes=[r_small[b][2]])
                        P.emit("act", lambda e, b=b: e.activation(out=small[b][2][:], in_=small[b][2][:], func=AF.Sqrt, scale=1.0 / 64, bias=gneps[:, 0:1]),
                               reads=[r_small[b][2], r_c], writes=[r_small[b][2]])
                        P.emit("dve", lambda e, b=b: e.reciprocal(out=small[b][2][:], in_=small[b][2][:]), reads=[r_small[b][2]], writes=[r_small[b][2]])
                        ew([RS[1], r_small[b][2]], [RS[1]], lambda e, b=b: e.tensor_tensor(
                            out=hv(S_[1][:]), in0=hv(S_[1][:]), in1=small[b][2][:].unsqueeze(2).to_broadcast([128, 4, 64]), op=ALU.mult))
                        ew([RS[1], r_c], [RS[1]], lambda e, b=b: e.tensor_tensor(out=S_[1][:], in0=S_[1][:], in1=LNW_, op=ALU.mult))
                        ew([RS[1], r_c], [RS[1]], lambda e, b=b: e.tensor_tensor(out=S_[1][:], in0=S_[1][:], in1=LNB_, op=ALU.add))
                        ew([r_opt[b]["Vt"], r_small[b][3]], [RS[3]], lambda e, b=b: e.tensor_tensor(
                            out=hv(S_[3][:]), in0=hv(opt[b]["Vt"][:]), in1=small[b][3][:].unsqueeze(2).to_broadcast([128, 4, 64]), op=ALU.mult))
                        ew([RS[1], RS[3]], [RS[1]], lambda e, b=b: e.tensor_tensor(out=S_[1][:], in0=S_[1][:], in1=S_[3][:], op=ALU.add))
                        ew([RS[1], r_keep[b]], [r_ob[b]], lambda e, b=b: e.tensor_tensor(out=ob[b][:], in0=S_[1][:], in1=K_["g"][:], op=ALU.mult))
                        P.emit("sp", lambda e, b=b, tok0=tok0: e.dma_start(out=o_out[b, tok0:tok0 + 128, :], in_=ob[b][:]),
                               reads=[r_ob[b]], writes=[r_out], chan=pfx + "oo%d" % b)
                for ci in ch_order:
                    _ch(ci)
            order = list(sc_order)
            for i, sci in enumerate(order):
                if prefetch:
                    _sc(sci, i > 0, order[i + 1] if i + 1 < len(order) else None)
                else:
                    _sc(sci, False, None)
        for seg in segs:
            _seg(*seg)
    for z in range(2):
        _pass(z)


def build_l45(nlat=SEQ, nctx=CTX, nb=2, debug=False, same_sync=True, ew_engs=("dve",), reuse=True):
    nc = bass.Bass("TRN2", target_bir_lowering=False)
    dt = lambda name, shape, dty, kind: nc.dram_tensor(name, shape, dty, kind=kind).ap()
    hT = dt("hT", [nb, D, nctx + nlat], BF16, "ExternalInput")
    wd = {"c_mask": dt("c_mask", [128, 2, 3, 256], BF16, "ExternalInput"), "c_tri": dt("c_tri", [128, 2, 3, 128], F32, "ExternalInput"),
          "c_ident": dt("c_ident", [128, 128], BF16, "ExternalInput"), "vecs": dt("vecs", [128, 9, 256], F32, "ExternalInput"),
          "mu": dt("mu", [128, KC, 6], F32, "ExternalInput"), "w2": dt("w2", [96, 2, 256], F32, "ExternalInput"),
          "a2": dt("a2", [96, 2, 256], F32, "ExternalInput"), "g2": dt("g2", [128, 2, 256], F32, "ExternalInput"),
          "rkv": dt("rkv", [128, KC, 768], F32, "ExternalInput"), "lw": dt("lw", [128, KC, 640], F32, "ExternalInput")}
    rkvs = dt("rkvs", [nb, (nctx + nlat) // 128, 128, 1024], F32, "Internal") if reuse else None
    yscr = dt("yscr", [nb, nlat, 256], F32, "ExternalOutput")
    o_out = dt("o_out", [nb, nlat, 256], BF16, "ExternalOutput")
    with contextlib.ExitStack() as st:
        P = Prog(nc, same_engine_sync=same_sync)
        dbg = None
        if debug:
            def dbg(name, ap, reads):
                t = nc.dram_tensor("dbg_" + name, list(ap.shape), ap.dtype, kind="ExternalOutput").ap()
                P.emit("sp", lambda e: e.dma_start(out=t, in_=ap), reads=reads, chan="dbg_" + name)
        emit_rwkv(nc, P, st, hT, Res(), wd, yscr, o_out, Res(), rkvs=rkvs, nlat=nlat, nctx=nctx, nb=nb, dbg=dbg, ew_engs=ew_engs)
        P.run(final_waits=_all_dma_tails(P))
    return nc


def rwkv_host_weights(inp, g):
    cs = slice(256 * g, 256 * (g + 1))
    rkv = np.concatenate([inp["rwkv_w_rkv"][0, i][:, cs] for i in range(3)], axis=1)
    lw = np.concatenate([inp["rwkv_w1"][0, 0], inp["rwkv_w1"][0, 1], inp["rwkv_a1"][0, 0], inp["rwkv_a1"][0, 1], inp["rwkv_g1"][0]], axis=1)
    vec = np.stack([inp["rwkv_w0"][0, 0][cs], inp["rwkv_w0"][0, 1][cs], inp["rwkv_a0"][0, 0][cs], inp["rwkv_a0"][0, 1][cs],
                    inp["rwkv_k_k"][0][cs], inp["rwkv_k_a"][0][cs], inp["rwkv_r_k"][0].reshape(-1)[cs], inp["rwkv_ln_w"][0][cs], inp["rwkv_ln_b"][0][cs]])
    return {"rkv": lay_sq(rkv), "lw": lay_sq(lw),
            "vecs": np.ascontiguousarray(np.broadcast_to(vec[None], (128, 9, 256))).astype(np.float32),
            "mu": np.ascontiguousarray(lay_vec(inp["rwkv_mu"][0]).transpose(0, 2, 1)),
            "w2": np.ascontiguousarray(inp["rwkv_w2"][0][:, :, cs].transpose(1, 0, 2)),
            "a2": np.ascontiguousarray(inp["rwkv_a2"][0][:, :, cs].transpose(1, 0, 2)),
            "g2": np.ascontiguousarray(inp["rwkv_g2"][0][:, cs].reshape(2, 128, 256).transpose(1, 0, 2))}


def lay_wo(w):
    return np.ascontiguousarray(w.reshape(KC, 128, 8, 256).transpose(2, 1, 0, 3))


def build_l3(nlat, nctx):
    NT = nlat + nctx
    nc = bass.Bass("TRN2", target_bir_lowering=False)
    dt = lambda name, shape, dty, kind="ExternalInput": nc.dram_tensor(name, shape, dty, kind=kind).ap()
    X1 = dt("X1", [D, NT], F32)
    fT = dt("fT", [D, NT], BF16)
    modo = dt("modo", [128, 2 * 144 * 3], F32); gso = dt("gso", [128, 2 * 3 * KC * 3], F32); hgo = dt("hgo", [128, 2 * 3 * KC * 3], F32)
    wo = dt("wo", [8, 128, KC, 256], F32)
    bo = dt("bo", [128, KC], F32)
    w13a = dt("w13a", [JC, 128, KC, 256], F32); w2a = dt("w2a", [KC, 128, JC, 128], F32)
    w13b = dt("w13b", [JC, 128, KC, 256], F32); w2b = dt("w2b", [KC, 128, JC, 128], F32)
    X2 = dt("X2", [D, NT], F32, "Internal"); X3 = dt("X3", [D, NT], F32, "Internal")
    X4 = dt("X4", [D, NT], F32, "ExternalOutput")
    h1 = dt("h1", [D, NT], BF16, "ExternalOutput")
    with contextlib.ExitStack() as st:
        P = Prog(nc)
        dn = Dense(nc, P, st, 768, 0)
        dn.mod_load(modo, gso, hgo)
        bos = st.enter_context(nc.sbuf_tensor("sb_bos", [128, KC], F32))
        P.emit("sp", lambda e: e.dma_start(out=bos[:], in_=bo), writes=[dn.r_mod], chan="bo")
        r_in = Res()
        for blocks in make_passes(nlat, nctx, 0):
            r2, r3, r4, rh = Res(), Res(), Res(), Res()
            dn.linear_res(blocks, fT, r_in, wo, bos, X1, r_in, X2, r2, 0)
            dn.norm_mod(blocks, X2, r2, 0, 2)
            dn.ffn(blocks, w13a, w2a, X2, r2, X3, r3, 0, 2)
            dn.norm_mod(blocks, X3, r3, 1, 0)
            dn.ffn(blocks, w13b, w2b, X3, r3, X4, r4, 1, 0)
            dn.norm_mod(blocks, X4, r4, 1, 1, out_dram=h1, r_out=rh)
        P.run(final_waits=_all_dma_tails(P))
    return nc


def build_l6(nlat):
    NT = nlat
    nc = bass.Bass("TRN2", target_bir_lowering=False)
    dt = lambda name, shape, dty, kind="ExternalInput": nc.dram_tensor(name, shape, dty, kind=kind).ap()
    X4 = dt("X4", [D, NT], F32)
    oT = dt("oT", [D, NT], BF16)
    modo = dt("modo", [128, 2 * 144 * 3], F32); gso = dt("gso", [128, 2 * 3 * KC * 3], F32); hgo = dt("hgo", [128, 2 * 3 * KC * 3], F32)
    wo = dt("wo", [8, 128, KC, 256], F32)
    fng = dt("fng", [128, KC], F32)
    w13 = dt("w13", [JC, 128, KC, 256], F32); w2 = dt("w2", [KC, 128, JC, 128], F32)
    X5 = dt("X5", [D, NT], F32, "Internal"); X6 = dt("X6", [D, NT], F32, "Internal")
    out = dt("out", [D, NT], F32, "ExternalOutput")
    with contextlib.ExitStack() as st:
        P = Prog(nc)
        dn = Dense(nc, P, st, 768, 0)
        dn.mod_load(modo, gso, hgo)
        fgs = st.enter_context(nc.sbuf_tensor("sb_fgs", [128, KC], F32))
        P.emit("sp", lambda e: e.dma_start(out=fgs[:], in_=fng), writes=[dn.r_mod], chan="fg")
        r_in = Res()
        for blocks in make_passes(nlat, 0, 0):
            r5, r6, ro = Res(), Res(), Res()
            dn.linear_res(blocks, oT, r_in, wo, None, X4, r_in, X5, r5, 1)
            dn.norm_mod(blocks, X5, r5, 1, 2)
            dn.ffn(blocks, w13, w2, X5, r5, X6, r6, 1, 2)
            dn.norm_mod(blocks, X6, r6, 0, 0, out_dram=out, r_out=ro, final_g=fgs)
        P.run(final_waits=_all_dma_tails(P))
    return nc


_DBG = {}


def _run(nc, maps):
    res = run_bass_kernel_spmd(nc, maps, core_ids=list(range(NCORES)))
    return res.results


def kernel(x, c, ctx, c_ctx, mod_w, mod_b, norm_w, ffn_w13, ffn_w2, fnet_w_o, fnet_b_o,
           rwkv_mu, rwkv_w_rkv, rwkv_w0, rwkv_w1, rwkv_w2, rwkv_a0, rwkv_a1, rwkv_a2,
           rwkv_g1, rwkv_g2, rwkv_k_k, rwkv_k_a, rwkv_r_k, rwkv_ln_w, rwkv_ln_b, rwkv_w_o,
           final_norm_w):
    f32 = np.float32
    A = lambda a: np.asarray(a, dtype=f32)
    x, c, ctx, c_ctx = A(x), A(c), A(ctx), A(c_ctx)
    B, L, _ = x.shape
    NL = L // 4
    NCX = CTX // 4
    NT = NL + NCX
    sT = np.ascontiguousarray(lay_vec(np.stack([c[0], c[1], c_ctx])).transpose(0, 2, 1))
    mod_w = A(mod_w); mod_b = A(mod_b)
    maps = []
    for core in range(NCORES):
        cs = slice(core * 2304, (core + 1) * 2304)
        maps.append({"sT": sT,
                     "modw": np.ascontiguousarray(mod_w[:, :, cs].reshape(2, KC, 128, 2304).transpose(0, 2, 1, 3)),
                     "modb": np.ascontiguousarray(mod_b[:, cs].reshape(2, 18, 128).transpose(2, 0, 1))})
    r0 = _run(build_l0(), maps)
    modfull = np.concatenate([r0[i]["modo"].reshape(128, 2, 18, 3) for i in range(NCORES)], axis=2)
    modsw = modfull.copy(); modsw[..., 0] = modfull[..., 1]; modsw[..., 1] = modfull[..., 0]
    modin = [np.ascontiguousarray((modfull if core // 4 == 0 else modsw).reshape(128, -1)) for core in range(NCORES)]
    normw = lay_vec(A(norm_w))
    ffn_w13 = A(ffn_w13); ffn_w2 = A(ffn_w2)
    maps = []
    w13_00, w2_00 = lay_w13(ffn_w13[0, 0]), lay_w2(ffn_w2[0, 0])
    for core in range(NCORES):
        b, q = core // 4, core % 4
        xt = np.concatenate([x[b, q * NL:(q + 1) * NL], ctx[b, q * NCX:(q + 1) * NCX]], 0).T
        maps.append({"xT": np.ascontiguousarray(xt), "modi": modin[core], "normw": normw, "w13": w13_00, "w2": w2_00})
    r1 = _run(build_l1(NL, NCX), maps)
    del maps
    tabs = fft_tables()
    maps = []
    for g in range(NCORES):
        rows = slice(256 * g, 256 * (g + 1))
        hT = np.stack([np.concatenate([r1[b * 4 + q]["h0"][rows, 0:NL] for q in range(4)], axis=1) for b in range(B)])
        hcT = np.stack([np.concatenate([r1[b * 4 + q]["h0"][rows, NL:NT] for q in range(4)], axis=1) for b in range(B)])
        maps.append({"hT": np.ascontiguousarray(hT.reshape(B, 2, 128, L).transpose(0, 2, 1, 3)),
                     "hcT": np.ascontiguousarray(hcT.reshape(B, 2, 128, CTX).transpose(0, 2, 1, 3)), **tabs})
    r2 = _run(build_l2(B), maps)
    maps = []
    wo_f = lay_wo(A(fnet_w_o)[0]); bo_f = lay_vec(A(fnet_b_o)[0])
    w13a, w2a = lay_w13(ffn_w13[0, 1]), lay_w2(ffn_w2[0, 1])
    w13b, w2b = lay_w13(ffn_w13[1, 0]), lay_w2(ffn_w2[1, 0])
    for core in range(NCORES):
        b, q = core // 4, core % 4
        fT = np.concatenate([np.concatenate([r2[g]["fo"][b][:, q * NL:(q + 1) * NL] for g in range(NCORES)], axis=0),
                             np.concatenate([r2[g]["fco"][b][:, q * NCX:(q + 1) * NCX] for g in range(NCORES)], axis=0)], axis=1)
        maps.append({"X1": r1[core]["X1"], "fT": np.ascontiguousarray(fT), "modo": modin[core], "gso": r1[core]["gso"], "hgo": r1[core]["hgo"],
                     "wo": wo_f, "bo": bo_f, "w13a": w13a, "w2a": w2a, "w13b": w13b, "w2b": w2b})
    r3 = _run(build_l3(NL, NCX), maps)
    del r2, maps
    inp = {"rwkv_mu": A(rwkv_mu), "rwkv_w_rkv": A(rwkv_w_rkv), "rwkv_w0": A(rwkv_w0), "rwkv_w1": A(rwkv_w1), "rwkv_w2": A(rwkv_w2),
           "rwkv_a0": A(rwkv_a0), "rwkv_a1": A(rwkv_a1), "rwkv_a2": A(rwkv_a2), "rwkv_g1": A(rwkv_g1), "rwkv_g2": A(rwkv_g2),
           "rwkv_k_k": A(rwkv_k_k), "rwkv_k_a": A(rwkv_k_a), "rwkv_r_k": A(rwkv_r_k), "rwkv_ln_w": A(rwkv_ln_w), "rwkv_ln_b": A(rwkv_ln_b)}
    hT = np.stack([np.concatenate([r3[b * 4 + q]["h1"][:, NL:NT] for q in range(4)] + [r3[b * 4 + q]["h1"][:, 0:NL] for q in range(4)], axis=1)
                   for b in range(B)])
    hT = np.ascontiguousarray(hT)
    cst = rwkv_consts()
    maps = [{"hT": hT, **cst, **rwkv_host_weights(inp, g)} for g in range(NCORES)]
    r45 = _run(build_l45(L, CTX, B), maps)
    del hT, maps
    maps = []
    wo_r = lay_wo(A(rwkv_w_o)[0]); fng = lay_vec(A(final_norm_w))
    w13c, w2c = lay_w13(ffn_w13[1, 1]), lay_w2(ffn_w2[1, 1])
    for core in range(NCORES):
        b, q = core // 4, core % 4
        oT = np.concatenate([r45[g]["o_out"][b][q * NL:(q + 1) * NL, :] for g in range(NCORES)], axis=1).T
        maps.append({"X4": np.ascontiguousarray(r3[core]["X4"][:, 0:NL]), "oT": np.ascontiguousarray(oT),
                     "modo": modin[core], "gso": r1[core]["gso"], "hgo": r1[core]["hgo"],
                     "wo": wo_r, "fng": fng, "w13": w13c, "w2": w2c})
    r6 = _run(build_l6(NL), maps)
    _DBG.update(r1=r1, r3=r3, r45=r45, modfull=modfull)
    out = np.empty((B, L, D), f32)
    for core in range(NCORES):
        b, q = core // 4, core % 4
        out[b, q * NL:(q + 1) * NL] = r6[core]["out"].T
    return out
```

```python
import contextlib
import types
import numpy as np
import ml_dtypes
import concourse.bass as bass
import concourse.mybir as mybir
from concourse.bass_utils import run_bass_kernel_spmd

F32 = mybir.dt.float32
BF16 = mybir.dt.bfloat16
ALU = mybir.AluOpType
AF = mybir.ActivationFunctionType
AX = mybir.AxisListType

D = 2048
KC = 16
DFF = 5632
JC = 44
NMOD = 9
SEQ = 16384
CTX = 256
EPS = 1e-6
NCORES = 8


class Res:
    __slots__ = ("lastw", "readers")

    def __init__(self):
        self.lastw = None
        self.readers = []


class Op:
    __slots__ = ("eng", "fn", "deps", "sig", "cnt", "chan")

    def __init__(self, eng, fn, chan=None):
        self.eng = eng
        self.fn = fn
        self.deps = []
        self.sig = False
        self.cnt = 0
        self.chan = chan


ENGS = ("pe", "act", "dve", "pool", "sp")


def _freeze(fn):
    cl = fn.__closure__
    if cl is None:
        return fn
    cells = []
    for c in cl:
        try:
            cells.append(types.CellType(c.cell_contents))
        except ValueError:
            cells.append(c)
    return types.FunctionType(fn.__code__, fn.__globals__, fn.__name__, fn.__defaults__, tuple(cells))


class Prog:
    def __init__(self, nc, same_engine_sync=True):
        self.nc = nc
        self.ops = {e: [] for e in ENGS}
        self.chan_last = {}
        self.same = same_engine_sync

    def emit(self, eng, fn, reads=(), writes=(), chan=None):
        op = Op(eng, _freeze(fn), chan)
        deps = []
        for r in reads:
            if r.lastw is not None:
                deps.append(r.lastw)
        for w in writes:
            if w.lastw is not None:
                deps.append(w.lastw)
            deps.extend(w.readers)
        if chan is not None:
            prev = self.chan_last.get(chan)
            if prev is not None:
                deps.append(prev)
                op.cnt = prev.cnt + 16
            else:
                op.cnt = 16
            self.chan_last[chan] = op
        seen = set()
        for d in deps:
            if id(d) in seen or d is op:
                continue
            seen.add(id(d))
            if d.chan is None and d.eng == eng and (eng == "pe" or not self.same):
                continue
            op.deps.append(d)
            d.sig = True
        for r in reads:
            r.readers.append(op)
        for w in writes:
            w.lastw = op
            w.readers = []
        self.ops[eng].append(op)
        return op

    def run(self, final_waits=()):
        nc = self.nc
        chans = dict(self.chan_last)
        for e in ENGS:
            c = 0
            for op in self.ops[e]:
                if op.chan is None and op.sig:
                    c += 1
                    op.cnt = c
        with contextlib.ExitStack() as st:
            esem = {e: st.enter_context(nc.semaphore("s_" + e)) for e in ENGS}
            csem = {c: st.enter_context(nc.semaphore("c_%s" % (str(c),))) for c in chans}
            block = st.enter_context(nc.Block())

            def mk(ename):
                oplist = self.ops[ename]
                fw = list(final_waits) if ename == "sp" else []

                def body(eng):
                    known = {}
                    for op in oplist:
                        need = {}
                        for d in op.deps:
                            if d.chan is not None:
                                key = ("c", d.chan)
                                sem = csem[d.chan]
                            else:
                                key = ("e", d.eng)
                                sem = esem[d.eng]
                            if known.get(key, 0) >= d.cnt:
                                continue
                            if key not in need or need[key][1] < d.cnt:
                                need[key] = (sem, d.cnt)
                        for key, (sem, cnt) in need.items():
                            known[key] = cnt
                            eng.wait_ge(sem, cnt)
                        ins = op.fn(eng)
                        if op.chan is not None:
                            ins.then_inc(csem[op.chan], 16)
                        elif op.sig:
                            ins.then_inc(esem[op.eng], 1)
                    for d in fw:
                        eng.wait_ge(csem[d.chan], d.cnt)
                return body

            block.tensor(mk("pe"))
            block.scalar(mk("act"))
            block.vector(mk("dve"))
            block.gpsimd(mk("pool"))
            block.sync(mk("sp"))


class Rot:
    def __init__(self, bufs, name):
        self.bufs = bufs
        self.res = [Res() for _ in bufs]
        self.name = name
        self.i = 0

    def next(self):
        k = self.i % len(self.bufs)
        self.i += 1
        return self.bufs[k], self.res[k], "%s%d" % (self.name, k)


class Dense:
    def __init__(self, nc, P, st, tmax, batch_row):
        self.nc, self.P, self.st = nc, P, st
        self.tmax = tmax
        self.brow = batch_row
        sb = lambda name, shape, dt: st.enter_context(nc.sbuf_tensor("sb_" + name, shape, dt))
        self.h = sb("h", [128, KC, tmax], BF16)
        self.r_h = Res()
        self.hid = sb("hid", [128, JC, tmax], BF16)
        self.r_hid = Res()
        self.xs = Rot([sb("xs%d" % i, [128, tmax], F32) for i in range(3)], "xs")
        self.sq = Rot([sb("sq%d" % i, [128, tmax], BF16) for i in range(2)], "sq")
        self.rstd = sb("rstd", [128, tmax], F32)
        self.r_rstd = Res()
        self.tmp = Rot([sb("tmp%d" % i, [128, tmax], F32) for i in range(2)], "tmp")
        self.sg = Rot([sb("sg%d" % i, [128, 512], F32) for i in range(2)], "sg")
        self.w13t = Rot([sb("w13t%d" % i, [128, KC, 256], BF16) for i in range(3)], "w13t")
        self.w2t = Rot([sb("w2t%d" % i, [128, JC, 128], BF16) for i in range(2)], "w2t")
        self.xo = Rot([sb("xo%d" % i, [128, tmax], F32) for i in range(2)], "xo")
        self.ones = sb("ones", [128, 128], BF16)
        self.r_ones = Res()
        P.emit("pool", lambda e: e.memset(self.ones[:], 1.0 / D), writes=[self.r_ones])
        self.epsb = sb("epsb", [128, 1], F32)
        P.emit("pool", lambda e: e.memset(self.epsb[:], EPS), writes=[self.r_ones])
        self.ps = Rot([st.enter_context(nc.psum_tensor("ps%d" % i, [128, 512], F32)) for i in range(8)], "ps")

    def mod_compute(self, sT_d, modw_d, modb_d, nlayers=2, nchunks=144):
        nc, P, st = self.nc, self.P, self.st
        sb = lambda name, shape, dt: st.enter_context(nc.sbuf_tensor("sb_" + name, shape, dt))
        sT = sb("sT", [128, KC, 3], F32)
        r_sT = Res()
        self.mod = sb("mod", [128, nlayers, nchunks, 3], F32)
        self.r_mod = Res()
        modb = sb("modb", [128, nlayers, nchunks], F32)
        r_modb = Res()
        wm = Rot([sb("wm%d" % i, [128, KC, 256], F32) for i in range(2)], "wm")
        P.emit("sp", lambda e: e.dma_start(out=sT[:], in_=sT_d), writes=[r_sT], chan="msc0")
        P.emit("sp", lambda e: e.dma_start(out=modb[:], in_=modb_d), writes=[r_modb], chan="msc1")
        P.emit("act", lambda e: e.activation(out=sT[:], in_=sT[:], func=AF.Silu), reads=[r_sT], writes=[r_sT])
        for l in range(nlayers):
            pst, r_ps, _ = self.ps.next()
            psv = pst[:, 0:nchunks * 3].rearrange("p (n r) -> p n r", r=3)
            for nb in range(nchunks // 2):
                wt, r_wt, ch = wm.next()
                P.emit("sp" if nb % 2 else "act",
                       lambda e, wt=wt, l=l, nb=nb: e.dma_start(out=wt[:], in_=modw_d[l, :, :, nb * 256:(nb + 1) * 256]),
                       writes=[r_wt], chan=ch)
                for q in range(2):
                    n = nb * 2 + q
                    for kc in range(KC):
                        P.emit("pe", lambda e, wt=wt, q=q, kc=kc, n=n, psv=psv: e.matmul(
                            psv[:, n, :], lhsT=wt[:, kc, q * 128:(q + 1) * 128], rhs=sT[:, kc, :],
                            start=(kc == 0), stop=(kc == KC - 1)), reads=[r_wt, r_sT], writes=[r_ps])
            P.emit("dve", lambda e, l=l, psv=psv: e.tensor_tensor(
                out=self.mod[:, l], in0=psv, in1=modb[:, l, :].unsqueeze(2).to_broadcast([128, nchunks, 3]), op=ALU.add),
                reads=[r_ps, r_modb], writes=[self.r_mod])

    def mod_derive(self, mod_d, normw_d, nlayers=2):
        nc, P, st = self.nc, self.P, self.st
        sb = lambda name, shape, dt: st.enter_context(nc.sbuf_tensor("sb_" + name, shape, dt))
        self.mod = sb("mod", [128, nlayers, 144, 3], F32)
        self.r_mod = Res()
        self.normw = sb("normw", [128, nlayers, 3, KC], F32)
        r_nw = Res()
        self.gs = sb("gs", [128, nlayers, 3, KC, 3], F32)
        self.hg = sb("hg", [128, nlayers, 3, KC, 3], F32)
        P.emit("sp", lambda e: e.dma_start(out=self.mod[:].rearrange("p a b c -> p (a b c)"), in_=mod_d), writes=[self.r_mod], chan="md0")
        P.emit("sp", lambda e: e.dma_start(out=self.normw[:], in_=normw_d), writes=[r_nw], chan="md1")
        for l in range(nlayers):
            for s in range(3):
                sc = self.mod[:, l, (3 * s + 1) * 16:(3 * s + 2) * 16, :]
                gt = self.mod[:, l, (3 * s + 2) * 16:(3 * s + 3) * 16, :]
                P.emit("dve", lambda e, l=l, s=s, sc=sc: e.scalar_tensor_tensor(
                    out=self.gs[:, l, s], in0=sc, scalar=1.0, in1=self.normw[:, l, s, :].unsqueeze(2).to_broadcast([128, KC, 3]),
                    op0=ALU.add, op1=ALU.mult), reads=[self.r_mod, r_nw], writes=[self.r_mod])
                P.emit("dve", lambda e, l=l, s=s, gt=gt: e.tensor_scalar(
                    out=self.hg[:, l, s], in0=gt, scalar1=(1.0 if s == 1 else 0.5), scalar2=None, op0=ALU.mult),
                    reads=[self.r_mod], writes=[self.r_mod])

    def mod_load(self, modo, gso, hgo, nlayers=2):
        nc, P, st = self.nc, self.P, self.st
        sb = lambda name, shape, dt: st.enter_context(nc.sbuf_tensor("sb_" + name, shape, dt))
        self.mod = sb("mod", [128, nlayers, 144, 3], F32)
        self.gs = sb("gs", [128, nlayers, 3, KC, 3], F32)
        self.hg = sb("hg", [128, nlayers, 3, KC, 3], F32)
        self.r_mod = Res()
        P.emit("sp", lambda e: e.dma_start(out=self.mod[:].rearrange("p a b c -> p (a b c)"), in_=modo), writes=[self.r_mod], chan="ml0")
        P.emit("sp", lambda e: e.dma_start(out=self.gs[:].rearrange("p a b c d -> p (a b c d)"), in_=gso), writes=[self.r_mod], chan="ml1")
        P.emit("sp", lambda e: e.dma_start(out=self.hg[:].rearrange("p a b c d -> p (a b c d)"), in_=hgo), writes=[self.r_mod], chan="ml2")

    def linear_res(self, blocks, src_d, r_src, w_d, bias_sb, X, r_X, Xo, r_Xo, l):
        P = self.P
        offs = np.cumsum([0] + [b[1] for b in blocks])
        sv = src_d.rearrange("(c p) t -> p c t", p=128)
        for bi, (c0, n, row) in enumerate(blocks):
            for half in range(2):
                P.emit("sp", lambda e, c0=c0, n=n, o=offs[bi], half=half: e.dma_start(
                    out=self.h[:, half * 8:(half + 1) * 8, o:o + n], in_=sv[:, half * 8:(half + 1) * 8, c0:c0 + n]),
                    reads=[r_src], writes=[self.r_h], chan="lrh%d_%d" % (bi, half))
        Xv = X.rearrange("(c p) t -> p c t", p=128)
        Xov = Xo.rearrange("(c p) t -> p c t", p=128)
        for n2 in range(KC // 2):
            wt, r_wt, ch = self.w13t.next()
            P.emit("pool", lambda e, wt=wt, n2=n2: e.dma_start(out=wt[:], in_=w_d[n2]), writes=[r_wt], chan=ch)
            for q in range(2):
                nn = n2 * 2 + q
                xt, r_xt, chx = self.xs.next()
                xo, r_xo, cho = self.xo.next()
                for bi, (c0, n, row) in enumerate(blocks):
                    o = offs[bi]
                    P.emit("sp", lambda e, xt=xt, nn=nn, c0=c0, n=n, o=o: e.dma_start(
                        out=xt[:, o:o + n], in_=Xv[:, nn, c0:c0 + n]), reads=[r_X], writes=[r_xt], chan=chx + "_%d" % bi)
                    po, r_po, _ = self.ps.next()
                    for kc in range(KC):
                        P.emit("pe", lambda e, po=po, wt=wt, kc=kc, q=q, o=o, n=n: e.matmul(
                            po[:, 0:n], lhsT=wt[:, kc, q * 128:(q + 1) * 128], rhs=self.h[:, kc, o:o + n],
                            start=(kc == 0), stop=(kc == KC - 1)), reads=[r_wt, self.r_h], writes=[r_po])
                    src_ap = po[:, 0:n]
                    rd = [r_po]
                    if bias_sb is not None:
                        sg, r_sg, _ = self.sg.next()
                        P.emit("act", lambda e, sg=sg, po=po, n=n, nn=nn: e.activation(
                            out=sg[:, 0:n], in_=po[:, 0:n], func=AF.Identity, bias=bias_sb[:, nn:nn + 1], scale=1.0),
                            reads=[r_po, self.r_mod], writes=[r_sg])
                        src_ap = sg[:, 0:n]
                        rd = [r_sg]
                    P.emit("dve", lambda e, src_ap=src_ap, xt=xt, xo=xo, nn=nn, o=o, n=n, row=row: e.scalar_tensor_tensor(
                        out=xo[:, o:o + n], in0=src_ap, scalar=self.hg[:, l, 1, nn, row:row + 1], in1=xt[:, o:o + n],
                        op0=ALU.mult, op1=ALU.add), reads=rd + [r_xt, self.r_mod], writes=[r_xo])
                    P.emit("sp", lambda e, xo=xo, nn=nn, c0=c0, n=n, o=o: e.dma_start(
                        out=Xov[:, nn, c0:c0 + n], in_=xo[:, o:o + n]), reads=[r_xo], writes=[r_Xo], chan=cho + "_o%d" % bi)

    def shift_ap(self, l, s, c, row):
        return self.mod[:, l, (3 * s) * 16 + c, row:row + 1]

    def norm_mod(self, blocks, X, r_X, l, s, out_dram=None, r_out=None, final_g=None):
        P = self.P
        T = sum(b[1] for b in blocks)
        offs = np.cumsum([0] + [b[1] for b in blocks])
        stat = [self.ps.next() for _ in blocks]
        Xv = X.rearrange("(c p) t -> p c t", p=128)

        def load_x(c):
            xt, r_xt, ch = self.xs.next()
            for bi, (c0, n, row) in enumerate(blocks):
                P.emit("sp", lambda e, xt=xt, c=c, c0=c0, n=n, o=offs[bi]: e.dma_start(
                    out=xt[:, o:o + n], in_=Xv[:, c, c0:c0 + n]), reads=[r_X], writes=[r_xt], chan=ch + "_%d" % bi)
            return xt, r_xt

        for c in range(KC):
            xt, r_xt = load_x(c)
            sq, r_sq, _ = self.sq.next()
            P.emit("act", lambda e, xt=xt, sq=sq: e.activation(out=sq[:, 0:T], in_=xt[:, 0:T], func=AF.Square),
                   reads=[r_xt], writes=[r_sq])
            for bi, (c0, n, row) in enumerate(blocks):
                pst, r_ps, _ = stat[bi]
                P.emit("pe", lambda e, pst=pst, sq=sq, o=offs[bi], n=n, c=c: e.matmul(
                    pst[:, 0:n], lhsT=self.ones[:], rhs=sq[:, o:o + n], start=(c == 0), stop=(c == KC - 1)),
                    reads=[r_sq, self.r_ones], writes=[r_ps])
        for bi, (c0, n, row) in enumerate(blocks):
            pst, r_ps, _ = stat[bi]
            P.emit("act", lambda e, pst=pst, o=offs[bi], n=n: e.activation(
                out=self.rstd[:, o:o + n], in_=pst[:, 0:n], func=AF.Sqrt, bias=self.epsb[:, 0:1], scale=1.0),
                reads=[r_ps, self.r_ones], writes=[self.r_rstd])
            P.emit("dve", lambda e, o=offs[bi], n=n: e.reciprocal(
                out=self.rstd[:, o:o + n], in_=self.rstd[:, o:o + n]),
                reads=[self.r_rstd], writes=[self.r_rstd])
        for c in range(KC):
            xt, r_xt = load_x(c)
            tmp, r_tmp, _ = self.tmp.next()
            P.emit("dve", lambda e, xt=xt, tmp=tmp: e.tensor_tensor(
                out=tmp[:, 0:T], in0=xt[:, 0:T], in1=self.rstd[:, 0:T], op=ALU.mult),
                reads=[r_xt, self.r_rstd], writes=[r_tmp])
            if out_dram is None:
                for bi, (c0, n, row) in enumerate(blocks):
                    P.emit("act", lambda e, tmp=tmp, o=offs[bi], n=n, c=c, row=row: e.activation(
                        out=self.h[:, c, o:o + n], in_=tmp[:, o:o + n], func=AF.Identity,
                        scale=self.gs[:, l, s, c, row:row + 1], bias=self.shift_ap(l, s, c, row)),
                        reads=[r_tmp, self.r_mod], writes=[self.r_h])
            else:
                xo, r_xo, ch = self.xo.next()
                odt = out_dram.dtype
                xov = xo if odt == F32 else xo[:].bitcast(BF16)
                for bi, (c0, n, row) in enumerate(blocks):
                    if final_g is not None:
                        P.emit("act", lambda e, tmp=tmp, xov=xov, o=offs[bi], n=n, c=c: e.activation(
                            out=xov[:, o:o + n], in_=tmp[:, o:o + n], func=AF.Identity, scale=final_g[:, c:c + 1], bias=0.0),
                            reads=[r_tmp, self.r_mod], writes=[r_xo])
                    else:
                        P.emit("act", lambda e, tmp=tmp, xov=xov, o=offs[bi], n=n, c=c, row=row: e.activation(
                            out=xov[:, o:o + n], in_=tmp[:, o:o + n], func=AF.Identity,
                            scale=self.gs[:, l, s, c, row:row + 1], bias=self.shift_ap(l, s, c, row)),
                            reads=[r_tmp, self.r_mod], writes=[r_xo])
                ov = out_dram.rearrange("(c p) t -> p c t", p=128)
                for bi, (c0, n, row) in enumerate(blocks):
                    P.emit("sp", lambda e, xov=xov, o=offs[bi], n=n, c=c, c0=c0: e.dma_start(
                        out=ov[:, c, c0:c0 + n], in_=xov[:, o:o + n]), reads=[r_xo], writes=[r_out], chan=ch + "_o%d" % bi)

    def ffn(self, blocks, w13_d, w2_d, X, r_X, Xo, r_Xo, l, s):
        P = self.P
        offs = np.cumsum([0] + [b[1] for b in blocks])
        for j in range(JC):
            wt, r_wt, ch = self.w13t.next()
            P.emit("pool", lambda e, wt=wt, j=j: e.dma_start(out=wt[:], in_=w13_d[j]), writes=[r_wt], chan=ch)
            for bi, (c0, n, row) in enumerate(blocks):
                o = offs[bi]
                pg, r_pg, _ = self.ps.next()
                pu, r_pu, _ = self.ps.next()
                for half, (pp, r_pp) in enumerate(((pg, r_pg), (pu, r_pu))):
                    for kc in range(KC):
                        P.emit("pe", lambda e, pp=pp, wt=wt, kc=kc, half=half, o=o, n=n: e.matmul(
                            pp[:, 0:n], lhsT=wt[:, kc, half * 128:(half + 1) * 128], rhs=self.h[:, kc, o:o + n],
                            start=(kc == 0), stop=(kc == KC - 1)), reads=[r_wt, self.r_h], writes=[r_pp])
                sg, r_sg, _ = self.sg.next()
                P.emit("act", lambda e, sg=sg, pg=pg, n=n: e.activation(out=sg[:, 0:n], in_=pg[:, 0:n], func=AF.Silu),
                       reads=[r_pg], writes=[r_sg])
                P.emit("dve", lambda e, sg=sg, pu=pu, j=j, o=o, n=n: e.tensor_tensor(
                    out=self.hid[:, j, o:o + n], in0=sg[:, 0:n], in1=pu[:, 0:n], op=ALU.mult),
                    reads=[r_sg, r_pu], writes=[self.r_hid])
        Xv = X.rearrange("(c p) t -> p c t", p=128)
        Xov = Xo.rearrange("(c p) t -> p c t", p=128)
        for nn in range(KC):
            wt, r_wt, ch = self.w2t.next()
            P.emit("pool", lambda e, wt=wt, nn=nn: e.dma_start(out=wt[:], in_=w2_d[nn]), writes=[r_wt], chan=ch)
            xt, r_xt, chx = self.xs.next()
            xo, r_xo, cho = self.xo.next()
            for bi, (c0, n, row) in enumerate(blocks):
                o = offs[bi]
                P.emit("sp", lambda e, xt=xt, nn=nn, c0=c0, n=n, o=o: e.dma_start(
                    out=xt[:, o:o + n], in_=Xv[:, nn, c0:c0 + n]), reads=[r_X], writes=[r_xt], chan=chx + "_%d" % bi)
                po, r_po, _ = self.ps.next()
                for jc in range(JC):
                    P.emit("pe", lambda e, po=po, wt=wt, jc=jc, o=o, n=n: e.matmul(
                        po[:, 0:n], lhsT=wt[:, jc, :], rhs=self.hid[:, jc, o:o + n],
                        start=(jc == 0), stop=(jc == JC - 1)), reads=[r_wt, self.r_hid], writes=[r_po])
                P.emit("dve", lambda e, po=po, xt=xt, xo=xo, nn=nn, o=o, n=n, row=row: e.scalar_tensor_tensor(
                    out=xo[:, o:o + n], in0=po[:, 0:n], scalar=self.hg[:, l, s, nn, row:row + 1], in1=xt[:, o:o + n],
                    op0=ALU.mult, op1=ALU.add), reads=[r_po, r_xt, self.r_mod], writes=[r_xo])
                P.emit("sp", lambda e, xo=xo, nn=nn, c0=c0, n=n, o=o: e.dma_start(
                    out=Xov[:, nn, c0:c0 + n], in_=xo[:, o:o + n]), reads=[r_xo], writes=[r_Xo], chan=cho + "_o%d" % bi)


def make_passes(nlat, nctx, brow):
    blks = []
    c = 0
    while c < nlat:
        n = min(512, nlat - c)
        blks.append((c, n, brow))
        c += n
    if nctx:
        blks.append((nlat, nctx, 2))
    fine = []
    for (c0, n, row) in blks:
        fine.append((c0, n, row))
    passes = []
    cur, tot = [], 0
    queue = list(fine)
    while queue:
        c0, n, row = queue.pop(0)
        if tot + n <= 768:
            cur.append((c0, n, row)); tot += n
        elif n == 512 and tot + 256 <= 768:
            cur.append((c0, 256, row)); tot += 256
            queue.insert(0, (c0 + 256, 256, row))
        else:
            passes.append(cur); cur, tot = [], 0
            queue.insert(0, (c0, n, row))
    if cur:
        passes.append(cur)
    return passes


def lay_w13(w):
    return np.ascontiguousarray(w.reshape(KC, 128, 2, JC, 128).transpose(3, 1, 0, 2, 4)).reshape(JC, 128, KC, 256)


def lay_w2(w):
    return np.ascontiguousarray(w.reshape(JC, 128, KC, 128).transpose(2, 1, 0, 3))


def lay_sq(w):
    return np.ascontiguousarray(w.reshape(KC, 128, -1).transpose(1, 0, 2))


def lay_vec(v):
    sh = v.shape[:-1]
    a = v.reshape(sh + (KC, 128))
    return np.ascontiguousarray(np.moveaxis(a, -1, 0))


def core_tokens(core):
    b = core // 4
    q = core % 4
    return b, q


def build_l0():
    nc = bass.Bass("TRN2", target_bir_lowering=False)
    dt = lambda name, shape, dty, kind: nc.dram_tensor(name, shape, dty, kind=kind).ap()
    sT = dt("sT", [128, KC, 3], F32, "ExternalInput")
    modw = dt("modw", [2, 128, KC, 18 * 128], F32, "ExternalInput")
    modb = dt("modb", [128, 2, 18], F32, "ExternalInput")
    modo = dt("modo", [128, 2 * 18 * 3], F32, "ExternalOutput")
    with contextlib.ExitStack() as st:
        P = Prog(nc)
        dn = Dense(nc, P, st, 64, 0)
        dn.mod_compute(sT, modw, modb, 2, 18)
        P.emit("sp", lambda e: e.dma_start(out=modo, in_=dn.mod[:].rearrange("p a b c -> p (a b c)")), reads=[dn.r_mod], chan="mo0")
        P.run(final_waits=_all_dma_tails(P))
    return nc


def build_l1(nlat, nctx):
    NT = nlat + nctx
    nc = bass.Bass("TRN2", target_bir_lowering=False)
    dt = lambda name, shape, dty, kind: nc.dram_tensor(name, shape, dty, kind=kind).ap()
    xT = dt("xT", [D, NT], F32, "ExternalInput")
    modi = dt("modi", [128, 2 * 144 * 3], F32, "ExternalInput")
    normw = dt("normw", [128, 2, 3, KC], F32, "ExternalInput")
    w13 = dt("w13", [JC, 128, KC, 256], F32, "ExternalInput")
    w2 = dt("w2", [KC, 128, JC, 128], F32, "ExternalInput")
    X1 = dt("X1", [D, NT], F32, "ExternalOutput")
    h0 = dt("h0", [D, NT], BF16, "ExternalOutput")
    gso = dt("gso", [128, 2 * 3 * KC * 3], F32, "ExternalOutput")
    hgo = dt("hgo", [128, 2 * 3 * KC * 3], F32, "ExternalOutput")
    outs = []
    with contextlib.ExitStack() as st:
        P = Prog(nc)
        dn = Dense(nc, P, st, 768, 0)
        dn.mod_derive(modi, normw)
        r_o = Res()
        outs.append(P.emit("sp", lambda e: e.dma_start(out=gso, in_=dn.gs[:].rearrange("p a b c d -> p (a b c d)")), reads=[dn.r_mod], writes=[r_o], chan="mo1"))
        outs.append(P.emit("sp", lambda e: e.dma_start(out=hgo, in_=dn.hg[:].rearrange("p a b c d -> p (a b c d)")), reads=[dn.r_mod], writes=[r_o], chan="mo2"))
        r_xin = Res()
        passes = make_passes(nlat, nctx, 0)
        for blocks in passes:
            r_X1 = Res()
            r_h0 = Res()
            dn.norm_mod(blocks, xT, r_xin, 0, 0)
            dn.ffn(blocks, w13, w2, xT, r_xin, X1, r_X1, 0, 0)
            dn.norm_mod(blocks, X1, r_X1, 0, 1, out_dram=h0, r_out=r_h0)
            outs.append(r_X1)
            outs.append(r_h0)
        fw = [o.lastw if isinstance(o, Res) else o for o in outs]
        P.run(final_waits=_all_dma_tails(P))
    return nc


def _all_dma_tails(P):
    return list(P.chan_last.values())


def fft_tables():
    bf = ml_dtypes.bfloat16
    ch = np.arange(256, dtype=np.float64)
    ang = 2 * np.pi * np.outer(ch, ch) / 256.0
    sc = 1.0 / 2048.0
    cs = np.zeros((128, 2, 4, 128), np.float64)
    for kc in range(2):
        for q in range(4):
            a = ang[kc * 128:(kc + 1) * 128, q * 64:(q + 1) * 64]
            cs[:, kc, q, 0:64] = np.cos(a) * sc
            cs[:, kc, q, 64:128] = -np.sin(a) * sc
    l = np.arange(128, dtype=np.float64)
    a1 = 2 * np.pi * np.outer(l, l) / 128.0
    f1 = np.zeros((128, 2, 256), np.float64)
    f1[:, 0, 0:128] = np.cos(a1); f1[:, 0, 128:256] = -np.sin(a1)
    f1[:, 1, 0:128] = np.sin(a1); f1[:, 1, 128:256] = np.cos(a1)
    k = np.arange(128)[None, :] * 128 + np.arange(128)[:, None]
    ae = 2 * np.pi * (l[:, None, None] * k[None]) / 16384.0
    E = np.concatenate([np.cos(ae), np.sin(ae)], axis=2)
    scc = 1.0 / 256.0
    csc = np.zeros((128, 2, 512), np.float64)
    for kc in range(2):
        a = ang[kc * 128:(kc + 1) * 128, :]
        csc[:, kc, 0:256] = np.cos(a) * scc
        csc[:, kc, 256:512] = -np.sin(a) * scc
    g = np.zeros((128, 2, 512), np.float64)
    for tc in range(2):
        a = ang[tc * 128:(tc + 1) * 128, :]
        g[:, tc, 0:256] = np.cos(a)
        g[:, tc, 256:512] = np.sin(a)
    return {"t_cs": cs.astype(bf), "t_f1": f1.astype(bf), "t_E": E.astype(bf), "t_csc": csc.astype(bf), "t_g": g.astype(bf)}


def emit_fft(nc, P, st, hT, r_hT, hcT, r_hcT, tabs, fo, r_fo, fco, r_fco, nb=2, pfx="ff"):
    sb = lambda name, shape, dt: st.enter_context(nc.sbuf_tensor("sb_" + pfx + name, shape, dt))
    hs = sb("hs", [128, 2, 16384], BF16); r_hs = Res()
    W = sb("W", [128, 128, 128], BF16); r_W = Res()
    Z = sb("Z", [128, 64, 256], BF16); r_Z = Res()
    fT = sb("fT", [64, 16384], BF16); r_fT = Res()
    Eb = Rot([sb("E%d" % i, [128, 16, 256], BF16) for i in range(2)], pfx + "E")
    cs = sb("cs", [128, 2, 4, 128], BF16)
    f1 = sb("f1", [128, 2, 256], BF16)
    csc = sb("csc", [128, 2, 512], BF16)
    gt = sb("gt", [128, 2, 512], BF16)
    hcs = sb("hcs", [128, 2, 256], BF16); r_hcs = Res()
    Wc = sb("Wc", [128, 2, 512], BF16); r_Wc = Res()
    fcs = sb("fcs", [128, 256], BF16); r_fcs = Res()
    r_tab = Res()
    ps = Rot([st.enter_context(nc.psum_tensor(pfx + "ps%d" % i, [128, 512], F32)) for i in range(8)], pfx + "ps")
    for i, (dst, src) in enumerate(((cs, tabs["t_cs"]), (f1, tabs["t_f1"]), (csc, tabs["t_csc"]), (gt, tabs["t_g"]))):
        P.emit("sp", lambda e, dst=dst, src=src: e.dma_start(out=dst[:], in_=src), writes=[r_tab], chan=pfx + "tab%d" % i)
    evac_i = [0]

    def evac(out_ap, in_ap, reads, writes):
        eng = "act" if evac_i[0] % 2 == 0 else "dve"
        evac_i[0] += 1
        if eng == "act":
            P.emit("act", lambda e: e.activation(out=out_ap, in_=in_ap, func=AF.Copy), reads=reads, writes=writes)
        else:
            P.emit("dve", lambda e: e.tensor_copy(out=out_ap, in_=in_ap), reads=reads, writes=writes)

    for b in range(nb):
        P.emit("sp", lambda e, b=b: e.dma_start(out=hcs[:], in_=hcT[b]), reads=[r_hcT], writes=[r_hcs], chan=pfx + "hc")
        for tc in range(2):
            pt, r_pt, _ = ps.next()
            for kc in range(2):
                P.emit("pe", lambda e, pt=pt, tc=tc, kc=kc: e.matmul(
                    pt[:, :], lhsT=hcs[:, kc, tc * 128:(tc + 1) * 128], rhs=csc[:, kc, :], start=(kc == 0), stop=(kc == 1)),
                    reads=[r_hcs, r_tab], writes=[r_pt])
            evac(Wc[:, tc, :], pt[:, :], [r_pt], [r_Wc])
        for half in range(2):
            pt, r_pt, _ = ps.next()
            k = 0
            for tc in range(2):
                for ri in range(2):
                    P.emit("pe", lambda e, pt=pt, tc=tc, ri=ri, half=half, k=k: e.matmul(
                        pt[:, 0:256], lhsT=Wc[:, tc, ri * 256 + half * 128: ri * 256 + (half + 1) * 128],
                        rhs=gt[:, tc, ri * 256:(ri + 1) * 256], start=(k == 0), stop=(k == 3)),
                        reads=[r_Wc, r_tab], writes=[r_pt])
                    k += 1
            evac(fcs[:, :], pt[:, 0:256], [r_pt], [r_fcs])
            P.emit("sp", lambda e, b=b, half=half: e.dma_start(out=fco[b, half * 128:(half + 1) * 128, :], in_=fcs[:, :]),
                   reads=[r_fcs], writes=[r_fco], chan=pfx + "fco")
        for kc in range(2):
            P.emit("sp" if kc == 0 else "act", lambda e, b=b, kc=kc: e.dma_start(out=hs[:, kc, :], in_=hT[b, :, kc, :]),
                   reads=[r_hT], writes=[r_hs], chan=pfx + "hs%d" % kc)
        hv = hs[:].rearrange("p k (a l) -> p k l a", l=128)
        for q in range(4):
            for g4 in range(32):
                pt, r_pt, _ = ps.next()
                for li in range(4):
                    l2 = g4 * 4 + li
                    for kc in range(2):
                        P.emit("pe", lambda e, pt=pt, li=li, l2=l2, kc=kc, q=q: e.matmul(
                            pt[:, li * 128:(li + 1) * 128], lhsT=hv[:, kc, l2, :], rhs=cs[:, kc, q, :],
                            start=(kc == 0), stop=(kc == 1)), reads=[r_hs, r_tab], writes=[r_pt])
                evac(W[:, g4 * 4:(g4 + 1) * 4, :].rearrange("p a b -> p (a b)"), pt[:, :], [r_pt], [r_W])
            for c2 in range(32):
                pt, r_pt, _ = ps.next()
                for ci in range(2):
                    c = c2 * 2 + ci
                    for ri in range(2):
                        P.emit("pe", lambda e, pt=pt, ci=ci, c=c, ri=ri: e.matmul(
                            pt[:, ci * 256:(ci + 1) * 256], lhsT=W[:, :, ri * 64 + c], rhs=f1[:, ri, :],
                            start=(ri == 0), stop=(ri == 1)), reads=[r_W, r_tab], writes=[r_pt])
                evac(Z[:, c2 * 2:(c2 + 1) * 2, :].rearrange("p a b -> p (a b)"), pt[:, :], [r_pt], [r_Z])
            fv = fT[:].rearrange("p (k2 k1) -> p k1 k2", k1=128)
            for eb in range(8):
                Et, r_Et, ch = Eb.next()
                P.emit("sp", lambda e, Et=Et, eb=eb: e.dma_start(out=Et[:], in_=tabs["t_E"][:, eb * 16:(eb + 1) * 16, :]),
                       writes=[r_Et], chan=ch)
                for k4 in range(4):
                    pt, r_pt, _ = ps.next()
                    for ki in range(4):
                        kl = k4 * 4 + ki
                        k1 = eb * 16 + kl
                        for ri in range(2):
                            P.emit("pe", lambda e, pt=pt, ki=ki, kl=kl, k1=k1, ri=ri, Et=Et: e.matmul(
                                pt[0:64, ki * 128:(ki + 1) * 128], lhsT=Z[:, :, ri * 128 + k1], rhs=Et[:, kl, ri * 128:(ri + 1) * 128],
                                start=(ri == 0), stop=(ri == 1)), reads=[r_Z, r_Et], writes=[r_pt])
                    k10 = eb * 16 + k4 * 4
                    evac(fv[:, k10:k10 + 4, :], pt[0:64, :].rearrange("p (a b) -> p a b", a=4), [r_pt], [r_fT])
            P.emit("sp", lambda e, b=b, q=q: e.dma_start(out=fo[b, q * 64:(q + 1) * 64, :], in_=fT[:, :]),
                   reads=[r_fT], writes=[r_fo], chan=pfx + "fo")


def build_l2(nb=2):
    nc = bass.Bass("TRN2", target_bir_lowering=False)
    dt = lambda name, shape, dty, kind: nc.dram_tensor(name, shape, dty, kind=kind).ap()
    hT = dt("hT", [nb, 128, 2, 16384], BF16, "ExternalInput")
    hcT = dt("hcT", [nb, 128, 2, 256], BF16, "ExternalInput")
    tabs = {"t_cs": dt("t_cs", [128, 2, 4, 128], BF16, "ExternalInput"), "t_f1": dt("t_f1", [128, 2, 256], BF16, "ExternalInput"),
            "t_E": dt("t_E", [128, 128, 256], BF16, "ExternalInput"), "t_csc": dt("t_csc", [128, 2, 512], BF16, "ExternalInput"),
            "t_g": dt("t_g", [128, 2, 512], BF16, "ExternalInput")}
    fo = dt("fo", [nb, 256, 16384], BF16, "ExternalOutput")
    fco = dt("fco", [nb, 256, 256], BF16, "ExternalOutput")
    with contextlib.ExitStack() as st:
        P = Prog(nc)
        emit_fft(nc, P, st, hT, Res(), hcT, Res(), tabs, fo, Res(), fco, Res(), nb=nb)
        P.run(final_waits=_all_dma_tails(P))
    return nc


LDC = -0.6065306597126334
GN_EPS = 64e-5


def rwkv_consts():
    bf = ml_dtypes.bfloat16
    idx = np.arange(128)
    cm = np.zeros((128, 2, 3, 256), np.float32)
    ct = np.zeros((128, 2, 3, 128), np.float32)
    for z in range(2):
        before = (idx[:, None] < idx[None, :]) if z == 0 else (idx[:, None] > idx[None, :])
        beq = before | np.eye(128, dtype=bool)
        cm[:, z, 0, 0:128] = before
        cm[:, z, 0, 128:256] = beq
        cm[:, z, 1, 0:128] = before.T
        cm[:, z, 1, 128:256] = before.T
        cm[:, z, 2, 0:128] = beq
        cm[:, z, 2, 128:256] = beq
        ct[:, z, 0, :] = LDC * beq
        ct[:, z, 1, :] = LDC * before
        ct[:, z, 2, :] = LDC
    ident = np.eye(128, dtype=np.float32)
    return {"c_mask": cm.astype(bf), "c_tri": ct, "c_ident": ident.astype(bf)}


def emit_rwkv(nc, P, st, hT, r_hT, wd, yscr, o_out, r_out, rkvs=None, nlat=SEQ, nctx=CTX, nb=2, pfx="rw", dbg=None, dbg_at=(0, "ctx", 0, 0), ew_engs=("dve",), prefetch=True):
    sbt = lambda name, shape, dt: st.enter_context(nc.sbuf_tensor("sb_" + pfx + name, shape, dt))
    ps = Rot([st.enter_context(nc.psum_tensor(pfx + "ps%d" % i, [128, 512], F32)) for i in range(8)], pfx + "ps")
    r_c = Res()

    def ld(dst, src, eng="sp", chan=None, **kw):
        return P.emit(eng, lambda e: e.dma_start(out=dst, in_=src, **kw), writes=[r_c], chan=chan)
    cmask = sbt("cmask", [128, 1, 3, 256], BF16)
    ctri = sbt("ctri", [128, 1, 3, 128], F32)
    ident = sbt("ident", [128, 128], BF16); ld(ident[:], wd["c_ident"], chan=pfx + "k2")
    vecs = sbt("vecs", [128, 9, 256], F32); ld(vecs[:], wd["vecs"], chan=pfx + "k3")
    mu = sbt("mu", [128, KC, 6], F32); ld(mu[:], wd["mu"], chan=pfx + "k4")
    om = sbt("om", [128, KC, 6], F32)
    W2s = sbt("W2s", [96, 2, 256], BF16); ld(W2s[:], wd["w2"], eng="pool", chan=pfx + "k5")
    A2s = sbt("A2s", [96, 2, 256], BF16); ld(A2s[:], wd["a2"], eng="pool", chan=pfx + "k6")
    G2s = sbt("G2s", [128, 2, 256], BF16); ld(G2s[:], wd["g2"], eng="pool", chan=pfx + "k7")
    negcol = sbt("negcol", [128, 1], F32)
    P.emit("pool", lambda e: e.memset(negcol[:], LDC), writes=[r_c])
    gneps = sbt("gneps", [128, 1], F32)
    P.emit("pool", lambda e: e.memset(gneps[:], GN_EPS), writes=[r_c])
    P.emit("dve", lambda e: e.tensor_scalar(out=om[:], in0=mu[:], scalar1=-1.0, scalar2=1.0, op0=ALU.mult, op1=ALU.add),
           reads=[r_c], writes=[r_c])
    RKa = sbt("RKa", [128, KC, 768], BF16); RKb = sbt("RKb", [128, KC, 768], BF16)
    LWa = sbt("LWa", [128, KC, 640], BF16); LWb = sbt("LWb", [128, KC, 640], BF16)
    stg = Rot([sbt("stg%d" % i, [128, 256], F32) for i in range(1)], pfx + "stg")
    blocks = [(0, 256, 0), (256, 512, 1), (512, 768, 2), (768, 960, 3), (960, 1152, 4), (1152, 1408, 5)]
    for kc in range(KC):
        for (c0, c1, p) in blocks:
            sg, r_sg, ch = stg.next()
            n_ = c1 - c0
            if c0 < 768:
                P.emit("sp", lambda e, sg=sg, kc=kc, c0=c0, c1=c1, n_=n_: e.dma_start(out=sg[:, 0:n_], in_=wd["rkv"][:, kc, c0:c1]), writes=[r_sg], chan=ch)
            else:
                P.emit("sp", lambda e, sg=sg, kc=kc, c0=c0, c1=c1, n_=n_: e.dma_start(out=sg[:, 0:n_], in_=wd["lw"][:, kc, c0 - 768:c1 - 768]), writes=[r_sg], chan=ch)
            da = RKa[:, kc, c0:c1] if c0 < 768 else LWa[:, kc, c0 - 768:c1 - 768]
            db = RKb[:, kc, c0:c1] if c0 < 768 else LWb[:, kc, c0 - 768:c1 - 768]
            P.emit("dve", lambda e, sg=sg, n_=n_, p=p, kc=kc, db=db: e.tensor_scalar(
                out=db, in0=sg[:, 0:n_], scalar1=mu[:, kc, p:p + 1], scalar2=None, op0=ALU.mult), reads=[r_sg, r_c], writes=[r_c])
            P.emit("pool", lambda e, sg=sg, n_=n_, p=p, kc=kc, da=da: e.tensor_scalar(
                out=da, in0=sg[:, 0:n_], scalar1=om[:, kc, p:p + 1], scalar2=None, op0=ALU.mult), reads=[r_sg, r_c], writes=[r_c])

    SC = 128
    hw0 = sbt("hw", [128, KC, 64 + SC + 64], BF16); r_hw0 = Res()
    hsb0 = sbt("hsb", [128, KC, SC], BF16); r_hs0 = Res()
    hw = [hw0 for b in range(nb)]; r_hw = [Res() for _ in range(nb)]
    hsb1 = sbt("hsb1", [128, KC, SC], BF16)
    hsb = [hsb0, hsb1]; r_hs = [Res() for _ in range(nb)]
    xw = [sbt("xw%d" % b, [96, SC], BF16) for b in range(nb)]
    xa = [sbt("xa%d" % b, [96, 2, SC], BF16) for b in range(nb)]
    xg = [sbt("xg%d" % b, [128, 2, SC], BF16) for b in range(nb)]
    r_x = [Res() for _ in range(nb)]
    rkv = [sbt("rkv%d" % b, [128, 768], F32) for b in range(nb)]; r_rkv = [Res() for _ in range(nb)]
    NSCR = 10
    scr = [[sbt("scr%d_%d" % (b, i), [128, 256], F32) for i in range(NSCR)] for b in range(nb)]
    r_scr = [[Res() for _ in range(NSCR)] for b in range(nb)]
    scr0, r_scr0 = scr[0], r_scr[0]
    small = [[sbt("sm%d_%d" % (b, i), [128, 4], F32) for i in range(4)] for b in range(nb)]
    r_small = [[Res() for _ in range(4)] for b in range(nb)]
    opn = ["At", "Rt", "Bt", "Kt", "Bh", "Kh", "Vt"]
    opt = [{n: sbt("%s%d" % (n, b), [128, 256], BF16) for n in opn} for b in range(nb)]
    r_opt = [{n: Res() for n in opn} for b in range(nb)]
    keep = [{n: sbt("kp%s%d" % (n, b), [128, 256], F32) for n in ("g",)} for b in range(nb)]
    r_keep = [Res() for _ in range(nb)]
    gend = [sbt("gend%d" % b, [64, 4], F32) for b in range(nb)]; r_gend = [Res() for _ in range(nb)]
    Yt = [sbt("Y%d" % b, [128, 256], F32) for b in range(nb)]; r_Y = [Res() for _ in range(nb)]
    yfin = [scr[b][5] for b in range(nb)]; r_yf = [r_scr[b][5] for b in range(nb)]
    ob = [sbt("ob%d" % b, [128, 256], BF16) for b in range(nb)]; r_ob = [Res() for _ in range(nb)]
    units = [(b, h) for b in range(nb) for h in range(4)]
    U = {}
    for (b, h) in units:
        u = {}
        n = "%d_%d" % (b, h)
        u["fm"] = sbt("fm" + n, [64, 4, 128], BF16); u["r_fm"] = Res()
        u["Mrk"] = sbt("Mrk" + n, [128, 256], BF16)
        u["MkaT"] = sbt("MkaT" + n, [128, 128], BF16)
        u["r_M"] = Res()
        _T = sbt("T0" + n, [128, 128], F32); _rT = Res()
        u["T"] = [_T, _T]; u["r_T"] = [_rT, _rT]
        u["Tbf"] = sbt("Tbf" + n, [128, 128], BF16); u["r_Tbf"] = Res()
        u["PP"] = [sbt("PP%d" % i + n, [128, 256], F32) for i in range(2)]; u["r_PP"] = [Res(), Res()]
        u["X"] = sbt("X" + n, [128, 128], BF16); u["Ah"] = sbt("Ah" + n, [64, 128], BF16); u["r_XA"] = Res()
        u["Ut"] = sbt("Ut" + n, [128, 64], BF16); u["r_Ut"] = Res()
        u["S32"] = sbt("S32" + n, [64, 64], F32); u["Sbf"] = sbt("Sbf" + n, [64, 64], BF16); u["r_S"] = Res(); u["r_Sbf"] = Res()
        U[(b, h)] = u

    W0 = lambda z: vecs[:, 0 + z, :]
    A0 = lambda z: vecs[:, 2 + z, :]
    KK_, KA_, RK_, LNW_, LNB_ = vecs[:, 4, :], vecs[:, 5, :], vecs[:, 6, :], vecs[:, 7, :], vecs[:, 8, :]
    hv = lambda t: t.rearrange("p (h j) -> p h j", h=4)
    rr = [0]
    r_rkvs = Res()

    def ew(reads, writes, fn_dve, allow=None):
        allow = allow or ew_engs
        eng = allow[rr[0] % len(allow)]
        rr[0] += 1
        P.emit(eng, fn_dve, reads=reads, writes=writes)

    def _pass(z):
        P.emit("sp", lambda e: e.dma_start(out=cmask[:, 0], in_=wd["c_mask"][:, z]), reads=[r_c], writes=[r_c], chan=pfx + "k0")
        P.emit("sp", lambda e: e.dma_start(out=ctri[:, 0], in_=wd["c_tri"][:, z]), reads=[r_c], writes=[r_c], chan=pfx + "k1")
        for (b, h) in units:
            u = U[(b, h)]
            P.emit("pool", lambda e, u=u: e.memset(u["S32"][:], 0.0), writes=[u["r_S"]])
            P.emit("pool", lambda e, u=u: e.memset(u["Sbf"][:], 0.0), writes=[u["r_Sbf"]])
        segs = [("ctx", 0, nctx), ("lat", nctx, nlat)]
        def _seg(sname, soff, slen):
            nsc = slen // SC
            sc_order = range(nsc) if z == 0 else range(nsc - 1, -1, -1)
            def _proj(sci):
                t0 = sci * SC
                for b in range(nb):
                    hTv = hT[b].rearrange("(c p) t -> p c t", p=128)
                    for half in range(2):
                        P.emit("sp", lambda e, b=b, half=half, hTv=hTv: e.dma_start(
                            out=hw[b][:, half * 8:(half + 1) * 8, b * SC:(b + 1) * SC],
                            in_=hTv[:, half * 8:(half + 1) * 8, soff + t0: soff + t0 + SC]),
                            reads=[r_hT], writes=[r_hw[b]], chan=pfx + "hw%d_%d" % (b, half))
                    shifts = (-1, 1, -64, 64) if sname == "lat" else (-1, 1, -1, 1)
                    for qd in range(4):
                        sh = shifts[qd]
                        lo = t0 + sh; hi = t0 + sh + SC
                        clo, chi = max(lo, 0), min(hi, slen)
                        if clo > lo or chi < hi:
                            P.emit("dve", lambda e, b=b, qd=qd: e.memset(hsb[b][:, qd * 4:(qd + 1) * 4, :], 0.0), writes=[r_hs[b]])
                        P.emit("sp", lambda e, b=b, qd=qd, lo=lo, clo=clo, chi=chi, hTv=hTv: e.dma_start(
                            out=hsb[b][:, qd * 4:(qd + 1) * 4, clo - lo:chi - lo],
                            in_=hTv[:, qd * 4:(qd + 1) * 4, soff + clo: soff + chi]),
                            reads=[r_hT], writes=[r_hs[b]], chan=pfx + "hs%d_%d" % (b, qd))
                    if sname == "lat":
                        P.emit("dve", lambda e, b=b: e.memset(hsb[b][:, 0:4, 0:SC:64], 0.0), writes=[r_hs[b]])
                        P.emit("dve", lambda e, b=b: e.memset(hsb[b][:, 4:8, 63:SC:64], 0.0), writes=[r_hs[b]])
                    groups = [(0 + 96 * z, 96, "w", 0)]
                    if z == 0:
                        groups += [(192, 96, "a", 0)]
                    else:
                        if rkvs is None:
                            groups += [(192, 96, "a", 0)]
                        groups += [(288, 96, "a", 1), (384, 128, "g", 0), (512, 128, "g", 1)]
                    for (c0, m, kind, gi) in groups:
                        pt, r_pt, _ = ps.next()
                        k = 0
                        for kc in range(KC):
                            for (wt, src) in ((LWa, hw[b][:, kc, b * SC:(b + 1) * SC]), (LWb, hsb[b][:, kc, :])):
                                P.emit("pe", lambda e, pt=pt, wt=wt, src=src, kc=kc, c0=c0, m=m, k=k: e.matmul(
                                    pt[0:m, 0:SC], lhsT=wt[:, kc, c0:c0 + m], rhs=src, start=(k == 0), stop=(k == 2 * KC - 1)),
                                    reads=[r_c, r_hw[b], r_hs[b]], writes=[r_pt])
                                k += 1
                        if kind == "w":
                            P.emit("act", lambda e, pt=pt, b=b: e.activation(out=xw[b][:, :], in_=pt[0:96, 0:SC], func=AF.Tanh),
                                   reads=[r_pt], writes=[r_x[b]])
                        elif kind == "a":
                            P.emit("act", lambda e, pt=pt, b=b, gi=gi: e.activation(out=xa[b][:, gi, :], in_=pt[0:96, 0:SC], func=AF.Copy),
                                   reads=[r_pt], writes=[r_x[b]])
                        else:
                            P.emit("act", lambda e, pt=pt, b=b, gi=gi: e.activation(out=xg[b][:, gi, :], in_=pt[0:128, 0:SC], func=AF.Sigmoid),
                                   reads=[r_pt], writes=[r_x[b]])
                    gidx = (soff + t0) // 128
                    if rkvs is not None and z == 1:
                        P.emit("sp", lambda e, b=b, gidx=gidx: e.dma_start(out=rkv[b][:, :], in_=rkvs[b, gidx, :, 0:768]),
                               reads=[r_rkvs], writes=[r_rkv[b]], chan=pfx + "rkl%d" % b)
                    else:
                        pa, r_pa, _ = ps.next()
                        pb_, r_pb, _ = ps.next()
                        k = 0
                        for kc in range(KC):
                            for (src, wt) in ((hw[b][:, kc, b * SC:(b + 1) * SC], RKa), (hsb[b][:, kc, 0 * 128:(0 + 1) * 128], RKb)):
                                P.emit("pe", lambda e, pa=pa, src=src, wt=wt, kc=kc, k=k: e.matmul(
                                    pa[:, 0:512], lhsT=src, rhs=wt[:, kc, 0:512], start=(k == 0), stop=(k == 2 * KC - 1)),
                                    reads=[r_c, r_hw[b], r_hs[b]], writes=[r_pa])
                                P.emit("pe", lambda e, pb_=pb_, src=src, wt=wt, kc=kc, k=k: e.matmul(
                                    pb_[:, 0:256], lhsT=src, rhs=wt[:, kc, 512:768], start=(k == 0), stop=(k == 2 * KC - 1)),
                                    reads=[r_c, r_hw[b], r_hs[b]], writes=[r_pb])
                                k += 1
                        P.emit("act", lambda e, b=b, pa=pa: e.activation(out=rkv[b][:, 0:512], in_=pa[:, 0:512], func=AF.Copy),
                               reads=[r_pa], writes=[r_rkv[b]])
                        P.emit("act", lambda e, b=b, pb_=pb_: e.activation(out=rkv[b][:, 512:768], in_=pb_[:, 0:256], func=AF.Copy),
                               reads=[r_pb], writes=[r_rkv[b]])
                        if rkvs is not None:
                            P.emit("sp", lambda e, b=b, gidx=gidx: e.dma_start(out=rkvs[b, gidx, :, 0:768], in_=rkv[b][:, :]),
                                   reads=[r_rkv[b]], writes=[r_rkvs], chan=pfx + "rks%d" % b)
            def _sc(sci, pre_done, nxt):
                t0 = sci * SC
                if not pre_done:
                    _proj(sci)
                ch_order = range(SC // 128) if z == 0 else range(SC // 128 - 1, -1, -1)
                def _ch(ci):
                    tok0 = t0 + ci * 128
                    recs = []
                    real_emit = P.emit
                    for b in range(nb):
                        rec = []
                        P.emit = (lambda rec: (lambda eng, fn, reads=(), writes=(), chan=None: rec.append((eng, _freeze(fn), reads, writes, chan))))(rec)
                        S_ = scr[b]; RS = r_scr[b]
                        r_t, k_t, v_t = rkv[b][:, 0:256], rkv[b][:, 256:512], rkv[b][:, 512:768]
                        cs = slice(ci * 128, (ci + 1) * 128)
                        pw, r_pw, _ = ps.next()
                        P.emit("pe", lambda e, pw=pw, b=b, cs=cs: e.matmul(pw[:, 0:256], lhsT=xw[b][:, cs], rhs=W2s[:, z, :], start=True, stop=True),
                               reads=[r_x[b], r_c], writes=[r_pw])
                        P.emit("dve", lambda e, pw=pw, b=b: e.tensor_tensor(out=S_[0][:], in0=pw[:, 0:256], in1=W0(z), op=ALU.add),
                               reads=[r_pw, r_c], writes=[RS[0]])
                        P.emit("act", lambda e, b=b: e.activation(out=S_[0][:], in_=S_[0][:], func=AF.Sigmoid), reads=[RS[0]], writes=[RS[0]])
                        gidx = (soff + tok0) // 128
                        zs = [z] if (z == 0 or rkvs is not None) else [1, 0]
                        if rkvs is not None and z == 1:
                            P.emit("sp", lambda e, b=b, gidx=gidx: e.dma_start(out=S_[2][:], in_=rkvs[b, gidx, :, 768:1024]),
                                   reads=[r_rkvs], writes=[RS[2]], chan=pfx + "agl")
                        for ai, za in enumerate(zs):
                            pw, r_pw, _ = ps.next()
                            P.emit("pe", lambda e, pw=pw, b=b, cs=cs, za=za: e.matmul(pw[:, 0:256], lhsT=xa[b][:, za, cs], rhs=A2s[:, za, :], start=True, stop=True),
                                   reads=[r_x[b], r_c], writes=[r_pw])
                            P.emit("dve", lambda e, pw=pw, b=b, ai=ai, za=za: e.tensor_tensor(out=S_[1 + ai][:], in0=pw[:, 0:256], in1=A0(za), op=ALU.add),
                                   reads=[r_pw, r_c], writes=[RS[1 + ai]])
                            P.emit("act", lambda e, b=b, ai=ai: e.activation(out=S_[1 + ai][:], in_=S_[1 + ai][:], func=AF.Sigmoid),
                                   reads=[RS[1 + ai]], writes=[RS[1 + ai]])
                        if rkvs is not None and z == 0:
                            P.emit("sp", lambda e, b=b, gidx=gidx: e.dma_start(out=rkvs[b, gidx, :, 768:1024], in_=S_[1][:]),
                                   reads=[RS[1]], writes=[r_rkvs], chan=pfx + "ags")
                        if z == 1:
                            pw, r_pw, _ = ps.next()
                            for kc2 in range(2):
                                P.emit("pe", lambda e, pw=pw, b=b, cs=cs, kc2=kc2: e.matmul(pw[:, 0:256], lhsT=xg[b][:, kc2, cs], rhs=G2s[:, kc2, :],
                                                                                           start=(kc2 == 0), stop=(kc2 == 1)), reads=[r_x[b], r_c], writes=[r_pw])
                            P.emit("act", lambda e, pw=pw, b=b: e.activation(out=keep[b]["g"][:], in_=pw[:, 0:256], func=AF.Copy),
                                   reads=[r_pw], writes=[r_keep[b]])
                        ew([r_rkv[b], r_c], [RS[3]], lambda e, b=b, k_t=k_t: e.tensor_tensor(out=S_[3][:], in0=k_t, in1=KK_, op=ALU.mult))
                        ew([RS[3]], [RS[9]], lambda e, b=b: e.tensor_tensor(out=S_[9][:], in0=S_[3][:], in1=S_[3][:], op=ALU.mult))
                        P.emit("dve", lambda e, b=b: e.tensor_reduce(out=small[b][0][:], in_=hv(S_[9][:]), axis=AX.X, op=ALU.add),
                               reads=[RS[9]], writes=[r_small[b][0]])
                        P.emit("dve", lambda e, b=b: e.tensor_scalar(out=small[b][0][:], in0=small[b][0][:], scalar1=1e-24, scalar2=None, op0=ALU.max),
                               reads=[r_small[b][0]], writes=[r_small[b][0]])
                        P.emit("act", lambda e, b=b: e.activation(out=small[b][0][:], in_=small[b][0][:], func=AF.Sqrt),
                               reads=[r_small[b][0]], writes=[r_small[b][0]])
                        P.emit("dve", lambda e, b=b: e.reciprocal(out=small[b][0][:], in_=small[b][0][:]),
                               reads=[r_small[b][0]], writes=[r_small[b][0]])
                        ew([RS[3], r_small[b][0]], [RS[3]], lambda e, b=b: e.tensor_tensor(
                            out=hv(S_[3][:]), in0=hv(S_[3][:]), in1=small[b][0][:].unsqueeze(2).to_broadcast([128, 4, 64]), op=ALU.mult))
                        pl, r_pl, _ = ps.next()
                        pe2, r_pe2, _ = ps.next()
                        P.emit("pe", lambda e, pl=pl, b=b: e.matmul(pl[:, 0:256], lhsT=ctri[:, 0, 0, :], rhs=S_[0][:], start=True, stop=True),
                               reads=[RS[0], r_c], writes=[r_pl])
                        P.emit("pe", lambda e, pl=pl, b=b: e.matmul(pl[:, 256:512], lhsT=ctri[:, 0, 1, :], rhs=S_[0][:], start=True, stop=True),
                               reads=[RS[0], r_c], writes=[r_pl])
                        P.emit("pe", lambda e, pe2=pe2, b=b: e.matmul(pe2[:, 0:256], lhsT=ctri[:, 0, 2, :], rhs=S_[0][:], start=True, stop=True),
                               reads=[RS[0], r_c], writes=[r_pe2])
                        for hh in range(4):
                            P.emit("pe", lambda e, pe2=pe2, b=b, hh=hh: e.matmul(pe2[0:64, 256 + hh:257 + hh], lhsT=S_[0][:, hh * 64:(hh + 1) * 64], rhs=negcol[:, 0:1],
                                                                                 start=True, stop=True), reads=[RS[0], r_c], writes=[r_pe2])
                        P.emit("act", lambda e, pl=pl, b=b: e.activation(out=S_[5][:], in_=pl[:, 0:256], func=AF.Exp), reads=[r_pl], writes=[RS[5]])
                        P.emit("act", lambda e, pl=pl, b=b: e.activation(out=S_[6][:], in_=pl[:, 0:256], func=AF.Exp, scale=-1.0), reads=[r_pl], writes=[RS[6]])
                        P.emit("act", lambda e, pl=pl, b=b: e.activation(out=S_[7][:], in_=pl[:, 256:512], func=AF.Exp), reads=[r_pl], writes=[RS[7]])
                        P.emit("act", lambda e, pe2=pe2, b=b: e.activation(out=S_[8][:], in_=pe2[:, 0:256], func=AF.Exp), reads=[r_pe2], writes=[RS[8]])
                        P.emit("act", lambda e, pe2=pe2, b=b: e.activation(out=gend[b][:], in_=pe2[0:64, 256:260], func=AF.Exp), reads=[r_pe2], writes=[r_gend[b]])
                        O_ = opt[b]; RO = r_opt[b]
                        ew([RS[3], RS[7]], [RO["At"]], lambda e, b=b: e.scalar_tensor_tensor(
                            out=O_["At"][:], in0=S_[3][:], scalar=-1.0, in1=S_[7][:], op0=ALU.mult, op1=ALU.mult), allow=("dve",))
                        ew([r_rkv[b], RS[5]], [RO["Rt"]], lambda e, b=b, r_t=r_t: e.tensor_tensor(out=O_["Rt"][:], in0=r_t, in1=S_[5][:], op=ALU.mult))
                        ew([r_rkv[b]], [RO["Vt"]], lambda e, b=b, v_t=v_t: e.tensor_copy(out=O_["Vt"][:], in_=v_t))
                        ew([RS[3], RS[1]], [RS[9]], lambda e, b=b: e.tensor_tensor(out=S_[9][:], in0=S_[3][:], in1=S_[1][:], op=ALU.mult))
                        ew([RS[9], RS[6]], [RO["Bt"]], lambda e, b=b: e.tensor_tensor(out=O_["Bt"][:], in0=S_[9][:], in1=S_[6][:], op=ALU.mult))
                        ew([RS[1], r_c], [RS[4]], lambda e, b=b: e.scalar_tensor_tensor(
                            out=S_[4][:], in0=S_[1][:], scalar=-1.0, in1=KA_, op0=ALU.add, op1=ALU.mult), allow=("dve",))
                        ew([RS[4], r_rkv[b]], [RS[4]], lambda e, b=b, k_t=k_t: e.scalar_tensor_tensor(
                            out=S_[4][:], in0=S_[4][:], scalar=1.0, in1=k_t, op0=ALU.add, op1=ALU.mult), allow=("dve",))
                        ew([RS[4], RS[6]], [RO["Kt"]], lambda e, b=b: e.tensor_tensor(out=O_["Kt"][:], in0=S_[4][:], in1=S_[6][:], op=ALU.mult))
                        ew([RO["Bt"], RS[8]], [RO["Bh"]], lambda e, b=b: e.tensor_tensor(out=O_["Bh"][:], in0=O_["Bt"][:], in1=S_[8][:], op=ALU.mult))
                        ew([RO["Kt"], RS[8]], [RO["Kh"]], lambda e, b=b: e.tensor_tensor(out=O_["Kh"][:], in0=O_["Kt"][:], in1=S_[8][:], op=ALU.mult))
                        if z == 1 and sname == "lat":
                            ew([RS[2], r_c], [RS[9]], lambda e, b=b: e.scalar_tensor_tensor(
                                out=S_[9][:], in0=S_[2][:], scalar=-1.0, in1=KA_, op0=ALU.add, op1=ALU.mult), allow=("dve",))
                            ew([RS[9], r_rkv[b]], [RS[9]], lambda e, b=b, k_t=k_t: e.scalar_tensor_tensor(
                                out=S_[9][:], in0=S_[9][:], scalar=1.0, in1=k_t, op0=ALU.add, op1=ALU.mult), allow=("dve",))
                            ew([RS[9], RS[4]], [RS[9]], lambda e, b=b: e.tensor_tensor(out=S_[9][:], in0=S_[9][:], in1=S_[4][:], op=ALU.add))
                            ew([RS[9], r_rkv[b]], [RS[9]], lambda e, b=b, r_t=r_t: e.tensor_tensor(out=S_[9][:], in0=S_[9][:], in1=r_t, op=ALU.mult))
                            ew([RS[9], r_c], [RS[9]], lambda e, b=b: e.scalar_tensor_tensor(
                                out=S_[9][:], in0=S_[9][:], scalar=0.5, in1=RK_, op0=ALU.mult, op1=ALU.mult), allow=("dve",))
                            P.emit("dve", lambda e, b=b: e.tensor_reduce(out=small[b][3][:], in_=hv(S_[9][:]), axis=AX.X, op=ALU.add),
                                   reads=[RS[9]], writes=[r_small[b][3]])
                        recs.append(rec)
                    P.emit = real_emit
                    for i_ in range(max(len(r_) for r_ in recs)):
                        for rec in recs:
                            if i_ < len(rec):
                                eng_, fn_, rd_, wr_, ch_ = rec[i_]
                                real_emit(eng_, fn_, reads=rd_, writes=wr_, chan=ch_)
                    if dbg is not None and (z, sname, sci, ci) == dbg_at:
                        b = 0
                        dbg("rkv", rkv[b][:], [r_rkv[b]])
                        for i in (0, 1, 3, 5, 6, 7, 8, 9, 4):
                            dbg("s%d" % i, scr[b][i][:], [r_scr[b][i]])
                        for nme in opn:
                            dbg(nme, opt[b][nme][:], [r_opt[b][nme]])
                        dbg("gend", gend[b][:], [r_gend[b]])
                        dbg("hw", hw[b][:], [r_hw[b]])
                        dbg("hsb", hsb[b][:], [r_hs[b]])
                        dbg("xw", xw[b][:], [r_x[b]])
                        dbg("xa", xa[b][:], [r_x[b]])
                    if nxt is not None:
                        _proj(nxt)
                    for (b, h) in units:
                        u = U[(b, h)]; O_ = opt[b]; RO = r_opt[b]
                        hs_ = slice(h * 64, (h + 1) * 64)
                        pt, r_pt, _ = ps.next()
                        ptb = pt[:].bitcast(BF16)
                        for i, nme in enumerate(("At", "Rt", "Bt", "Kt")):
                            P.emit("pe", lambda e, ptb=ptb, i=i, nme=nme, b=b, hs_=hs_: e.transpose(
                                ptb[0:64, i * 128:(i + 1) * 128], O_[nme][:, hs_], ident[:]), reads=[RO[nme], r_c], writes=[r_pt])
                        P.emit("act", lambda e, ptb=ptb, u=u: e.activation(out=u["fm"][:].rearrange("p a t -> p (a t)"), in_=ptb[0:64, 0:512], func=AF.Copy),
                               reads=[r_pt], writes=[u["r_fm"]])
                        fm = u["fm"]
                        p1, r_p1, _ = ps.next()
                        p3, r_p3, _ = ps.next()
                        P.emit("pe", lambda e, p1=p1, fm=fm: e.matmul(p1[:, 0:256], lhsT=fm[:, 2, :], rhs=fm[:, 0:2, :].rearrange("p a t -> p (a t)"), start=True, stop=True),
                               reads=[u["r_fm"]], writes=[r_p1])
                        P.emit("pe", lambda e, p1=p1, fm=fm: e.matmul(p1[:, 256:384], lhsT=fm[:, 3, :], rhs=fm[:, 1, :], start=True, stop=True),
                               reads=[u["r_fm"]], writes=[r_p1])
                        P.emit("pe", lambda e, p3=p3, fm=fm: e.matmul(p3[:, 0:256], lhsT=fm[:, 0, :], rhs=fm[:, 2:4, :].rearrange("p a t -> p (a t)"), start=True, stop=True),
                               reads=[u["r_fm"]], writes=[r_p3])
                        P.emit("dve", lambda e, p1=p1, u=u: e.tensor_tensor(out=u["Mrk"][:], in0=p1[:, 128:384], in1=cmask[:, 0, 2, :], op=ALU.mult),
                               reads=[r_p1, r_c], writes=[u["r_M"]])
                        P.emit("dve", lambda e, p3=p3, u=u: e.tensor_tensor(out=u["MkaT"][:], in0=p3[:, 128:256], in1=cmask[:, 0, 1, 128:256], op=ALU.mult),
                               reads=[r_p3, r_c], writes=[u["r_M"]])
                        P.emit("dve", lambda e, p1=p1, u=u: e.tensor_tensor(out=u["PP"][1][:, 0:128], in0=p1[:, 0:128], in1=cmask[:, 0, 0, 0:128], op=ALU.mult),
                               reads=[r_p1, r_c], writes=[u["r_PP"][1]])
                        P.emit("dve", lambda e, p3=p3, u=u: e.tensor_tensor(out=u["PP"][1][:, 128:256], in0=p3[:, 0:128], in1=cmask[:, 0, 1, 0:128], op=ALU.mult),
                               reads=[r_p3, r_c], writes=[u["r_PP"][1]])
                        P.emit("dve", lambda e, u=u: e.tensor_tensor(out=u["T"][0][:], in0=u["PP"][1][:, 0:128], in1=ident[:], op=ALU.add),
                               reads=[u["r_PP"][1], r_c], writes=[u["r_T"][0]])
                    for kk_ in range(1, 7):
                        for (b, h) in units:
                            u = U[(b, h)]
                            Pm, PTm, rd = u["PP"][kk_ % 2][:, 0:128], u["PP"][kk_ % 2][:, 128:256], u["r_PP"][kk_ % 2]
                            dst, r_dst = u["PP"][(kk_ + 1) % 2], u["r_PP"][(kk_ + 1) % 2]
                            pp, r_pp, _ = ps.next()
                            P.emit("pe", lambda e, pp=pp, Pm=Pm, PTm=PTm: e.matmul(pp[:, 128:256], lhsT=Pm, rhs=PTm, start=True, stop=True),
                                   reads=[rd], writes=[r_pp])
                            if kk_ < 6:
                                P.emit("pe", lambda e, pp=pp, Pm=Pm, PTm=PTm: e.matmul(pp[:, 0:128], lhsT=PTm, rhs=Pm, start=True, stop=True),
                                       reads=[rd], writes=[r_pp])
                                P.emit("act", lambda e, pp=pp, dst=dst: e.activation(out=dst[:], in_=pp[:, 0:256], func=AF.Copy), reads=[r_pp], writes=[r_dst])
                            else:
                                P.emit("act", lambda e, pp=pp, dst=dst: e.activation(out=dst[:, 128:256], in_=pp[:, 128:256], func=AF.Copy), reads=[r_pp], writes=[r_dst])
                        for (b, h) in units:
                            u = U[(b, h)]
                            PTk, r_ptk = u["PP"][(kk_ + 1) % 2][:, 128:256], u["r_PP"][(kk_ + 1) % 2]
                            Told, r_told = u["T"][(kk_ - 1) % 2], u["r_T"][(kk_ - 1) % 2]
                            pt, r_pt, _ = ps.next()
                            P.emit("pe", lambda e, pt=pt, PTk=PTk, Told=Told: e.matmul(pt[:, 0:128], lhsT=PTk, rhs=Told[:], start=True, stop=True),
                                   reads=[r_ptk, r_told], writes=[r_pt])
                            if kk_ < 6:
                                Tnew, r_tnew = u["T"][kk_ % 2], u["r_T"][kk_ % 2]
                            else:
                                Tnew, r_tnew = u["Tbf"], u["r_Tbf"]
                            P.emit("dve", lambda e, pt=pt, Tnew=Tnew, Told=Told: e.tensor_tensor(out=Tnew[:], in0=pt[:, 0:128], in1=Told[:], op=ALU.add),
                                   reads=[r_pt, r_told], writes=[r_tnew])
                    for (b, h) in units:
                        u = U[(b, h)]; O_ = opt[b]; RO = r_opt[b]
                        hs_ = slice(h * 64, (h + 1) * 64)
                        Tf, r_tf = u["Tbf"], u["r_Tbf"]
                        px, r_px, _ = ps.next()
                        P.emit("pe", lambda e, px=px, u=u, Tf=Tf: e.matmul(px[:, 0:128], lhsT=u["MkaT"][:], rhs=Tf[:], start=True, stop=True),
                               reads=[u["r_M"], r_tf], writes=[r_px])
                        P.emit("pe", lambda e, px=px, b=b, hs_=hs_, Tf=Tf: e.matmul(px[0:64, 128:256], lhsT=O_["At"][:, hs_], rhs=Tf[:], start=True, stop=True),
                               reads=[RO["At"], r_tf], writes=[r_px])
                        P.emit("act", lambda e, px=px, u=u: e.activation(out=u["X"][:], in_=px[:, 0:128], func=AF.Copy), reads=[r_px], writes=[u["r_XA"]])
                        P.emit("dve", lambda e, px=px, u=u: e.tensor_copy(out=u["Ah"][:], in_=px[0:64, 128:256]), reads=[r_px], writes=[u["r_XA"]])
                    for (b, h) in units:
                        u = U[(b, h)]; O_ = opt[b]; RO = r_opt[b]
                        hs_ = slice(h * 64, (h + 1) * 64)
                        pu, r_pu, _ = ps.next()
                        P.emit("pe", lambda e, pu=pu, u=u: e.matmul(pu[:, 0:64], lhsT=u["Ah"][:], rhs=u["Sbf"][:], start=True, stop=False),
                               reads=[u["r_XA"], u["r_Sbf"]], writes=[r_pu])
                        P.emit("pe", lambda e, pu=pu, u=u, b=b, hs_=hs_: e.matmul(pu[:, 0:64], lhsT=u["X"][:], rhs=O_["Vt"][:, hs_], start=False, stop=True),
                               reads=[u["r_XA"], RO["Vt"]], writes=[r_pu])
                        P.emit("act", lambda e, pu=pu, u=u: e.activation(out=u["Ut"][:], in_=pu[:, 0:64], func=AF.Copy), reads=[r_pu], writes=[u["r_Ut"]])
                    for (b, h) in units:
                        u = U[(b, h)]; O_ = opt[b]; RO = r_opt[b]
                        hs_ = slice(h * 64, (h + 1) * 64)
                        py, r_py, _ = ps.next()
                        P.emit("pe", lambda e, py=py, u=u: e.matmul(py[:, 0:64], lhsT=u["fm"][:, 1, :], rhs=u["Sbf"][:], start=True, stop=False),
                               reads=[u["r_fm"], u["r_Sbf"]], writes=[r_py])
                        P.emit("pe", lambda e, py=py, u=u: e.matmul(py[:, 0:64], lhsT=u["Mrk"][:, 0:128], rhs=u["Ut"][:], start=False, stop=False),
                               reads=[u["r_M"], u["r_Ut"]], writes=[r_py])
                        P.emit("pe", lambda e, py=py, u=u, b=b, hs_=hs_: e.matmul(py[:, 0:64], lhsT=u["Mrk"][:, 128:256], rhs=O_["Vt"][:, hs_], start=False, stop=True),
                               reads=[u["r_M"], RO["Vt"]], writes=[r_py])
                        P.emit("act", lambda e, py=py, b=b, hs_=hs_: e.activation(out=Yt[b][:, hs_], in_=py[:, 0:64], func=AF.Copy), reads=[r_py], writes=[r_Y[b]])
                        pss, r_pss, _ = ps.next()
                        P.emit("pe", lambda e, pss=pss, u=u, b=b, hs_=hs_: e.matmul(pss[0:64, 0:64], lhsT=O_["Bh"][:, hs_], rhs=u["Ut"][:], start=True, stop=False),
                               reads=[RO["Bh"], u["r_Ut"]], writes=[r_pss])
                        P.emit("pe", lambda e, pss=pss, u=u, b=b, hs_=hs_: e.matmul(pss[0:64, 0:64], lhsT=O_["Kh"][:, hs_], rhs=O_["Vt"][:, hs_], start=False, stop=True),
                               reads=[RO["Kh"], RO["Vt"]], writes=[r_pss])
                        P.emit("dve", lambda e, pss=pss, u=u, b=b, h=h: e.scalar_tensor_tensor(
                            out=u["S32"][:], in0=u["S32"][:], scalar=gend[b][:, h:h + 1], in1=pss[0:64, 0:64], op0=ALU.mult, op1=ALU.add),
                            reads=[r_pss, r_gend[b], u["r_S"]], writes=[u["r_S"]])
                        P.emit("pool", lambda e, u=u: e.tensor_copy(out=u["Sbf"][:], in_=u["S32"][:]), reads=[u["r_S"]], writes=[u["r_Sbf"]])
                    if dbg is not None and (z, sname, sci, ci) == dbg_at:
                        u = U[(0, 0)]
                        dbg("fm", u["fm"][:], [u["r_fm"]])
                        dbg("Mrk", u["Mrk"][:], [u["r_M"]])
                        dbg("T", u["Tbf"][:], [u["r_Tbf"]])
                        dbg("X", u["X"][:], [u["r_XA"]]); dbg("Ah", u["Ah"][:], [u["r_XA"]])
                        dbg("Ut", u["Ut"][:], [u["r_Ut"]])
                        dbg("Y", Yt[0][:], [r_Y[0]])
                        dbg("S32", u["S32"][:], [u["r_S"]])
                    if sname != "lat":
                        return
                    for b in range(nb):
                        if z == 0:
                            P.emit("sp", lambda e, b=b, tok0=tok0: e.dma_start(out=yscr[b, tok0:tok0 + 128, :], in_=Yt[b][:]),
                                   reads=[r_Y[b]], writes=[r_out], chan=pfx + "ys%d" % b)
                            continue
                        S_ = scr[b]; RS = r_scr[b]; K_ = keep[b]
                        P.emit("sp", lambda e, b=b, tok0=tok0: e.dma_start(out=yfin[b][:], in_=yscr[b, tok0:tok0 + 128, :]),
                               reads=[r_out], writes=[r_yf[b]], chan=pfx + "yl%d" % b)
                        ew([r_Y[b], r_yf[b]], [RS[0]], lambda e, b=b: e.tensor_tensor(out=S_[0][:], in0=Yt[b][:], in1=yfin[b][:], op=ALU.add))
                        P.emit("dve", lambda e, b=b: e.tensor_reduce(out=small[b][1][:], in_=hv(S_[0][:]), axis=AX.X, op=ALU.add),
                               reads=[RS[0]], writes=[r_small[b][1]])
                        P.emit("dve", lambda e, b=b: e.tensor_scalar(out=small[b][1][:], in0=small[b][1][:], scalar1=1.0 / 64, scalar2=None, op0=ALU.mult),
                               reads=[r_small[b][1]], writes=[r_small[b][1]])
                        ew([RS[0], r_small[b][1]], [RS[1]], lambda e, b=b: e.tensor_tensor(
                            out=hv(S_[1][:]), in0=hv(S_[0][:]), in1=small[b][1][:].unsqueeze(2).to_broadcast([128, 4, 64]), op=ALU.subtract))
                        ew([RS[1]], [RS[2]], lambda e, b=b: e.tensor_tensor(out=S_[2][:], in0=S_[1][:], in1=S_[1][:], op=ALU.mult))
                        P.emit("dve", lambda e, b=b: e.tensor_reduce(out=small[b][2][:], in_=hv(S_[2][:]), axis=AX.X, op=ALU.add),
                               reads=[RS[2]], writes=[r_small[b][2]])
                        P.emit("act", lambda e, b=b: e.activation(out=small[b][2][:], in_=small[b][2][:], func=AF.Sqrt, scale=1.0 / 64, bias=gneps[:, 0:1]),
                               reads=[r_small[b][2], r_c], writes=[r_small[b][2]])
                        P.emit("dve", lambda e, b=b: e.reciprocal(out=small[b][2][:], in_=small[b][2][:]), reads=[r_small[b][2]], writes=[r_small[b][2]])
                        ew([RS[1], r_small[b][2]], [RS[1]], lambda e, b=b: e.tensor_tensor(
                            out=hv(S_[1][:]), in0=hv(S_[1][:]), in1=small[b][2][:].unsqueeze(2).to_broadcast([128, 4, 64]), op=ALU.mult))
                        ew([RS[1], r_c], [RS[1]], lambda e, b=b: e.tensor_tensor(out=S_[1][:], in0=S_[1][:], in1=LNW_, op=ALU.mult))
                        ew([RS[1], r_c], [RS[1]], lambda e, b=b: e.tensor_tensor(out=S_[1][:], in0=S_[1][:], in1=LNB_, op=ALU.add))
                        ew([r_opt[b]["Vt"], r_small[b][3]], [RS[3]], lambda e, b=b: e.tensor_tensor(
                            out=hv(S_[3][:]), in0=hv(opt[b]["Vt"][:]), in1=small[b][3][:].unsqueeze(2).to_broadcast([128, 4, 64]), op=ALU.mult))
                        ew([RS[1], RS[3]], [RS[1]], lambda e, b=b: e.tensor_tensor(out=S_[1][:], in0=S_[1][:], in1=S_[3][:], op=ALU.add))
                        ew([RS[1], r_keep[b]], [r_ob[b]], lambda e, b=b: e.tensor_tensor(out=ob[b][:], in0=S_[1][:], in1=K_["g"][:], op=ALU.mult))
                        P.emit("sp", lambda e, b=b, tok0=tok0: e.dma_start(out=o_out[b, tok0:tok0 + 128, :], in_=ob[b][:]),
                               reads=[r_ob[b]], writes=[r_out], chan=pfx + "oo%d" % b)
                for ci in ch_order:
                    _ch(ci)
            order = list(sc_order)
            for i, sci in enumerate(order):
                if prefetch:
                    _sc(sci, i > 0, order[i + 1] if i + 1 < len(order) else None)
                else:
                    _sc(sci, False, None)
        for seg in segs:
            _seg(*seg)
    for z in range(2):
        _pass(z)


def build_l45(nlat=SEQ, nctx=CTX, nb=2, debug=False, same_sync=True, ew_engs=("dve",), reuse=True):
    nc = bass.Bass("TRN2", target_bir_lowering=False)
    dt = lambda name, shape, dty, kind: nc.dram_tensor(name, shape, dty, kind=kind).ap()
    hT = dt("hT", [nb, D, nctx + nlat], BF16, "ExternalInput")
    wd = {"c_mask": dt("c_mask", [128, 2, 3, 256], BF16, "ExternalInput"), "c_tri": dt("c_tri", [128, 2, 3, 128], F32, "ExternalInput"),
          "c_ident": dt("c_ident", [128, 128], BF16, "ExternalInput"), "vecs": dt("vecs", [128, 9, 256], F32, "ExternalInput"),
          "mu": dt("mu", [128, KC, 6], F32, "ExternalInput"), "w2": dt("w2", [96, 2, 256], F32, "ExternalInput"),
          "a2": dt("a2", [96, 2, 256], F32, "ExternalInput"), "g2": dt("g2", [128, 2, 256], F32, "ExternalInput"),
          "rkv": dt("rkv", [128, KC, 768], F32, "ExternalInput"), "lw": dt("lw", [128, KC, 640], F32, "ExternalInput")}
    rkvs = dt("rkvs", [nb, (nctx + nlat) // 128, 128, 1024], F32, "Internal") if reuse else None
    yscr = dt("yscr", [nb, nlat, 256], F32, "ExternalOutput")
    o_out = dt("o_out", [nb, nlat, 256], BF16, "ExternalOutput")
    with contextlib.ExitStack() as st:
        P = Prog(nc, same_engine_sync=same_sync)
        dbg = None
        if debug:
            def dbg(name, ap, reads):
                t = nc.dram_tensor("dbg_" + name, list(ap.shape), ap.dtype, kind="ExternalOutput").ap()
                P.emit("sp", lambda e: e.dma_start(out=t, in_=ap), reads=reads, chan="dbg_" + name)
        emit_rwkv(nc, P, st, hT, Res(), wd, yscr, o_out, Res(), rkvs=rkvs, nlat=nlat, nctx=nctx, nb=nb, dbg=dbg, ew_engs=ew_engs)
        P.run(final_waits=_all_dma_tails(P))
    return nc


def rwkv_host_weights(inp, g):
    cs = slice(256 * g, 256 * (g + 1))
    rkv = np.concatenate([inp["rwkv_w_rkv"][0, i][:, cs] for i in range(3)], axis=1)
    lw = np.concatenate([inp["rwkv_w1"][0, 0], inp["rwkv_w1"][0, 1], inp["rwkv_a1"][0, 0], inp["rwkv_a1"][0, 1], inp["rwkv_g1"][0]], axis=1)
    vec = np.stack([inp["rwkv_w0"][0, 0][cs], inp["rwkv_w0"][0, 1][cs], inp["rwkv_a0"][0, 0][cs], inp["rwkv_a0"][0, 1][cs],
                    inp["rwkv_k_k"][0][cs], inp["rwkv_k_a"][0][cs], inp["rwkv_r_k"][0].reshape(-1)[cs], inp["rwkv_ln_w"][0][cs], inp["rwkv_ln_b"][0][cs]])
    return {"rkv": lay_sq(rkv), "lw": lay_sq(lw),
            "vecs": np.ascontiguousarray(np.broadcast_to(vec[None], (128, 9, 256))).astype(np.float32),
            "mu": np.ascontiguousarray(lay_vec(inp["rwkv_mu"][0]).transpose(0, 2, 1)),
            "w2": np.ascontiguousarray(inp["rwkv_w2"][0][:, :, cs].transpose(1, 0, 2)),
            "a2": np.ascontiguousarray(inp["rwkv_a2"][0][:, :, cs].transpose(1, 0, 2)),
            "g2": np.ascontiguousarray(inp["rwkv_g2"][0][:, cs].reshape(2, 128, 256).transpose(1, 0, 2))}


def lay_wo(w):
    return np.ascontiguousarray(w.reshape(KC, 128, 8, 256).transpose(2, 1, 0, 3))


def build_l3(nlat, nctx):
    NT = nlat + nctx
    nc = bass.Bass("TRN2", target_bir_lowering=False)
    dt = lambda name, shape, dty, kind="ExternalInput": nc.dram_tensor(name, shape, dty, kind=kind).ap()
    X1 = dt("X1", [D, NT], F32)
    fT = dt("fT", [D, NT], BF16)
    modo = dt("modo", [128, 2 * 144 * 3], F32); gso = dt("gso", [128, 2 * 3 * KC * 3], F32); hgo = dt("hgo", [128, 2 * 3 * KC * 3], F32)
    wo = dt("wo", [8, 128, KC, 256], F32)
    bo = dt("bo", [128, KC], F32)
    w13a = dt("w13a", [JC, 128, KC, 256], F32); w2a = dt("w2a", [KC, 128, JC, 128], F32)
    w13b = dt("w13b", [JC, 128, KC, 256], F32); w2b = dt("w2b", [KC, 128, JC, 128], F32)
    X2 = dt("X2", [D, NT], F32, "Internal"); X3 = dt("X3", [D, NT], F32, "Internal")
    X4 = dt("X4", [D, NT], F32, "ExternalOutput")
    h1 = dt("h1", [D, NT], BF16, "ExternalOutput")
    with contextlib.ExitStack() as st:
        P = Prog(nc)
        dn = Dense(nc, P, st, 768, 0)
        dn.mod_load(modo, gso, hgo)
        bos = st.enter_context(nc.sbuf_tensor("sb_bos", [128, KC], F32))
        P.emit("sp", lambda e: e.dma_start(out=bos[:], in_=bo), writes=[dn.r_mod], chan="bo")
        r_in = Res()
        for blocks in make_passes(nlat, nctx, 0):
            r2, r3, r4, rh = Res(), Res(), Res(), Res()
            dn.linear_res(blocks, fT, r_in, wo, bos, X1, r_in, X2, r2, 0)
            dn.norm_mod(blocks, X2, r2, 0, 2)
            dn.ffn(blocks, w13a, w2a, X2, r2, X3, r3, 0, 2)
            dn.norm_mod(blocks, X3, r3, 1, 0)
            dn.ffn(blocks, w13b, w2b, X3, r3, X4, r4, 1, 0)
            dn.norm_mod(blocks, X4, r4, 1, 1, out_dram=h1, r_out=rh)
        P.run(final_waits=_all_dma_tails(P))
    return nc


def build_l6(nlat):
    NT = nlat
    nc = bass.Bass("TRN2", target_bir_lowering=False)
    dt = lambda name, shape, dty, kind="ExternalInput": nc.dram_tensor(name, shape, dty, kind=kind).ap()
    X4 = dt("X4", [D, NT], F32)
    oT = dt("oT", [D, NT], BF16)
    modo = dt("modo", [128, 2 * 144 * 3], F32); gso = dt("gso", [128, 2 * 3 * KC * 3], F32); hgo = dt("hgo", [128, 2 * 3 * KC * 3], F32)
    wo = dt("wo", [8, 128, KC, 256], F32)
    fng = dt("fng", [128, KC], F32)
    w13 = dt("w13", [JC, 128, KC, 256], F32); w2 = dt("w2", [KC, 128, JC, 128], F32)
    X5 = dt("X5", [D, NT], F32, "Internal"); X6 = dt("X6", [D, NT], F32, "Internal")
    out = dt("out", [D, NT], F32, "ExternalOutput")
    with contextlib.ExitStack() as st:
        P = Prog(nc)
        dn = Dense(nc, P, st, 768, 0)
        dn.mod_load(modo, gso, hgo)
        fgs = st.enter_context(nc.sbuf_tensor("sb_fgs", [128, KC], F32))
        P.emit("sp", lambda e: e.dma_start(out=fgs[:], in_=fng), writes=[dn.r_mod], chan="fg")
        r_in = Res()
        for blocks in make_passes(nlat, 0, 0):
            r5, r6, ro = Res(), Res(), Res()
            dn.linear_res(blocks, oT, r_in, wo, None, X4, r_in, X5, r5, 1)
            dn.norm_mod(blocks, X5, r5, 1, 2)
            dn.ffn(blocks, w13, w2, X5, r5, X6, r6, 1, 2)
            dn.norm_mod(blocks, X6, r6, 0, 0, out_dram=out, r_out=ro, final_g=fgs)
        P.run(final_waits=_all_dma_tails(P))
    return nc


_DBG = {}


def _run(nc, maps):
    res = run_bass_kernel_spmd(nc, maps, core_ids=list(range(NCORES)))
    return res.results


def kernel(x, c, ctx, c_ctx, mod_w, mod_b, norm_w, ffn_w13, ffn_w2, fnet_w_o, fnet_b_o,
           rwkv_mu, rwkv_w_rkv, rwkv_w0, rwkv_w1, rwkv_w2, rwkv_a0, rwkv_a1, rwkv_a2,
           rwkv_g1, rwkv_g2, rwkv_k_k, rwkv_k_a, rwkv_r_k, rwkv_ln_w, rwkv_ln_b, rwkv_w_o,
           final_norm_w):
    f32 = np.float32
    A = lambda a: np.asarray(a, dtype=f32)
    x, c, ctx, c_ctx = A(x), A(c), A(ctx), A(c_ctx)
    B, L, _ = x.shape
    NL = L // 4
    NCX = CTX // 4
    NT = NL + NCX
    sT = np.ascontiguousarray(lay_vec(np.stack([c[0], c[1], c_ctx])).transpose(0, 2, 1))
    mod_w = A(mod_w); mod_b = A(mod_b)
    maps = []
    for core in range(NCORES):
        cs = slice(core * 2304, (core + 1) * 2304)
        maps.append({"sT": sT,
                     "modw": np.ascontiguousarray(mod_w[:, :, cs].reshape(2, KC, 128, 2304).transpose(0, 2, 1, 3)),
                     "modb": np.ascontiguousarray(mod_b[:, cs].reshape(2, 18, 128).transpose(2, 0, 1))})
    r0 = _run(build_l0(), maps)
    modfull = np.concatenate([r0[i]["modo"].reshape(128, 2, 18, 3) for i in range(NCORES)], axis=2)
    modsw = modfull.copy(); modsw[..., 0] = modfull[..., 1]; modsw[..., 1] = modfull[..., 0]
    modin = [np.ascontiguousarray((modfull if core // 4 == 0 else modsw).reshape(128, -1)) for core in range(NCORES)]
    normw = lay_vec(A(norm_w))
    ffn_w13 = A(ffn_w13); ffn_w2 = A(ffn_w2)
    maps = []
    w13_00, w2_00 = lay_w13(ffn_w13[0, 0]), lay_w2(ffn_w2[0, 0])
    for core in range(NCORES):
        b, q = core // 4, core % 4
        xt = np.concatenate([x[b, q * NL:(q + 1) * NL], ctx[b, q * NCX:(q + 1) * NCX]], 0).T
        maps.append({"xT": np.ascontiguousarray(xt), "modi": modin[core], "normw": normw, "w13": w13_00, "w2": w2_00})
    r1 = _run(build_l1(NL, NCX), maps)
    del maps
    tabs = fft_tables()
    maps = []
    for g in range(NCORES):
        rows = slice(256 * g, 256 * (g + 1))
        hT = np.stack([np.concatenate([r1[b * 4 + q]["h0"][rows, 0:NL] for q in range(4)], axis=1) for b in range(B)])
        hcT = np.stack([np.concatenate([r1[b * 4 + q]["h0"][rows, NL:NT] for q in range(4)], axis=1) for b in range(B)])
        maps.append({"hT": np.ascontiguousarray(hT.reshape(B, 2, 128, L).transpose(0, 2, 1, 3)),
                     "hcT": np.ascontiguousarray(hcT.reshape(B, 2, 128, CTX).transpose(0, 2, 1, 3)), **tabs})
    r2 = _run(build_l2(B), maps)
    maps = []
    wo_f = lay_wo(A(fnet_w_o)[0]); bo_f = lay_vec(A(fnet_b_o)[0])
    w13a, w2a = lay_w13(ffn_w13[0, 1]), lay_w2(ffn_w2[0, 1])
    w13b, w2b = lay_w13(ffn_w13[1, 0]), lay_w2(ffn_w2[1, 0])
    for core in range(NCORES):
        b, q = core // 4, core % 4
        fT = np.concatenate([np.concatenate([r2[g]["fo"][b][:, q * NL:(q + 1) * NL] for g in range(NCORES)], axis=0),
                             np.concatenate([r2[g]["fco"][b][:, q * NCX:(q + 1) * NCX] for g in range(NCORES)], axis=0)], axis=1)
        maps.append({"X1": r1[core]["X1"], "fT": np.ascontiguousarray(fT), "modo": modin[core], "gso": r1[core]["gso"], "hgo": r1[core]["hgo"],
                     "wo": wo_f, "bo": bo_f, "w13a": w13a, "w2a": w2a, "w13b": w13b, "w2b": w2b})
    r3 = _run(build_l3(NL, NCX), maps)
    del r2, maps
    inp = {"rwkv_mu": A(rwkv_mu), "rwkv_w_rkv": A(rwkv_w_rkv), "rwkv_w0": A(rwkv_w0), "rwkv_w1": A(rwkv_w1), "rwkv_w2": A(rwkv_w2),
           "rwkv_a0": A(rwkv_a0), "rwkv_a1": A(rwkv_a1), "rwkv_a2": A(rwkv_a2), "rwkv_g1": A(rwkv_g1), "rwkv_g2": A(rwkv_g2),
           "rwkv_k_k": A(rwkv_k_k), "rwkv_k_a": A(rwkv_k_a), "rwkv_r_k": A(rwkv_r_k), "rwkv_ln_w": A(rwkv_ln_w), "rwkv_ln_b": A(rwkv_ln_b)}
    hT = np.stack([np.concatenate([r3[b * 4 + q]["h1"][:, NL:NT] for q in range(4)] + [r3[b * 4 + q]["h1"][:, 0:NL] for q in range(4)], axis=1)
                   for b in range(B)])
    hT = np.ascontiguousarray(hT)
    cst = rwkv_consts()
    maps = [{"hT": hT, **cst, **rwkv_host_weights(inp, g)} for g in range(NCORES)]
    r45 = _run(build_l45(L, CTX, B), maps)
    del hT, maps
    maps = []
    wo_r = lay_wo(A(rwkv_w_o)[0]); fng = lay_vec(A(final_norm_w))
    w13c, w2c = lay_w13(ffn_w13[1, 1]), lay_w2(ffn_w2[1, 1])
    for core in range(NCORES):
        b, q = core // 4, core % 4
        oT = np.concatenate([r45[g]["o_out"][b][q * NL:(q + 1) * NL, :] for g in range(NCORES)], axis=1).T
        maps.append({"X4": np.ascontiguousarray(r3[core]["X4"][:, 0:NL]), "oT": np.ascontiguousarray(oT),
                     "modo": modin[core], "gso": r1[core]["gso"], "hgo": r1[core]["hgo"],
                     "wo": wo_r, "fng": fng, "w13": w13c, "w2": w2c})
    r6 = _run(build_l6(NL), maps)
    _DBG.update(r1=r1, r3=r3, r45=r45, modfull=modfull)
    out = np.empty((B, L, D), f32)
    for core in range(NCORES):
        b, q = core // 4, core % 4
        out[b, q * NL:(q + 1) * NL] = r6[core]["out"].T
    return out
```
